# Optimizing a Trainium2 kernel written in Bass

```python
import math
import jax, jax.numpy as jnp
from jax import lax
import numpy as np

D_MODEL = 1024
BATCH = 4
SEQ = 4096
DEPTH = 2
DEC_BATCH = 2
DEC_SEQ = 8192
PAST_LEN = 128

N_EVEN = (DEPTH + 1) // 2
N_ODD = DEPTH // 2
D_FF = 4 * D_MODEL
EPS = 1e-5
A_WIDTH = D_MODEL // 2
A_GROUPS = 4
A_GROUP_DIM = A_WIDTH // A_GROUPS
CHUNK = 128
B_HEADS = 4
B_QK_DIM = 64
B_V_DIM = 2 * B_QK_DIM
B_WIDTH = B_HEADS * B_V_DIM
Q_BLOCK = 128
ROT_DIM = B_QK_DIM // 4
ROPE_THETA = 500000.0
E_IN = 2 * A_WIDTH + 2 * B_HEADS * 2 * B_QK_DIM + B_WIDTH
E_MIX = A_WIDTH + B_WIDTH
CONV_W = 3

kernel_name = "hybrid_gmlp_diffattn_shortconv_encoder"


def _rmsnorm(x, g):
    x32 = x.astype(jnp.float32)
    y = x32 * lax.rsqrt(jnp.mean(x32 * x32, axis=-1, keepdims=True) + EPS)
    return y.astype(x.dtype) * g


def _lambda_init(layer_idx):
    return 0.8 - 0.6 * math.exp(-0.3 * layer_idx)


def _rope_tables(seq_len, dtype):
    inv = ROPE_THETA ** (-jnp.arange(0, ROT_DIM, 2, dtype=jnp.float32) / ROT_DIM)
    ang = jnp.arange(seq_len, dtype=jnp.float32)[:, None] * inv[None, :]
    return jnp.cos(ang).astype(dtype), jnp.sin(ang).astype(dtype)


def _rope(x, cos, sin):
    xr, xp = x[..., :ROT_DIM], x[..., ROT_DIM:]
    half = ROT_DIM // 2
    x1, x2 = xr[..., :half], xr[..., half:]
    c, s = cos[None, :, None, :], sin[None, :, None, :]
    return jnp.concatenate([x1 * c - x2 * s, x2 * c + x1 * s, xp], axis=-1)


def _gmlp_spatial(a_u, a_v, vnorm_g, w_s, b_s):
    bsz, s, _ = a_u.shape
    u = jax.nn.gelu(a_u)
    v = jax.nn.gelu(a_v).reshape(bsz, s, A_GROUPS, A_GROUP_DIM)
    v = _rmsnorm(v, vnorm_g.reshape(A_GROUPS, A_GROUP_DIM))
    v = v.reshape(bsz, s // CHUNK, CHUNK, A_GROUPS, A_GROUP_DIM)
    mixed = jnp.einsum('gpq,bcqge->bcpge', w_s, v) + b_s.T[None, None, :, :, None]
    return u * mixed.reshape(bsz, s, A_WIDTH)


def _diff_attention(q, k, v, lq1, lk1, lq2, lk2, subln_g, layer_idx):
    bsz, s, _ = q.shape
    q = q.reshape(bsz, s, B_HEADS, 2, B_QK_DIM)
    k = k.reshape(bsz, s, B_HEADS, 2, B_QK_DIM)
    v = v.reshape(bsz, s, B_HEADS, B_V_DIM)
    cos, sin = _rope_tables(s, q.dtype)
    q1, q2 = _rope(q[:, :, :, 0], cos, sin), _rope(q[:, :, :, 1], cos, sin)
    k1, k2 = _rope(k[:, :, :, 0], cos, sin), _rope(k[:, :, :, 1], cos, sin)
    lam_init = _lambda_init(layer_idx)
    lam = (jnp.exp(jnp.sum(lq1.astype(jnp.float32) * lk1.astype(jnp.float32)))
           - jnp.exp(jnp.sum(lq2.astype(jnp.float32) * lk2.astype(jnp.float32))) + lam_init)
    scale = 1.0 / math.sqrt(B_QK_DIM)
    nblk = s // Q_BLOCK

    def to_blocks(t):
        return t.reshape(bsz, nblk, Q_BLOCK, B_HEADS, B_QK_DIM).transpose(1, 0, 2, 3, 4)

    def block(args):
        q1b, q2b = args
        s1 = jnp.einsum('bqhd,bkhd->bhqk', q1b, k1).astype(jnp.float32) * scale
        s2 = jnp.einsum('bqhd,bkhd->bhqk', q2b, k2).astype(jnp.float32) * scale
        p = jax.nn.softmax(s1, axis=-1) - lam * jax.nn.softmax(s2, axis=-1)
        return jnp.einsum('bhqk,bkhe->bqhe', p.astype(v.dtype), v)

    o = lax.map(block, (to_blocks(q1), to_blocks(q2)))
    o = o.transpose(1, 0, 2, 3, 4).reshape(bsz, s, B_HEADS, B_V_DIM)
    o = _rmsnorm(o, subln_g) * (1.0 - lam_init)
    return o.reshape(bsz, s, B_WIDTH)


def _short_conv(xn, w_in, conv_w, w_out):
    z = xn @ w_in
    bg, cg, h = jnp.split(z, 3, axis=-1)
    t = cg * h
    tp = jnp.pad(t, ((0, 0), (1, 1), (0, 0)))
    y = tp[:, :-2] * conv_w[0] + tp[:, 1:-1] * conv_w[1] + tp[:, 2:] * conv_w[2]
    return (bg * y) @ w_out


def _ffn(xn, w1, w2):
    return jnp.square(jax.nn.relu(xn @ w1)) @ w2


def _trunk(x, norm_mix_g, norm_ffn_g, ffn_w1, ffn_w2, e_w_in, e_w_out, a_vnorm_g, a_w_s, a_b_s,
           b_lq1, b_lk1, b_lq2, b_lk2, b_subln_g, c_w_in, c_conv_w, c_w_out, final_g):
    h = x
    splits = [A_WIDTH, 2 * A_WIDTH, 2 * A_WIDTH + 2 * B_HEADS * B_QK_DIM,
              2 * A_WIDTH + 4 * B_HEADS * B_QK_DIM]
    for i in range(DEPTH):
        xn = _rmsnorm(h, norm_mix_g[i])
        if i % 2 == 0:
            e = i // 2
            z = xn @ e_w_in[e]
            a_u, a_v, q, k, v = jnp.split(z, splits, axis=-1)
            out_a = _gmlp_spatial(a_u, a_v, a_vnorm_g[e], a_w_s[e], a_b_s[e])
            out_b = _diff_attention(q, k, v, b_lq1[e], b_lk1[e], b_lq2[e], b_lk2[e], b_subln_g[e], i)
            h = h + jnp.concatenate([out_a, out_b], axis=-1) @ e_w_out[e]
        else:
            o = i // 2
            h = h + _short_conv(xn, c_w_in[o], c_conv_w[o], c_w_out[o])
        h = h + _ffn(_rmsnorm(h, norm_ffn_g[i]), ffn_w1[i], ffn_w2[i])
    return _rmsnorm(h, final_g)


def setup_inputs(seed: int = 0) -> dict:
    key = jax.random.key(seed)
    ks = jax.random.split(key, 24)
    f32 = jnp.float32
    n = lambda k, shape, s: jax.random.normal(k, shape, f32) * s
    return {
        "x_prompt": n(ks[0], (BATCH, SEQ, D_MODEL), 1.0),
        "x_sample": n(ks[1], (DEC_BATCH, DEC_SEQ, D_MODEL), 1.0),
        "norm_mix_g": 1.0 + n(ks[2], (DEPTH, D_MODEL), 0.02),
        "norm_ffn_g": 1.0 + n(ks[3], (DEPTH, D_MODEL), 0.02),
        "ffn_w1": n(ks[4], (DEPTH, D_MODEL, D_FF), D_MODEL ** -0.5),
        "ffn_w2": n(ks[5], (DEPTH, D_FF, D_MODEL), 0.5 * D_FF ** -0.5),
        "e_w_in": n(ks[6], (N_EVEN, D_MODEL, E_IN), D_MODEL ** -0.5),
        "e_w_out": n(ks[7], (N_EVEN, E_MIX, D_MODEL), E_MIX ** -0.5),
        "a_vnorm_g": 1.0 + n(ks[8], (N_EVEN, A_WIDTH), 0.02),
        "a_w_s": n(ks[9], (N_EVEN, A_GROUPS, CHUNK, CHUNK), CHUNK ** -0.5),
        "a_b_s": 1.0 + n(ks[10], (N_EVEN, A_GROUPS, CHUNK), 0.02),
        "b_lq1": n(ks[11], (N_EVEN, B_QK_DIM), 0.1),
        "b_lk1": n(ks[12], (N_EVEN, B_QK_DIM), 0.1),
        "b_lq2": n(ks[13], (N_EVEN, B_QK_DIM), 0.1),
        "b_lk2": n(ks[14], (N_EVEN, B_QK_DIM), 0.1),
        "b_subln_g": 1.0 + n(ks[15], (N_EVEN, B_V_DIM), 0.02),
        "c_w_in": n(ks[16], (N_ODD, D_MODEL, 3 * D_MODEL), D_MODEL ** -0.5),
        "c_conv_w": n(ks[17], (N_ODD, CONV_W, D_MODEL), CONV_W ** -0.5),
        "c_w_out": n(ks[18], (N_ODD, D_MODEL, D_MODEL), D_MODEL ** -0.5),
        "final_g": 1.0 + n(ks[19], (D_MODEL,), 0.02),
    }


def reference(x_prompt, x_sample, norm_mix_g, norm_ffn_g, ffn_w1, ffn_w2, e_w_in, e_w_out,
              a_vnorm_g, a_w_s, a_b_s, b_lq1, b_lk1, b_lq2, b_lk2, b_subln_g,
              c_w_in, c_conv_w, c_w_out, final_g):
    y_prompt = _trunk(x_prompt, norm_mix_g, norm_ffn_g, ffn_w1, ffn_w2, e_w_in, e_w_out, a_vnorm_g,
                      a_w_s, a_b_s, b_lq1, b_lk1, b_lq2, b_lk2, b_subln_g, c_w_in, c_conv_w,
                      c_w_out, final_g)
    y_sample = _trunk(x_sample, norm_mix_g, norm_ffn_g, ffn_w1, ffn_w2, e_w_in, e_w_out, a_vnorm_g,
                      a_w_s, a_b_s, b_lq1, b_lk1, b_lq2, b_lk2, b_subln_g, c_w_in, c_conv_w,
                      c_w_out, final_g)
    return (y_prompt, y_sample)
```

```python
import math
import os
import numpy as np
import concourse.bass as bass
import concourse.mybir as mybir
from concourse.bass_utils import run_bass_kernel_spmd

F32 = mybir.dt.float32
BF16 = mybir.dt.bfloat16
I32 = mybir.dt.int32
AF = mybir.ActivationFunctionType
ALU = mybir.AluOpType
AX = mybir.AxisListType

D = 1024
DFF = 4096
EPS = 1e-5
NCORES = 8
SEQ_S = 8192
SEQ_P = 4096
OWN = 2048
RNG = 1024
LAM_INIT = 0.8 - 0.6 * math.exp(0.0)
TWO_PI = 2.0 * math.pi


class Op:
    __slots__ = ("eng", "fn", "waits", "signal", "tick", "dma", "idx", "name", "rw")


def I(name, *args, **kw):
    return (name, args, kw)


def _mkfn(spec):
    if spec is None:
        return None
    if callable(spec):
        return spec
    if isinstance(spec, tuple):
        spec = [spec]

    def f(e, spec=spec):
        ins = None
        for (name, args, kw) in spec:
            ins = getattr(e, name)(*args, **kw)
        return ins
    return f


class Prog:
    ENGS = ("pe", "act", "dve", "pool", "sp")

    def __init__(self):
        self.ops = []
        self.last_w = {}
        self.readers = {}
        self.dma_count = {}
        self.total_groups = set()
        self.extra = {e: [] for e in self.ENGS}
        self.dma_last = {}

    def barrier(self):
        lasts = {}
        for op in reversed(self.ops):
            if op.dma is None and op.eng not in lasts and op.fn is not None:
                lasts[op.eng] = op
        deps = list(lasts.values()) + list(self.dma_last.values())
        for e in self.ENGS:
            self.extra[e] = [d for d in deps if not (d.dma is None and d.eng == e)]
        for d in lasts.values():
            d.signal = True
        self.last_w = {}
        self.readers = {}

    def add(self, eng, fn, reads=(), writes=(), dma=None, wait_total=False):
        op = Op()
        op.eng = eng
        op.fn = _mkfn(fn)
        try:
            op.name = fn[0] if isinstance(fn, tuple) else (fn[-1][0] + "x%d" % len(fn) if isinstance(fn, list) else str(fn))
        except Exception:
            op.name = "?"
        op.signal = False
        op.tick = None
        op.dma = dma
        op.idx = len(self.ops)
        bank_r = [k for k in reads if isinstance(k, tuple) and k and k[0] == "bank" and k not in writes]
        if bank_r:
            writes = list(writes) + bank_r
        deps = {}
        for k in reads:
            w = self.last_w.get(k)
            if w is not None:
                deps[w.idx] = (w, True)
        for k in writes:
            w = self.last_w.get(k)
            if w is not None and w.idx not in deps:
                deps[w.idx] = (w, False)
            for r in self.readers.get(k, ()):
                if r.idx not in deps:
                    deps[r.idx] = (r, False)
        waits = []
        for d, raw in deps.values():
            if d.dma is None and d.eng == eng:
                if eng == "pe" or not raw:
                    continue
            waits.append(d)
            if d.dma is None:
                d.signal = True
        if self.extra[eng]:
            waits = waits + self.extra[eng]
            self.extra[eng] = []
        op.waits = waits
        op.rw = (list(reads), list(writes))
        for k in reads:
            self.readers.setdefault(k, []).append(op)
        for k in writes:
            self.last_w[k] = op
            self.readers[k] = []
        if dma is not None:
            self.dma_count[dma] = self.dma_count.get(dma, 0) + 1
            op.tick = 16 * self.dma_count[dma]
            self.dma_last[dma] = op
            if wait_total:
                self.total_groups.add(dma)
        self.ops.append(op)
        return op

    def simulate(self):
        cnt = {e: 0 for e in self.ENGS}
        for op in self.ops:
            if op.dma is None and op.signal:
                cnt[op.eng] += 1
                op.tick = cnt[op.eng]
        per = {e: [op for op in self.ops if op.eng == e] for e in self.ENGS}
        pos = {e: 0 for e in self.ENGS}
        sem = {}
        progress = True
        while progress:
            progress = False
            for e in self.ENGS:
                while pos[e] < len(per[e]):
                    op = per[e][pos[e]]
                    ok = True
                    for d in op.waits:
                        if d.dma is not None:
                            key = ("d", d.dma)
                            val = 16 * self.dma_count[d.dma] if d.dma in self.total_groups else d.tick
                        else:
                            key = ("e", d.eng)
                            val = d.tick
                        if sem.get(key, 0) < val:
                            ok = False
                            break
                    if not ok:
                        break
                    if op.fn is not None:
                        if op.dma is not None:
                            sem[("d", op.dma)] = sem.get(("d", op.dma), 0) + 16
                        elif op.signal:
                            sem[("e", op.eng)] = sem.get(("e", op.eng), 0) + 1
                    pos[e] += 1
                    progress = True
        stuck = {e: (pos[e], len(per[e])) for e in self.ENGS if pos[e] < len(per[e])}
        return stuck, per, pos

    def emit(self, nc, stack):
        cnt = {e: 0 for e in self.ENGS}
        for op in self.ops:
            if op.dma is None and op.signal:
                cnt[op.eng] += 1
                op.tick = cnt[op.eng]
        sems = {e: stack.enter_context(nc.semaphore("s_" + e)) for e in self.ENGS}
        dsems = {}
        for i, g in enumerate(self.dma_count):
            dsems[g] = stack.enter_context(nc.semaphore("d%d" % i))
        per = {e: [] for e in self.ENGS}
        for op in self.ops:
            per[op.eng].append(op)
        block = stack.enter_context(nc.Block())

        def run(engh, ops):
            seen = {}
            for op in ops:
                need = {}
                for d in op.waits:
                    if d.dma is not None:
                        key = ("d", d.dma)
                        val = 16 * self.dma_count[d.dma] if d.dma in self.total_groups else d.tick
                    else:
                        key = ("e", d.eng)
                        val = d.tick
                    if val > need.get(key, 0):
                        need[key] = val
                for key, val in need.items():
                    if seen.get(key, 0) >= val:
                        continue
                    seen[key] = val
                    sem = dsems[key[1]] if key[0] == "d" else sems[key[1]]
                    engh.wait_ge(sem, val)
                if op.fn is None:
                    continue
                ins = op.fn(engh)
                if op.dma is not None:
                    ins.then_inc(dsems[op.dma], 16)
                elif op.signal:
                    ins.then_inc(sems[op.eng], 1)

        @block.tensor
        def _(e):
            run(e, per["pe"])

        @block.scalar
        def _(e):
            run(e, per["act"])

        @block.vector
        def _(e):
            run(e, per["dve"])

        @block.gpsimd
        def _(e):
            run(e, per["pool"])

        @block.sync
        def _(e):
            run(e, per["sp"])


class Arena:
    def __init__(self, ap, nwords):
        self.ap = ap
        self.n = nwords
        self.off = 0

    def alloc(self, free_shape, dtype):
        nel = 1
        for s in free_shape:
            nel *= s
        bpe = 2 if dtype == BF16 else 4
        words = (nel * bpe + 3) // 4
        words = (words + 7) // 8 * 8
        assert self.off + words <= self.n, ("arena overflow", self.off, words, self.n)
        v = self.ap[:, self.off:self.off + words]
        self.off += words
        if dtype != F32:
            v = v.bitcast(dtype)
        v = v[:, 0:nel]
        if len(free_shape) == 2:
            v = v.rearrange("p (a b) -> p a b", a=free_shape[0])
        elif len(free_shape) == 3:
            v = v.rearrange("p (a b c) -> p a b c", a=free_shape[0], b=free_shape[1])
        return v


def build_program():
    from contextlib import ExitStack
    nc = bass.Bass("TRN2", target_bir_lowering=False)
    stack = ExitStack()
    P = Prog()

    def din(name, shape):
        return nc.dram_tensor(name, list(shape), F32, kind="ExternalInput").ap()

    xseq = {"s": din("xs", (SEQ_S, D)), "p": din("xp", (SEQ_P, D))}
    xtail = {"s": din("xts", (4, D)), "p": din("xtp", (4, D))}
    posd = {"s": din("poss", (1, SEQ_S + 4)), "p": din("posp", (1, SEQ_P + 4))}
    maskd = din("masks", (1, 8))
    ewin = din("e_w_in", (D, 2560)).rearrange("(c p) n -> p c n", p=128)
    ewout = din("e_w_out", (D, D)).rearrange("(c p) n -> p c n", p=128)
    w1d = [din("ffn_w1_%d" % l, (D, DFF)).rearrange("(c p) n -> p c n", p=128) for l in range(2)]
    w2d = [din("ffn_w2_%d" % l, (DFF, D)).rearrange("(f p) n -> p f n", p=128) for l in range(2)]
    cwin = din("c_w_in", (D, 3 * D)).rearrange("(c p) n -> p c n", p=128)
    cwout = din("c_w_out", (D, D)).rearrange("(c p) n -> p c n", p=128)
    wstd = din("wsT", (4, 128, 128)).rearrange("g q p -> q g p")
    identd = din("ident", (128, 128))
    rtd = din("rt", (128, 128))
    gcolsd = din("gcols", (128, 32))
    cwd = din("cw", (128, 24))
    rcd = din("rc", (128, 2))
    lvecd = din("lvec", (1, 256))
    vngd = din("vng", (1, 512))
    bsd = din("bs", (1, 512))
    sublnd = din("subln", (1, 128))
    fgd = din("fg", (1, D))
    xnd = {"s": nc.dram_tensor("xnd_s", [SEQ_S // 512, 128, 8 * 512], BF16, kind="Internal").ap(),
           "p": nc.dram_tensor("xnd_p", [SEQ_P // 512, 128, 8 * 512], BF16, kind="Internal").ap()}
    tabd = {"s": nc.dram_tensor("tabd_s", [SEQ_S // 512, 128, 2 * 512], F32, kind="Internal").ap(),
            "p": nc.dram_tensor("tabd_p", [SEQ_P // 512, 128, 2 * 512], F32, kind="Internal").ap()}
    yout = {"s": nc.dram_tensor("ys", [OWN, D], F32, kind="ExternalOutput").ap(),
            "p": nc.dram_tensor("yp", [OWN, D], F32, kind="ExternalOutput").ap()}

    ARENA_W = 43520
    PERS_W = 9216
    arena_t = stack.enter_context(nc.sbuf_tensor("arena", [128, ARENA_W], F32))
    pers_t = stack.enter_context(nc.sbuf_tensor("pers", [128, PERS_W], F32))
    psum = stack.enter_context(nc.psum_tensor("ps", [128, 8, 512], F32))
    pers = Arena(pers_t[:], PERS_W)
    arena_ap = arena_t[:]

    out_bT = pers.alloc((4, OWN + 4), BF16)
    ident = pers.alloc((128,), BF16)
    rt = pers.alloc((128,), BF16)
    wsT = pers.alloc((4, 128), BF16)
    gcols = pers.alloc((4, 8), F32)
    cw = pers.alloc((3, 8), F32)
    rc = pers.alloc((2,), F32)
    lvec = pers.alloc((256,), F32)
    ltmp = pers.alloc((64,), F32)
    lsc = pers.alloc((8,), F32)
    gBv = pers.alloc((512,), F32)
    bsB = pers.alloc((4, 128), F32)
    sublnB = pers.alloc((128,), F32)
    fgB = pers.alloc((D,), F32)
    maskB = pers.alloc((8,), F32)
    NSTAT = 16
    stat = pers.alloc((NSTAT, 12), F32)
    vstash = pers.alloc((4, 512), BF16)
    junk = pers.alloc((D,), BF16)
    epsb = pers.alloc((1,), F32)
    halfpi = pers.alloc((1,), F32)
    cst = pers.alloc((8,), F32)
    statC = pers.alloc((3, 12), F32)

    CONST = "const"
    ALLC = []

    def cload(eng, dst, src, name):
        P.add(eng, I("dma_start", out=dst, in_=src), writes=[("c", name)], dma=CONST + "_" + eng, wait_total=True)
        ALLC.append(("c", name))

    cload("pool", ident, identd[:, :], "ident")
    cload("pool", rt, rtd[:, :], "rt")
    cload("pool", wsT, wstd, "wsT")
    cload("sp", gcols, gcolsd.rearrange("p (a b) -> p a b", a=4), "gcols")
    cload("sp", cw, cwd.rearrange("p (a b) -> p a b", a=3), "cw")
    cload("sp", rc, rcd[:, :], "rc")
    cload("sp", lvec, lvecd.partition_broadcast(128), "lvec")
    cload("sp", gBv, vngd.partition_broadcast(128), "gBv")
    cload("sp", bsB, bsd.partition_broadcast(128).rearrange("p o (a b) -> p (o a) b", a=4), "bsB")
    cload("sp", sublnB, sublnd.partition_broadcast(128), "sublnB")
    cload("sp", fgB, fgd.partition_broadcast(128), "fgB")
    cload("sp", maskB, maskd.partition_broadcast(128), "maskB")

    P.add("dve", I("memset", epsb, EPS), writes=["epsb"])
    P.add("dve", I("memset", halfpi, math.pi / 2), writes=["halfpi"])
    P.add("dve", I("memset", cst[:, 0:1], 0.5), writes=["cst0"])
    P.add("dve", I("memset", statC, 1.0), writes=["statC"])
    P.add("dve", I("memset", cst[:, 1:2], 0.0), writes=["cst1"])
    P.add("dve", I("memset", cst[:, 2:3], -0.5), writes=["cst2"])
    P.add("dve", I("memset", cst[:, 3:4], D * EPS), writes=["cst3"])
    P.add("dve", I("memset", cst[:, 4:5], 128 * EPS), writes=["cst4"])
    for j in range(2):
        P.add("dve", I("tensor_tensor", out=ltmp, in0=lvec[:, 128 * j:128 * j + 64],
                       in1=lvec[:, 128 * j + 64:128 * j + 128], op=ALU.mult), reads=ALLC, writes=["ltmp"])
        P.add("dve", I("tensor_reduce", out=lsc[:, j:j + 1], in_=ltmp, axis=AX.X, op=ALU.add),
              reads=["ltmp"], writes=[("lsc", j)])
    P.add("act", I("activation", out=lsc[:, 2:4], in_=lsc[:, 0:2], func=AF.Exp),
          reads=[("lsc", 0), ("lsc", 1)], writes=["lsce"])
    P.add("dve", I("tensor_tensor", out=lsc[:, 4:5], in0=lsc[:, 2:3], in1=lsc[:, 3:4], op=ALU.subtract),
          reads=["lsce"], writes=["lam0"])
    P.add("dve", I("tensor_scalar", out=lsc[:, 5:6], in0=lsc[:, 4:5], scalar1=LAM_INIT, scalar2=-1.0,
                   op0=ALU.add, op1=ALU.mult), reads=["lam0"], writes=["neglam"])
    P.add("dve", I("tensor_scalar", out=sublnB, in0=sublnB, scalar1=(1.0 - LAM_INIT) * math.sqrt(128.0), scalar2=None,
                   op0=ALU.mult), reads=ALLC, writes=["sublnS"])
    P.add("dve", I("tensor_scalar", out=gcols, in0=gcols, scalar1=math.sqrt(float(D)), scalar2=None, op0=ALU.mult),
          reads=ALLC, writes=["gcolsS"])
    P.add("dve", I("tensor_scalar", out=fgB, in0=fgB, scalar1=math.sqrt(float(D)), scalar2=None, op0=ALU.mult),
          reads=ALLC, writes=["fgBS"])
    P.add("dve", I("tensor_scalar", out=gBv, in0=gBv, scalar1=math.sqrt(128.0), scalar2=None, op0=ALU.mult),
          reads=ALLC, writes=["gBvS"])
    P.barrier()

    def pool_rstd(src, dst, r, n, ccol, reads, wkey):
        P.add("pool", I("tensor_tensor", out=dst, in0=src, in1=cst[:r, ccol:ccol + 1].broadcast_to([r, n]), op=ALU.add),
              reads=reads, writes=[(wkey, "a")])
        P.add("pool", I("tensor_tensor", out=dst, in0=dst, in1=cst[:r, 2:3].broadcast_to([r, n]), op=ALU.pow),
              reads=[(wkey, "a")], writes=[wkey])

    state = {"stat": 0, "pt": 0, "ps": 0}

    def new_stat():
        s = state["stat"]
        state["stat"] = (s + 1) % NSTAT
        return s

    def bank(i):
        return psum[:, i, :]

    PS_RING = [0, 1, 2, 3, 6, 7]

    def next_ps():
        s = state["ps"]
        state["ps"] = (s + 1) % len(PS_RING)
        return PS_RING[s]

    def next_pt():
        s = state["pt"]
        state["pt"] = (s + 1) % 2
        return 4 + s

    def rstd_chain(src, r, srckey, scale):
        s = new_stat()
        sk = ("stat", s)
        n = src.shape[-1]
        P.add("act", I("activation", out=junk[:r, 0:n], in_=src, func=AF.Square, accum_out=stat[:r, s, 0:1]),
              reads=[srckey], writes=[(sk, 0)])
        pool_rstd(stat[:r, s, 0:1], stat[:r, s, 2:3], r, 1, 3 if n == D else 4, [(sk, 0)], (sk, 2))
        return stat[:r, s, 2:3], (sk, 2)

    def norm_to_xnT(src, r, srckey, xs_buf, xs_key, layer, dst, dst_key):
        rs, rsk = rstd_chain(src, r, srckey, 1.0 / D)
        P.add("act", I("activation", out=xs_buf[:r, :], in_=src, func=AF.Identity, scale=rs),
              reads=[srckey, rsk], writes=[xs_key])
        b = next_pt()
        ptv = bank(b).bitcast(BF16).rearrange("p (c n) -> p c n", c=8)
        P.add("pe", [I("transpose", out=ptv[:, c, 0:r], in_=xs_buf[:r, c * 128:(c + 1) * 128], identity=ident[:r, :r])
                     for c in range(8)], reads=[xs_key], writes=[("bank", b)])
        P.add("dve", I("tensor_tensor", out=dst, in0=ptv[:, :, 0:r],
                       in1=gcols[:, layer, :].unsqueeze(2).broadcast_to([128, 8, r]), op=ALU.mult),
              reads=[("bank", b)], writes=[dst_key])

    def mm(out_ap, lhs, rhs):
        nk = len(lhs)
        return [I("matmul", out_ap, lhsT=lhs[c], rhs=rhs[c], start=(c == 0), stop=(c == nk - 1)) for c in range(nk)]

    NW = 4
    WSLOT = 4096

    class WRing:
        def __init__(self, ar):
            self.slots = [ar.alloc((WSLOT,), BF16) for _ in range(NW)]
            self.i = 0

        def load(self, pieces):
            s = self.i % NW
            self.i += 1
            sl = self.slots[s]
            key = ("w", s)
            for dv, src in pieces:
                P.add("pool", I("dma_start", out=dv(sl), in_=src), writes=[key], dma=("w", s))
            return sl, key

    def v8(sl):
        return sl.rearrange("p (c n) -> p c n", c=8)

    def v4(sl):
        return sl.rearrange("p (c n) -> p c n", c=4)

    def v3(sl):
        return sl[:, 0:3072].rearrange("p (j c n) -> p j c n", j=3, c=8)

    def rope_tables(posB, n, poskey, Ct, St, A1, A2, tkey):
        A2i = A2.bitcast(I32)
        TE = os.environ.get("KTE", "pool")
        bc = lambda ap: ap.broadcast_to([128, n])
        P.add(TE, I("tensor_tensor", out=A1[:, 0:n], in0=posB[:, 0:n], in1=bc(rc[:, 0:1]), op=ALU.mult),
              reads=[poskey], writes=["tabA1"])
        P.add(TE, I("tensor_copy", out=A2i[:, 0:n], in_=A1[:, 0:n]), reads=["tabA1"], writes=["tabA2"])
        P.add(TE, I("tensor_tensor", out=A1[:, 0:n], in0=A1[:, 0:n], in1=A2i[:, 0:n], op=ALU.subtract),
              reads=["tabA1", "tabA2"], writes=["tabA1"])
        P.add("dve", I("scalar_tensor_tensor", out=A2[:, 0:n], in0=A1[:, 0:n], scalar=0.5, in1=A1[:, 0:n],
                       op0=ALU.is_gt, op1=ALU.subtract), reads=["tabA1"], writes=["tabA2"])
        P.add("act", I("activation", out=St[:, 0:n], in_=A2[:, 0:n], func=AF.Sin, scale=rc[:, 1:2]),
              reads=["tabA2"], writes=[(tkey, "S")])
        P.add("dve", I("scalar_tensor_tensor", out=A1[:, 0:n], in0=A2[:, 0:n], scalar=-1.0, in1=A2[:, 0:n],
                       op0=ALU.mult, op1=ALU.min), reads=["tabA2"], writes=["tabA1"])
        P.add("act", I("activation", out=Ct[:, 0:n], in_=A1[:, 0:n], func=AF.Sin, scale=TWO_PI, bias=halfpi[:, 0:1]),
              reads=["tabA1"], writes=[(tkey, "C")])

    def gmlp_v1(ps_ap, pskey, r, gv, sq, kx=0):
        P.add("act", I("activation", out=gv[:r, :], in_=ps_ap, func=AF.Gelu_apprx_tanh), reads=[pskey],
              writes=[("gv", kx)])
        P.add("dve", I("tensor_tensor", out=sq[:r, :], in0=gv[:r, :], in1=gv[:r, :], op=ALU.mult),
              reads=[("gv", kx)], writes=[("sq", kx)])
        s = new_stat()
        sk = ("stat", s)
        P.add("dve", I("tensor_reduce", out=stat[:r, s, 0:4], in_=sq[:r, :].rearrange("p (g n) -> p g n", g=4),
                       axis=AX.X, op=ALU.add), reads=[("sq", kx)], writes=[(sk, 0)])
        return s

    def gmlp_v2(s, r, gv, vout, vkey, kx=0):
        sk = ("stat", s)
        pool_rstd(stat[:r, s, 0:4], stat[:r, s, 8:12], r, 4, 4, [(sk, 0)], (sk, 2))
        for g in range(4):
            P.add("dve", I("scalar_tensor_tensor", out=vout[:r, g * 128:(g + 1) * 128], in0=gv[:r, g * 128:(g + 1) * 128],
                           scalar=stat[:r, s, 8 + g:9 + g], in1=gBv[:r, g * 128:(g + 1) * 128], op0=ALU.mult, op1=ALU.mult),
                  reads=[("gv", kx), (sk, 2)], writes=[vkey])

    def gmlp_v(ps_ap, pskey, r, gv, sq, vout, vkey, kx=0):
        s = gmlp_v1(ps_ap, pskey, r, gv, sq, kx)
        gmlp_v2(s, r, gv, vout, vkey, kx)

    def phase_AB(sq_name):
        S = SEQ_S if sq_name == "s" else SEQ_P
        NB = S // 512
        NKB = S // 128
        xd = xseq[sq_name]
        ar = Arena(arena_ap, ARENA_W)
        Kt = ar.alloc((S,), BF16)
        Vaug = ar.alloc((NKB, 130), BF16)
        Qt = ar.alloc((OWN + 4,), BF16)
        NXR = 8
        NXS = 4
        xring = [ar.alloc((D,), F32) for _ in range(NXR)]
        xsb = [ar.alloc((D,), BF16) for _ in range(NXS)]
        xnA = [ar.alloc((8, 512), BF16) for _ in range(2)]
        xnTl = ar.alloc((8, 4), BF16)
        kraw = [ar.alloc((512,), BF16) for _ in range(2)]
        t1 = ar.alloc((512,), F32)
        t2 = ar.alloc((512,), F32)
        Ct = [ar.alloc((512,), F32) for _ in range(2)]
        St = [ar.alloc((512,), F32) for _ in range(2)]
        A1 = ar.alloc((512,), F32)
        A2 = ar.alloc((512,), F32)
        posB = [ar.alloc((512,), F32) for _ in range(2)]
        Pt = [ar.alloc((2, 512), BF16) for _ in range(3)]
        otmp = ar.alloc((128,), F32)
        accS = ar.alloc((3, 388), F32)
        obS = ar.alloc((4, 128), F32)
        obn4 = ar.alloc((4, 128), BF16)
        gv = ar.alloc((512,), F32)
        sq = ar.alloc((512,), F32)
        ring = WRing(ar)
        tail_chunks = [NKB - 1, 8, 7, 16]
        cnt = {"x": 0, "xs": 0, "kraw": 0, "pt": 0}
        accs = [("bank", 4), ("bank", 5), ("bank", 6)]

        def acc(j, qt):
            i = j * 4 + qt
            return psum[:, 4 + i // 3, (i % 3) * 129:(i % 3) * 129 + 129]

        import os
        for hd in range(int(os.environ.get("KHEADS", "4"))):
            wsl, wkey = ring.load([(lambda sl, j=j: v3(sl)[:, j],
                                    ewin[:, :, 1024 + 512 * j + hd * 128:1024 + 512 * j + (hd + 1) * 128]) for j in range(3)])
            W3 = v3(wsl)
            if hd == 0:
                wav_sl, wavkey = ring.load([(v8, ewin[:, :, 512:1024])])
                Wav = v8(wav_sl)
            P.add("dve", I("memset", Vaug[:, :, 128:129], 1.0), writes=["Vaug1"])

            def rope_proj(j, rhs, n, ctab, stab, tabkey, dst, dstkey, xkeys, part=6):
                b1 = next_ps()
                P.add("pe", mm(bank(b1)[:, 0:n], [W3[:, j, c, :] for c in range(8)], rhs),
                      reads=xkeys + [wkey], writes=[("bank", b1)])
                if part < 2:
                    return
                ks = cnt["kraw"] % 2
                cnt["kraw"] += 1
                P.add("act", I("activation", out=kraw[ks][:, 0:n], in_=bank(b1)[:, 0:n], func=AF.Copy),
                      reads=[("bank", b1)], writes=[("kraw", ks)])
                if part < 3:
                    return
                b2 = next_ps()
                P.add("pe", I("matmul", bank(b2)[:, 0:n], lhsT=rt, rhs=kraw[ks][:, 0:n], start=True, stop=True),
                      reads=[("kraw", ks)], writes=[("bank", b2)])
                if part < 4:
                    return
                P.add("dve", I("tensor_tensor", out=t1[:, 0:n], in0=bank(b1)[:, 0:n], in1=ctab[:, 0:n], op=ALU.mult),
                      reads=[("bank", b1), (tabkey, "C")], writes=["t1"])
                if part < 5:
                    return
                P.add("dve", I("tensor_tensor", out=t2[:, 0:n], in0=bank(b2)[:, 0:n], in1=stab[:, 0:n], op=ALU.mult),
                      reads=[("bank", b2), (tabkey, "S")], writes=["t2"])
                if part < 6:
                    return
                P.add("pool", I("tensor_tensor", out=dst, in0=t1[:, 0:n], in1=t2[:, 0:n], op=ALU.add),
                      reads=["t1", "t2"], writes=[dstkey])

            def rope_a(j, rhs, n, xkeys):
                b1 = next_ps()
                P.add("pe", mm(bank(b1)[:, 0:n], [W3[:, j, c, :] for c in range(8)], rhs),
                      reads=xkeys + [wkey], writes=[("bank", b1)])
                ks = cnt["kraw"] % 2
                cnt["kraw"] += 1
                P.add("act", I("activation", out=kraw[ks][:, 0:n], in_=bank(b1)[:, 0:n], func=AF.Copy),
                      reads=[("bank", b1)], writes=[("kraw", ks)])
                return (b1, ks)

            def rope_b(h_, n, ctab, stab, tabkey, dst, dstkey):
                b1, ks = h_
                b2 = next_ps()
                P.add("pe", I("matmul", bank(b2)[:, 0:n], lhsT=rt, rhs=kraw[ks][:, 0:n], start=True, stop=True),
                      reads=[("kraw", ks)], writes=[("bank", b2)])
                P.add("dve", I("tensor_tensor", out=t1[:, 0:n], in0=bank(b1)[:, 0:n], in1=ctab[:, 0:n], op=ALU.mult),
                      reads=[("bank", b1), (tabkey, "C")], writes=["t1"])
                P.add("dve", I("tensor_tensor", out=t2[:, 0:n], in0=bank(b2)[:, 0:n], in1=stab[:, 0:n], op=ALU.mult),
                      reads=[("bank", b2), (tabkey, "S")], writes=["t2"])
                P.add("pool", I("tensor_tensor", out=dst, in0=t1[:, 0:n], in1=t2[:, 0:n], op=ALU.add),
                      reads=["t1", "t2"], writes=[dstkey])

            xsl = cnt["x"] % NXR
            cnt["x"] += 1
            xt = xring[xsl]
            xk = ("xr", xsl)
            P.add("sp", I("dma_start", out=xt[0:4, :], in_=xtail[sq_name][:, :]), writes=[xk], dma=xk)
            xss = cnt["xs"] % NXS
            cnt["xs"] += 1
            norm_to_xnT(xt[0:4, :], 4, xk, xsb[xss], ("xsb", xss), 0, xnTl, "xnTl")
            P.add("sp", I("dma_start", out=posB[0][:, 0:4], in_=posd[sq_name][:, S:S + 4].partition_broadcast(128)),
                  writes=[("posB", 0)], dma=("posB", 0))
            rope_tables(posB[0], 4, ("posB", 0), Ct[0], St[0], A1, A2, ("tab", 0))
            rope_proj(0, [xnTl[:, c, :] for c in range(8)], 4, Ct[0], St[0], ("tab", 0), Qt[:, OWN:OWN + 4],
                      ("Qt", 4), ["xnTl"])

            if os.environ.get("KSTOP", "9") == "1":
                return
            NBr = int(os.environ.get("KNB", NB))
            blk = {}

            def par_of(b):
                return (b + 1) % 2

            def stage_load(b):
                par = par_of(b)
                P.add("sp", I("dma_start", out=posB[par],
                              in_=posd[sq_name][:, b * 512:(b + 1) * 512].partition_broadcast(128)),
                      writes=[("posB", par)], dma=("posB", par))
                xs_ = []
                for t in range(4):
                    xsl = (b % 2) * 4 + t
                    row = b * 512 + t * 128
                    P.add("sp", I("dma_start", out=xring[xsl], in_=xd[row:row + 128, :]), writes=[("xr", xsl)],
                          dma=("xr", xsl))
                    xs_.append(xsl)
                blk[b] = {"x": xs_}

            def stage_stats(b):
                s_ = new_stat()
                sk = ("stat", s_)
                for t in range(4):
                    xsl = blk[b]["x"][t]
                    P.add("act", I("activation", out=junk[:, :], in_=xring[xsl], func=AF.Square,
                                   accum_out=stat[:, s_, t:t + 1]), reads=[("xr", xsl)], writes=[(sk, 0, t)])
                pool_rstd(stat[:, s_, 0:4], stat[:, s_, 8:12], 128, 4, 3, [(sk, 0, t) for t in range(4)], (sk, 2))
                blk[b]["stat"] = s_

            def stage_mid(b):
                par = par_of(b)
                s_ = blk[b]["stat"]
                sk = ("stat", s_)
                for t in range(4):
                    xsl = blk[b]["x"][t]
                    rs_ap = stat[:, s_, 8 + t:9 + t]
                    if t == 2:
                        P.add("act", I("activation", out=xsb[t], in_=xring[xsl], func=AF.Identity, scale=rs_ap),
                              reads=[("xr", xsl), (sk, 2)], writes=[("xsb", t)])
                    elif t == 3:
                        P.add("pool", I("tensor_tensor", out=xsb[t], in0=xring[xsl], in1=rs_ap.broadcast_to([128, D]),
                                        op=ALU.mult), reads=[("xr", xsl), (sk, 2)], writes=[("xsb", t)])
                    else:
                        P.add("dve", I("tensor_scalar", out=xsb[t], in0=xring[xsl], scalar1=rs_ap,
                                       scalar2=None, op0=ALU.mult), reads=[("xr", xsl), (sk, 2)], writes=[("xsb", t)])
                for t in range(4):
                    bnk = next_pt()
                    ptv = bank(bnk).bitcast(BF16).rearrange("p (c n) -> p c n", c=8)
                    P.add("pe", [I("transpose", out=ptv[:, c, :], in_=xsb[t][:, c * 128:(c + 1) * 128], identity=ident)
                                 for c in range(8)], reads=[("xsb", t)], writes=[("bank", bnk)])
                    P.add("dve", I("tensor_tensor", out=xnA[par][:, :, t * 128:(t + 1) * 128], in0=ptv,
                                   in1=gcols[:, 0, :].unsqueeze(2).broadcast_to([128, 8, 128]), op=ALU.mult),
                          reads=[("bank", bnk)], writes=[("xnA", par, t)])

            def stage_tables(b):
                par = par_of(b)
                rope_tables(posB[par], 512, ("posB", par), Ct[par], St[par], A1, A2, ("tab", par))

            cache = not os.environ.get("KNOCACHE")

            def store_tab(b):
                par = par_of(b)
                for (p0, p1) in ((0, 16), (64, 80)):
                    P.add("sp", I("dma_start", out=tabd[sq_name][b, p0:p1, 0:512], in_=Ct[par][p0:p1, :]),
                          reads=[(("tab", par), "C")], writes=[("tabd", b, 0)], dma=("tst", 0))
                    P.add("sp", I("dma_start", out=tabd[sq_name][b, p0:p1, 512:1024], in_=St[par][p0:p1, :]),
                          reads=[(("tab", par), "S")], writes=[("tabd", b, 1)], dma=("tst", 1))

            def store_x(b):
                par = par_of(b)
                P.add("sp", I("dma_start", out=xnd[sq_name][b], in_=xnA[par].rearrange("p c n -> p (c n)")),
                      reads=[("xnA", par, t) for t in range(4)], writes=[("xnd", b)], dma=("xst", par))

            def stage_reload(b):
                par = par_of(b)
                P.add("sp", I("dma_start", out=xnA[par].rearrange("p c n -> p (c n)"), in_=xnd[sq_name][b]),
                      reads=[("xnd", b)], writes=[("xnA", par, t) for t in range(4)], dma=("xrl", par))
                for (p0, p1) in ((0, 16), (64, 80)):
                    P.add("sp", I("dma_start", out=Ct[par][p0:p1, :], in_=tabd[sq_name][b, p0:p1, 0:512]),
                          reads=[("tabd", b, 0)], writes=[(("tab", par), "C")], dma=("trl", par, 0))
                    P.add("sp", I("dma_start", out=St[par][p0:p1, :], in_=tabd[sq_name][b, p0:p1, 512:1024]),
                          reads=[("tabd", b, 1)], writes=[(("tab", par), "S")], dma=("trl", par, 1))

            first_pass = (hd == 0) or not cache
            if first_pass:
                stage_load(0)
                if NBr > 1:
                    stage_load(1)
                stage_stats(0)
                stage_tables(0)
                if cache:
                    store_tab(0)
            else:
                stage_reload(0)
            for b in range(NBr):
                par = par_of(b)
                if first_pass:
                    stage_mid(b)
                    if b + 2 < NBr:
                        stage_load(b + 2)
                    if b + 1 < NBr:
                        stage_stats(b + 1)
                        stage_tables(b + 1)
                    if cache:
                        store_x(b)
                        if b + 1 < NBr:
                            store_tab(b + 1)
                elif b + 1 < NBr:
                    stage_reload(b + 1)
                xkeys = [("xnA", par, t) for t in range(4)]
                xr = [xnA[par][:, c, :] for c in range(8)]
                ka = rope_a(1, xr, 512, xkeys)
                qa = rope_a(0, xr, 512, xkeys) if b < 4 else None
                bv = next_ps()
                vm = []
                for t in range(4):
                    vm += mm(bank(bv)[:, t * 128:(t + 1) * 128], [xnA[par][:, c, t * 128:(t + 1) * 128] for c in range(8)],
                             [W3[:, 2, c, :] for c in range(8)])
                P.add("pe", vm, reads=xkeys + [wkey], writes=[("bank", bv)])
                for t in range(4):
                    P.add("dve", I("tensor_copy", out=Vaug[:, b * 4 + t, 0:128], in_=bank(bv)[:, t * 128:(t + 1) * 128]),
                          reads=[("bank", bv), "Vaug1"], writes=[("Vaug", b, t)])
                rope_b(ka, 512, Ct[par], St[par], ("tab", par), Kt[:, b * 512:(b + 1) * 512], ("Kt", b))
                if qa is not None:
                    rope_b(qa, 512, Ct[par], St[par], ("tab", par), Qt[:, b * 512:(b + 1) * 512], ("Qt", b))
                if hd == 0 and not os.environ.get("KNOAV"):
                    for ti, ch in enumerate(tail_chunks):
                        if ch // 4 == b:
                            t = ch % 4
                            ba = next_ps()
                            P.add("pe", mm(bank(ba), [xnA[par][:, c, t * 128:(t + 1) * 128] for c in range(8)],
                                           [Wav[:, c, :] for c in range(8)]),
                                  reads=xkeys + [wavkey], writes=[("bank", ba)])
                            gmlp_v(bank(ba), ("bank", ba), 128, gv, sq, vstash[:, ti, :], ("vstash", ti))

            if os.environ.get("KSTOP", "9") == "2":
                return
            qblocks = [(0, 512), (512, 512), (1024, 512), (1536, 512), (OWN, 4)]
            allK = [("Kt", b) for b in range(NB)]
            allV = [("Vaug", b, t) for b in range(NB) for t in range(4)]
            allQ = [("Qt", b) for b in range(5)]
            def accv(j, qt):
                i = j * 4 + qt
                return accS[:, i // 3, (i % 3) * 129:(i % 3) * 129 + 129]

            def make_fin(q0, nq, nqt, r):
                def fin():
                    akeys = [("accS", bi) for bi in range(3)]
                    s_ = new_stat()
                    sk = ("stat", s_)
                    for qt in range(nqt):
                        P.add("dve", I("reciprocal", out=stat[:r, s_, 0:1], in_=accv(0, qt)[0:r, 128:129]),
                              reads=akeys, writes=[(sk, 0)])
                        P.add("dve", I("reciprocal", out=stat[:r, s_, 1:2], in_=accv(1, qt)[0:r, 128:129]),
                              reads=akeys, writes=[(sk, 1)])
                        P.add("dve", I("tensor_tensor", out=stat[:r, s_, 2:3], in0=stat[:r, s_, 1:2], in1=lsc[:r, 5:6],
                                       op=ALU.mult), reads=[(sk, 1)], writes=[(sk, 2)])
                        P.add("dve", I("tensor_scalar", out=otmp[:r, :], in0=accv(0, qt)[0:r, 0:128],
                                       scalar1=stat[:r, s_, 0:1], scalar2=None, op0=ALU.mult),
                              reads=akeys + [(sk, 0)], writes=["otmp"])
                        P.add("dve", I("scalar_tensor_tensor", out=obS[:r, qt, :], in0=accv(1, qt)[0:r, 0:128],
                                       scalar=stat[:r, s_, 2:3], in1=otmp[:r, :], op0=ALU.mult, op1=ALU.add),
                              reads=akeys + [(sk, 2), "otmp"], writes=[("obS", qt)])
                        P.add("dve", I("tensor_tensor", out=otmp[:r, :], in0=obS[:r, qt, :], in1=obS[:r, qt, :], op=ALU.mult),
                              reads=[("obS", qt)], writes=["otmp"])
                        P.add("dve", I("tensor_reduce", out=stat[:r, s_, 4 + qt:5 + qt], in_=otmp[:r, :], axis=AX.X, op=ALU.add),
                              reads=["otmp"], writes=[(sk, 4, qt)])
                    return (s_, sk)

                def fin2(s_, sk):
                    pool_rstd(stat[:r, s_, 4:4 + nqt], stat[:r, s_, 8:8 + nqt], r, nqt, 4,
                              [(sk, 4, qt) for qt in range(nqt)], (sk, 9))
                    ptb = bank(7).bitcast(BF16)
                    for qt in range(nqt):
                        P.add("dve", I("scalar_tensor_tensor", out=obn4[:r, qt, :], in0=obS[:r, qt, :],
                                       scalar=stat[:r, s_, 8 + qt:9 + qt], in1=sublnB[:r, :], op0=ALU.mult, op1=ALU.mult),
                              reads=[("obS", qt), (sk, 9)], writes=[("obn", qt)])
                    P.add("pe", [I("transpose", out=ptb[:, qt * 128:qt * 128 + r], in_=obn4[:r, qt, :], identity=ident[:r, :r])
                                 for qt in range(nqt)], reads=[("obn", qt) for qt in range(nqt)], writes=[("bank", 7)])
                    P.add("dve", I("tensor_copy", out=out_bT[:, hd, q0:q0 + nq], in_=ptb[:, 0:nq]),
                          reads=[("bank", 7)], writes=[("obT", hd, q0)])
                return fin, fin2

            pending_fin = []
            for (q0, nq) in qblocks:
                nqt = (nq + 127) // 128
                r = min(128, nq)
                pend = None
                for kb in range(NKB + 1):
                    if kb == 14 and pending_fin:
                        f2_, args_ = pending_fin.pop(0)
                        f2_(*args_)
                    cur = None
                    if kb < NKB:
                        sb = (kb % 2) * 2
                        P.add("pe", [I("matmul", psum[:, sb, 0:nq], lhsT=Kt[0:64, kb * 128:(kb + 1) * 128],
                                       rhs=Qt[0:64, q0:q0 + nq], start=True, stop=True),
                                     I("matmul", psum[:, sb + 1, 0:nq], lhsT=Kt[64:128, kb * 128:(kb + 1) * 128],
                                       rhs=Qt[64:128, q0:q0 + nq], start=True, stop=True)],
                              reads=allK + allQ, writes=[("bank", sb), ("bank", sb + 1)])
                        psl = cnt["pt"] % 3
                        cnt["pt"] += 1
                        P.add("act", I("activation", out=Pt[psl][:, :, 0:nq], in_=psum[:, sb:sb + 2, 0:nq], func=AF.Exp,
                                       scale=0.125), reads=[("bank", sb), ("bank", sb + 1)], writes=[("Pt", psl)])
                        cur = (kb, psl)
                    if pend is not None:
                        pkb, pps = pend
                        pv = []
                        seen_banks = set()
                        for j in range(2):
                            for qt in range(nqt):
                                ai = j * 4 + qt
                                first_in_bank = (ai // 3) not in seen_banks
                                seen_banks.add(ai // 3)
                                pv.append(I("matmul", acc(j, qt)[0:r, :], lhsT=Pt[pps][:, j, qt * 128:qt * 128 + r],
                                            rhs=Vaug[:, pkb, 0:129], start=(pkb == 0 and first_in_bank),
                                            stop=(pkb == NKB - 1), skip_group_check=True))
                        P.add("pe", pv, reads=[("Pt", pps)] + allV, writes=accs)
                    pend = cur
                for bi in range(3):
                    ncol = 387 if bi < 2 else 258
                    P.add("dve", I("tensor_copy", out=accS[:r, bi, 0:ncol], in_=psum[0:r, 4 + bi, 0:ncol]),
                          reads=[("bank", 4 + bi)], writes=[("accS", bi)])
                f1_, f2_ = make_fin(q0, nq, nqt, r)
                pending_fin.append((f2_, f1_()))
            while pending_fin:
                f2_, args_ = pending_fin.pop(0)
                f2_(*args_)

    def phase_C(sq_name, ri):
        xd = xseq[sq_name]
        roff = ri * RNG
        sidx = 0 if sq_name == "s" else 1
        ar = Arena(arena_ap, ARENA_W)
        h = ar.alloc((9, D), F32)
        xnT = ar.alloc((8, RNG + 2), BF16)
        uT = ar.alloc((4, RNG + 2), BF16)
        oaT = ar.alloc((4, RNG + 2), BF16)
        hTall = ar.alloc((8, RNG + 2), BF16)
        hT = [hTall[:, 0:4, :], hTall[:, 4:8, :]]
        mTb = hTall
        xsb = [ar.alloc((D,), BF16) for _ in range(9)]
        uni = ar.alloc((4640,), F32)
        gv2 = [uni[:, i * 512:(i + 1) * 512] for i in range(3)]
        sq2 = [uni[:, 1536 + i * 512:1536 + (i + 1) * 512] for i in range(3)]
        mixt2 = [uni[:, 3072 + i * 512:3072 + (i + 1) * 512].rearrange("p (g n) -> p g n", g=4) for i in range(3)]
        vb2 = [ar.alloc((512,), BF16) for _ in range(3)]
        mixt = mixt2[0]
        rl = [ar.alloc((512,), F32) for _ in range(2)]
        bgs = uni[:, 0:RNG]
        cgs = uni[:, 1024:1024 + RNG + 2]
        tb = uni[:, 2056:2056 + RNG + 2]
        yb = uni[:, 3088:3088 + RNG]
        ring = WRing(ar)
        cnt = {"xs": 0, "rl": 0, "hT": 0}
        tiles = [(t, 128) for t in range(8)] + [(8, 2)]
        cblocks = [(0, 512), (512, 512), (RNG, 2)]

        def tile_cols(t):
            return (t * 128, 128) if t < 8 else (RNG, 2)

        def hkey(t):
            return ("h", t)

        for t in range(8):
            P.add("sp", I("dma_start", out=h[:, t, :], in_=xd[roff + t * 128:roff + (t + 1) * 128, :]),
                  writes=[hkey(t)], dma=("hl", t))
        P.add("sp", I("dma_start", out=h[0:2, 8, :], in_=xtail[sq_name][2 * ri:2 * ri + 2, :]),
              writes=[hkey(8)], dma=("hl", 8))

        def do_norm(layer, tl):
            nt = len(tl)
            for (t, r) in tl:
                P.add("act", I("activation", out=junk[:r, :], in_=h[0:r, t, :], func=AF.Square,
                               accum_out=statC[:r, 0, t:t + 1]), reads=[hkey(t)], writes=[("sC", 0, t)])
            pool_rstd(statC[:, 0, 0:nt], statC[:, 2, 0:nt], 128, nt, 3, [("sC", 0, t) for (t, r) in tl], ("sC", 2))
            for i, (t, r) in enumerate(tl):
                rs_ap = statC[:r, 2, t:t + 1]
                src = h[0:r, t, :]
                if i % 3 == 0:
                    P.add("act", I("activation", out=xsb[t][:r, :], in_=src, func=AF.Identity, scale=rs_ap),
                          reads=[hkey(t), ("sC", 2)], writes=[("xsbC", t)])
                elif i % 3 == 1:
                    P.add("dve", I("tensor_scalar", out=xsb[t][:r, :], in0=src, scalar1=rs_ap, scalar2=None, op0=ALU.mult),
                          reads=[hkey(t), ("sC", 2)], writes=[("xsbC", t)])
                else:
                    P.add("pool", I("tensor_tensor", out=xsb[t][:r, :], in0=src, in1=rs_ap.broadcast_to([r, D]), op=ALU.mult),
                          reads=[hkey(t), ("sC", 2)], writes=[("xsbC", t)])
            for (t, r) in tl:
                c0, n = tile_cols(t)
                bnk = next_pt()
                ptv = bank(bnk).bitcast(BF16).rearrange("p (c n) -> p c n", c=8)
                P.add("pe", [I("transpose", out=ptv[:, c, 0:r], in_=xsb[t][:r, c * 128:(c + 1) * 128], identity=ident[:r, :r])
                             for c in range(8)], reads=[("xsbC", t)], writes=[("bank", bnk)])
                P.add("dve", I("tensor_tensor", out=xnT[:, :, c0:c0 + n], in0=ptv[:, :, 0:r],
                               in1=gcols[:, layer, :].unsqueeze(2).broadcast_to([128, 8, r]), op=ALU.mult),
                      reads=[("bank", bnk)], writes=[("xnT", t)])

        def xkeys_cb(cb):
            c0, n = cb
            if n == 2:
                return [("xnT", 8)]
            return [("xnT", t) for t in range(c0 // 128, c0 // 128 + 4)]

        def resid_add(t, r, half, b):
            P.add("dve", I("tensor_tensor", out=h[0:r, t, half * 512:(half + 1) * 512],
                           in0=h[0:r, t, half * 512:(half + 1) * 512], in1=bank(b)[0:r, :], op=ALU.add),
                  reads=[("bank", b), hkey(t)], writes=[hkey(t)])

        do_norm(0, tiles)
        wau_sl, waukey = ring.load([(v8, ewin[:, :, 0:512])])
        wav_sl, wavkey = ring.load([(v8, ewin[:, :, 512:1024])])
        Wau = v8(wau_sl)
        Wav = v8(wav_sl)
        wo = []
        for half in range(2):
            sl, k = ring.load([(v4, ewout[:, half * 4:(half + 1) * 4, :])])
            wo.append((v4(sl), k))
        for cb in cblocks:
            c0, n = cb
            for g in range(4):
                b = next_ps()
                P.add("pe", mm(bank(b)[:, 0:n], [Wau[:, c, g * 128:(g + 1) * 128] for c in range(8)],
                               [xnT[:, c, c0:c0 + n] for c in range(8)]),
                      reads=xkeys_cb(cb) + [waukey], writes=[("bank", b)])
                P.add("act", I("activation", out=uT[:, g, c0:c0 + n], in_=bank(b)[:, 0:n], func=AF.Gelu_apprx_tanh),
                      reads=[("bank", b)], writes=[("uT", g, c0)])

        def ukeys(c0):
            cc = RNG if c0 >= RNG else (c0 // 512) * 512
            return [("uT", g, cc) for g in range(4)]
        gst = {}

        def g_a1(t):
            b = next_ps()
            P.add("pe", mm(bank(b), [xnT[:, c, t * 128:(t + 1) * 128] for c in range(8)], [Wav[:, c, :] for c in range(8)]),
                  reads=[("xnT", t), wavkey], writes=[("bank", b)])
            kx = t % 3
            gst[t] = gmlp_v1(bank(b), ("bank", b), 128, gv2[kx], sq2[kx], kx=kx)

        def g_a2(t):
            kx = t % 3
            gmlp_v2(gst[t], 128, gv2[kx], vb2[kx], ("vb", kx), kx=kx)
            b2 = next_ps()
            P.add("pe", [I("matmul", bank(b2)[:, g * 128:(g + 1) * 128], lhsT=vb2[kx][:, g * 128:(g + 1) * 128],
                           rhs=wsT[:, g, :], start=True, stop=True) for g in range(4)],
                  reads=[("vb", kx)], writes=[("bank", b2)])
            gst[("b2", t)] = b2

        def g_b(t):
            kx = t % 3
            b2 = gst[("b2", t)]
            mx = mixt2[kx]
            P.add("dve", I("tensor_tensor", out=mx, in0=bank(b2).rearrange("p (g n) -> p g n", g=4), in1=bsB, op=ALU.add),
                  reads=[("bank", b2)], writes=[("mixt", kx)])
            P.add("dve", I("tensor_tensor", out=oaT[:, :, t * 128:(t + 1) * 128], in0=mx,
                           in1=uT[:, :, t * 128:(t + 1) * 128], op=ALU.mult),
                  reads=[("mixt", kx)] + ukeys(t * 128), writes=[("oaT", t)])

        if os.environ.get("KGSEQ"):
            for t in range(8):
                g_a1(t)
                g_a2(t)
                g_b(t)
        else:
            g_a1(0)
            for t in range(8):
                if t + 1 < 8:
                    g_a1(t + 1)
                g_a2(t)
                if t >= 1:
                    g_b(t - 1)
            g_b(7)
        for j in range(2):
            ti = 2 * ri + j
            pcol = 127 if j == 0 else 0
            b2 = next_ps()
            P.add("pe", [I("matmul", bank(b2)[:, g:g + 1], lhsT=vstash[:, ti, g * 128:(g + 1) * 128],
                           rhs=wsT[:, g, pcol:pcol + 1], start=True, stop=True) for g in range(4)],
                  reads=[], writes=[("bank", b2)])
            P.add("dve", I("tensor_tensor", out=mixt[:, :, 0], in0=bank(b2)[:, 0:4], in1=bsB[:, :, pcol], op=ALU.add),
                  reads=[("bank", b2)], writes=[("mixt", 0)])
            P.add("dve", I("tensor_tensor", out=oaT[:, :, RNG + j], in0=mixt[:, :, 0], in1=uT[:, :, RNG + j], op=ALU.mult),
                  reads=[("mixt", 0)] + ukeys(RNG), writes=[("oaT", 8, j)])
        for (t, r) in tiles:
            c0, n = tile_cols(t)
            for half in range(2):
                b = next_ps()
                lhs = []
                for c in range(8):
                    if c < 4:
                        lhs.append(oaT[:, c, c0:c0 + r])
                    elif t < 8:
                        lhs.append(out_bT[:, c - 4, roff + c0:roff + c0 + r])
                    else:
                        lhs.append(out_bT[:, c - 4, OWN + 2 * ri:OWN + 2 * ri + 2])
                okeys = [("oaT", t)] if t < 8 else [("oaT", 8, 0), ("oaT", 8, 1)]
                P.add("pe", mm(bank(b)[0:r, :], lhs, [wo[c // 4][0][:, c % 4, half * 512:(half + 1) * 512] for c in range(8)]),
                      reads=okeys + [wo[0][1], wo[1][1]], writes=[("bank", b)])
                resid_add(t, r, half, b)

        ALLHT = [("hT", hs) for hs in range(2)]

        def ffn(layer, tl, cbl):
            do_norm(1 + 2 * layer, tl)

            def ld(fg):
                return (ring.load([(v8, w1d[layer][:, :, fg * 512:(fg + 1) * 512])]),
                        ring.load([(v4, w2d[layer][:, fg * 4:(fg + 1) * 4, :])]))
            nxt = ld(0)
            for fg in range(8):
                (s1, k1), (s2, k2) = nxt
                if fg + 1 < 8:
                    nxt = ld(fg + 1)
                W1 = v8(s1)
                W2 = v4(s2)
                hs = cnt["hT"] % 2
                cnt["hT"] += 1
                hb = hT[hs]
                for f in range(4):
                    for cb in cbl:
                        c0, n = cb
                        b = next_ps()
                        P.add("pe", mm(bank(b)[:, 0:n], [W1[:, c, f * 128:(f + 1) * 128] for c in range(8)],
                                       [xnT[:, c, c0:c0 + n] for c in range(8)]),
                              reads=xkeys_cb(cb) + [k1], writes=[("bank", b)])
                        rsl = cnt["rl"] % 2
                        cnt["rl"] += 1
                        P.add("act", I("activation", out=rl[rsl][:, 0:n], in_=bank(b)[:, 0:n], func=AF.Relu),
                              reads=[("bank", b)], writes=[("rl", rsl)])
                        P.add("dve", I("tensor_tensor", out=hb[:, f, c0:c0 + n], in0=rl[rsl][:, 0:n], in1=rl[rsl][:, 0:n],
                                       op=ALU.mult), reads=[("rl", rsl)], writes=[("hTw", hs, f, c0)])
                hkeys = [("hTw", hs, f, c0) for f in range(4) for (c0, n) in cbl]
                first = True
                for (t, r) in tl:
                    c0, n = tile_cols(t)
                    for half in range(2):
                        b = next_ps()
                        P.add("pe", mm(bank(b)[0:r, :], [hb[:, f, c0:c0 + r] for f in range(4)],
                                       [W2[:, f, half * 512:(half + 1) * 512] for f in range(4)]),
                              reads=hkeys + [k2], writes=[("bank", b)])
                        resid_add(t, r, half, b)

        ffn(0, tiles, cblocks)

        do_norm(2, tiles)
        m0 = sidx * 4 + 2 * ri
        for c in range(8):
            wsl, wk = ring.load([(lambda sl, j=j: v3(sl)[:, j], cwin[:, :, j * D + c * 128:j * D + (c + 1) * 128])
                                 for j in range(3)])
            Wc = v3(wsl)
            for cb in cblocks:
                c0, n = cb
                xk = xkeys_cb(cb)
                xr = [xnT[:, k, c0:c0 + n] for k in range(8)]
                if n > 2:
                    b = next_ps()
                    P.add("pe", mm(bank(b)[:, 0:n], [Wc[:, 0, k, :] for k in range(8)], xr),
                          reads=xk + [wk], writes=[("bank", b)])
                    P.add("act", I("activation", out=bgs[:, c0:c0 + n], in_=bank(b)[:, 0:n], func=AF.Copy),
                          reads=[("bank", b)], writes=[("bgs", c0)])
                b = next_ps()
                P.add("pe", mm(bank(b)[:, 0:n], [Wc[:, 1, k, :] for k in range(8)], xr),
                      reads=xk + [wk], writes=[("bank", b)])
                P.add("act", I("activation", out=cgs[:, c0:c0 + n], in_=bank(b)[:, 0:n], func=AF.Copy),
                      reads=[("bank", b)], writes=[("cgs", c0)])
                b = next_ps()
                P.add("pe", mm(bank(b)[:, 0:n], [Wc[:, 2, k, :] for k in range(8)], xr),
                      reads=xk + [wk], writes=[("bank", b)])
                if n > 2:
                    P.add("dve", I("tensor_tensor", out=tb[:, 1 + c0:1 + c0 + n], in0=bank(b)[:, 0:n], in1=cgs[:, c0:c0 + n],
                                   op=ALU.mult), reads=[("bank", b), ("cgs", c0)], writes=[("tb", c0)])
                else:
                    P.add("dve", I("tensor_tensor", out=cgs[:, RNG:RNG + 2], in0=bank(b)[:, 0:2], in1=cgs[:, RNG:RNG + 2],
                                   op=ALU.mult), reads=[("bank", b), ("cgs", RNG)], writes=[("cgs2", RNG)])
                    P.add("dve", I("tensor_tensor", out=tb[:, 0:1], in0=cgs[:, RNG:RNG + 1], in1=maskB[:, m0:m0 + 1],
                                   op=ALU.mult), reads=[("cgs2", RNG)], writes=[("tb", "l")])
                    P.add("dve", I("tensor_tensor", out=tb[:, RNG + 1:RNG + 2], in0=cgs[:, RNG + 1:RNG + 2],
                                   in1=maskB[:, m0 + 1:m0 + 2], op=ALU.mult), reads=[("cgs2", RNG)], writes=[("tb", "r")])
            tkeys = [("tb", 0), ("tb", 512), ("tb", "l"), ("tb", "r")]
            P.add("dve", I("tensor_scalar", out=yb, in0=tb[:, 1:RNG + 1], scalar1=cw[:, 1, c:c + 1], scalar2=None,
                           op0=ALU.mult), reads=tkeys, writes=["yb"])
            P.add("dve", I("scalar_tensor_tensor", out=yb, in0=tb[:, 0:RNG], scalar=cw[:, 0, c:c + 1], in1=yb,
                           op0=ALU.mult, op1=ALU.add), reads=tkeys + ["yb"], writes=["yb"])
            P.add("dve", I("scalar_tensor_tensor", out=yb, in0=tb[:, 2:RNG + 2], scalar=cw[:, 2, c:c + 1], in1=yb,
                           op0=ALU.mult, op1=ALU.add), reads=tkeys + ["yb"], writes=["yb"])
            P.add("dve", I("tensor_tensor", out=mTb[:, c, 0:RNG], in0=bgs, in1=yb, op=ALU.mult),
                  reads=["yb", ("bgs", 0), ("bgs", 512)],
                  writes=[("mT", c), ("hTw", c // 4, c % 4, 0), ("hTw", c // 4, c % 4, 512)])
        wco = []
        for half in range(2):
            sl, k = ring.load([(v4, cwout[:, half * 4:(half + 1) * 4, :])])
            wco.append((v4(sl), k))
        mkeys = [("mT", c) for c in range(8)] + [("hTw", hs_, f_, c0_) for hs_ in range(2) for f_ in range(4) for c0_ in (0, 512)]
        own_tiles = tiles[:8]
        for (t, r) in own_tiles:
            for half in range(2):
                b = next_ps()
                P.add("pe", mm(bank(b), [mTb[:, c, t * 128:(t + 1) * 128] for c in range(8)],
                               [wco[c // 4][0][:, c % 4, half * 512:(half + 1) * 512] for c in range(8)]),
                      reads=mkeys + [wco[0][1], wco[1][1]], writes=[("bank", b)])
                resid_add(t, r, half, b)

        ffn(1, own_tiles, cblocks[:2])

        ystage = [bgs, yb]
        yskeys = [[("bgs", 0), ("bgs", 512)], ["yb"]]
        for half in range(2):
            for t in range(half * 4, half * 4 + 4):
                P.add("act", I("activation", out=junk[:, :], in_=h[:, t, :], func=AF.Square,
                               accum_out=statC[:, 0, t:t + 1]), reads=[hkey(t)], writes=[("sC", 0, t)])
            pool_rstd(statC[:, 0, half * 4:half * 4 + 4], statC[:, 2, half * 4:half * 4 + 4], 128, 4, 3,
                      [("sC", 0, t) for t in range(half * 4, half * 4 + 4)], ("sCf", half))
        for (t, r) in own_tiles:
            rs = statC[:, 2, t:t + 1]
            rsk = ("sCf", t // 4)
            ys = t % 2
            P.add("dve", I("scalar_tensor_tensor", out=ystage[ys], in0=h[:, t, :], scalar=rs, in1=fgB,
                           op0=ALU.mult, op1=ALU.mult), reads=[hkey(t), rsk], writes=yskeys[ys])
            P.add("sp", I("dma_start", out=yout[sq_name][roff + t * 128:roff + (t + 1) * 128, :], in_=ystage[ys]),
                  reads=yskeys[ys], writes=[("yout", sq_name, ri, t)], dma=("yst", ys))

    import os
    dbg = os.environ.get("KDBG", "")
    for sq_name in ("s", "p"):
        if dbg and sq_name not in dbg:
            continue
        if not dbg or "A" in dbg:
            phase_AB(sq_name)
            P.barrier()
        if not dbg or "C" in dbg:
            phase_C(sq_name, 0)
            P.barrier()
        if not dbg or "D" in dbg:
            phase_C(sq_name, 1)
            P.barrier()
    for e in Prog.ENGS:
        P.add(e, None)
    stuck, per_, pos_ = P.simulate()
    if stuck:
        for e, (p_, n_) in stuck.items():
            op = per_[e][p_]
            print("DEADLOCK", e, p_, n_, "op idx", op.idx, "waits", [(d.eng, d.idx, d.dma, d.tick, d.signal) for d in op.waits])
        raise RuntimeError("semaphore protocol deadlock")
    print("program ops:", len(P.ops), {e: len(v) for e, v in per_.items()})
    if os.environ.get("KDUMP"):
        for op in P.ops:
            print(op.idx, op.eng, op.name, "tick", op.tick if (op.signal or op.dma) else None,
                  "W:", [(d.eng, d.idx, d.tick) for d in op.waits], "w=", op.rw[1][:3])
    P.emit(nc, stack)
    stack.close()
    return nc


_NC_CACHE = {}


def _host_constants():
    ident = np.eye(128, dtype=np.float32)
    rt = np.zeros((128, 128), np.float32)
    invf = np.zeros(128, np.float32)
    sgn = np.zeros(128, np.float32)
    inv = (500000.0 ** (-np.arange(0, 16, 2, dtype=np.float32) / 16.0)).astype(np.float32)
    for base in (0, 64):
        for d in range(16):
            p = base + d
            partner = p + 8 if d < 8 else p - 8
            rt[partner, p] = 1.0
            invf[p] = inv[d % 8] / np.float32(2 * np.pi)
            sgn[p] = -1.0 if d < 8 else 1.0
    rc = np.stack([invf, (-2.0 * np.pi * sgn).astype(np.float32)], axis=1).astype(np.float32)
    return ident, rt, rc


def kernel(**inputs):
    f32 = lambda a: np.ascontiguousarray(np.asarray(a, dtype=np.float32))
    xpr = f32(inputs["x_prompt"])
    xsa = f32(inputs["x_sample"])
    ident, rt, rc = _host_constants()
    gains = np.stack([f32(inputs["norm_mix_g"])[0], f32(inputs["norm_ffn_g"])[0],
                      f32(inputs["norm_mix_g"])[1], f32(inputs["norm_ffn_g"])[1]], axis=0)
    gcols = np.ascontiguousarray(gains.reshape(4, 8, 128).transpose(2, 0, 1).reshape(128, 32))
    cwh = np.ascontiguousarray(f32(inputs["c_conv_w"])[0].reshape(3, 8, 128).transpose(2, 0, 1).reshape(128, 24))
    lvec = np.concatenate([f32(inputs["b_lq1"])[0], f32(inputs["b_lk1"])[0],
                           f32(inputs["b_lq2"])[0], f32(inputs["b_lk2"])[0]])[None, :]
    shared = {
        "e_w_in": f32(inputs["e_w_in"])[0], "e_w_out": f32(inputs["e_w_out"])[0],
        "ffn_w1_0": f32(inputs["ffn_w1"])[0], "ffn_w1_1": f32(inputs["ffn_w1"])[1],
        "ffn_w2_0": f32(inputs["ffn_w2"])[0], "ffn_w2_1": f32(inputs["ffn_w2"])[1],
        "c_w_in": f32(inputs["c_w_in"])[0], "c_w_out": f32(inputs["c_w_out"])[0],
        "wsT": np.ascontiguousarray(f32(inputs["a_w_s"])[0].transpose(0, 2, 1)),
        "ident": ident, "rt": rt, "gcols": gcols, "cw": cwh, "rc": rc, "lvec": np.ascontiguousarray(lvec),
        "vng": f32(inputs["a_vnorm_g"]).reshape(1, 512), "bs": f32(inputs["a_b_s"]).reshape(1, 512),
        "subln": f32(inputs["b_subln_g"]).reshape(1, 128), "fg": f32(inputs["final_g"]).reshape(1, D),
    }
    in_maps = []
    meta = []
    for c in range(NCORES):
        bp, op_ = c // 2, (c % 2) * OWN
        bs_, os_ = c // 4, (c % 4) * OWN
        m = dict(shared)
        masks = np.zeros((1, 8), np.float32)
        for key, x, b, off, S, mi in (("s", xsa, bs_, os_, SEQ_S, 0), ("p", xpr, bp, op_, SEQ_P, 4)):
            xr = np.ascontiguousarray(np.roll(x[b], -off, axis=0))
            tidx = np.array([S - 1, RNG, RNG - 1, OWN])
            pos = ((np.arange(S) + off) % S).astype(np.float32)
            post = ((tidx + off) % S).astype(np.float32)
            m["x" + key] = xr
            m["xt" + key] = np.ascontiguousarray(xr[tidx])
            m["pos" + key] = np.concatenate([pos, post])[None, :].astype(np.float32)
            masks[0, mi + 0] = 1.0 if off > 0 else 0.0
            masks[0, mi + 1] = 1.0
            masks[0, mi + 2] = 1.0
            masks[0, mi + 3] = 1.0 if off + OWN < S else 0.0
        m["masks"] = masks
        in_maps.append(m)
        meta.append((bp, op_, bs_, os_))
    if "nc" not in _NC_CACHE:
        _NC_CACHE["nc"] = build_program()
    import os as _os
    ncr = int(_os.environ.get("KCORES", NCORES))
    if _os.environ.get("KTRACE"):
        res = run_bass_kernel_spmd(_NC_CACHE["nc"], in_maps[:ncr], core_ids=list(range(ncr)), trace=True)
        print("KTRACE exec_time_ns", res.exec_time_ns)
    else:
        res = run_bass_kernel_spmd(_NC_CACHE["nc"], in_maps[:ncr], core_ids=list(range(ncr)))
    y_p = np.zeros_like(xpr)
    y_s = np.zeros_like(xsa)
    for c, (bp, op_, bs_, os_) in enumerate(meta[:ncr]):
        y_p[bp, op_:op_ + OWN] = res.results[c]["yp"]
        y_s[bs_, os_:os_ + OWN] = res.results[c]["ys"]
    return (y_p, y_s)
```

```python
import math
import os
import numpy as np
import concourse.bass as bass
import concourse.mybir as mybir
from concourse.bass_utils import run_bass_kernel_spmd

F32 = mybir.dt.float32
BF16 = mybir.dt.bfloat16
I32 = mybir.dt.int32
AF = mybir.ActivationFunctionType
ALU = mybir.AluOpType
AX = mybir.AxisListType

D = 1024
DFF = 4096
EPS = 1e-5
NCORES = 8
SEQ_S = 8192
SEQ_P = 4096
OWN = 2048
RNG = 1024
LAM_INIT = 0.8 - 0.6 * math.exp(0.0)
TWO_PI = 2.0 * math.pi


class Op:
    __slots__ = ("eng", "fn", "waits", "signal", "tick", "dma", "idx", "name", "rw")


def I(name, *args, **kw):
    return (name, args, kw)


def _mkfn(spec):
    if spec is None:
        return None
    if callable(spec):
        return spec
    if isinstance(spec, tuple):
        spec = [spec]

    def f(e, spec=spec):
        ins = None
        for (name, args, kw) in spec:
            ins = getattr(e, name)(*args, **kw)
        return ins
    return f


class Prog:
    ENGS = ("pe", "act", "dve", "pool", "sp")

    def __init__(self):
        self.ops = []
        self.last_w = {}
        self.readers = {}
        self.dma_count = {}
        self.total_groups = set()
        self.extra = {e: [] for e in self.ENGS}
        self.dma_last = {}

    def barrier(self):
        lasts = {}
        for op in reversed(self.ops):
            if op.dma is None and op.eng not in lasts and op.fn is not None:
                lasts[op.eng] = op
        deps = list(lasts.values()) + list(self.dma_last.values())
        for e in self.ENGS:
            self.extra[e] = [d for d in deps if not (d.dma is None and d.eng == e)]
        for d in lasts.values():
            d.signal = True
        self.last_w = {}
        self.readers = {}

    def add(self, eng, fn, reads=(), writes=(), dma=None, wait_total=False):
        op = Op()
        op.eng = eng
        op.fn = _mkfn(fn)
        try:
            op.name = fn[0] if isinstance(fn, tuple) else (fn[-1][0] + "x%d" % len(fn) if isinstance(fn, list) else str(fn))
        except Exception:
            op.name = "?"
        op.signal = False
        op.tick = None
        op.dma = dma
        op.idx = len(self.ops)
        bank_r = [k for k in reads if isinstance(k, tuple) and k and k[0] == "bank" and k not in writes]
        if bank_r:
            writes = list(writes) + bank_r
        deps = {}
        for k in reads:
            w = self.last_w.get(k)
            if w is not None:
                deps[w.idx] = (w, True)
        for k in writes:
            w = self.last_w.get(k)
            if w is not None and w.idx not in deps:
                deps[w.idx] = (w, False)
            for r in self.readers.get(k, ()):
                if r.idx not in deps:
                    deps[r.idx] = (r, False)
        waits = []
        for d, raw in deps.values():
            if d.dma is None and d.eng == eng:
                if eng == "pe" or not raw:
                    continue
            waits.append(d)
            if d.dma is None:
                d.signal = True
        if self.extra[eng]:
            waits = waits + self.extra[eng]
            self.extra[eng] = []
        op.waits = waits
        op.rw = (list(reads), list(writes))
        for k in reads:
            self.readers.setdefault(k, []).append(op)
        for k in writes:
            self.last_w[k] = op
            self.readers[k] = []
        if dma is not None:
            self.dma_count[dma] = self.dma_count.get(dma, 0) + 1
            op.tick = 16 * self.dma_count[dma]
            self.dma_last[dma] = op
            if wait_total:
                self.total_groups.add(dma)
        self.ops.append(op)
        return op

    def simulate(self):
        cnt = {e: 0 for e in self.ENGS}
        for op in self.ops:
            if op.dma is None and op.signal:
                cnt[op.eng] += 1
                op.tick = cnt[op.eng]
        per = {e: [op for op in self.ops if op.eng == e] for e in self.ENGS}
        pos = {e: 0 for e in self.ENGS}
        sem = {}
        progress = True
        while progress:
            progress = False
            for e in self.ENGS:
                while pos[e] < len(per[e]):
                    op = per[e][pos[e]]
                    ok = True
                    for d in op.waits:
                        if d.dma is not None:
                            key = ("d", d.dma)
                            val = 16 * self.dma_count[d.dma] if d.dma in self.total_groups else d.tick
                        else:
                            key = ("e", d.eng)
                            val = d.tick
                        if sem.get(key, 0) < val:
                            ok = False
                            break
                    if not ok:
                        break
                    if op.fn is not None:
                        if op.dma is not None:
                            sem[("d", op.dma)] = sem.get(("d", op.dma), 0) + 16
                        elif op.signal:
                            sem[("e", op.eng)] = sem.get(("e", op.eng), 0) + 1
                    pos[e] += 1
                    progress = True
        stuck = {e: (pos[e], len(per[e])) for e in self.ENGS if pos[e] < len(per[e])}
        return stuck, per, pos

    def emit(self, nc, stack):
        cnt = {e: 0 for e in self.ENGS}
        for op in self.ops:
            if op.dma is None and op.signal:
                cnt[op.eng] += 1
                op.tick = cnt[op.eng]
        sems = {e: stack.enter_context(nc.semaphore("s_" + e)) for e in self.ENGS}
        dsems = {}
        for i, g in enumerate(self.dma_count):
            dsems[g] = stack.enter_context(nc.semaphore("d%d" % i))
        per = {e: [] for e in self.ENGS}
        for op in self.ops:
            per[op.eng].append(op)
        block = stack.enter_context(nc.Block())

        def run(engh, ops):
            seen = {}
            for op in ops:
                need = {}
                for d in op.waits:
                    if d.dma is not None:
                        key = ("d", d.dma)
                        val = 16 * self.dma_count[d.dma] if d.dma in self.total_groups else d.tick
                    else:
                        key = ("e", d.eng)
                        val = d.tick
                    if val > need.get(key, 0):
                        need[key] = val
                for key, val in need.items():
                    if seen.get(key, 0) >= val:
                        continue
                    seen[key] = val
                    sem = dsems[key[1]] if key[0] == "d" else sems[key[1]]
                    engh.wait_ge(sem, val)
                if op.fn is None:
                    continue
                ins = op.fn(engh)
                if op.dma is not None:
                    ins.then_inc(dsems[op.dma], 16)
                elif op.signal:
                    ins.then_inc(sems[op.eng], 1)

        @block.tensor
        def _(e):
            run(e, per["pe"])

        @block.scalar
        def _(e):
            run(e, per["act"])

        @block.vector
        def _(e):
            run(e, per["dve"])

        @block.gpsimd
        def _(e):
            run(e, per["pool"])

        @block.sync
        def _(e):
            run(e, per["sp"])


class Arena:
    def __init__(self, ap, nwords):
        self.ap = ap
        self.n = nwords
        self.off = 0

    def alloc(self, free_shape, dtype):
        nel = 1
        for s in free_shape:
            nel *= s
        bpe = 2 if dtype == BF16 else 4
        words = (nel * bpe + 3) // 4
        words = (words + 7) // 8 * 8
        assert self.off + words <= self.n, ("arena overflow", self.off, words, self.n)
        v = self.ap[:, self.off:self.off + words]
        self.off += words
        if dtype != F32:
            v = v.bitcast(dtype)
        v = v[:, 0:nel]
        if len(free_shape) == 2:
            v = v.rearrange("p (a b) -> p a b", a=free_shape[0])
        elif len(free_shape) == 3:
            v = v.rearrange("p (a b c) -> p a b c", a=free_shape[0], b=free_shape[1])
        return v


def build_program():
    from contextlib import ExitStack
    nc = bass.Bass("TRN2", target_bir_lowering=False)
    stack = ExitStack()
    P = Prog()

    def din(name, shape):
        return nc.dram_tensor(name, list(shape), F32, kind="ExternalInput").ap()

    xseq = {"s": din("xs", (SEQ_S, D)), "p": din("xp", (SEQ_P, D))}
    xtail = {"s": din("xts", (4, D)), "p": din("xtp", (4, D))}
    posd = {"s": din("poss", (1, SEQ_S + 4)), "p": din("posp", (1, SEQ_P + 4))}
    maskd = din("masks", (1, 8))
    ewin = din("e_w_in", (D, 2560)).rearrange("(c p) n -> p c n", p=128)
    ewout = din("e_w_out", (D, D)).rearrange("(c p) n -> p c n", p=128)
    w1d = [din("ffn_w1_%d" % l, (D, DFF)).rearrange("(c p) n -> p c n", p=128) for l in range(2)]
    w2d = [din("ffn_w2_%d" % l, (DFF, D)).rearrange("(f p) n -> p f n", p=128) for l in range(2)]
    cwin = din("c_w_in", (D, 3 * D)).rearrange("(c p) n -> p c n", p=128)
    cwout = din("c_w_out", (D, D)).rearrange("(c p) n -> p c n", p=128)
    wstd = din("wsT", (4, 128, 128)).rearrange("g q p -> q g p")
    identd = din("ident", (128, 128))
    rtd = din("rt", (128, 128))
    gcolsd = din("gcols", (128, 32))
    cwd = din("cw", (128, 24))
    rcd = din("rc", (128, 2))
    lvecd = din("lvec", (1, 256))
    vngd = din("vng", (1, 512))
    bsd = din("bs", (1, 512))
    sublnd = din("subln", (1, 128))
    fgd = din("fg", (1, D))
    xnd = {"s": nc.dram_tensor("xnd_s", [SEQ_S // 512, 128, 8 * 512], BF16, kind="Internal").ap(),
           "p": nc.dram_tensor("xnd_p", [SEQ_P // 512, 128, 8 * 512], BF16, kind="Internal").ap()}
    tabd = {"s": nc.dram_tensor("tabd_s", [SEQ_S // 512, 128, 2 * 512], F32, kind="Internal").ap(),
            "p": nc.dram_tensor("tabd_p", [SEQ_P // 512, 128, 2 * 512], F32, kind="Internal").ap()}
    yout = {"s": nc.dram_tensor("ys", [OWN, D], F32, kind="ExternalOutput").ap(),
            "p": nc.dram_tensor("yp", [OWN, D], F32, kind="ExternalOutput").ap()}

    ARENA_W = 43520
    PERS_W = 9216
    arena_t = stack.enter_context(nc.sbuf_tensor("arena", [128, ARENA_W], F32))
    pers_t = stack.enter_context(nc.sbuf_tensor("pers", [128, PERS_W], F32))
    psum = stack.enter_context(nc.psum_tensor("ps", [128, 8, 512], F32))
    pers = Arena(pers_t[:], PERS_W)
    arena_ap = arena_t[:]

    out_bT = pers.alloc((4, OWN + 4), BF16)
    ident = pers.alloc((128,), BF16)
    rt = pers.alloc((128,), BF16)
    wsT = pers.alloc((4, 128), BF16)
    gcols = pers.alloc((4, 8), F32)
    cw = pers.alloc((3, 8), F32)
    rc = pers.alloc((2,), F32)
    lvec = pers.alloc((256,), F32)
    ltmp = pers.alloc((64,), F32)
    lsc = pers.alloc((8,), F32)
    gBv = pers.alloc((512,), F32)
    bsB = pers.alloc((4, 128), F32)
    sublnB = pers.alloc((128,), F32)
    fgB = pers.alloc((D,), F32)
    maskB = pers.alloc((8,), F32)
    NSTAT = 16
    stat = pers.alloc((NSTAT, 12), F32)
    vstash = pers.alloc((4, 512), BF16)
    junk = pers.alloc((D,), BF16)
    epsb = pers.alloc((1,), F32)
    halfpi = pers.alloc((1,), F32)
    cst = pers.alloc((8,), F32)
    statC = pers.alloc((3, 12), F32)

    CONST = "const"
    ALLC = []

    def cload(eng, dst, src, name):
        P.add(eng, I("dma_start", out=dst, in_=src), writes=[("c", name)], dma=CONST + "_" + eng, wait_total=True)
        ALLC.append(("c", name))

    cload("pool", ident, identd[:, :], "ident")
    cload("pool", rt, rtd[:, :], "rt")
    cload("pool", wsT, wstd, "wsT")
    cload("sp", gcols, gcolsd.rearrange("p (a b) -> p a b", a=4), "gcols")
    cload("sp", cw, cwd.rearrange("p (a b) -> p a b", a=3), "cw")
    cload("sp", rc, rcd[:, :], "rc")
    cload("sp", lvec, lvecd.partition_broadcast(128), "lvec")
    cload("sp", gBv, vngd.partition_broadcast(128), "gBv")
    cload("sp", bsB, bsd.partition_broadcast(128).rearrange("p o (a b) -> p (o a) b", a=4), "bsB")
    cload("sp", sublnB, sublnd.partition_broadcast(128), "sublnB")
    cload("sp", fgB, fgd.partition_broadcast(128), "fgB")
    cload("sp", maskB, maskd.partition_broadcast(128), "maskB")

    P.add("dve", I("memset", epsb, EPS), writes=["epsb"])
    P.add("dve", I("memset", halfpi, math.pi / 2), writes=["halfpi"])
    P.add("dve", I("memset", cst[:, 0:1], 0.5), writes=["cst0"])
    P.add("dve", I("memset", statC, 1.0), writes=["statC"])
    P.add("dve", I("memset", cst[:, 1:2], 0.0), writes=["cst1"])
    P.add("dve", I("memset", cst[:, 2:3], -0.5), writes=["cst2"])
    P.add("dve", I("memset", cst[:, 3:4], D * EPS), writes=["cst3"])
    P.add("dve", I("memset", cst[:, 4:5], 128 * EPS), writes=["cst4"])
    for j in range(2):
        P.add("dve", I("tensor_tensor", out=ltmp, in0=lvec[:, 128 * j:128 * j + 64],
                       in1=lvec[:, 128 * j + 64:128 * j + 128], op=ALU.mult), reads=ALLC, writes=["ltmp"])
        P.add("dve", I("tensor_reduce", out=lsc[:, j:j + 1], in_=ltmp, axis=AX.X, op=ALU.add),
              reads=["ltmp"], writes=[("lsc", j)])
    P.add("act", I("activation", out=lsc[:, 2:4], in_=lsc[:, 0:2], func=AF.Exp),
          reads=[("lsc", 0), ("lsc", 1)], writes=["lsce"])
    P.add("dve", I("tensor_tensor", out=lsc[:, 4:5], in0=lsc[:, 2:3], in1=lsc[:, 3:4], op=ALU.subtract),
          reads=["lsce"], writes=["lam0"])
    P.add("dve", I("tensor_scalar", out=lsc[:, 5:6], in0=lsc[:, 4:5], scalar1=LAM_INIT, scalar2=-1.0,
                   op0=ALU.add, op1=ALU.mult), reads=["lam0"], writes=["neglam"])
    P.add("dve", I("tensor_scalar", out=sublnB, in0=sublnB, scalar1=(1.0 - LAM_INIT) * math.sqrt(128.0), scalar2=None,
                   op0=ALU.mult), reads=ALLC, writes=["sublnS"])
    P.add("dve", I("tensor_scalar", out=gcols, in0=gcols, scalar1=math.sqrt(float(D)), scalar2=None, op0=ALU.mult),
          reads=ALLC, writes=["gcolsS"])
    P.add("dve", I("tensor_scalar", out=fgB, in0=fgB, scalar1=math.sqrt(float(D)), scalar2=None, op0=ALU.mult),
          reads=ALLC, writes=["fgBS"])
    P.add("dve", I("tensor_scalar", out=gBv, in0=gBv, scalar1=math.sqrt(128.0), scalar2=None, op0=ALU.mult),
          reads=ALLC, writes=["gBvS"])
    P.barrier()

    def pool_rstd(src, dst, r, n, ccol, reads, wkey):
        P.add("pool", I("tensor_tensor", out=dst, in0=src, in1=cst[:r, ccol:ccol + 1].broadcast_to([r, n]), op=ALU.add),
              reads=reads, writes=[(wkey, "a")])
        P.add("pool", I("tensor_tensor", out=dst, in0=dst, in1=cst[:r, 2:3].broadcast_to([r, n]), op=ALU.pow),
              reads=[(wkey, "a")], writes=[wkey])

    state = {"stat": 0, "pt": 0, "ps": 0}

    def new_stat():
        s = state["stat"]
        state["stat"] = (s + 1) % NSTAT
        return s

    def bank(i):
        return psum[:, i, :]

    PS_RING = [0, 1, 2, 3, 6, 7]

    def next_ps():
        s = state["ps"]
        state["ps"] = (s + 1) % len(PS_RING)
        return PS_RING[s]

    def next_pt():
        s = state["pt"]
        state["pt"] = (s + 1) % 2
        return 4 + s

    def rstd_chain(src, r, srckey, scale):
        s = new_stat()
        sk = ("stat", s)
        n = src.shape[-1]
        P.add("act", I("activation", out=junk[:r, 0:n], in_=src, func=AF.Square, accum_out=stat[:r, s, 0:1]),
              reads=[srckey], writes=[(sk, 0)])
        pool_rstd(stat[:r, s, 0:1], stat[:r, s, 2:3], r, 1, 3 if n == D else 4, [(sk, 0)], (sk, 2))
        return stat[:r, s, 2:3], (sk, 2)

    def norm_to_xnT(src, r, srckey, xs_buf, xs_key, layer, dst, dst_key):
        rs, rsk = rstd_chain(src, r, srckey, 1.0 / D)
        P.add("act", I("activation", out=xs_buf[:r, :], in_=src, func=AF.Identity, scale=rs),
              reads=[srckey, rsk], writes=[xs_key])
        b = next_pt()
        ptv = bank(b).bitcast(BF16).rearrange("p (c n) -> p c n", c=8)
        P.add("pe", [I("transpose", out=ptv[:, c, 0:r], in_=xs_buf[:r, c * 128:(c + 1) * 128], identity=ident[:r, :r])
                     for c in range(8)], reads=[xs_key], writes=[("bank", b)])
        P.add("dve", I("tensor_tensor", out=dst, in0=ptv[:, :, 0:r],
                       in1=gcols[:, layer, :].unsqueeze(2).broadcast_to([128, 8, r]), op=ALU.mult),
              reads=[("bank", b)], writes=[dst_key])

    def mm(out_ap, lhs, rhs):
        nk = len(lhs)
        return [I("matmul", out_ap, lhsT=lhs[c], rhs=rhs[c], start=(c == 0), stop=(c == nk - 1)) for c in range(nk)]

    NW = 4
    WSLOT = 4096

    class WRing:
        def __init__(self, ar):
            self.slots = [ar.alloc((WSLOT,), BF16) for _ in range(NW)]
            self.i = 0

        def load(self, pieces):
            s = self.i % NW
            self.i += 1
            sl = self.slots[s]
            key = ("w", s)
            for dv, src in pieces:
                P.add("pool", I("dma_start", out=dv(sl), in_=src), writes=[key], dma=("w", s))
            return sl, key

    def v8(sl):
        return sl.rearrange("p (c n) -> p c n", c=8)

    def v4(sl):
        return sl.rearrange("p (c n) -> p c n", c=4)

    def v3(sl):
        return sl[:, 0:3072].rearrange("p (j c n) -> p j c n", j=3, c=8)

    def rope_tables(posB, n, poskey, Ct, St, A1, A2, tkey):
        A2i = A2.bitcast(I32)
        TE = os.environ.get("KTE", "pool")
        bc = lambda ap: ap.broadcast_to([128, n])
        P.add(TE, I("tensor_tensor", out=A1[:, 0:n], in0=posB[:, 0:n], in1=bc(rc[:, 0:1]), op=ALU.mult),
              reads=[poskey], writes=["tabA1"])
        P.add(TE, I("tensor_copy", out=A2i[:, 0:n], in_=A1[:, 0:n]), reads=["tabA1"], writes=["tabA2"])
        P.add(TE, I("tensor_tensor", out=A1[:, 0:n], in0=A1[:, 0:n], in1=A2i[:, 0:n], op=ALU.subtract),
              reads=["tabA1", "tabA2"], writes=["tabA1"])
        P.add("dve", I("scalar_tensor_tensor", out=A2[:, 0:n], in0=A1[:, 0:n], scalar=0.5, in1=A1[:, 0:n],
                       op0=ALU.is_gt, op1=ALU.subtract), reads=["tabA1"], writes=["tabA2"])
        P.add("act", I("activation", out=St[:, 0:n], in_=A2[:, 0:n], func=AF.Sin, scale=rc[:, 1:2]),
              reads=["tabA2"], writes=[(tkey, "S")])
        P.add("dve", I("scalar_tensor_tensor", out=A1[:, 0:n], in0=A2[:, 0:n], scalar=-1.0, in1=A2[:, 0:n],
                       op0=ALU.mult, op1=ALU.min), reads=["tabA2"], writes=["tabA1"])
        P.add("act", I("activation", out=Ct[:, 0:n], in_=A1[:, 0:n], func=AF.Sin, scale=TWO_PI, bias=halfpi[:, 0:1]),
              reads=["tabA1"], writes=[(tkey, "C")])

    def gmlp_v1(ps_ap, pskey, r, gv, sq, kx=0):
        P.add("act", I("activation", out=gv[:r, :], in_=ps_ap, func=AF.Gelu_apprx_tanh), reads=[pskey],
              writes=[("gv", kx)])
        P.add("dve", I("tensor_tensor", out=sq[:r, :], in0=gv[:r, :], in1=gv[:r, :], op=ALU.mult),
              reads=[("gv", kx)], writes=[("sq", kx)])
        s = new_stat()
        sk = ("stat", s)
        P.add("dve", I("tensor_reduce", out=stat[:r, s, 0:4], in_=sq[:r, :].rearrange("p (g n) -> p g n", g=4),
                       axis=AX.X, op=ALU.add), reads=[("sq", kx)], writes=[(sk, 0)])
        return s

    def gmlp_v2(s, r, gv, vout, vkey, kx=0):
        sk = ("stat", s)
        pool_rstd(stat[:r, s, 0:4], stat[:r, s, 8:12], r, 4, 4, [(sk, 0)], (sk, 2))
        for g in range(4):
            P.add("dve", I("scalar_tensor_tensor", out=vout[:r, g * 128:(g + 1) * 128], in0=gv[:r, g * 128:(g + 1) * 128],
                           scalar=stat[:r, s, 8 + g:9 + g], in1=gBv[:r, g * 128:(g + 1) * 128], op0=ALU.mult, op1=ALU.mult),
                  reads=[("gv", kx), (sk, 2)], writes=[vkey])

    def gmlp_v(ps_ap, pskey, r, gv, sq, vout, vkey, kx=0):
        s = gmlp_v1(ps_ap, pskey, r, gv, sq, kx)
        gmlp_v2(s, r, gv, vout, vkey, kx)

    def phase_AB(sq_name):
        S = SEQ_S if sq_name == "s" else SEQ_P
        NB = S // 512
        NKB = S // 128
        xd = xseq[sq_name]
        ar = Arena(arena_ap, ARENA_W)
        Kt = ar.alloc((S,), BF16)
        Vaug = ar.alloc((NKB, 130), BF16)
        Qt = ar.alloc((OWN + 4,), BF16)
        NXR = 8
        NXS = 4
        xring = [ar.alloc((D,), F32) for _ in range(NXR)]
        xsb = [ar.alloc((D,), BF16) for _ in range(NXS)]
        xnA = [ar.alloc((8, 512), BF16) for _ in range(2)]
        xnTl = ar.alloc((8, 4), BF16)
        kraw = [ar.alloc((512,), BF16) for _ in range(2)]
        t1 = ar.alloc((512,), F32)
        t2 = ar.alloc((512,), F32)
        Ct = [ar.alloc((512,), F32) for _ in range(2)]
        St = [ar.alloc((512,), F32) for _ in range(2)]
        A1 = ar.alloc((512,), F32)
        A2 = ar.alloc((512,), F32)
        posB = [ar.alloc((512,), F32) for _ in range(2)]
        NPT = 4
        Pt = [ar.alloc((2, 512), BF16) for _ in range(NPT)]
        otmp = ar.alloc((128,), F32)
        accS = ar.alloc((3, 388), F32)
        obS = ar.alloc((4, 128), F32)
        obn4 = ar.alloc((4, 128), BF16)
        gv = ar.alloc((512,), F32)
        sq = ar.alloc((512,), F32)
        ring = WRing(ar)
        tail_chunks = [NKB - 1, 8, 7, 16]
        cnt = {"x": 0, "xs": 0, "kraw": 0, "pt": 0}
        accs = [("bank", 4), ("bank", 5), ("bank", 6)]

        def acc(j, qt):
            i = j * 4 + qt
            return psum[:, 4 + i // 3, (i % 3) * 129:(i % 3) * 129 + 129]

        import os
        for hd in range(int(os.environ.get("KHEADS", "4"))):
            wsl, wkey = ring.load([(lambda sl, j=j: v3(sl)[:, j],
                                    ewin[:, :, 1024 + 512 * j + hd * 128:1024 + 512 * j + (hd + 1) * 128]) for j in range(3)])
            W3 = v3(wsl)
            if hd == 0:
                wav_sl, wavkey = ring.load([(v8, ewin[:, :, 512:1024])])
                Wav = v8(wav_sl)
            P.add("dve", I("memset", Vaug[:, :, 128:129], 1.0), writes=["Vaug1"])

            def rope_proj(j, rhs, n, ctab, stab, tabkey, dst, dstkey, xkeys, part=6):
                b1 = next_ps()
                P.add("pe", mm(bank(b1)[:, 0:n], [W3[:, j, c, :] for c in range(8)], rhs),
                      reads=xkeys + [wkey], writes=[("bank", b1)])
                if part < 2:
                    return
                ks = cnt["kraw"] % 2
                cnt["kraw"] += 1
                P.add("act", I("activation", out=kraw[ks][:, 0:n], in_=bank(b1)[:, 0:n], func=AF.Copy),
                      reads=[("bank", b1)], writes=[("kraw", ks)])
                if part < 3:
                    return
                b2 = next_ps()
                P.add("pe", I("matmul", bank(b2)[:, 0:n], lhsT=rt, rhs=kraw[ks][:, 0:n], start=True, stop=True),
                      reads=[("kraw", ks)], writes=[("bank", b2)])
                if part < 4:
                    return
                P.add("dve", I("tensor_tensor", out=t1[:, 0:n], in0=bank(b1)[:, 0:n], in1=ctab[:, 0:n], op=ALU.mult),
                      reads=[("bank", b1), (tabkey, "C")], writes=["t1"])
                if part < 5:
                    return
                P.add("dve", I("tensor_tensor", out=t2[:, 0:n], in0=bank(b2)[:, 0:n], in1=stab[:, 0:n], op=ALU.mult),
                      reads=[("bank", b2), (tabkey, "S")], writes=["t2"])
                if part < 6:
                    return
                P.add("pool", I("tensor_tensor", out=dst, in0=t1[:, 0:n], in1=t2[:, 0:n], op=ALU.add),
                      reads=["t1", "t2"], writes=[dstkey])

            xsl = cnt["x"] % NXR
            cnt["x"] += 1
            xt = xring[xsl]
            xk = ("xr", xsl)
            P.add("sp", I("dma_start", out=xt[0:4, :], in_=xtail[sq_name][:, :]), writes=[xk], dma=xk)
            xss = cnt["xs"] % NXS
            cnt["xs"] += 1
            norm_to_xnT(xt[0:4, :], 4, xk, xsb[xss], ("xsb", xss), 0, xnTl, "xnTl")
            P.add("sp", I("dma_start", out=posB[0][:, 0:4], in_=posd[sq_name][:, S:S + 4].partition_broadcast(128)),
                  writes=[("posB", 0)], dma=("posB", 0))
            rope_tables(posB[0], 4, ("posB", 0), Ct[0], St[0], A1, A2, ("tab", 0))
            rope_proj(0, [xnTl[:, c, :] for c in range(8)], 4, Ct[0], St[0], ("tab", 0), Qt[:, OWN:OWN + 4],
                      ("Qt", 4), ["xnTl"])

            if os.environ.get("KSTOP", "9") == "1":
                return
            NBr = int(os.environ.get("KNB", NB))
            blk = {}

            def par_of(b):
                return (b + 1) % 2

            def stage_load(b):
                par = par_of(b)
                P.add("sp", I("dma_start", out=posB[par],
                              in_=posd[sq_name][:, b * 512:(b + 1) * 512].partition_broadcast(128)),
                      writes=[("posB", par)], dma=("posB", par))
                xs_ = []
                for t in range(4):
                    xsl = (b % 2) * 4 + t
                    row = b * 512 + t * 128
                    P.add("sp", I("dma_start", out=xring[xsl], in_=xd[row:row + 128, :]), writes=[("xr", xsl)],
                          dma=("xr", xsl))
                    xs_.append(xsl)
                blk[b] = {"x": xs_}

            def stage_stats(b):
                s_ = new_stat()
                sk = ("stat", s_)
                for t in range(4):
                    xsl = blk[b]["x"][t]
                    P.add("act", I("activation", out=junk[:, :], in_=xring[xsl], func=AF.Square,
                                   accum_out=stat[:, s_, t:t + 1]), reads=[("xr", xsl)], writes=[(sk, 0, t)])
                pool_rstd(stat[:, s_, 0:4], stat[:, s_, 8:12], 128, 4, 3, [(sk, 0, t) for t in range(4)], (sk, 2))
                blk[b]["stat"] = s_

            def stage_mid(b):
                par = par_of(b)
                s_ = blk[b]["stat"]
                sk = ("stat", s_)
                for t in range(4):
                    xsl = blk[b]["x"][t]
                    rs_ap = stat[:, s_, 8 + t:9 + t]
                    if t == 2:
                        P.add("act", I("activation", out=xsb[t], in_=xring[xsl], func=AF.Identity, scale=rs_ap),
                              reads=[("xr", xsl), (sk, 2)], writes=[("xsb", t)])
                    elif t == 3:
                        P.add("pool", I("tensor_tensor", out=xsb[t], in0=xring[xsl], in1=rs_ap.broadcast_to([128, D]),
                                        op=ALU.mult), reads=[("xr", xsl), (sk, 2)], writes=[("xsb", t)])
                    else:
                        P.add("dve", I("tensor_scalar", out=xsb[t], in0=xring[xsl], scalar1=rs_ap,
                                       scalar2=None, op0=ALU.mult), reads=[("xr", xsl), (sk, 2)], writes=[("xsb", t)])
                for t in range(4):
                    bnk = next_pt()
                    ptv = bank(bnk).bitcast(BF16).rearrange("p (c n) -> p c n", c=8)
                    P.add("pe", [I("transpose", out=ptv[:, c, :], in_=xsb[t][:, c * 128:(c + 1) * 128], identity=ident)
                                 for c in range(8)], reads=[("xsb", t)], writes=[("bank", bnk)])
                    P.add("dve", I("tensor_tensor", out=xnA[par][:, :, t * 128:(t + 1) * 128], in0=ptv,
                                   in1=gcols[:, 0, :].unsqueeze(2).broadcast_to([128, 8, 128]), op=ALU.mult),
                          reads=[("bank", bnk)], writes=[("xnA", par, t)])

            def stage_tables(b):
                par = par_of(b)
                rope_tables(posB[par], 512, ("posB", par), Ct[par], St[par], A1, A2, ("tab", par))

            cache = not os.environ.get("KNOCACHE")

            def store_tab(b):
                par = par_of(b)
                P.add("sp", I("dma_start", out=tabd[sq_name][b, :, 0:512], in_=Ct[par]),
                      reads=[(("tab", par), "C")], writes=[("tabd", b, 0)], dma=("tst", 0))
                P.add("sp", I("dma_start", out=tabd[sq_name][b, :, 512:1024], in_=St[par]),
                      reads=[(("tab", par), "S")], writes=[("tabd", b, 1)], dma=("tst", 1))

            def store_x(b):
                par = par_of(b)
                P.add("sp", I("dma_start", out=xnd[sq_name][b], in_=xnA[par].rearrange("p c n -> p (c n)")),
                      reads=[("xnA", par, t) for t in range(4)], writes=[("xnd", b)], dma=("xst", par))

            def stage_reload(b):
                par = par_of(b)
                P.add("sp", I("dma_start", out=xnA[par].rearrange("p c n -> p (c n)"), in_=xnd[sq_name][b]),
                      reads=[("xnd", b)], writes=[("xnA", par, t) for t in range(4)], dma=("xrl", par))
                P.add("sp", I("dma_start", out=Ct[par], in_=tabd[sq_name][b, :, 0:512]),
                      reads=[("tabd", b, 0)], writes=[(("tab", par), "C")], dma=("trl", par, 0))
                P.add("sp", I("dma_start", out=St[par], in_=tabd[sq_name][b, :, 512:1024]),
                      reads=[("tabd", b, 1)], writes=[(("tab", par), "S")], dma=("trl", par, 1))

            first_pass = (hd == 0) or not cache
            if first_pass:
                stage_load(0)
                if NBr > 1:
                    stage_load(1)
                stage_stats(0)
                stage_tables(0)
                if cache:
                    store_tab(0)
            else:
                stage_reload(0)
            for b in range(NBr):
                par = par_of(b)
                if first_pass:
                    stage_mid(b)
                    if b + 2 < NBr:
                        stage_load(b + 2)
                    if b + 1 < NBr:
                        stage_stats(b + 1)
                        stage_tables(b + 1)
                    if cache:
                        store_x(b)
                        if b + 1 < NBr:
                            store_tab(b + 1)
                elif b + 1 < NBr:
                    stage_reload(b + 1)
                xkeys = [("xnA", par, t) for t in range(4)]
                xr = [xnA[par][:, c, :] for c in range(8)]
                kv = os.environ.get("KVAR", "")
                if kv != "noK":
                    rope_proj(1, xr, 512, Ct[par], St[par], ("tab", par), Kt[:, b * 512:(b + 1) * 512], ("Kt", b), xkeys)
                if kv == "KbK":
                    P.barrier()
                    kv = "KK"
                if kv == "KK":
                    for _ in range(int(os.environ.get("KSKIP", "0"))):
                        next_ps()
                    rope_proj(1, xr, 512, Ct[par], St[par], ("tab", par), Kt[:, b * 512:(b + 1) * 512], ("Kt", b), xkeys,
                              part=int(os.environ.get("KPART", "6")))
                    continue
                if os.environ.get("KNOV"):
                    if b < 4 and os.environ.get("KNOV") == "q":
                        rope_proj(0, xr, 512, Ct[par], St[par], ("tab", par), Qt[:, b * 512:(b + 1) * 512], ("Qt", b), xkeys)
                    continue
                bv = next_ps()
                vm = []
                for t in range(4):
                    vm += mm(bank(bv)[:, t * 128:(t + 1) * 128], [xnA[par][:, c, t * 128:(t + 1) * 128] for c in range(8)],
                             [W3[:, 2, c, :] for c in range(8)])
                P.add("pe", vm, reads=xkeys + [wkey], writes=[("bank", bv)])
                for t in range(4):
                    P.add("dve", I("tensor_copy", out=Vaug[:, b * 4 + t, 0:128], in_=bank(bv)[:, t * 128:(t + 1) * 128]),
                          reads=[("bank", bv), "Vaug1"], writes=[("Vaug", b, t)])
                if b < 4:
                    rope_proj(0, xr, 512, Ct[par], St[par], ("tab", par), Qt[:, b * 512:(b + 1) * 512], ("Qt", b), xkeys)
                if hd == 0 and not os.environ.get("KNOAV"):
                    for ti, ch in enumerate(tail_chunks):
                        if ch // 4 == b:
                            t = ch % 4
                            ba = next_ps()
                            P.add("pe", mm(bank(ba), [xnA[par][:, c, t * 128:(t + 1) * 128] for c in range(8)],
                                           [Wav[:, c, :] for c in range(8)]),
                                  reads=xkeys + [wavkey], writes=[("bank", ba)])
                            gmlp_v(bank(ba), ("bank", ba), 128, gv, sq, vstash[:, ti, :], ("vstash", ti))

            if os.environ.get("KSTOP", "9") == "2":
                return
            qblocks = [(0, 512), (512, 512), (1024, 512), (1536, 512), (OWN, 4)]
            allK = [("Kt", b) for b in range(NB)]
            allV = [("Vaug", b, t) for b in range(NB) for t in range(4)]
            allQ = [("Qt", b) for b in range(5)]
            def accv(j, qt):
                i = j * 4 + qt
                return accS[:, i // 3, (i % 3) * 129:(i % 3) * 129 + 129]

            def make_fin(q0, nq, nqt, r):
                def fin():
                    akeys = [("accS", bi) for bi in range(3)]
                    s_ = new_stat()
                    sk = ("stat", s_)
                    for qt in range(nqt):
                        P.add("dve", I("reciprocal", out=stat[:r, s_, 0:1], in_=accv(0, qt)[0:r, 128:129]),
                              reads=akeys, writes=[(sk, 0)])
                        P.add("dve", I("reciprocal", out=stat[:r, s_, 1:2], in_=accv(1, qt)[0:r, 128:129]),
                              reads=akeys, writes=[(sk, 1)])
                        P.add("dve", I("tensor_tensor", out=stat[:r, s_, 2:3], in0=stat[:r, s_, 1:2], in1=lsc[:r, 5:6],
                                       op=ALU.mult), reads=[(sk, 1)], writes=[(sk, 2)])
                        P.add("dve", I("tensor_scalar", out=otmp[:r, :], in0=accv(0, qt)[0:r, 0:128],
                                       scalar1=stat[:r, s_, 0:1], scalar2=None, op0=ALU.mult),
                              reads=akeys + [(sk, 0)], writes=["otmp"])
                        P.add("dve", I("scalar_tensor_tensor", out=obS[:r, qt, :], in0=accv(1, qt)[0:r, 0:128],
                                       scalar=stat[:r, s_, 2:3], in1=otmp[:r, :], op0=ALU.mult, op1=ALU.add),
                              reads=akeys + [(sk, 2), "otmp"], writes=[("obS", qt)])
                        P.add("dve", I("tensor_tensor", out=otmp[:r, :], in0=obS[:r, qt, :], in1=obS[:r, qt, :], op=ALU.mult),
                              reads=[("obS", qt)], writes=["otmp"])
                        P.add("dve", I("tensor_reduce", out=stat[:r, s_, 4 + qt:5 + qt], in_=otmp[:r, :], axis=AX.X, op=ALU.add),
                              reads=["otmp"], writes=[(sk, 4, qt)])
                    return (s_, sk)

                def fin2(s_, sk):
                    pool_rstd(stat[:r, s_, 4:4 + nqt], stat[:r, s_, 8:8 + nqt], r, nqt, 4,
                              [(sk, 4, qt) for qt in range(nqt)], (sk, 9))
                    ptb = bank(7).bitcast(BF16)
                    for qt in range(nqt):
                        P.add("dve", I("scalar_tensor_tensor", out=obn4[:r, qt, :], in0=obS[:r, qt, :],
                                       scalar=stat[:r, s_, 8 + qt:9 + qt], in1=sublnB[:r, :], op0=ALU.mult, op1=ALU.mult),
                              reads=[("obS", qt), (sk, 9)], writes=[("obn", qt)])
                    P.add("pe", [I("transpose", out=ptb[:, qt * 128:qt * 128 + r], in_=obn4[:r, qt, :], identity=ident[:r, :r])
                                 for qt in range(nqt)], reads=[("obn", qt) for qt in range(nqt)], writes=[("bank", 7)])
                    P.add("dve", I("tensor_copy", out=out_bT[:, hd, q0:q0 + nq], in_=ptb[:, 0:nq]),
                          reads=[("bank", 7)], writes=[("obT", hd, q0)])
                return fin, fin2

            pending_fin = []
            for (q0, nq) in qblocks:
                nqt = (nq + 127) // 128
                r = min(128, nq)
                pq = []
                for kb in range(NKB + 2):
                    if kb == 14 and pending_fin:
                        f2_, args_ = pending_fin.pop(0)
                        f2_(*args_)
                    pend = None
                    if kb < NKB:
                        sb = (kb % 2) * 2
                        P.add("pe", [I("matmul", psum[:, sb, 0:nq], lhsT=Kt[0:64, kb * 128:(kb + 1) * 128],
                                       rhs=Qt[0:64, q0:q0 + nq], start=True, stop=True),
                                     I("matmul", psum[:, sb + 1, 0:nq], lhsT=Kt[64:128, kb * 128:(kb + 1) * 128],
                                       rhs=Qt[64:128, q0:q0 + nq], start=True, stop=True)],
                              reads=allK + allQ, writes=[("bank", sb), ("bank", sb + 1)])
                        psl = cnt["pt"] % NPT
                        cnt["pt"] += 1
                        P.add("act", I("activation", out=Pt[psl][:, :, 0:nq], in_=psum[:, sb:sb + 2, 0:nq], func=AF.Exp,
                                       scale=0.125), reads=[("bank", sb), ("bank", sb + 1)], writes=[("Pt", psl)])
                        pq.append((kb, psl))
                    if pq and (len(pq) > 2 or kb >= NKB):
                        pend = pq.pop(0)
                    if pend is not None:
                        pkb, pps = pend
                        pv = []
                        seen_banks = set()
                        for j in range(2):
                            for qt in range(nqt):
                                ai = j * 4 + qt
                                first_in_bank = (ai // 3) not in seen_banks
                                seen_banks.add(ai // 3)
                                pv.append(I("matmul", acc(j, qt)[0:r, :], lhsT=Pt[pps][:, j, qt * 128:qt * 128 + r],
                                            rhs=Vaug[:, pkb, 0:129], start=(pkb == 0 and first_in_bank),
                                            stop=(pkb == NKB - 1), skip_group_check=True))
                        P.add("pe", pv, reads=[("Pt", pps)] + allV, writes=accs)
                for bi in range(3):
                    ncol = 387 if bi < 2 else 258
                    P.add("dve", I("tensor_copy", out=accS[:r, bi, 0:ncol], in_=psum[0:r, 4 + bi, 0:ncol]),
                          reads=[("bank", 4 + bi)], writes=[("accS", bi)])
                f1_, f2_ = make_fin(q0, nq, nqt, r)
                pending_fin.append((f2_, f1_()))
            while pending_fin:
                f2_, args_ = pending_fin.pop(0)
                f2_(*args_)

    def phase_C(sq_name, ri):
        xd = xseq[sq_name]
        roff = ri * RNG
        sidx = 0 if sq_name == "s" else 1
        ar = Arena(arena_ap, ARENA_W)
        h = ar.alloc((9, D), F32)
        xnT = ar.alloc((8, RNG + 2), BF16)
        uT = ar.alloc((4, RNG + 2), BF16)
        oaT = ar.alloc((4, RNG + 2), BF16)
        hTall = ar.alloc((8, RNG + 2), BF16)
        hT = [hTall[:, 0:4, :], hTall[:, 4:8, :]]
        mTb = hTall
        xsb = [ar.alloc((D,), BF16) for _ in range(9)]
        uni = ar.alloc((4640,), F32)
        gv2 = [uni[:, i * 512:(i + 1) * 512] for i in range(3)]
        sq2 = [uni[:, 1536 + i * 512:1536 + (i + 1) * 512] for i in range(3)]
        mixt2 = [uni[:, 3072 + i * 512:3072 + (i + 1) * 512].rearrange("p (g n) -> p g n", g=4) for i in range(3)]
        vb2 = [ar.alloc((512,), BF16) for _ in range(3)]
        mixt = mixt2[0]
        rl = [ar.alloc((512,), F32) for _ in range(2)]
        bgs = uni[:, 0:RNG]
        cgs = uni[:, 1024:1024 + RNG + 2]
        tb = uni[:, 2056:2056 + RNG + 2]
        yb = uni[:, 3088:3088 + RNG]
        ring = WRing(ar)
        cnt = {"xs": 0, "rl": 0, "hT": 0}
        tiles = [(t, 128) for t in range(8)] + [(8, 2)]
        cblocks = [(0, 512), (512, 512), (RNG, 2)]

        def tile_cols(t):
            return (t * 128, 128) if t < 8 else (RNG, 2)

        def hkey(t):
            return ("h", t)

        for t in range(8):
            P.add("sp", I("dma_start", out=h[:, t, :], in_=xd[roff + t * 128:roff + (t + 1) * 128, :]),
                  writes=[hkey(t)], dma=("hl", t))
        P.add("sp", I("dma_start", out=h[0:2, 8, :], in_=xtail[sq_name][2 * ri:2 * ri + 2, :]),
              writes=[hkey(8)], dma=("hl", 8))

        def do_norm(layer, tl):
            nt = len(tl)
            for (t, r) in tl:
                P.add("act", I("activation", out=junk[:r, :], in_=h[0:r, t, :], func=AF.Square,
                               accum_out=statC[:r, 0, t:t + 1]), reads=[hkey(t)], writes=[("sC", 0, t)])
            pool_rstd(statC[:, 0, 0:nt], statC[:, 2, 0:nt], 128, nt, 3, [("sC", 0, t) for (t, r) in tl], ("sC", 2))
            for i, (t, r) in enumerate(tl):
                rs_ap = statC[:r, 2, t:t + 1]
                src = h[0:r, t, :]
                if i % 3 == 0:
                    P.add("act", I("activation", out=xsb[t][:r, :], in_=src, func=AF.Identity, scale=rs_ap),
                          reads=[hkey(t), ("sC", 2)], writes=[("xsbC", t)])
                elif i % 3 == 1:
                    P.add("dve", I("tensor_scalar", out=xsb[t][:r, :], in0=src, scalar1=rs_ap, scalar2=None, op0=ALU.mult),
                          reads=[hkey(t), ("sC", 2)], writes=[("xsbC", t)])
                else:
                    P.add("pool", I("tensor_tensor", out=xsb[t][:r, :], in0=src, in1=rs_ap.broadcast_to([r, D]), op=ALU.mult),
                          reads=[hkey(t), ("sC", 2)], writes=[("xsbC", t)])
            for (t, r) in tl:
                c0, n = tile_cols(t)
                bnk = next_pt()
                ptv = bank(bnk).bitcast(BF16).rearrange("p (c n) -> p c n", c=8)
                P.add("pe", [I("transpose", out=ptv[:, c, 0:r], in_=xsb[t][:r, c * 128:(c + 1) * 128], identity=ident[:r, :r])
                             for c in range(8)], reads=[("xsbC", t)], writes=[("bank", bnk)])
                P.add("dve", I("tensor_tensor", out=xnT[:, :, c0:c0 + n], in0=ptv[:, :, 0:r],
                               in1=gcols[:, layer, :].unsqueeze(2).broadcast_to([128, 8, r]), op=ALU.mult),
                      reads=[("bank", bnk)], writes=[("xnT", t)])

        def xkeys_cb(cb):
            c0, n = cb
            if n == 2:
                return [("xnT", 8)]
            return [("xnT", t) for t in range(c0 // 128, c0 // 128 + 4)]

        def resid_add(t, r, half, b):
            P.add("dve", I("tensor_tensor", out=h[0:r, t, half * 512:(half + 1) * 512],
                           in0=h[0:r, t, half * 512:(half + 1) * 512], in1=bank(b)[0:r, :], op=ALU.add),
                  reads=[("bank", b), hkey(t)], writes=[hkey(t)])

        do_norm(0, tiles)
        wau_sl, waukey = ring.load([(v8, ewin[:, :, 0:512])])
        wav_sl, wavkey = ring.load([(v8, ewin[:, :, 512:1024])])
        Wau = v8(wau_sl)
        Wav = v8(wav_sl)
        wo = []
        for half in range(2):
            sl, k = ring.load([(v4, ewout[:, half * 4:(half + 1) * 4, :])])
            wo.append((v4(sl), k))
        for cb in cblocks:
            c0, n = cb
            for g in range(4):
                b = next_ps()
                P.add("pe", mm(bank(b)[:, 0:n], [Wau[:, c, g * 128:(g + 1) * 128] for c in range(8)],
                               [xnT[:, c, c0:c0 + n] for c in range(8)]),
                      reads=xkeys_cb(cb) + [waukey], writes=[("bank", b)])
                P.add("act", I("activation", out=uT[:, g, c0:c0 + n], in_=bank(b)[:, 0:n], func=AF.Gelu_apprx_tanh),
                      reads=[("bank", b)], writes=[("uT", g, c0)])

        def ukeys(c0):
            cc = RNG if c0 >= RNG else (c0 // 512) * 512
            return [("uT", g, cc) for g in range(4)]
        gst = {}

        def g_a1(t):
            b = next_ps()
            P.add("pe", mm(bank(b), [xnT[:, c, t * 128:(t + 1) * 128] for c in range(8)], [Wav[:, c, :] for c in range(8)]),
                  reads=[("xnT", t), wavkey], writes=[("bank", b)])
            kx = t % 3
            gst[t] = gmlp_v1(bank(b), ("bank", b), 128, gv2[kx], sq2[kx], kx=kx)

        def g_a2(t):
            kx = t % 3
            gmlp_v2(gst[t], 128, gv2[kx], vb2[kx], ("vb", kx), kx=kx)
            b2 = next_ps()
            P.add("pe", [I("matmul", bank(b2)[:, g * 128:(g + 1) * 128], lhsT=vb2[kx][:, g * 128:(g + 1) * 128],
                           rhs=wsT[:, g, :], start=True, stop=True) for g in range(4)],
                  reads=[("vb", kx)], writes=[("bank", b2)])
            gst[("b2", t)] = b2

        def g_b(t):
            kx = t % 3
            b2 = gst[("b2", t)]
            mx = mixt2[kx]
            P.add("dve", I("tensor_tensor", out=mx, in0=bank(b2).rearrange("p (g n) -> p g n", g=4), in1=bsB, op=ALU.add),
                  reads=[("bank", b2)], writes=[("mixt", kx)])
            P.add("dve", I("tensor_tensor", out=oaT[:, :, t * 128:(t + 1) * 128], in0=mx,
                           in1=uT[:, :, t * 128:(t + 1) * 128], op=ALU.mult),
                  reads=[("mixt", kx)] + ukeys(t * 128), writes=[("oaT", t)])

        if os.environ.get("KGSEQ"):
            for t in range(8):
                g_a1(t)
                g_a2(t)
                g_b(t)
        else:
            g_a1(0)
            for t in range(8):
                if t + 1 < 8:
                    g_a1(t + 1)
                g_a2(t)
                if t >= 1:
                    g_b(t - 1)
            g_b(7)
        for j in range(2):
            ti = 2 * ri + j
            pcol = 127 if j == 0 else 0
            b2 = next_ps()
            P.add("pe", [I("matmul", bank(b2)[:, g:g + 1], lhsT=vstash[:, ti, g * 128:(g + 1) * 128],
                           rhs=wsT[:, g, pcol:pcol + 1], start=True, stop=True) for g in range(4)],
                  reads=[], writes=[("bank", b2)])
            P.add("dve", I("tensor_tensor", out=mixt[:, :, 0], in0=bank(b2)[:, 0:4], in1=bsB[:, :, pcol], op=ALU.add),
                  reads=[("bank", b2)], writes=[("mixt", 0)])
            P.add("dve", I("tensor_tensor", out=oaT[:, :, RNG + j], in0=mixt[:, :, 0], in1=uT[:, :, RNG + j], op=ALU.mult),
                  reads=[("mixt", 0)] + ukeys(RNG), writes=[("oaT", 8, j)])
        for (t, r) in tiles:
            c0, n = tile_cols(t)
            for half in range(2):
                b = next_ps()
                lhs = []
                for c in range(8):
                    if c < 4:
                        lhs.append(oaT[:, c, c0:c0 + r])
                    elif t < 8:
                        lhs.append(out_bT[:, c - 4, roff + c0:roff + c0 + r])
                    else:
                        lhs.append(out_bT[:, c - 4, OWN + 2 * ri:OWN + 2 * ri + 2])
                okeys = [("oaT", t)] if t < 8 else [("oaT", 8, 0), ("oaT", 8, 1)]
                P.add("pe", mm(bank(b)[0:r, :], lhs, [wo[c // 4][0][:, c % 4, half * 512:(half + 1) * 512] for c in range(8)]),
                      reads=okeys + [wo[0][1], wo[1][1]], writes=[("bank", b)])
                resid_add(t, r, half, b)

        ALLHT = [("hT", hs) for hs in range(2)]

        def ffn(layer, tl, cbl):
            do_norm(1 + 2 * layer, tl)

            def ld(fg):
                return (ring.load([(v8, w1d[layer][:, :, fg * 512:(fg + 1) * 512])]),
                        ring.load([(v4, w2d[layer][:, fg * 4:(fg + 1) * 4, :])]))
            nxt = ld(0)
            for fg in range(8):
                (s1, k1), (s2, k2) = nxt
                if fg + 1 < 8:
                    nxt = ld(fg + 1)
                W1 = v8(s1)
                W2 = v4(s2)
                hs = cnt["hT"] % 2
                cnt["hT"] += 1
                hb = hT[hs]
                for f in range(4):
                    for cb in cbl:
                        c0, n = cb
                        b = next_ps()
                        P.add("pe", mm(bank(b)[:, 0:n], [W1[:, c, f * 128:(f + 1) * 128] for c in range(8)],
                                       [xnT[:, c, c0:c0 + n] for c in range(8)]),
                              reads=xkeys_cb(cb) + [k1], writes=[("bank", b)])
                        rsl = cnt["rl"] % 2
                        cnt["rl"] += 1
                        P.add("act", I("activation", out=rl[rsl][:, 0:n], in_=bank(b)[:, 0:n], func=AF.Relu),
                              reads=[("bank", b)], writes=[("rl", rsl)])
                        P.add("dve", I("tensor_tensor", out=hb[:, f, c0:c0 + n], in0=rl[rsl][:, 0:n], in1=rl[rsl][:, 0:n],
                                       op=ALU.mult), reads=[("rl", rsl)], writes=[("hTw", hs, f, c0)])
                hkeys = [("hTw", hs, f, c0) for f in range(4) for (c0, n) in cbl]
                first = True
                for (t, r) in tl:
                    c0, n = tile_cols(t)
                    for half in range(2):
                        b = next_ps()
                        P.add("pe", mm(bank(b)[0:r, :], [hb[:, f, c0:c0 + r] for f in range(4)],
                                       [W2[:, f, half * 512:(half + 1) * 512] for f in range(4)]),
                              reads=hkeys + [k2], writes=[("bank", b)])
                        resid_add(t, r, half, b)

        ffn(0, tiles, cblocks)

        do_norm(2, tiles)
        m0 = sidx * 4 + 2 * ri
        for c in range(8):
            wsl, wk = ring.load([(lambda sl, j=j: v3(sl)[:, j], cwin[:, :, j * D + c * 128:j * D + (c + 1) * 128])
                                 for j in range(3)])
            Wc = v3(wsl)
            for cb in cblocks:
                c0, n = cb
                xk = xkeys_cb(cb)
                xr = [xnT[:, k, c0:c0 + n] for k in range(8)]
                if n > 2:
                    b = next_ps()
                    P.add("pe", mm(bank(b)[:, 0:n], [Wc[:, 0, k, :] for k in range(8)], xr),
                          reads=xk + [wk], writes=[("bank", b)])
                    P.add("act", I("activation", out=bgs[:, c0:c0 + n], in_=bank(b)[:, 0:n], func=AF.Copy),
                          reads=[("bank", b)], writes=[("bgs", c0)])
                b = next_ps()
                P.add("pe", mm(bank(b)[:, 0:n], [Wc[:, 1, k, :] for k in range(8)], xr),
                      reads=xk + [wk], writes=[("bank", b)])
                P.add("act", I("activation", out=cgs[:, c0:c0 + n], in_=bank(b)[:, 0:n], func=AF.Copy),
                      reads=[("bank", b)], writes=[("cgs", c0)])
                b = next_ps()
                P.add("pe", mm(bank(b)[:, 0:n], [Wc[:, 2, k, :] for k in range(8)], xr),
                      reads=xk + [wk], writes=[("bank", b)])
                if n > 2:
                    P.add("dve", I("tensor_tensor", out=tb[:, 1 + c0:1 + c0 + n], in0=bank(b)[:, 0:n], in1=cgs[:, c0:c0 + n],
                                   op=ALU.mult), reads=[("bank", b), ("cgs", c0)], writes=[("tb", c0)])
                else:
                    P.add("dve", I("tensor_tensor", out=cgs[:, RNG:RNG + 2], in0=bank(b)[:, 0:2], in1=cgs[:, RNG:RNG + 2],
                                   op=ALU.mult), reads=[("bank", b), ("cgs", RNG)], writes=[("cgs2", RNG)])
                    P.add("dve", I("tensor_tensor", out=tb[:, 0:1], in0=cgs[:, RNG:RNG + 1], in1=maskB[:, m0:m0 + 1],
                                   op=ALU.mult), reads=[("cgs2", RNG)], writes=[("tb", "l")])
                    P.add("dve", I("tensor_tensor", out=tb[:, RNG + 1:RNG + 2], in0=cgs[:, RNG + 1:RNG + 2],
                                   in1=maskB[:, m0 + 1:m0 + 2], op=ALU.mult), reads=[("cgs2", RNG)], writes=[("tb", "r")])
            tkeys = [("tb", 0), ("tb", 512), ("tb", "l"), ("tb", "r")]
            P.add("dve", I("tensor_scalar", out=yb, in0=tb[:, 1:RNG + 1], scalar1=cw[:, 1, c:c + 1], scalar2=None,
                           op0=ALU.mult), reads=tkeys, writes=["yb"])
            P.add("dve", I("scalar_tensor_tensor", out=yb, in0=tb[:, 0:RNG], scalar=cw[:, 0, c:c + 1], in1=yb,
                           op0=ALU.mult, op1=ALU.add), reads=tkeys + ["yb"], writes=["yb"])
            P.add("dve", I("scalar_tensor_tensor", out=yb, in0=tb[:, 2:RNG + 2], scalar=cw[:, 2, c:c + 1], in1=yb,
                           op0=ALU.mult, op1=ALU.add), reads=tkeys + ["yb"], writes=["yb"])
            P.add("dve", I("tensor_tensor", out=mTb[:, c, 0:RNG], in0=bgs, in1=yb, op=ALU.mult),
                  reads=["yb", ("bgs", 0), ("bgs", 512)],
                  writes=[("mT", c), ("hTw", c // 4, c % 4, 0), ("hTw", c // 4, c % 4, 512)])
        wco = []
        for half in range(2):
            sl, k = ring.load([(v4, cwout[:, half * 4:(half + 1) * 4, :])])
            wco.append((v4(sl), k))
        mkeys = [("mT", c) for c in range(8)] + [("hTw", hs_, f_, c0_) for hs_ in range(2) for f_ in range(4) for c0_ in (0, 512)]
        own_tiles = tiles[:8]
        for (t, r) in own_tiles:
            for half in range(2):
                b = next_ps()
                P.add("pe", mm(bank(b), [mTb[:, c, t * 128:(t + 1) * 128] for c in range(8)],
                               [wco[c // 4][0][:, c % 4, half * 512:(half + 1) * 512] for c in range(8)]),
                      reads=mkeys + [wco[0][1], wco[1][1]], writes=[("bank", b)])
                resid_add(t, r, half, b)

        ffn(1, own_tiles, cblocks[:2])

        ystage = [bgs, yb]
        yskeys = [[("bgs", 0), ("bgs", 512)], ["yb"]]
        for half in range(2):
            for t in range(half * 4, half * 4 + 4):
                P.add("act", I("activation", out=junk[:, :], in_=h[:, t, :], func=AF.Square,
                               accum_out=statC[:, 0, t:t + 1]), reads=[hkey(t)], writes=[("sC", 0, t)])
            pool_rstd(statC[:, 0, half * 4:half * 4 + 4], statC[:, 2, half * 4:half * 4 + 4], 128, 4, 3,
                      [("sC", 0, t) for t in range(half * 4, half * 4 + 4)], ("sCf", half))
        for (t, r) in own_tiles:
            rs = statC[:, 2, t:t + 1]
            rsk = ("sCf", t // 4)
            ys = t % 2
            P.add("dve", I("scalar_tensor_tensor", out=ystage[ys], in0=h[:, t, :], scalar=rs, in1=fgB,
                           op0=ALU.mult, op1=ALU.mult), reads=[hkey(t), rsk], writes=yskeys[ys])
            P.add("sp", I("dma_start", out=yout[sq_name][roff + t * 128:roff + (t + 1) * 128, :], in_=ystage[ys]),
                  reads=yskeys[ys], writes=[("yout", sq_name, ri, t)], dma=("yst", ys))

    import os
    dbg = os.environ.get("KDBG", "")
    for sq_name in ("s", "p"):
        if dbg and sq_name not in dbg:
            continue
        if not dbg or "A" in dbg:
            phase_AB(sq_name)
            P.barrier()
        if not dbg or "C" in dbg:
            phase_C(sq_name, 0)
            P.barrier()
        if not dbg or "D" in dbg:
            phase_C(sq_name, 1)
            P.barrier()
    for e in Prog.ENGS:
        P.add(e, None)
    stuck, per_, pos_ = P.simulate()
    if stuck:
        for e, (p_, n_) in stuck.items():
            op = per_[e][p_]
            print("DEADLOCK", e, p_, n_, "op idx", op.idx, "waits", [(d.eng, d.idx, d.dma, d.tick, d.signal) for d in op.waits])
        raise RuntimeError("semaphore protocol deadlock")
    print("program ops:", len(P.ops), {e: len(v) for e, v in per_.items()})
    if os.environ.get("KDUMP"):
        for op in P.ops:
            print(op.idx, op.eng, op.name, "tick", op.tick if (op.signal or op.dma) else None,
                  "W:", [(d.eng, d.idx, d.tick) for d in op.waits], "w=", op.rw[1][:3])
    P.emit(nc, stack)
    stack.close()
    return nc


_NC_CACHE = {}


def _host_constants():
    ident = np.eye(128, dtype=np.float32)
    rt = np.zeros((128, 128), np.float32)
    invf = np.zeros(128, np.float32)
    sgn = np.zeros(128, np.float32)
    inv = (500000.0 ** (-np.arange(0, 16, 2, dtype=np.float32) / 16.0)).astype(np.float32)
    for base in (0, 64):
        for d in range(16):
            p = base + d
            partner = p + 8 if d < 8 else p - 8
            rt[partner, p] = 1.0
            invf[p] = inv[d % 8] / np.float32(2 * np.pi)
            sgn[p] = -1.0 if d < 8 else 1.0
    rc = np.stack([invf, (-2.0 * np.pi * sgn).astype(np.float32)], axis=1).astype(np.float32)
    return ident, rt, rc


def kernel(**inputs):
    f32 = lambda a: np.ascontiguousarray(np.asarray(a, dtype=np.float32))
    xpr = f32(inputs["x_prompt"])
    xsa = f32(inputs["x_sample"])
    ident, rt, rc = _host_constants()
    gains = np.stack([f32(inputs["norm_mix_g"])[0], f32(inputs["norm_ffn_g"])[0],
                      f32(inputs["norm_mix_g"])[1], f32(inputs["norm_ffn_g"])[1]], axis=0)
    gcols = np.ascontiguousarray(gains.reshape(4, 8, 128).transpose(2, 0, 1).reshape(128, 32))
    cwh = np.ascontiguousarray(f32(inputs["c_conv_w"])[0].reshape(3, 8, 128).transpose(2, 0, 1).reshape(128, 24))
    lvec = np.concatenate([f32(inputs["b_lq1"])[0], f32(inputs["b_lk1"])[0],
                           f32(inputs["b_lq2"])[0], f32(inputs["b_lk2"])[0]])[None, :]
    shared = {
        "e_w_in": f32(inputs["e_w_in"])[0], "e_w_out": f32(inputs["e_w_out"])[0],
        "ffn_w1_0": f32(inputs["ffn_w1"])[0], "ffn_w1_1": f32(inputs["ffn_w1"])[1],
        "ffn_w2_0": f32(inputs["ffn_w2"])[0], "ffn_w2_1": f32(inputs["ffn_w2"])[1],
        "c_w_in": f32(inputs["c_w_in"])[0], "c_w_out": f32(inputs["c_w_out"])[0],
        "wsT": np.ascontiguousarray(f32(inputs["a_w_s"])[0].transpose(0, 2, 1)),
        "ident": ident, "rt": rt, "gcols": gcols, "cw": cwh, "rc": rc, "lvec": np.ascontiguousarray(lvec),
        "vng": f32(inputs["a_vnorm_g"]).reshape(1, 512), "bs": f32(inputs["a_b_s"]).reshape(1, 512),
        "subln": f32(inputs["b_subln_g"]).reshape(1, 128), "fg": f32(inputs["final_g"]).reshape(1, D),
    }
    in_maps = []
    meta = []
    for c in range(NCORES):
        bp, op_ = c // 2, (c % 2) * OWN
        bs_, os_ = c // 4, (c % 4) * OWN
        m = dict(shared)
        masks = np.zeros((1, 8), np.float32)
        for key, x, b, off, S, mi in (("s", xsa, bs_, os_, SEQ_S, 0), ("p", xpr, bp, op_, SEQ_P, 4)):
            xr = np.ascontiguousarray(np.roll(x[b], -off, axis=0))
            tidx = np.array([S - 1, RNG, RNG - 1, OWN])
            pos = ((np.arange(S) + off) % S).astype(np.float32)
            post = ((tidx + off) % S).astype(np.float32)
            m["x" + key] = xr
            m["xt" + key] = np.ascontiguousarray(xr[tidx])
            m["pos" + key] = np.concatenate([pos, post])[None, :].astype(np.float32)
            masks[0, mi + 0] = 1.0 if off > 0 else 0.0
            masks[0, mi + 1] = 1.0
            masks[0, mi + 2] = 1.0
            masks[0, mi + 3] = 1.0 if off + OWN < S else 0.0
        m["masks"] = masks
        in_maps.append(m)
        meta.append((bp, op_, bs_, os_))
    if "nc" not in _NC_CACHE:
        _NC_CACHE["nc"] = build_program()
    import os as _os
    ncr = int(_os.environ.get("KCORES", NCORES))
    if _os.environ.get("KTRACE"):
        res = run_bass_kernel_spmd(_NC_CACHE["nc"], in_maps[:ncr], core_ids=list(range(ncr)), trace=True)
        print("KTRACE exec_time_ns", res.exec_time_ns)
    else:
        res = run_bass_kernel_spmd(_NC_CACHE["nc"], in_maps[:ncr], core_ids=list(range(ncr)))
    y_p = np.zeros_like(xpr)
    y_s = np.zeros_like(xsa)
    for c, (bp, op_, bs_, os_) in enumerate(meta[:ncr]):
        y_p[bp, op_:op_ + OWN] = res.results[c]["yp"]
        y_s[bs_, os_:os_ + OWN] = res.results[c]["ys"]
    return (y_p, y_s)
```

```python
import math
import os
import numpy as np
import concourse.bass as bass
import concourse.mybir as mybir
from concourse.bass_utils import run_bass_kernel_spmd

F32 = mybir.dt.float32
BF16 = mybir.dt.bfloat16
I32 = mybir.dt.int32
AF = mybir.ActivationFunctionType
ALU = mybir.AluOpType
AX = mybir.AxisListType

D = 1024
DFF = 4096
EPS = 1e-5
NCORES = 8
SEQ_S = 8192
SEQ_P = 4096
OWN = 2048
RNG = 1024
LAM_INIT = 0.8 - 0.6 * math.exp(0.0)
TWO_PI = 2.0 * math.pi


class Op:
    __slots__ = ("eng", "fn", "waits", "signal", "tick", "dma", "idx", "name", "rw")


def I(name, *args, **kw):
    return (name, args, kw)


def _mkfn(spec):
    if spec is None:
        return None
    if callable(spec):
        return spec
    if isinstance(spec, tuple):
        spec = [spec]

    def f(e, spec=spec):
        ins = None
        for (name, args, kw) in spec:
            ins = getattr(e, name)(*args, **kw)
        return ins
    return f


class Prog:
    ENGS = ("pe", "act", "dve", "pool", "sp")

    def __init__(self):
        self.ops = []
        self.last_w = {}
        self.readers = {}
        self.dma_count = {}
        self.total_groups = set()
        self.extra = {e: [] for e in self.ENGS}
        self.dma_last = {}

    def barrier(self):
        lasts = {}
        for op in reversed(self.ops):
            if op.dma is None and op.eng not in lasts and op.fn is not None:
                lasts[op.eng] = op
        deps = list(lasts.values()) + list(self.dma_last.values())
        for e in self.ENGS:
            self.extra[e] = [d for d in deps if not (d.dma is None and d.eng == e)]
        for d in lasts.values():
            d.signal = True
        self.last_w = {}
        self.readers = {}

    def add(self, eng, fn, reads=(), writes=(), dma=None, wait_total=False):
        op = Op()
        op.eng = eng
        op.fn = _mkfn(fn)
        try:
            op.name = fn[0] if isinstance(fn, tuple) else (fn[-1][0] + "x%d" % len(fn) if isinstance(fn, list) else str(fn))
        except Exception:
            op.name = "?"
        op.signal = False
        op.tick = None
        op.dma = dma
        op.idx = len(self.ops)
        bank_r = [k for k in reads if isinstance(k, tuple) and k and k[0] == "bank" and k not in writes]
        if bank_r:
            writes = list(writes) + bank_r
        deps = {}
        for k in reads:
            w = self.last_w.get(k)
            if w is not None:
                deps[w.idx] = (w, True)
        for k in writes:
            w = self.last_w.get(k)
            if w is not None and w.idx not in deps:
                deps[w.idx] = (w, False)
            for r in self.readers.get(k, ()):
                if r.idx not in deps:
                    deps[r.idx] = (r, False)
        waits = []
        for d, raw in deps.values():
            if d.dma is None and d.eng == eng:
                if eng == "pe" or not raw:
                    continue
            waits.append(d)
            if d.dma is None:
                d.signal = True
        if self.extra[eng]:
            waits = waits + self.extra[eng]
            self.extra[eng] = []
        op.waits = waits
        op.rw = (list(reads), list(writes))
        for k in reads:
            self.readers.setdefault(k, []).append(op)
        for k in writes:
            self.last_w[k] = op
            self.readers[k] = []
        if dma is not None:
            self.dma_count[dma] = self.dma_count.get(dma, 0) + 1
            op.tick = 16 * self.dma_count[dma]
            self.dma_last[dma] = op
            if wait_total:
                self.total_groups.add(dma)
        self.ops.append(op)
        return op

    def simulate(self):
        cnt = {e: 0 for e in self.ENGS}
        for op in self.ops:
            if op.dma is None and op.signal:
                cnt[op.eng] += 1
                op.tick = cnt[op.eng]
        per = {e: [op for op in self.ops if op.eng == e] for e in self.ENGS}
        pos = {e: 0 for e in self.ENGS}
        sem = {}
        progress = True
        while progress:
            progress = False
            for e in self.ENGS:
                while pos[e] < len(per[e]):
                    op = per[e][pos[e]]
                    ok = True
                    for d in op.waits:
                        if d.dma is not None:
                            key = ("d", d.dma)
                            val = 16 * self.dma_count[d.dma] if d.dma in self.total_groups else d.tick
                        else:
                            key = ("e", d.eng)
                            val = d.tick
                        if sem.get(key, 0) < val:
                            ok = False
                            break
                    if not ok:
                        break
                    if op.fn is not None:
                        if op.dma is not None:
                            sem[("d", op.dma)] = sem.get(("d", op.dma), 0) + 16
                        elif op.signal:
                            sem[("e", op.eng)] = sem.get(("e", op.eng), 0) + 1
                    pos[e] += 1
                    progress = True
        stuck = {e: (pos[e], len(per[e])) for e in self.ENGS if pos[e] < len(per[e])}
        return stuck, per, pos

    def emit(self, nc, stack):
        cnt = {e: 0 for e in self.ENGS}
        for op in self.ops:
            if op.dma is None and op.signal:
                cnt[op.eng] += 1
                op.tick = cnt[op.eng]
        sems = {e: stack.enter_context(nc.semaphore("s_" + e)) for e in self.ENGS}
        dsems = {}
        for i, g in enumerate(self.dma_count):
            dsems[g] = stack.enter_context(nc.semaphore("d%d" % i))
        per = {e: [] for e in self.ENGS}
        for op in self.ops:
            per[op.eng].append(op)
        block = stack.enter_context(nc.Block())

        def run(engh, ops):
            seen = {}
            for op in ops:
                need = {}
                for d in op.waits:
                    if d.dma is not None:
                        key = ("d", d.dma)
                        val = 16 * self.dma_count[d.dma] if d.dma in self.total_groups else d.tick
                    else:
                        key = ("e", d.eng)
                        val = d.tick
                    if val > need.get(key, 0):
                        need[key] = val
                for key, val in need.items():
                    if seen.get(key, 0) >= val:
                        continue
                    seen[key] = val
                    sem = dsems[key[1]] if key[0] == "d" else sems[key[1]]
                    engh.wait_ge(sem, val)
                if op.fn is None:
                    continue
                ins = op.fn(engh)
                if op.dma is not None:
                    ins.then_inc(dsems[op.dma], 16)
                elif op.signal:
                    ins.then_inc(sems[op.eng], 1)

        @block.tensor
        def _(e):
            run(e, per["pe"])

        @block.scalar
        def _(e):
            run(e, per["act"])

        @block.vector
        def _(e):
            run(e, per["dve"])

        @block.gpsimd
        def _(e):
            run(e, per["pool"])

        @block.sync
        def _(e):
            run(e, per["sp"])


class Arena:
    def __init__(self, ap, nwords):
        self.ap = ap
        self.n = nwords
        self.off = 0

    def alloc(self, free_shape, dtype):
        nel = 1
        for s in free_shape:
            nel *= s
        bpe = 2 if dtype == BF16 else 4
        words = (nel * bpe + 3) // 4
        words = (words + 7) // 8 * 8
        assert self.off + words <= self.n, ("arena overflow", self.off, words, self.n)
        v = self.ap[:, self.off:self.off + words]
        self.off += words
        if dtype != F32:
            v = v.bitcast(dtype)
        v = v[:, 0:nel]
        if len(free_shape) == 2:
            v = v.rearrange("p (a b) -> p a b", a=free_shape[0])
        elif len(free_shape) == 3:
            v = v.rearrange("p (a b c) -> p a b c", a=free_shape[0], b=free_shape[1])
        return v


def build_program():
    from contextlib import ExitStack
    nc = bass.Bass("TRN2", target_bir_lowering=False)
    stack = ExitStack()
    P = Prog()

    def din(name, shape):
        return nc.dram_tensor(name, list(shape), F32, kind="ExternalInput").ap()

    xseq = {"s": din("xs", (SEQ_S, D)), "p": din("xp", (SEQ_P, D))}
    xtail = {"s": din("xts", (4, D)), "p": din("xtp", (4, D))}
    posd = {"s": din("poss", (1, SEQ_S + 4)), "p": din("posp", (1, SEQ_P + 4))}
    maskd = din("masks", (1, 8))
    ewin = din("e_w_in", (D, 2560)).rearrange("(c p) n -> p c n", p=128)
    ewout = din("e_w_out", (D, D)).rearrange("(c p) n -> p c n", p=128)
    w1d = [din("ffn_w1_%d" % l, (D, DFF)).rearrange("(c p) n -> p c n", p=128) for l in range(2)]
    w2d = [din("ffn_w2_%d" % l, (DFF, D)).rearrange("(f p) n -> p f n", p=128) for l in range(2)]
    cwin = din("c_w_in", (D, 3 * D)).rearrange("(c p) n -> p c n", p=128)
    cwout = din("c_w_out", (D, D)).rearrange("(c p) n -> p c n", p=128)
    wstd = din("wsT", (4, 128, 128)).rearrange("g q p -> q g p")
    identd = din("ident", (128, 128))
    rtd = din("rt", (128, 128))
    gcolsd = din("gcols", (128, 32))
    cwd = din("cw", (128, 24))
    rcd = din("rc", (128, 2))
    lvecd = din("lvec", (1, 256))
    vngd = din("vng", (1, 512))
    bsd = din("bs", (1, 512))
    sublnd = din("subln", (1, 128))
    fgd = din("fg", (1, D))
    xnd = {"s": nc.dram_tensor("xnd_s", [SEQ_S // 512, 128, 8 * 512], BF16, kind="Internal").ap(),
           "p": nc.dram_tensor("xnd_p", [SEQ_P // 512, 128, 8 * 512], BF16, kind="Internal").ap()}
    tabd = {"s": nc.dram_tensor("tabd_s", [SEQ_S // 512, 128, 2 * 512], F32, kind="Internal").ap(),
            "p": nc.dram_tensor("tabd_p", [SEQ_P // 512, 128, 2 * 512], F32, kind="Internal").ap()}
    yout = {"s": nc.dram_tensor("ys", [OWN, D], F32, kind="ExternalOutput").ap(),
            "p": nc.dram_tensor("yp", [OWN, D], F32, kind="ExternalOutput").ap()}

    ARENA_W = 43520
    PERS_W = 9216
    arena_t = stack.enter_context(nc.sbuf_tensor("arena", [128, ARENA_W], F32))
    pers_t = stack.enter_context(nc.sbuf_tensor("pers", [128, PERS_W], F32))
    psum = stack.enter_context(nc.psum_tensor("ps", [128, 8, 512], F32))
    pers = Arena(pers_t[:], PERS_W)
    arena_ap = arena_t[:]

    out_bT = pers.alloc((4, OWN + 4), BF16)
    ident = pers.alloc((128,), BF16)
    rt = pers.alloc((128,), BF16)
    wsT = pers.alloc((4, 128), BF16)
    gcols = pers.alloc((4, 8), F32)
    cw = pers.alloc((3, 8), F32)
    rc = pers.alloc((2,), F32)
    lvec = pers.alloc((256,), F32)
    ltmp = pers.alloc((64,), F32)
    lsc = pers.alloc((8,), F32)
    gBv = pers.alloc((512,), F32)
    bsB = pers.alloc((4, 128), F32)
    sublnB = pers.alloc((128,), F32)
    fgB = pers.alloc((D,), F32)
    maskB = pers.alloc((8,), F32)
    NSTAT = 16
    stat = pers.alloc((NSTAT, 12), F32)
    vstash = pers.alloc((4, 512), BF16)
    junk = pers.alloc((D,), BF16)
    epsb = pers.alloc((1,), F32)
    halfpi = pers.alloc((1,), F32)
    cst = pers.alloc((8,), F32)
    statC = pers.alloc((3, 12), F32)

    CONST = "const"
    ALLC = []

    def cload(eng, dst, src, name):
        P.add(eng, I("dma_start", out=dst, in_=src), writes=[("c", name)], dma=CONST + "_" + eng, wait_total=True)
        ALLC.append(("c", name))

    cload("pool", ident, identd[:, :], "ident")
    cload("pool", rt, rtd[:, :], "rt")
    cload("pool", wsT, wstd, "wsT")
    cload("sp", gcols, gcolsd.rearrange("p (a b) -> p a b", a=4), "gcols")
    cload("sp", cw, cwd.rearrange("p (a b) -> p a b", a=3), "cw")
    cload("sp", rc, rcd[:, :], "rc")
    cload("sp", lvec, lvecd.partition_broadcast(128), "lvec")
    cload("sp", gBv, vngd.partition_broadcast(128), "gBv")
    cload("sp", bsB, bsd.partition_broadcast(128).rearrange("p o (a b) -> p (o a) b", a=4), "bsB")
    cload("sp", sublnB, sublnd.partition_broadcast(128), "sublnB")
    cload("sp", fgB, fgd.partition_broadcast(128), "fgB")
    cload("sp", maskB, maskd.partition_broadcast(128), "maskB")

    P.add("dve", I("memset", epsb, EPS), writes=["epsb"])
    P.add("dve", I("memset", halfpi, math.pi / 2), writes=["halfpi"])
    P.add("dve", I("memset", cst[:, 0:1], 0.5), writes=["cst0"])
    P.add("dve", I("memset", statC, 1.0), writes=["statC"])
    P.add("dve", I("memset", cst[:, 1:2], 0.0), writes=["cst1"])
    P.add("dve", I("memset", cst[:, 2:3], -0.5), writes=["cst2"])
    P.add("dve", I("memset", cst[:, 3:4], D * EPS), writes=["cst3"])
    P.add("dve", I("memset", cst[:, 4:5], 128 * EPS), writes=["cst4"])
    for j in range(2):
        P.add("dve", I("tensor_tensor", out=ltmp, in0=lvec[:, 128 * j:128 * j + 64],
                       in1=lvec[:, 128 * j + 64:128 * j + 128], op=ALU.mult), reads=ALLC, writes=["ltmp"])
        P.add("dve", I("tensor_reduce", out=lsc[:, j:j + 1], in_=ltmp, axis=AX.X, op=ALU.add),
              reads=["ltmp"], writes=[("lsc", j)])
    P.add("act", I("activation", out=lsc[:, 2:4], in_=lsc[:, 0:2], func=AF.Exp),
          reads=[("lsc", 0), ("lsc", 1)], writes=["lsce"])
    P.add("dve", I("tensor_tensor", out=lsc[:, 4:5], in0=lsc[:, 2:3], in1=lsc[:, 3:4], op=ALU.subtract),
          reads=["lsce"], writes=["lam0"])
    P.add("dve", I("tensor_scalar", out=lsc[:, 5:6], in0=lsc[:, 4:5], scalar1=LAM_INIT, scalar2=-1.0,
                   op0=ALU.add, op1=ALU.mult), reads=["lam0"], writes=["neglam"])
    P.add("dve", I("tensor_scalar", out=sublnB, in0=sublnB, scalar1=(1.0 - LAM_INIT) * math.sqrt(128.0), scalar2=None,
                   op0=ALU.mult), reads=ALLC, writes=["sublnS"])
    P.add("dve", I("tensor_scalar", out=gcols, in0=gcols, scalar1=math.sqrt(float(D)), scalar2=None, op0=ALU.mult),
          reads=ALLC, writes=["gcolsS"])
    P.add("dve", I("tensor_scalar", out=fgB, in0=fgB, scalar1=math.sqrt(float(D)), scalar2=None, op0=ALU.mult),
          reads=ALLC, writes=["fgBS"])
    P.add("dve", I("tensor_scalar", out=gBv, in0=gBv, scalar1=math.sqrt(128.0), scalar2=None, op0=ALU.mult),
          reads=ALLC, writes=["gBvS"])
    P.barrier()

    def pool_rstd(src, dst, r, n, ccol, reads, wkey):
        P.add("pool", I("tensor_tensor", out=dst, in0=src, in1=cst[:r, ccol:ccol + 1].broadcast_to([r, n]), op=ALU.add),
              reads=reads, writes=[(wkey, "a")])
        P.add("pool", I("tensor_tensor", out=dst, in0=dst, in1=cst[:r, 2:3].broadcast_to([r, n]), op=ALU.pow),
              reads=[(wkey, "a")], writes=[wkey])

    state = {"stat": 0, "pt": 0, "ps": 0}

    def new_stat():
        s = state["stat"]
        state["stat"] = (s + 1) % NSTAT
        return s

    def bank(i):
        return psum[:, i, :]

    PS_RING = [0, 1, 2, 3, 6, 7]

    def next_ps():
        s = state["ps"]
        state["ps"] = (s + 1) % len(PS_RING)
        return PS_RING[s]

    def next_pt():
        s = state["pt"]
        state["pt"] = (s + 1) % 2
        return 4 + s

    def rstd_chain(src, r, srckey, scale):
        s = new_stat()
        sk = ("stat", s)
        n = src.shape[-1]
        P.add("act", I("activation", out=junk[:r, 0:n], in_=src, func=AF.Square, accum_out=stat[:r, s, 0:1]),
              reads=[srckey], writes=[(sk, 0)])
        pool_rstd(stat[:r, s, 0:1], stat[:r, s, 2:3], r, 1, 3 if n == D else 4, [(sk, 0)], (sk, 2))
        return stat[:r, s, 2:3], (sk, 2)

    def norm_to_xnT(src, r, srckey, xs_buf, xs_key, layer, dst, dst_key):
        rs, rsk = rstd_chain(src, r, srckey, 1.0 / D)
        P.add("act", I("activation", out=xs_buf[:r, :], in_=src, func=AF.Identity, scale=rs),
              reads=[srckey, rsk], writes=[xs_key])
        b = next_pt()
        ptv = bank(b).bitcast(BF16).rearrange("p (c n) -> p c n", c=8)
        P.add("pe", [I("transpose", out=ptv[:, c, 0:r], in_=xs_buf[:r, c * 128:(c + 1) * 128], identity=ident[:r, :r])
                     for c in range(8)], reads=[xs_key], writes=[("bank", b)])
        P.add("dve", I("tensor_tensor", out=dst, in0=ptv[:, :, 0:r],
                       in1=gcols[:, layer, :].unsqueeze(2).broadcast_to([128, 8, r]), op=ALU.mult),
              reads=[("bank", b)], writes=[dst_key])

    def mm(out_ap, lhs, rhs):
        nk = len(lhs)
        return [I("matmul", out_ap, lhsT=lhs[c], rhs=rhs[c], start=(c == 0), stop=(c == nk - 1)) for c in range(nk)]

    NW = 4
    WSLOT = 4096

    class WRing:
        def __init__(self, ar):
            self.slots = [ar.alloc((WSLOT,), BF16) for _ in range(NW)]
            self.i = 0

        def load(self, pieces):
            s = self.i % NW
            self.i += 1
            sl = self.slots[s]
            key = ("w", s)
            for dv, src in pieces:
                P.add("pool", I("dma_start", out=dv(sl), in_=src), writes=[key], dma=("w", s))
            return sl, key

    def v8(sl):
        return sl.rearrange("p (c n) -> p c n", c=8)

    def v4(sl):
        return sl.rearrange("p (c n) -> p c n", c=4)

    def v3(sl):
        return sl[:, 0:3072].rearrange("p (j c n) -> p j c n", j=3, c=8)

    def rope_tables(posB, n, poskey, Ct, St, A1, A2, tkey):
        A2i = A2.bitcast(I32)
        TE = os.environ.get("KTE", "pool")
        bc = lambda ap: ap.broadcast_to([128, n])
        P.add(TE, I("tensor_tensor", out=A1[:, 0:n], in0=posB[:, 0:n], in1=bc(rc[:, 0:1]), op=ALU.mult),
              reads=[poskey], writes=["tabA1"])
        P.add(TE, I("tensor_copy", out=A2i[:, 0:n], in_=A1[:, 0:n]), reads=["tabA1"], writes=["tabA2"])
        P.add(TE, I("tensor_tensor", out=A1[:, 0:n], in0=A1[:, 0:n], in1=A2i[:, 0:n], op=ALU.subtract),
              reads=["tabA1", "tabA2"], writes=["tabA1"])
        P.add("dve", I("scalar_tensor_tensor", out=A2[:, 0:n], in0=A1[:, 0:n], scalar=0.5, in1=A1[:, 0:n],
                       op0=ALU.is_gt, op1=ALU.subtract), reads=["tabA1"], writes=["tabA2"])
        P.add("act", I("activation", out=St[:, 0:n], in_=A2[:, 0:n], func=AF.Sin, scale=rc[:, 1:2]),
              reads=["tabA2"], writes=[(tkey, "S")])
        P.add("dve", I("scalar_tensor_tensor", out=A1[:, 0:n], in0=A2[:, 0:n], scalar=-1.0, in1=A2[:, 0:n],
                       op0=ALU.mult, op1=ALU.min), reads=["tabA2"], writes=["tabA1"])
        P.add("act", I("activation", out=Ct[:, 0:n], in_=A1[:, 0:n], func=AF.Sin, scale=TWO_PI, bias=halfpi[:, 0:1]),
              reads=["tabA1"], writes=[(tkey, "C")])

    def gmlp_v1(ps_ap, pskey, r, gv, sq, kx=0):
        P.add("act", I("activation", out=gv[:r, :], in_=ps_ap, func=AF.Gelu_apprx_tanh), reads=[pskey],
              writes=[("gv", kx)])
        P.add("dve", I("tensor_tensor", out=sq[:r, :], in0=gv[:r, :], in1=gv[:r, :], op=ALU.mult),
              reads=[("gv", kx)], writes=[("sq", kx)])
        s = new_stat()
        sk = ("stat", s)
        P.add("dve", I("tensor_reduce", out=stat[:r, s, 0:4], in_=sq[:r, :].rearrange("p (g n) -> p g n", g=4),
                       axis=AX.X, op=ALU.add), reads=[("sq", kx)], writes=[(sk, 0)])
        return s

    def gmlp_v2(s, r, gv, vout, vkey, kx=0):
        sk = ("stat", s)
        pool_rstd(stat[:r, s, 0:4], stat[:r, s, 8:12], r, 4, 4, [(sk, 0)], (sk, 2))
        for g in range(4):
            P.add("dve", I("scalar_tensor_tensor", out=vout[:r, g * 128:(g + 1) * 128], in0=gv[:r, g * 128:(g + 1) * 128],
                           scalar=stat[:r, s, 8 + g:9 + g], in1=gBv[:r, g * 128:(g + 1) * 128], op0=ALU.mult, op1=ALU.mult),
                  reads=[("gv", kx), (sk, 2)], writes=[vkey])

    def gmlp_v(ps_ap, pskey, r, gv, sq, vout, vkey, kx=0):
        s = gmlp_v1(ps_ap, pskey, r, gv, sq, kx)
        gmlp_v2(s, r, gv, vout, vkey, kx)

    def phase_AB(sq_name):
        S = SEQ_S if sq_name == "s" else SEQ_P
        NB = S // 512
        NKB = S // 128
        xd = xseq[sq_name]
        ar = Arena(arena_ap, ARENA_W)
        Kt = ar.alloc((S,), BF16)
        Vaug = ar.alloc((NKB, 130), BF16)
        Qt = ar.alloc((OWN + 4,), BF16)
        NXR = 8
        NXS = 4
        xring = [ar.alloc((D,), F32) for _ in range(NXR)]
        xsb = [ar.alloc((D,), BF16) for _ in range(NXS)]
        xnA = [ar.alloc((8, 512), BF16) for _ in range(2)]
        xnTl = ar.alloc((8, 4), BF16)
        kraw = [ar.alloc((512,), BF16) for _ in range(2)]
        t1 = ar.alloc((512,), F32)
        t2 = ar.alloc((512,), F32)
        Ct = [ar.alloc((512,), F32) for _ in range(2)]
        St = [ar.alloc((512,), F32) for _ in range(2)]
        A1 = ar.alloc((512,), F32)
        A2 = ar.alloc((512,), F32)
        posB = [ar.alloc((512,), F32) for _ in range(2)]
        NPT = 4
        Pt = [ar.alloc((2, 512), BF16) for _ in range(NPT)]
        otmp = ar.alloc((128,), F32)
        accS = ar.alloc((3, 388), F32)
        obS = ar.alloc((4, 128), F32)
        obn4 = ar.alloc((4, 128), BF16)
        gv = ar.alloc((512,), F32)
        sq = ar.alloc((512,), F32)
        ring = WRing(ar)
        tail_chunks = [NKB - 1, 8, 7, 16]
        cnt = {"x": 0, "xs": 0, "kraw": 0, "pt": 0}
        accs = [("bank", 4), ("bank", 5), ("bank", 6)]

        def acc(j, qt):
            i = j * 4 + qt
            return psum[:, 4 + i // 3, (i % 3) * 129:(i % 3) * 129 + 129]

        import os
        for hd in range(int(os.environ.get("KHEADS", "4"))):
            wsl, wkey = ring.load([(lambda sl, j=j: v3(sl)[:, j],
                                    ewin[:, :, 1024 + 512 * j + hd * 128:1024 + 512 * j + (hd + 1) * 128]) for j in range(3)])
            W3 = v3(wsl)
            if hd == 0:
                wav_sl, wavkey = ring.load([(v8, ewin[:, :, 512:1024])])
                Wav = v8(wav_sl)
            P.add("dve", I("memset", Vaug[:, :, 128:129], 1.0), writes=["Vaug1"])

            def rope_proj(j, rhs, n, ctab, stab, tabkey, dst, dstkey, xkeys, part=6):
                b1 = next_ps()
                P.add("pe", mm(bank(b1)[:, 0:n], [W3[:, j, c, :] for c in range(8)], rhs),
                      reads=xkeys + [wkey], writes=[("bank", b1)])
                if part < 2:
                    return
                ks = cnt["kraw"] % 2
                cnt["kraw"] += 1
                P.add("act", I("activation", out=kraw[ks][:, 0:n], in_=bank(b1)[:, 0:n], func=AF.Copy),
                      reads=[("bank", b1)], writes=[("kraw", ks)])
                if part < 3:
                    return
                b2 = next_ps()
                P.add("pe", I("matmul", bank(b2)[:, 0:n], lhsT=rt, rhs=kraw[ks][:, 0:n], start=True, stop=True),
                      reads=[("kraw", ks)], writes=[("bank", b2)])
                if part < 4:
                    return
                P.add("dve", I("tensor_tensor", out=t1[:, 0:n], in0=bank(b1)[:, 0:n], in1=ctab[:, 0:n], op=ALU.mult),
                      reads=[("bank", b1), (tabkey, "C")], writes=["t1"])
                if part < 5:
                    return
                P.add("dve", I("tensor_tensor", out=t2[:, 0:n], in0=bank(b2)[:, 0:n], in1=stab[:, 0:n], op=ALU.mult),
                      reads=[("bank", b2), (tabkey, "S")], writes=["t2"])
                if part < 6:
                    return
                P.add("pool", I("tensor_tensor", out=dst, in0=t1[:, 0:n], in1=t2[:, 0:n], op=ALU.add),
                      reads=["t1", "t2"], writes=[dstkey])

            xsl = cnt["x"] % NXR
            cnt["x"] += 1
            xt = xring[xsl]
            xk = ("xr", xsl)
            P.add("sp", I("dma_start", out=xt[0:4, :], in_=xtail[sq_name][:, :]), writes=[xk], dma=xk)
            xss = cnt["xs"] % NXS
            cnt["xs"] += 1
            norm_to_xnT(xt[0:4, :], 4, xk, xsb[xss], ("xsb", xss), 0, xnTl, "xnTl")
            P.add("sp", I("dma_start", out=posB[0][:, 0:4], in_=posd[sq_name][:, S:S + 4].partition_broadcast(128)),
                  writes=[("posB", 0)], dma=("posB", 0))
            rope_tables(posB[0], 4, ("posB", 0), Ct[0], St[0], A1, A2, ("tab", 0))
            rope_proj(0, [xnTl[:, c, :] for c in range(8)], 4, Ct[0], St[0], ("tab", 0), Qt[:, OWN:OWN + 4],
                      ("Qt", 4), ["xnTl"])

            if os.environ.get("KSTOP", "9") == "1":
                return
            NBr = int(os.environ.get("KNB", NB))
            blk = {}

            def par_of(b):
                return (b + 1) % 2

            def stage_load(b):
                par = par_of(b)
                P.add("sp", I("dma_start", out=posB[par],
                              in_=posd[sq_name][:, b * 512:(b + 1) * 512].partition_broadcast(128)),
                      writes=[("posB", par)], dma=("posB", par))
                xs_ = []
                for t in range(4):
                    xsl = (b % 2) * 4 + t
                    row = b * 512 + t * 128
                    P.add("sp", I("dma_start", out=xring[xsl], in_=xd[row:row + 128, :]), writes=[("xr", xsl)],
                          dma=("xr", xsl))
                    xs_.append(xsl)
                blk[b] = {"x": xs_}

            def stage_stats(b):
                s_ = new_stat()
                sk = ("stat", s_)
                for t in range(4):
                    xsl = blk[b]["x"][t]
                    P.add("act", I("activation", out=junk[:, :], in_=xring[xsl], func=AF.Square,
                                   accum_out=stat[:, s_, t:t + 1]), reads=[("xr", xsl)], writes=[(sk, 0, t)])
                pool_rstd(stat[:, s_, 0:4], stat[:, s_, 8:12], 128, 4, 3, [(sk, 0, t) for t in range(4)], (sk, 2))
                blk[b]["stat"] = s_

            def stage_mid(b):
                par = par_of(b)
                s_ = blk[b]["stat"]
                sk = ("stat", s_)
                for t in range(4):
                    xsl = blk[b]["x"][t]
                    rs_ap = stat[:, s_, 8 + t:9 + t]
                    if t == 2:
                        P.add("act", I("activation", out=xsb[t], in_=xring[xsl], func=AF.Identity, scale=rs_ap),
                              reads=[("xr", xsl), (sk, 2)], writes=[("xsb", t)])
                    elif t == 3:
                        P.add("pool", I("tensor_tensor", out=xsb[t], in0=xring[xsl], in1=rs_ap.broadcast_to([128, D]),
                                        op=ALU.mult), reads=[("xr", xsl), (sk, 2)], writes=[("xsb", t)])
                    else:
                        P.add("dve", I("tensor_scalar", out=xsb[t], in0=xring[xsl], scalar1=rs_ap,
                                       scalar2=None, op0=ALU.mult), reads=[("xr", xsl), (sk, 2)], writes=[("xsb", t)])
                for t in range(4):
                    bnk = next_pt()
                    ptv = bank(bnk).bitcast(BF16).rearrange("p (c n) -> p c n", c=8)
                    P.add("pe", [I("transpose", out=ptv[:, c, :], in_=xsb[t][:, c * 128:(c + 1) * 128], identity=ident)
                                 for c in range(8)], reads=[("xsb", t)], writes=[("bank", bnk)])
                    P.add("dve", I("tensor_tensor", out=xnA[par][:, :, t * 128:(t + 1) * 128], in0=ptv,
                                   in1=gcols[:, 0, :].unsqueeze(2).broadcast_to([128, 8, 128]), op=ALU.mult),
                          reads=[("bank", bnk)], writes=[("xnA", par, t)])

            def stage_tables(b):
                par = par_of(b)
                rope_tables(posB[par], 512, ("posB", par), Ct[par], St[par], A1, A2, ("tab", par))

            cache = not os.environ.get("KNOCACHE")

            def store_tab(b):
                par = par_of(b)
                P.add("sp", I("dma_start", out=tabd[sq_name][b, :, 0:512], in_=Ct[par]),
                      reads=[(("tab", par), "C")], writes=[("tabd", b, 0)], dma=("tst", 0))
                P.add("sp", I("dma_start", out=tabd[sq_name][b, :, 512:1024], in_=St[par]),
                      reads=[(("tab", par), "S")], writes=[("tabd", b, 1)], dma=("tst", 1))

            def store_x(b):
                par = par_of(b)
                P.add("sp", I("dma_start", out=xnd[sq_name][b], in_=xnA[par].rearrange("p c n -> p (c n)")),
                      reads=[("xnA", par, t) for t in range(4)], writes=[("xnd", b)], dma=("xst", par))

            def stage_reload(b):
                par = par_of(b)
                P.add("sp", I("dma_start", out=xnA[par].rearrange("p c n -> p (c n)"), in_=xnd[sq_name][b]),
                      reads=[("xnd", b)], writes=[("xnA", par, t) for t in range(4)], dma=("xrl", par))
                P.add("sp", I("dma_start", out=Ct[par], in_=tabd[sq_name][b, :, 0:512]),
                      reads=[("tabd", b, 0)], writes=[(("tab", par), "C")], dma=("trl", par, 0))
                P.add("sp", I("dma_start", out=St[par], in_=tabd[sq_name][b, :, 512:1024]),
                      reads=[("tabd", b, 1)], writes=[(("tab", par), "S")], dma=("trl", par, 1))

            first_pass = (hd == 0) or not cache
            if first_pass:
                stage_load(0)
                if NBr > 1:
                    stage_load(1)
                stage_stats(0)
                stage_tables(0)
                if cache:
                    store_tab(0)
            else:
                stage_reload(0)
            for b in range(NBr):
                par = par_of(b)
                if first_pass:
                    stage_mid(b)
                    if b + 2 < NBr:
                        stage_load(b + 2)
                    if b + 1 < NBr:
                        stage_stats(b + 1)
                        stage_tables(b + 1)
                    if cache:
                        store_x(b)
                        if b + 1 < NBr:
                            store_tab(b + 1)
                elif b + 1 < NBr:
                    stage_reload(b + 1)
                xkeys = [("xnA", par, t) for t in range(4)]
                xr = [xnA[par][:, c, :] for c in range(8)]
                kv = os.environ.get("KVAR", "")
                if kv != "noK":
                    rope_proj(1, xr, 512, Ct[par], St[par], ("tab", par), Kt[:, b * 512:(b + 1) * 512], ("Kt", b), xkeys)
                if kv == "KbK":
                    P.barrier()
                    kv = "KK"
                if kv == "KK":
                    for _ in range(int(os.environ.get("KSKIP", "0"))):
                        next_ps()
                    rope_proj(1, xr, 512, Ct[par], St[par], ("tab", par), Kt[:, b * 512:(b + 1) * 512], ("Kt", b), xkeys,
                              part=int(os.environ.get("KPART", "6")))
                    continue
                if os.environ.get("KNOV"):
                    if b < 4 and os.environ.get("KNOV") == "q":
                        rope_proj(0, xr, 512, Ct[par], St[par], ("tab", par), Qt[:, b * 512:(b + 1) * 512], ("Qt", b), xkeys)
                    continue
                bv = next_ps()
                vm = []
                for t in range(4):
                    vm += mm(bank(bv)[:, t * 128:(t + 1) * 128], [xnA[par][:, c, t * 128:(t + 1) * 128] for c in range(8)],
                             [W3[:, 2, c, :] for c in range(8)])
                P.add("pe", vm, reads=xkeys + [wkey], writes=[("bank", bv)])
                for t in range(4):
                    P.add("dve", I("tensor_copy", out=Vaug[:, b * 4 + t, 0:128], in_=bank(bv)[:, t * 128:(t + 1) * 128]),
                          reads=[("bank", bv), "Vaug1"], writes=[("Vaug", b, t)])
                if b < 4:
                    rope_proj(0, xr, 512, Ct[par], St[par], ("tab", par), Qt[:, b * 512:(b + 1) * 512], ("Qt", b), xkeys)
                if hd == 0 and not os.environ.get("KNOAV"):
                    for ti, ch in enumerate(tail_chunks):
                        if ch // 4 == b:
                            t = ch % 4
                            ba = next_ps()
                            P.add("pe", mm(bank(ba), [xnA[par][:, c, t * 128:(t + 1) * 128] for c in range(8)],
                                           [Wav[:, c, :] for c in range(8)]),
                                  reads=xkeys + [wavkey], writes=[("bank", ba)])
                            gmlp_v(bank(ba), ("bank", ba), 128, gv, sq, vstash[:, ti, :], ("vstash", ti))

            if os.environ.get("KSTOP", "9") == "2":
                return
            qblocks = [(0, 512), (512, 512), (1024, 512), (1536, 512), (OWN, 4)]
            allK = [("Kt", b) for b in range(NB)]
            allV = [("Vaug", b, t) for b in range(NB) for t in range(4)]
            allQ = [("Qt", b) for b in range(5)]
            def accv(j, qt):
                i = j * 4 + qt
                return accS[:, i // 3, (i % 3) * 129:(i % 3) * 129 + 129]

            def make_fin(q0, nq, nqt, r):
                def fin():
                    akeys = [("accS", bi) for bi in range(3)]
                    s_ = new_stat()
                    sk = ("stat", s_)
                    for qt in range(nqt):
                        P.add("dve", I("reciprocal", out=stat[:r, s_, 0:1], in_=accv(0, qt)[0:r, 128:129]),
                              reads=akeys, writes=[(sk, 0)])
                        P.add("dve", I("reciprocal", out=stat[:r, s_, 1:2], in_=accv(1, qt)[0:r, 128:129]),
                              reads=akeys, writes=[(sk, 1)])
                        P.add("dve", I("tensor_tensor", out=stat[:r, s_, 2:3], in0=stat[:r, s_, 1:2], in1=lsc[:r, 5:6],
                                       op=ALU.mult), reads=[(sk, 1)], writes=[(sk, 2)])
                        P.add("dve", I("tensor_scalar", out=otmp[:r, :], in0=accv(0, qt)[0:r, 0:128],
                                       scalar1=stat[:r, s_, 0:1], scalar2=None, op0=ALU.mult),
                              reads=akeys + [(sk, 0)], writes=["otmp"])
                        P.add("dve", I("scalar_tensor_tensor", out=obS[:r, qt, :], in0=accv(1, qt)[0:r, 0:128],
                                       scalar=stat[:r, s_, 2:3], in1=otmp[:r, :], op0=ALU.mult, op1=ALU.add),
                              reads=akeys + [(sk, 2), "otmp"], writes=[("obS", qt)])
                        P.add("dve", I("tensor_tensor", out=otmp[:r, :], in0=obS[:r, qt, :], in1=obS[:r, qt, :], op=ALU.mult),
                              reads=[("obS", qt)], writes=["otmp"])
                        P.add("dve", I("tensor_reduce", out=stat[:r, s_, 4 + qt:5 + qt], in_=otmp[:r, :], axis=AX.X, op=ALU.add),
                              reads=["otmp"], writes=[(sk, 4, qt)])
                    return (s_, sk)

                def fin2(s_, sk):
                    pool_rstd(stat[:r, s_, 4:4 + nqt], stat[:r, s_, 8:8 + nqt], r, nqt, 4,
                              [(sk, 4, qt) for qt in range(nqt)], (sk, 9))
                    ptb = bank(7).bitcast(BF16)
                    for qt in range(nqt):
                        P.add("dve", I("scalar_tensor_tensor", out=obn4[:r, qt, :], in0=obS[:r, qt, :],
                                       scalar=stat[:r, s_, 8 + qt:9 + qt], in1=sublnB[:r, :], op0=ALU.mult, op1=ALU.mult),
                              reads=[("obS", qt), (sk, 9)], writes=[("obn", qt)])
                    P.add("pe", [I("transpose", out=ptb[:, qt * 128:qt * 128 + r], in_=obn4[:r, qt, :], identity=ident[:r, :r])
                                 for qt in range(nqt)], reads=[("obn", qt) for qt in range(nqt)], writes=[("bank", 7)])
                    P.add("dve", I("tensor_copy", out=out_bT[:, hd, q0:q0 + nq], in_=ptb[:, 0:nq]),
                          reads=[("bank", 7)], writes=[("obT", hd, q0)])
                return fin, fin2

            pending_fin = []
            for (q0, nq) in qblocks:
                nqt = (nq + 127) // 128
                r = min(128, nq)
                pq = []
                for kb in range(NKB + 2):
                    if kb == 14 and pending_fin:
                        f2_, args_ = pending_fin.pop(0)
                        f2_(*args_)
                    pend = None
                    if kb < NKB:
                        sb = (kb % 2) * 2
                        P.add("pe", [I("matmul", psum[:, sb, 0:nq], lhsT=Kt[0:64, kb * 128:(kb + 1) * 128],
                                       rhs=Qt[0:64, q0:q0 + nq], start=True, stop=True),
                                     I("matmul", psum[:, sb + 1, 0:nq], lhsT=Kt[64:128, kb * 128:(kb + 1) * 128],
                                       rhs=Qt[64:128, q0:q0 + nq], start=True, stop=True)],
                              reads=allK + allQ, writes=[("bank", sb), ("bank", sb + 1)])
                        psl = cnt["pt"] % NPT
                        cnt["pt"] += 1
                        P.add("act", I("activation", out=Pt[psl][:, :, 0:nq], in_=psum[:, sb:sb + 2, 0:nq], func=AF.Exp,
                                       scale=0.125), reads=[("bank", sb), ("bank", sb + 1)], writes=[("Pt", psl)])
                        pq.append((kb, psl))
                    if pq and (len(pq) > 2 or kb >= NKB):
                        pend = pq.pop(0)
                    if pend is not None:
                        pkb, pps = pend
                        pv = []
                        seen_banks = set()
                        for j in range(2):
                            for qt in range(nqt):
                                ai = j * 4 + qt
                                first_in_bank = (ai // 3) not in seen_banks
                                seen_banks.add(ai // 3)
                                pv.append(I("matmul", acc(j, qt)[0:r, :], lhsT=Pt[pps][:, j, qt * 128:qt * 128 + r],
                                            rhs=Vaug[:, pkb, 0:129], start=(pkb == 0 and first_in_bank),
                                            stop=(pkb == NKB - 1), skip_group_check=True))
                        P.add("pe", pv, reads=[("Pt", pps)] + allV, writes=accs)
                for bi in range(3):
                    ncol = 387 if bi < 2 else 258
                    P.add("dve", I("tensor_copy", out=accS[:r, bi, 0:ncol], in_=psum[0:r, 4 + bi, 0:ncol]),
                          reads=[("bank", 4 + bi)], writes=[("accS", bi)])
                f1_, f2_ = make_fin(q0, nq, nqt, r)
                pending_fin.append((f2_, f1_()))
            while pending_fin:
                f2_, args_ = pending_fin.pop(0)
                f2_(*args_)

    def phase_C(sq_name, ri):
        xd = xseq[sq_name]
        roff = ri * RNG
        sidx = 0 if sq_name == "s" else 1
        ar = Arena(arena_ap, ARENA_W)
        h = ar.alloc((9, D), F32)
        xnT = ar.alloc((8, RNG + 2), BF16)
        uT = ar.alloc((4, RNG + 2), BF16)
        oaT = ar.alloc((4, RNG + 2), BF16)
        hTall = ar.alloc((8, RNG + 2), BF16)
        hT = [hTall[:, 0:4, :], hTall[:, 4:8, :]]
        mTb = hTall
        xsb = [ar.alloc((D,), BF16) for _ in range(9)]
        uni = ar.alloc((4640,), F32)
        gv2 = [uni[:, i * 512:(i + 1) * 512] for i in range(3)]
        sq2 = [uni[:, 1536 + i * 512:1536 + (i + 1) * 512] for i in range(3)]
        mixt2 = [uni[:, 3072 + i * 512:3072 + (i + 1) * 512].rearrange("p (g n) -> p g n", g=4) for i in range(3)]
        vb2 = [ar.alloc((512,), BF16) for _ in range(3)]
        mixt = mixt2[0]
        rl = [ar.alloc((512,), F32) for _ in range(2)]
        bgs = uni[:, 0:RNG]
        cgs = uni[:, 1024:1024 + RNG + 2]
        tb = uni[:, 2056:2056 + RNG + 2]
        yb = uni[:, 3088:3088 + RNG]
        ring = WRing(ar)
        cnt = {"xs": 0, "rl": 0, "hT": 0}
        tiles = [(t, 128) for t in range(8)] + [(8, 2)]
        cblocks = [(0, 512), (512, 512), (RNG, 2)]

        def tile_cols(t):
            return (t * 128, 128) if t < 8 else (RNG, 2)

        def hkey(t):
            return ("h", t)

        for t in range(8):
            P.add("sp", I("dma_start", out=h[:, t, :], in_=xd[roff + t * 128:roff + (t + 1) * 128, :]),
                  writes=[hkey(t)], dma=("hl", t))
        P.add("sp", I("dma_start", out=h[0:2, 8, :], in_=xtail[sq_name][2 * ri:2 * ri + 2, :]),
              writes=[hkey(8)], dma=("hl", 8))

        def do_norm(layer, tl):
            nt = len(tl)
            for (t, r) in tl:
                P.add("act", I("activation", out=junk[:r, :], in_=h[0:r, t, :], func=AF.Square,
                               accum_out=statC[:r, 0, t:t + 1]), reads=[hkey(t)], writes=[("sC", 0, t)])
            pool_rstd(statC[:, 0, 0:nt], statC[:, 2, 0:nt], 128, nt, 3, [("sC", 0, t) for (t, r) in tl], ("sC", 2))
            for i, (t, r) in enumerate(tl):
                rs_ap = statC[:r, 2, t:t + 1]
                src = h[0:r, t, :]
                if i % 3 == 0:
                    P.add("act", I("activation", out=xsb[t][:r, :], in_=src, func=AF.Identity, scale=rs_ap),
                          reads=[hkey(t), ("sC", 2)], writes=[("xsbC", t)])
                elif i % 3 == 1:
                    P.add("dve", I("tensor_scalar", out=xsb[t][:r, :], in0=src, scalar1=rs_ap, scalar2=None, op0=ALU.mult),
                          reads=[hkey(t), ("sC", 2)], writes=[("xsbC", t)])
                else:
                    P.add("pool", I("tensor_tensor", out=xsb[t][:r, :], in0=src, in1=rs_ap.broadcast_to([r, D]), op=ALU.mult),
                          reads=[hkey(t), ("sC", 2)], writes=[("xsbC", t)])
            for (t, r) in tl:
                c0, n = tile_cols(t)
                bnk = next_pt()
                ptv = bank(bnk).bitcast(BF16).rearrange("p (c n) -> p c n", c=8)
                P.add("pe", [I("transpose", out=ptv[:, c, 0:r], in_=xsb[t][:r, c * 128:(c + 1) * 128], identity=ident[:r, :r])
                             for c in range(8)], reads=[("xsbC", t)], writes=[("bank", bnk)])
                P.add("dve", I("tensor_tensor", out=xnT[:, :, c0:c0 + n], in0=ptv[:, :, 0:r],
                               in1=gcols[:, layer, :].unsqueeze(2).broadcast_to([128, 8, r]), op=ALU.mult),
                      reads=[("bank", bnk)], writes=[("xnT", t)])

        def xkeys_cb(cb):
            c0, n = cb
            if n == 2:
                return [("xnT", 8)]
            return [("xnT", t) for t in range(c0 // 128, c0 // 128 + 4)]

        def resid_add(t, r, half, b):
            P.add("dve", I("tensor_tensor", out=h[0:r, t, half * 512:(half + 1) * 512],
                           in0=h[0:r, t, half * 512:(half + 1) * 512], in1=bank(b)[0:r, :], op=ALU.add),
                  reads=[("bank", b), hkey(t)], writes=[hkey(t)])

        do_norm(0, tiles)
        wau_sl, waukey = ring.load([(v8, ewin[:, :, 0:512])])
        wav_sl, wavkey = ring.load([(v8, ewin[:, :, 512:1024])])
        Wau = v8(wau_sl)
        Wav = v8(wav_sl)
        wo = []
        for half in range(2):
            sl, k = ring.load([(v4, ewout[:, half * 4:(half + 1) * 4, :])])
            wo.append((v4(sl), k))
        for cb in cblocks:
            c0, n = cb
            for g in range(4):
                b = next_ps()
                P.add("pe", mm(bank(b)[:, 0:n], [Wau[:, c, g * 128:(g + 1) * 128] for c in range(8)],
                               [xnT[:, c, c0:c0 + n] for c in range(8)]),
                      reads=xkeys_cb(cb) + [waukey], writes=[("bank", b)])
                P.add("act", I("activation", out=uT[:, g, c0:c0 + n], in_=bank(b)[:, 0:n], func=AF.Gelu_apprx_tanh),
                      reads=[("bank", b)], writes=[("uT", g, c0)])

        def ukeys(c0):
            cc = RNG if c0 >= RNG else (c0 // 512) * 512
            return [("uT", g, cc) for g in range(4)]
        gst = {}

        def g_a1(t):
            b = next_ps()
            P.add("pe", mm(bank(b), [xnT[:, c, t * 128:(t + 1) * 128] for c in range(8)], [Wav[:, c, :] for c in range(8)]),
                  reads=[("xnT", t), wavkey], writes=[("bank", b)])
            kx = t % 3
            gst[t] = gmlp_v1(bank(b), ("bank", b), 128, gv2[kx], sq2[kx], kx=kx)

        def g_a2(t):
            kx = t % 3
            gmlp_v2(gst[t], 128, gv2[kx], vb2[kx], ("vb", kx), kx=kx)
            b2 = next_ps()
            P.add("pe", [I("matmul", bank(b2)[:, g * 128:(g + 1) * 128], lhsT=vb2[kx][:, g * 128:(g + 1) * 128],
                           rhs=wsT[:, g, :], start=True, stop=True) for g in range(4)],
                  reads=[("vb", kx)], writes=[("bank", b2)])
            gst[("b2", t)] = b2

        def g_b(t):
            kx = t % 3
            b2 = gst[("b2", t)]
            mx = mixt2[kx]
            P.add("dve", I("tensor_tensor", out=mx, in0=bank(b2).rearrange("p (g n) -> p g n", g=4), in1=bsB, op=ALU.add),
                  reads=[("bank", b2)], writes=[("mixt", kx)])
            P.add("dve", I("tensor_tensor", out=oaT[:, :, t * 128:(t + 1) * 128], in0=mx,
                           in1=uT[:, :, t * 128:(t + 1) * 128], op=ALU.mult),
                  reads=[("mixt", kx)] + ukeys(t * 128), writes=[("oaT", t)])

        if os.environ.get("KGSEQ"):
            for t in range(8):
                g_a1(t)
                g_a2(t)
                g_b(t)
        else:
            g_a1(0)
            for t in range(8):
                if t + 1 < 8:
                    g_a1(t + 1)
                g_a2(t)
                if t >= 1:
                    g_b(t - 1)
            g_b(7)
        for j in range(2):
            ti = 2 * ri + j
            pcol = 127 if j == 0 else 0
            b2 = next_ps()
            P.add("pe", [I("matmul", bank(b2)[:, g:g + 1], lhsT=vstash[:, ti, g * 128:(g + 1) * 128],
                           rhs=wsT[:, g, pcol:pcol + 1], start=True, stop=True) for g in range(4)],
                  reads=[], writes=[("bank", b2)])
            P.add("dve", I("tensor_tensor", out=mixt[:, :, 0], in0=bank(b2)[:, 0:4], in1=bsB[:, :, pcol], op=ALU.add),
                  reads=[("bank", b2)], writes=[("mixt", 0)])
            P.add("dve", I("tensor_tensor", out=oaT[:, :, RNG + j], in0=mixt[:, :, 0], in1=uT[:, :, RNG + j], op=ALU.mult),
                  reads=[("mixt", 0)] + ukeys(RNG), writes=[("oaT", 8, j)])
        for (t, r) in tiles:
            c0, n = tile_cols(t)
            for half in range(2):
                b = next_ps()
                lhs = []
                for c in range(8):
                    if c < 4:
                        lhs.append(oaT[:, c, c0:c0 + r])
                    elif t < 8:
                        lhs.append(out_bT[:, c - 4, roff + c0:roff + c0 + r])
                    else:
                        lhs.append(out_bT[:, c - 4, OWN + 2 * ri:OWN + 2 * ri + 2])
                okeys = [("oaT", t)] if t < 8 else [("oaT", 8, 0), ("oaT", 8, 1)]
                P.add("pe", mm(bank(b)[0:r, :], lhs, [wo[c // 4][0][:, c % 4, half * 512:(half + 1) * 512] for c in range(8)]),
                      reads=okeys + [wo[0][1], wo[1][1]], writes=[("bank", b)])
                resid_add(t, r, half, b)

        ALLHT = [("hT", hs) for hs in range(2)]

        def ffn(layer, tl, cbl):
            do_norm(1 + 2 * layer, tl)
            st = {}

            def ld1(fg):
                st[("w1", fg)] = ring.load([(v8, w1d[layer][:, :, fg * 512:(fg + 1) * 512])])

            def ld2(fg):
                st[("w2", fg)] = ring.load([(v4, w2d[layer][:, fg * 4:(fg + 1) * 4, :])])

            def hidden(fg):
                s1, k1 = st[("w1", fg)]
                W1 = v8(s1)
                hs = fg % 2
                hb = hT[hs]
                for f in range(4):
                    for cb in cbl:
                        c0, n = cb
                        b = next_ps()
                        P.add("pe", mm(bank(b)[:, 0:n], [W1[:, c, f * 128:(f + 1) * 128] for c in range(8)],
                                       [xnT[:, c, c0:c0 + n] for c in range(8)]),
                              reads=xkeys_cb(cb) + [k1], writes=[("bank", b)])
                        rsl = cnt["rl"] % 2
                        cnt["rl"] += 1
                        P.add("act", I("activation", out=rl[rsl][:, 0:n], in_=bank(b)[:, 0:n], func=AF.Relu),
                              reads=[("bank", b)], writes=[("rl", rsl)])
                        P.add("dve", I("tensor_tensor", out=hb[:, f, c0:c0 + n], in0=rl[rsl][:, 0:n], in1=rl[rsl][:, 0:n],
                                       op=ALU.mult), reads=[("rl", rsl)], writes=[("hTw", hs, f, c0)])

            def second(fg):
                s2, k2 = st[("w2", fg)]
                W2 = v4(s2)
                hs = fg % 2
                hb = hT[hs]
                hkeys = [("hTw", hs, f, c0) for f in range(4) for (c0, n) in cbl]
                for (t, r) in tl:
                    c0, n = tile_cols(t)
                    for half in range(2):
                        b = next_ps()
                        P.add("pe", mm(bank(b)[0:r, :], [hb[:, f, c0:c0 + r] for f in range(4)],
                                       [W2[:, f, half * 512:(half + 1) * 512] for f in range(4)]),
                              reads=hkeys + [k2], writes=[("bank", b)])
                        resid_add(t, r, half, b)

            ld1(0)
            ld2(0)
            ld1(1)
            hidden(0)
            for fg in range(8):
                if fg + 1 < 8:
                    ld2(fg + 1)
                    hidden(fg + 1)
                if fg + 2 < 8:
                    ld1(fg + 2)
                second(fg)

        ffn(0, tiles, cblocks)

        do_norm(2, tiles)
        m0 = sidx * 4 + 2 * ri
        for c in range(8):
            wsl, wk = ring.load([(lambda sl, j=j: v3(sl)[:, j], cwin[:, :, j * D + c * 128:j * D + (c + 1) * 128])
                                 for j in range(3)])
            Wc = v3(wsl)
            for cb in cblocks:
                c0, n = cb
                xk = xkeys_cb(cb)
                xr = [xnT[:, k, c0:c0 + n] for k in range(8)]
                if n > 2:
                    b = next_ps()
                    P.add("pe", mm(bank(b)[:, 0:n], [Wc[:, 0, k, :] for k in range(8)], xr),
                          reads=xk + [wk], writes=[("bank", b)])
                    P.add("act", I("activation", out=bgs[:, c0:c0 + n], in_=bank(b)[:, 0:n], func=AF.Copy),
                          reads=[("bank", b)], writes=[("bgs", c0)])
                b = next_ps()
                P.add("pe", mm(bank(b)[:, 0:n], [Wc[:, 1, k, :] for k in range(8)], xr),
                      reads=xk + [wk], writes=[("bank", b)])
                P.add("act", I("activation", out=cgs[:, c0:c0 + n], in_=bank(b)[:, 0:n], func=AF.Copy),
                      reads=[("bank", b)], writes=[("cgs", c0)])
                b = next_ps()
                P.add("pe", mm(bank(b)[:, 0:n], [Wc[:, 2, k, :] for k in range(8)], xr),
                      reads=xk + [wk], writes=[("bank", b)])
                if n > 2:
                    P.add("dve", I("tensor_tensor", out=tb[:, 1 + c0:1 + c0 + n], in0=bank(b)[:, 0:n], in1=cgs[:, c0:c0 + n],
                                   op=ALU.mult), reads=[("bank", b), ("cgs", c0)], writes=[("tb", c0)])
                else:
                    P.add("dve", I("tensor_tensor", out=cgs[:, RNG:RNG + 2], in0=bank(b)[:, 0:2], in1=cgs[:, RNG:RNG + 2],
                                   op=ALU.mult), reads=[("bank", b), ("cgs", RNG)], writes=[("cgs2", RNG)])
                    P.add("dve", I("tensor_tensor", out=tb[:, 0:1], in0=cgs[:, RNG:RNG + 1], in1=maskB[:, m0:m0 + 1],
                                   op=ALU.mult), reads=[("cgs2", RNG)], writes=[("tb", "l")])
                    P.add("dve", I("tensor_tensor", out=tb[:, RNG + 1:RNG + 2], in0=cgs[:, RNG + 1:RNG + 2],
                                   in1=maskB[:, m0 + 1:m0 + 2], op=ALU.mult), reads=[("cgs2", RNG)], writes=[("tb", "r")])
            tkeys = [("tb", 0), ("tb", 512), ("tb", "l"), ("tb", "r")]
            P.add("dve", I("tensor_scalar", out=yb, in0=tb[:, 1:RNG + 1], scalar1=cw[:, 1, c:c + 1], scalar2=None,
                           op0=ALU.mult), reads=tkeys, writes=["yb"])
            P.add("dve", I("scalar_tensor_tensor", out=yb, in0=tb[:, 0:RNG], scalar=cw[:, 0, c:c + 1], in1=yb,
                           op0=ALU.mult, op1=ALU.add), reads=tkeys + ["yb"], writes=["yb"])
            P.add("dve", I("scalar_tensor_tensor", out=yb, in0=tb[:, 2:RNG + 2], scalar=cw[:, 2, c:c + 1], in1=yb,
                           op0=ALU.mult, op1=ALU.add), reads=tkeys + ["yb"], writes=["yb"])
            P.add("dve", I("tensor_tensor", out=mTb[:, c, 0:RNG], in0=bgs, in1=yb, op=ALU.mult),
                  reads=["yb", ("bgs", 0), ("bgs", 512)],
                  writes=[("mT", c), ("hTw", c // 4, c % 4, 0), ("hTw", c // 4, c % 4, 512)])
        wco = []
        for half in range(2):
            sl, k = ring.load([(v4, cwout[:, half * 4:(half + 1) * 4, :])])
            wco.append((v4(sl), k))
        mkeys = [("mT", c) for c in range(8)] + [("hTw", hs_, f_, c0_) for hs_ in range(2) for f_ in range(4) for c0_ in (0, 512)]
        own_tiles = tiles[:8]
        for (t, r) in own_tiles:
            for half in range(2):
                b = next_ps()
                P.add("pe", mm(bank(b), [mTb[:, c, t * 128:(t + 1) * 128] for c in range(8)],
                               [wco[c // 4][0][:, c % 4, half * 512:(half + 1) * 512] for c in range(8)]),
                      reads=mkeys + [wco[0][1], wco[1][1]], writes=[("bank", b)])
                resid_add(t, r, half, b)

        ffn(1, own_tiles, cblocks[:2])

        ystage = [bgs, yb]
        yskeys = [[("bgs", 0), ("bgs", 512)], ["yb"]]
        for half in range(2):
            for t in range(half * 4, half * 4 + 4):
                P.add("act", I("activation", out=junk[:, :], in_=h[:, t, :], func=AF.Square,
                               accum_out=statC[:, 0, t:t + 1]), reads=[hkey(t)], writes=[("sC", 0, t)])
            pool_rstd(statC[:, 0, half * 4:half * 4 + 4], statC[:, 2, half * 4:half * 4 + 4], 128, 4, 3,
                      [("sC", 0, t) for t in range(half * 4, half * 4 + 4)], ("sCf", half))
        for (t, r) in own_tiles:
            rs = statC[:, 2, t:t + 1]
            rsk = ("sCf", t // 4)
            ys = t % 2
            P.add("dve", I("scalar_tensor_tensor", out=ystage[ys], in0=h[:, t, :], scalar=rs, in1=fgB,
                           op0=ALU.mult, op1=ALU.mult), reads=[hkey(t), rsk], writes=yskeys[ys])
            P.add("sp", I("dma_start", out=yout[sq_name][roff + t * 128:roff + (t + 1) * 128, :], in_=ystage[ys]),
                  reads=yskeys[ys], writes=[("yout", sq_name, ri, t)], dma=("yst", ys))

    import os
    dbg = os.environ.get("KDBG", "")
    for sq_name in ("s", "p"):
        if dbg and sq_name not in dbg:
            continue
        if not dbg or "A" in dbg:
            phase_AB(sq_name)
            P.barrier()
        if not dbg or "C" in dbg:
            phase_C(sq_name, 0)
            P.barrier()
        if not dbg or "D" in dbg:
            phase_C(sq_name, 1)
            P.barrier()
    for e in Prog.ENGS:
        P.add(e, None)
    stuck, per_, pos_ = P.simulate()
    if stuck:
        for e, (p_, n_) in stuck.items():
            op = per_[e][p_]
            print("DEADLOCK", e, p_, n_, "op idx", op.idx, "waits", [(d.eng, d.idx, d.dma, d.tick, d.signal) for d in op.waits])
        raise RuntimeError("semaphore protocol deadlock")
    print("program ops:", len(P.ops), {e: len(v) for e, v in per_.items()})
    if os.environ.get("KDUMP"):
        for op in P.ops:
            print(op.idx, op.eng, op.name, "tick", op.tick if (op.signal or op.dma) else None,
                  "W:", [(d.eng, d.idx, d.tick) for d in op.waits], "w=", op.rw[1][:3])
    P.emit(nc, stack)
    stack.close()
    return nc


_NC_CACHE = {}


def _host_constants():
    ident = np.eye(128, dtype=np.float32)
    rt = np.zeros((128, 128), np.float32)
    invf = np.zeros(128, np.float32)
    sgn = np.zeros(128, np.float32)
    inv = (500000.0 ** (-np.arange(0, 16, 2, dtype=np.float32) / 16.0)).astype(np.float32)
    for base in (0, 64):
        for d in range(16):
            p = base + d
            partner = p + 8 if d < 8 else p - 8
            rt[partner, p] = 1.0
            invf[p] = inv[d % 8] / np.float32(2 * np.pi)
            sgn[p] = -1.0 if d < 8 else 1.0
    rc = np.stack([invf, (-2.0 * np.pi * sgn).astype(np.float32)], axis=1).astype(np.float32)
    return ident, rt, rc


def kernel(**inputs):
    f32 = lambda a: np.ascontiguousarray(np.asarray(a, dtype=np.float32))
    xpr = f32(inputs["x_prompt"])
    xsa = f32(inputs["x_sample"])
    ident, rt, rc = _host_constants()
    gains = np.stack([f32(inputs["norm_mix_g"])[0], f32(inputs["norm_ffn_g"])[0],
                      f32(inputs["norm_mix_g"])[1], f32(inputs["norm_ffn_g"])[1]], axis=0)
    gcols = np.ascontiguousarray(gains.reshape(4, 8, 128).transpose(2, 0, 1).reshape(128, 32))
    cwh = np.ascontiguousarray(f32(inputs["c_conv_w"])[0].reshape(3, 8, 128).transpose(2, 0, 1).reshape(128, 24))
    lvec = np.concatenate([f32(inputs["b_lq1"])[0], f32(inputs["b_lk1"])[0],
                           f32(inputs["b_lq2"])[0], f32(inputs["b_lk2"])[0]])[None, :]
    shared = {
        "e_w_in": f32(inputs["e_w_in"])[0], "e_w_out": f32(inputs["e_w_out"])[0],
        "ffn_w1_0": f32(inputs["ffn_w1"])[0], "ffn_w1_1": f32(inputs["ffn_w1"])[1],
        "ffn_w2_0": f32(inputs["ffn_w2"])[0], "ffn_w2_1": f32(inputs["ffn_w2"])[1],
        "c_w_in": f32(inputs["c_w_in"])[0], "c_w_out": f32(inputs["c_w_out"])[0],
        "wsT": np.ascontiguousarray(f32(inputs["a_w_s"])[0].transpose(0, 2, 1)),
        "ident": ident, "rt": rt, "gcols": gcols, "cw": cwh, "rc": rc, "lvec": np.ascontiguousarray(lvec),
        "vng": f32(inputs["a_vnorm_g"]).reshape(1, 512), "bs": f32(inputs["a_b_s"]).reshape(1, 512),
        "subln": f32(inputs["b_subln_g"]).reshape(1, 128), "fg": f32(inputs["final_g"]).reshape(1, D),
    }
    in_maps = []
    meta = []
    for c in range(NCORES):
        bp, op_ = c // 2, (c % 2) * OWN
        bs_, os_ = c // 4, (c % 4) * OWN
        m = dict(shared)
        masks = np.zeros((1, 8), np.float32)
        for key, x, b, off, S, mi in (("s", xsa, bs_, os_, SEQ_S, 0), ("p", xpr, bp, op_, SEQ_P, 4)):
            xr = np.ascontiguousarray(np.roll(x[b], -off, axis=0))
            tidx = np.array([S - 1, RNG, RNG - 1, OWN])
            pos = ((np.arange(S) + off) % S).astype(np.float32)
            post = ((tidx + off) % S).astype(np.float32)
            m["x" + key] = xr
            m["xt" + key] = np.ascontiguousarray(xr[tidx])
            m["pos" + key] = np.concatenate([pos, post])[None, :].astype(np.float32)
            masks[0, mi + 0] = 1.0 if off > 0 else 0.0
            masks[0, mi + 1] = 1.0
            masks[0, mi + 2] = 1.0
            masks[0, mi + 3] = 1.0 if off + OWN < S else 0.0
        m["masks"] = masks
        in_maps.append(m)
        meta.append((bp, op_, bs_, os_))
    if "nc" not in _NC_CACHE:
        _NC_CACHE["nc"] = build_program()
    import os as _os
    ncr = int(_os.environ.get("KCORES", NCORES))
    if _os.environ.get("KTRACE"):
        res = run_bass_kernel_spmd(_NC_CACHE["nc"], in_maps[:ncr], core_ids=list(range(ncr)), trace=True)
        print("KTRACE exec_time_ns", res.exec_time_ns)
    else:
        res = run_bass_kernel_spmd(_NC_CACHE["nc"], in_maps[:ncr], core_ids=list(range(ncr)))
    y_p = np.zeros_like(xpr)
    y_s = np.zeros_like(xsa)
    for c, (bp, op_, bs_, os_) in enumerate(meta[:ncr]):
        y_p[bp, op_:op_ + OWN] = res.results[c]["yp"]
        y_s[bs_, os_:os_ + OWN] = res.results[c]["ys"]
    return (y_p, y_s)
```

```python
import math
import os
import numpy as np
import concourse.bass as bass
import concourse.mybir as mybir
from concourse.bass_utils import run_bass_kernel_spmd

F32 = mybir.dt.float32
BF16 = mybir.dt.bfloat16
I32 = mybir.dt.int32
AF = mybir.ActivationFunctionType
ALU = mybir.AluOpType
AX = mybir.AxisListType

D = 1024
DFF = 4096
EPS = 1e-5
NCORES = 8
SEQ_S = 8192
SEQ_P = 4096
OWN = 2048
RNG = 1024
LAM_INIT = 0.8 - 0.6 * math.exp(0.0)
TWO_PI = 2.0 * math.pi


class Op:
    __slots__ = ("eng", "fn", "waits", "signal", "tick", "dma", "idx", "name", "rw")


def I(name, *args, **kw):
    return (name, args, kw)


def _mkfn(spec):
    if spec is None:
        return None
    if callable(spec):
        return spec
    if isinstance(spec, tuple):
        spec = [spec]

    def f(e, spec=spec):
        ins = None
        for (name, args, kw) in spec:
            ins = getattr(e, name)(*args, **kw)
        return ins
    return f


class Prog:
    ENGS = ("pe", "act", "dve", "pool", "sp")

    def __init__(self):
        self.ops = []
        self.last_w = {}
        self.readers = {}
        self.dma_count = {}
        self.total_groups = set()
        self.extra = {e: [] for e in self.ENGS}
        self.dma_last = {}

    def barrier(self):
        lasts = {}
        for op in reversed(self.ops):
            if op.dma is None and op.eng not in lasts and op.fn is not None:
                lasts[op.eng] = op
        deps = list(lasts.values()) + list(self.dma_last.values())
        for e in self.ENGS:
            self.extra[e] = [d for d in deps if not (d.dma is None and d.eng == e)]
        for d in lasts.values():
            d.signal = True
        self.last_w = {}
        self.readers = {}

    def add(self, eng, fn, reads=(), writes=(), dma=None, wait_total=False):
        op = Op()
        op.eng = eng
        op.fn = _mkfn(fn)
        try:
            op.name = fn[0] if isinstance(fn, tuple) else (fn[-1][0] + "x%d" % len(fn) if isinstance(fn, list) else str(fn))
        except Exception:
            op.name = "?"
        op.signal = False
        op.tick = None
        op.dma = dma
        op.idx = len(self.ops)
        bank_r = [k for k in reads if isinstance(k, tuple) and k and k[0] == "bank" and k not in writes]
        if bank_r:
            writes = list(writes) + bank_r
        deps = {}
        for k in reads:
            w = self.last_w.get(k)
            if w is not None:
                deps[w.idx] = (w, True)
        for k in writes:
            w = self.last_w.get(k)
            if w is not None and w.idx not in deps:
                deps[w.idx] = (w, False)
            for r in self.readers.get(k, ()):
                if r.idx not in deps:
                    deps[r.idx] = (r, False)
        waits = []
        for d, raw in deps.values():
            if d.dma is None and d.eng == eng:
                if eng == "pe" or not raw:
                    continue
            waits.append(d)
            if d.dma is None:
                d.signal = True
        if self.extra[eng]:
            waits = waits + self.extra[eng]
            self.extra[eng] = []
        op.waits = waits
        op.rw = (list(reads), list(writes))
        for k in reads:
            self.readers.setdefault(k, []).append(op)
        for k in writes:
            self.last_w[k] = op
            self.readers[k] = []
        if dma is not None:
            self.dma_count[dma] = self.dma_count.get(dma, 0) + 1
            op.tick = 16 * self.dma_count[dma]
            self.dma_last[dma] = op
            if wait_total:
                self.total_groups.add(dma)
        self.ops.append(op)
        return op

    def simulate(self):
        cnt = {e: 0 for e in self.ENGS}
        for op in self.ops:
            if op.dma is None and op.signal:
                cnt[op.eng] += 1
                op.tick = cnt[op.eng]
        per = {e: [op for op in self.ops if op.eng == e] for e in self.ENGS}
        pos = {e: 0 for e in self.ENGS}
        sem = {}
        progress = True
        while progress:
            progress = False
            for e in self.ENGS:
                while pos[e] < len(per[e]):
                    op = per[e][pos[e]]
                    ok = True
                    for d in op.waits:
                        if d.dma is not None:
                            key = ("d", d.dma)
                            val = 16 * self.dma_count[d.dma] if d.dma in self.total_groups else d.tick
                        else:
                            key = ("e", d.eng)
                            val = d.tick
                        if sem.get(key, 0) < val:
                            ok = False
                            break
                    if not ok:
                        break
                    if op.fn is not None:
                        if op.dma is not None:
                            sem[("d", op.dma)] = sem.get(("d", op.dma), 0) + 16
                        elif op.signal:
                            sem[("e", op.eng)] = sem.get(("e", op.eng), 0) + 1
                    pos[e] += 1
                    progress = True
        stuck = {e: (pos[e], len(per[e])) for e in self.ENGS if pos[e] < len(per[e])}
        return stuck, per, pos

    def emit(self, nc, stack):
        cnt = {e: 0 for e in self.ENGS}
        for op in self.ops:
            if op.dma is None and op.signal:
                cnt[op.eng] += 1
                op.tick = cnt[op.eng]
        sems = {e: stack.enter_context(nc.semaphore("s_" + e)) for e in self.ENGS}
        dsems = {}
        for i, g in enumerate(self.dma_count):
            dsems[g] = stack.enter_context(nc.semaphore("d%d" % i))
        per = {e: [] for e in self.ENGS}
        for op in self.ops:
            per[op.eng].append(op)
        block = stack.enter_context(nc.Block())

        def run(engh, ops):
            seen = {}
            for op in ops:
                need = {}
                for d in op.waits:
                    if d.dma is not None:
                        key = ("d", d.dma)
                        val = 16 * self.dma_count[d.dma] if d.dma in self.total_groups else d.tick
                    else:
                        key = ("e", d.eng)
                        val = d.tick
                    if val > need.get(key, 0):
                        need[key] = val
                for key, val in need.items():
                    if seen.get(key, 0) >= val:
                        continue
                    seen[key] = val
                    sem = dsems[key[1]] if key[0] == "d" else sems[key[1]]
                    engh.wait_ge(sem, val)
                if op.fn is None:
                    continue
                ins = op.fn(engh)
                if op.dma is not None:
                    ins.then_inc(dsems[op.dma], 16)
                elif op.signal:
                    ins.then_inc(sems[op.eng], 1)

        @block.tensor
        def _(e):
            run(e, per["pe"])

        @block.scalar
        def _(e):
            run(e, per["act"])

        @block.vector
        def _(e):
            run(e, per["dve"])

        @block.gpsimd
        def _(e):
            run(e, per["pool"])

        @block.sync
        def _(e):
            run(e, per["sp"])


class Arena:
    def __init__(self, ap, nwords):
        self.ap = ap
        self.n = nwords
        self.off = 0

    def alloc(self, free_shape, dtype):
        nel = 1
        for s in free_shape:
            nel *= s
        bpe = 2 if dtype == BF16 else 4
        words = (nel * bpe + 3) // 4
        words = (words + 7) // 8 * 8
        assert self.off + words <= self.n, ("arena overflow", self.off, words, self.n)
        v = self.ap[:, self.off:self.off + words]
        self.off += words
        if dtype != F32:
            v = v.bitcast(dtype)
        v = v[:, 0:nel]
        if len(free_shape) == 2:
            v = v.rearrange("p (a b) -> p a b", a=free_shape[0])
        elif len(free_shape) == 3:
            v = v.rearrange("p (a b c) -> p a b c", a=free_shape[0], b=free_shape[1])
        return v


def build_program():
    from contextlib import ExitStack
    nc = bass.Bass("TRN2", target_bir_lowering=False)
    stack = ExitStack()
    P = Prog()

    def din(name, shape):
        return nc.dram_tensor(name, list(shape), F32, kind="ExternalInput").ap()

    xseq = {"s": din("xs", (SEQ_S, D)), "p": din("xp", (SEQ_P, D))}
    xtail = {"s": din("xts", (4, D)), "p": din("xtp", (4, D))}
    posd = {"s": din("poss", (1, SEQ_S + 4)), "p": din("posp", (1, SEQ_P + 4))}
    maskd = din("masks", (1, 8))
    ewin = din("e_w_in", (D, 2560)).rearrange("(c p) n -> p c n", p=128)
    ewout = din("e_w_out", (D, D)).rearrange("(c p) n -> p c n", p=128)
    w1d = [din("ffn_w1_%d" % l, (D, DFF)).rearrange("(c p) n -> p c n", p=128) for l in range(2)]
    w2d = [din("ffn_w2_%d" % l, (DFF, D)).rearrange("(f p) n -> p f n", p=128) for l in range(2)]
    cwin = din("c_w_in", (D, 3 * D)).rearrange("(c p) n -> p c n", p=128)
    cwout = din("c_w_out", (D, D)).rearrange("(c p) n -> p c n", p=128)
    wstd = din("wsT", (4, 128, 128)).rearrange("g q p -> q g p")
    identd = din("ident", (128, 128))
    rtd = din("rt", (128, 128))
    gcolsd = din("gcols", (128, 32))
    cwd = din("cw", (128, 24))
    rcd = din("rc", (128, 2))
    lvecd = din("lvec", (1, 256))
    vngd = din("vng", (1, 512))
    bsd = din("bs", (1, 512))
    sublnd = din("subln", (1, 128))
    fgd = din("fg", (1, D))
    xnd = {"s": nc.dram_tensor("xnd_s", [SEQ_S // 512, 128, 8 * 512], BF16, kind="Internal").ap(),
           "p": nc.dram_tensor("xnd_p", [SEQ_P // 512, 128, 8 * 512], BF16, kind="Internal").ap()}
    tabd = {"s": nc.dram_tensor("tabd_s", [SEQ_S // 512, 128, 2 * 512], F32, kind="Internal").ap(),
            "p": nc.dram_tensor("tabd_p", [SEQ_P // 512, 128, 2 * 512], F32, kind="Internal").ap()}
    yout = {"s": nc.dram_tensor("ys", [OWN, D], F32, kind="ExternalOutput").ap(),
            "p": nc.dram_tensor("yp", [OWN, D], F32, kind="ExternalOutput").ap()}

    ARENA_W = 43520
    PERS_W = 9216
    arena_t = stack.enter_context(nc.sbuf_tensor("arena", [128, ARENA_W], F32))
    pers_t = stack.enter_context(nc.sbuf_tensor("pers", [128, PERS_W], F32))
    psum = stack.enter_context(nc.psum_tensor("ps", [128, 8, 512], F32))
    pers = Arena(pers_t[:], PERS_W)
    arena_ap = arena_t[:]

    out_bT = pers.alloc((4, OWN + 4), BF16)
    ident = pers.alloc((128,), BF16)
    rt = pers.alloc((128,), BF16)
    wsT = pers.alloc((4, 128), BF16)
    gcols = pers.alloc((4, 8), F32)
    cw = pers.alloc((3, 8), F32)
    rc = pers.alloc((2,), F32)
    lvec = pers.alloc((256,), F32)
    ltmp = pers.alloc((64,), F32)
    lsc = pers.alloc((8,), F32)
    gBv = pers.alloc((512,), F32)
    bsB = pers.alloc((4, 128), F32)
    sublnB = pers.alloc((128,), F32)
    fgB = pers.alloc((D,), F32)
    maskB = pers.alloc((8,), F32)
    NSTAT = 16
    stat = pers.alloc((NSTAT, 12), F32)
    vstash = pers.alloc((4, 512), BF16)
    junk = pers.alloc((D,), BF16)
    epsb = pers.alloc((1,), F32)
    halfpi = pers.alloc((1,), F32)
    cst = pers.alloc((8,), F32)
    statC = pers.alloc((3, 12), F32)

    CONST = "const"
    ALLC = []

    def cload(eng, dst, src, name):
        P.add(eng, I("dma_start", out=dst, in_=src), writes=[("c", name)], dma=CONST + "_" + eng, wait_total=True)
        ALLC.append(("c", name))

    cload("pool", ident, identd[:, :], "ident")
    cload("pool", rt, rtd[:, :], "rt")
    cload("pool", wsT, wstd, "wsT")
    cload("sp", gcols, gcolsd.rearrange("p (a b) -> p a b", a=4), "gcols")
    cload("sp", cw, cwd.rearrange("p (a b) -> p a b", a=3), "cw")
    cload("sp", rc, rcd[:, :], "rc")
    cload("sp", lvec, lvecd.partition_broadcast(128), "lvec")
    cload("sp", gBv, vngd.partition_broadcast(128), "gBv")
    cload("sp", bsB, bsd.partition_broadcast(128).rearrange("p o (a b) -> p (o a) b", a=4), "bsB")
    cload("sp", sublnB, sublnd.partition_broadcast(128), "sublnB")
    cload("sp", fgB, fgd.partition_broadcast(128), "fgB")
    cload("sp", maskB, maskd.partition_broadcast(128), "maskB")

    P.add("dve", I("memset", epsb, EPS), writes=["epsb"])
    P.add("dve", I("memset", halfpi, math.pi / 2), writes=["halfpi"])
    P.add("dve", I("memset", cst[:, 0:1], 0.5), writes=["cst0"])
    P.add("dve", I("memset", statC, 1.0), writes=["statC"])
    P.add("dve", I("memset", cst[:, 1:2], 0.0), writes=["cst1"])
    P.add("dve", I("memset", cst[:, 2:3], -0.5), writes=["cst2"])
    P.add("dve", I("memset", cst[:, 3:4], D * EPS), writes=["cst3"])
    P.add("dve", I("memset", cst[:, 4:5], 128 * EPS), writes=["cst4"])
    for j in range(2):
        P.add("dve", I("tensor_tensor", out=ltmp, in0=lvec[:, 128 * j:128 * j + 64],
                       in1=lvec[:, 128 * j + 64:128 * j + 128], op=ALU.mult), reads=ALLC, writes=["ltmp"])
        P.add("dve", I("tensor_reduce", out=lsc[:, j:j + 1], in_=ltmp, axis=AX.X, op=ALU.add),
              reads=["ltmp"], writes=[("lsc", j)])
    P.add("act", I("activation", out=lsc[:, 2:4], in_=lsc[:, 0:2], func=AF.Exp),
          reads=[("lsc", 0), ("lsc", 1)], writes=["lsce"])
    P.add("dve", I("tensor_tensor", out=lsc[:, 4:5], in0=lsc[:, 2:3], in1=lsc[:, 3:4], op=ALU.subtract),
          reads=["lsce"], writes=["lam0"])
    P.add("dve", I("tensor_scalar", out=lsc[:, 5:6], in0=lsc[:, 4:5], scalar1=LAM_INIT, scalar2=-1.0,
                   op0=ALU.add, op1=ALU.mult), reads=["lam0"], writes=["neglam"])
    P.add("dve", I("tensor_scalar", out=sublnB, in0=sublnB, scalar1=(1.0 - LAM_INIT) * math.sqrt(128.0), scalar2=None,
                   op0=ALU.mult), reads=ALLC, writes=["sublnS"])
    P.add("dve", I("tensor_scalar", out=gcols, in0=gcols, scalar1=math.sqrt(float(D)), scalar2=None, op0=ALU.mult),
          reads=ALLC, writes=["gcolsS"])
    P.add("dve", I("tensor_scalar", out=fgB, in0=fgB, scalar1=math.sqrt(float(D)), scalar2=None, op0=ALU.mult),
          reads=ALLC, writes=["fgBS"])
    P.add("dve", I("tensor_scalar", out=gBv, in0=gBv, scalar1=math.sqrt(128.0), scalar2=None, op0=ALU.mult),
          reads=ALLC, writes=["gBvS"])
    P.barrier()

    def pool_rstd(src, dst, r, n, ccol, reads, wkey):
        P.add("pool", I("tensor_tensor", out=dst, in0=src, in1=cst[:r, ccol:ccol + 1].broadcast_to([r, n]), op=ALU.add),
              reads=reads, writes=[(wkey, "a")])
        P.add("pool", I("tensor_tensor", out=dst, in0=dst, in1=cst[:r, 2:3].broadcast_to([r, n]), op=ALU.pow),
              reads=[(wkey, "a")], writes=[wkey])

    state = {"stat": 0, "pt": 0, "ps": 0}

    def new_stat():
        s = state["stat"]
        state["stat"] = (s + 1) % NSTAT
        return s

    def bank(i):
        return psum[:, i, :]

    PS_RING = [0, 1, 2, 3, 6, 7]

    def next_ps():
        s = state["ps"]
        state["ps"] = (s + 1) % len(PS_RING)
        return PS_RING[s]

    def next_pt():
        s = state["pt"]
        state["pt"] = (s + 1) % 2
        return 4 + s

    def rstd_chain(src, r, srckey, scale):
        s = new_stat()
        sk = ("stat", s)
        n = src.shape[-1]
        P.add("act", I("activation", out=junk[:r, 0:n], in_=src, func=AF.Square, accum_out=stat[:r, s, 0:1]),
              reads=[srckey], writes=[(sk, 0)])
        pool_rstd(stat[:r, s, 0:1], stat[:r, s, 2:3], r, 1, 3 if n == D else 4, [(sk, 0)], (sk, 2))
        return stat[:r, s, 2:3], (sk, 2)

    def norm_to_xnT(src, r, srckey, xs_buf, xs_key, layer, dst, dst_key):
        rs, rsk = rstd_chain(src, r, srckey, 1.0 / D)
        P.add("act", I("activation", out=xs_buf[:r, :], in_=src, func=AF.Identity, scale=rs),
              reads=[srckey, rsk], writes=[xs_key])
        b = next_pt()
        ptv = bank(b).bitcast(BF16).rearrange("p (c n) -> p c n", c=8)
        P.add("pe", [I("transpose", out=ptv[:, c, 0:r], in_=xs_buf[:r, c * 128:(c + 1) * 128], identity=ident[:r, :r])
                     for c in range(8)], reads=[xs_key], writes=[("bank", b)])
        P.add("dve", I("tensor_tensor", out=dst, in0=ptv[:, :, 0:r],
                       in1=gcols[:, layer, :].unsqueeze(2).broadcast_to([128, 8, r]), op=ALU.mult),
              reads=[("bank", b)], writes=[dst_key])

    def mm(out_ap, lhs, rhs):
        nk = len(lhs)
        return [I("matmul", out_ap, lhsT=lhs[c], rhs=rhs[c], start=(c == 0), stop=(c == nk - 1)) for c in range(nk)]

    NW = 4
    WSLOT = 4096

    class WRing:
        def __init__(self, ar):
            self.slots = [ar.alloc((WSLOT,), BF16) for _ in range(NW)]
            self.i = 0

        def load(self, pieces):
            s = self.i % NW
            self.i += 1
            sl = self.slots[s]
            key = ("w", s)
            for dv, src in pieces:
                P.add("pool", I("dma_start", out=dv(sl), in_=src), writes=[key], dma=("w", s))
            return sl, key

    def v8(sl):
        return sl.rearrange("p (c n) -> p c n", c=8)

    def v4(sl):
        return sl.rearrange("p (c n) -> p c n", c=4)

    def v3(sl):
        return sl[:, 0:3072].rearrange("p (j c n) -> p j c n", j=3, c=8)

    def rope_tables(posB, n, poskey, Ct, St, A1, A2, tkey):
        A2i = A2.bitcast(I32)
        TE = os.environ.get("KTE", "pool")
        bc = lambda ap: ap.broadcast_to([128, n])
        P.add(TE, I("tensor_tensor", out=A1[:, 0:n], in0=posB[:, 0:n], in1=bc(rc[:, 0:1]), op=ALU.mult),
              reads=[poskey], writes=["tabA1"])
        P.add(TE, I("tensor_copy", out=A2i[:, 0:n], in_=A1[:, 0:n]), reads=["tabA1"], writes=["tabA2"])
        P.add(TE, I("tensor_tensor", out=A1[:, 0:n], in0=A1[:, 0:n], in1=A2i[:, 0:n], op=ALU.subtract),
              reads=["tabA1", "tabA2"], writes=["tabA1"])
        P.add("dve", I("scalar_tensor_tensor", out=A2[:, 0:n], in0=A1[:, 0:n], scalar=0.5, in1=A1[:, 0:n],
                       op0=ALU.is_gt, op1=ALU.subtract), reads=["tabA1"], writes=["tabA2"])
        P.add("act", I("activation", out=St[:, 0:n], in_=A2[:, 0:n], func=AF.Sin, scale=rc[:, 1:2]),
              reads=["tabA2"], writes=[(tkey, "S")])
        P.add("dve", I("scalar_tensor_tensor", out=A1[:, 0:n], in0=A2[:, 0:n], scalar=-1.0, in1=A2[:, 0:n],
                       op0=ALU.mult, op1=ALU.min), reads=["tabA2"], writes=["tabA1"])
        P.add("act", I("activation", out=Ct[:, 0:n], in_=A1[:, 0:n], func=AF.Sin, scale=TWO_PI, bias=halfpi[:, 0:1]),
              reads=["tabA1"], writes=[(tkey, "C")])

    def gmlp_v1(ps_ap, pskey, r, gv, sq, kx=0):
        P.add("act", I("activation", out=gv[:r, :], in_=ps_ap, func=AF.Gelu_apprx_tanh), reads=[pskey],
              writes=[("gv", kx)])
        s = new_stat()
        sk = ("stat", s)
        for g in range(4):
            P.add("act", I("activation", out=junk[:r, 0:128], in_=gv[:r, g * 128:(g + 1) * 128], func=AF.Square,
                           accum_out=stat[:r, s, g:g + 1]), reads=[("gv", kx)], writes=[(sk, 0, g)])
        return s

    def gmlp_v2(s, r, gv, vout, vkey, kx=0):
        sk = ("stat", s)
        pool_rstd(stat[:r, s, 0:4], stat[:r, s, 8:12], r, 4, 4, [(sk, 0, g) for g in range(4)], (sk, 2))
        for g in range(4):
            P.add("dve", I("scalar_tensor_tensor", out=vout[:r, g * 128:(g + 1) * 128], in0=gv[:r, g * 128:(g + 1) * 128],
                           scalar=stat[:r, s, 8 + g:9 + g], in1=gBv[:r, g * 128:(g + 1) * 128], op0=ALU.mult, op1=ALU.mult),
                  reads=[("gv", kx), (sk, 2)], writes=[vkey])

    def gmlp_v(ps_ap, pskey, r, gv, sq, vout, vkey, kx=0):
        s = gmlp_v1(ps_ap, pskey, r, gv, sq, kx)
        gmlp_v2(s, r, gv, vout, vkey, kx)

    def phase_AB(sq_name):
        S = SEQ_S if sq_name == "s" else SEQ_P
        NB = S // 512
        NKB = S // 128
        xd = xseq[sq_name]
        ar = Arena(arena_ap, ARENA_W)
        Kt = ar.alloc((S,), BF16)
        Vaug = ar.alloc((NKB, 130), BF16)
        Qt = ar.alloc((OWN + 4,), BF16)
        NXR = 8
        NXS = 4
        xring = [ar.alloc((D,), F32) for _ in range(NXR)]
        xsb = [ar.alloc((D,), BF16) for _ in range(NXS)]
        xnA = [ar.alloc((8, 512), BF16) for _ in range(2)]
        xnTl = ar.alloc((8, 4), BF16)
        kraw = [ar.alloc((512,), BF16) for _ in range(2)]
        t1 = ar.alloc((512,), F32)
        t2 = ar.alloc((512,), F32)
        Ct = [ar.alloc((512,), F32) for _ in range(2)]
        St = [ar.alloc((512,), F32) for _ in range(2)]
        A1 = ar.alloc((512,), F32)
        A2 = ar.alloc((512,), F32)
        posB = [ar.alloc((512,), F32) for _ in range(2)]
        NPT = 4
        Pt = [ar.alloc((2, 512), BF16) for _ in range(NPT)]
        otmp = ar.alloc((128,), F32)
        accS = ar.alloc((3, 388), F32)
        obS = ar.alloc((4, 128), F32)
        obn4 = ar.alloc((4, 128), BF16)
        gv = ar.alloc((512,), F32)
        sq = ar.alloc((512,), F32)
        ring = WRing(ar)
        tail_chunks = [NKB - 1, 8, 7, 16]
        cnt = {"x": 0, "xs": 0, "kraw": 0, "pt": 0}
        accs = [("bank", 4), ("bank", 5), ("bank", 6)]

        def acc(j, qt):
            i = j * 4 + qt
            return psum[:, 4 + i // 3, (i % 3) * 129:(i % 3) * 129 + 129]

        import os
        for hd in range(int(os.environ.get("KHEADS", "4"))):
            wsl, wkey = ring.load([(lambda sl, j=j: v3(sl)[:, j],
                                    ewin[:, :, 1024 + 512 * j + hd * 128:1024 + 512 * j + (hd + 1) * 128]) for j in range(3)])
            W3 = v3(wsl)
            if hd == 0:
                wav_sl, wavkey = ring.load([(v8, ewin[:, :, 512:1024])])
                Wav = v8(wav_sl)
            P.add("dve", I("memset", Vaug[:, :, 128:129], 1.0), writes=["Vaug1"])

            def rope_proj(j, rhs, n, ctab, stab, tabkey, dst, dstkey, xkeys, part=6):
                b1 = next_ps()
                P.add("pe", mm(bank(b1)[:, 0:n], [W3[:, j, c, :] for c in range(8)], rhs),
                      reads=xkeys + [wkey], writes=[("bank", b1)])
                if part < 2:
                    return
                ks = cnt["kraw"] % 2
                cnt["kraw"] += 1
                P.add("act", I("activation", out=kraw[ks][:, 0:n], in_=bank(b1)[:, 0:n], func=AF.Copy),
                      reads=[("bank", b1)], writes=[("kraw", ks)])
                if part < 3:
                    return
                b2 = next_ps()
                P.add("pe", I("matmul", bank(b2)[:, 0:n], lhsT=rt, rhs=kraw[ks][:, 0:n], start=True, stop=True),
                      reads=[("kraw", ks)], writes=[("bank", b2)])
                if part < 4:
                    return
                P.add("dve", I("tensor_tensor", out=t1[:, 0:n], in0=bank(b1)[:, 0:n], in1=ctab[:, 0:n], op=ALU.mult),
                      reads=[("bank", b1), (tabkey, "C")], writes=["t1"])
                if part < 5:
                    return
                P.add("dve", I("tensor_tensor", out=t2[:, 0:n], in0=bank(b2)[:, 0:n], in1=stab[:, 0:n], op=ALU.mult),
                      reads=[("bank", b2), (tabkey, "S")], writes=["t2"])
                if part < 6:
                    return
                P.add("dve", I("tensor_tensor", out=dst, in0=t1[:, 0:n], in1=t2[:, 0:n], op=ALU.add),
                      reads=["t1", "t2"], writes=[dstkey])

            xsl = cnt["x"] % NXR
            cnt["x"] += 1
            xt = xring[xsl]
            xk = ("xr", xsl)
            P.add("sp", I("dma_start", out=xt[0:4, :], in_=xtail[sq_name][:, :]), writes=[xk], dma=xk)
            xss = cnt["xs"] % NXS
            cnt["xs"] += 1
            norm_to_xnT(xt[0:4, :], 4, xk, xsb[xss], ("xsb", xss), 0, xnTl, "xnTl")
            P.add("sp", I("dma_start", out=posB[0][:, 0:4], in_=posd[sq_name][:, S:S + 4].partition_broadcast(128)),
                  writes=[("posB", 0)], dma=("posB", 0))
            rope_tables(posB[0], 4, ("posB", 0), Ct[0], St[0], A1, A2, ("tab", 0))
            rope_proj(0, [xnTl[:, c, :] for c in range(8)], 4, Ct[0], St[0], ("tab", 0), Qt[:, OWN:OWN + 4],
                      ("Qt", 4), ["xnTl"])

            if os.environ.get("KSTOP", "9") == "1":
                return
            NBr = int(os.environ.get("KNB", NB))
            blk = {}

            def par_of(b):
                return (b + 1) % 2

            def stage_load(b):
                par = par_of(b)
                P.add("sp", I("dma_start", out=posB[par],
                              in_=posd[sq_name][:, b * 512:(b + 1) * 512].partition_broadcast(128)),
                      writes=[("posB", par)], dma=("posB", par))
                xs_ = []
                for t in range(4):
                    xsl = (b % 2) * 4 + t
                    row = b * 512 + t * 128
                    P.add("sp", I("dma_start", out=xring[xsl], in_=xd[row:row + 128, :]), writes=[("xr", xsl)],
                          dma=("xr", xsl))
                    xs_.append(xsl)
                blk[b] = {"x": xs_}

            def stage_stats(b):
                s_ = new_stat()
                sk = ("stat", s_)
                for t in range(4):
                    xsl = blk[b]["x"][t]
                    P.add("act", I("activation", out=junk[:, :], in_=xring[xsl], func=AF.Square,
                                   accum_out=stat[:, s_, t:t + 1]), reads=[("xr", xsl)], writes=[(sk, 0, t)])
                pool_rstd(stat[:, s_, 0:4], stat[:, s_, 8:12], 128, 4, 3, [(sk, 0, t) for t in range(4)], (sk, 2))
                blk[b]["stat"] = s_

            def stage_mid(b):
                par = par_of(b)
                s_ = blk[b]["stat"]
                sk = ("stat", s_)
                for t in range(4):
                    xsl = blk[b]["x"][t]
                    rs_ap = stat[:, s_, 8 + t:9 + t]
                    if t == 2:
                        P.add("act", I("activation", out=xsb[t], in_=xring[xsl], func=AF.Identity, scale=rs_ap),
                              reads=[("xr", xsl), (sk, 2)], writes=[("xsb", t)])
                    elif t == 3:
                        P.add("act", I("activation", out=xsb[t], in_=xring[xsl], func=AF.Identity, scale=rs_ap),
                              reads=[("xr", xsl), (sk, 2)], writes=[("xsb", t)])
                    else:
                        P.add("dve", I("tensor_scalar", out=xsb[t], in0=xring[xsl], scalar1=rs_ap,
                                       scalar2=None, op0=ALU.mult), reads=[("xr", xsl), (sk, 2)], writes=[("xsb", t)])
                for t in range(4):
                    bnk = next_pt()
                    ptv = bank(bnk).bitcast(BF16).rearrange("p (c n) -> p c n", c=8)
                    P.add("pe", [I("transpose", out=ptv[:, c, :], in_=xsb[t][:, c * 128:(c + 1) * 128], identity=ident)
                                 for c in range(8)], reads=[("xsb", t)], writes=[("bank", bnk)])
                    P.add("dve", I("tensor_tensor", out=xnA[par][:, :, t * 128:(t + 1) * 128], in0=ptv,
                                   in1=gcols[:, 0, :].unsqueeze(2).broadcast_to([128, 8, 128]), op=ALU.mult),
                          reads=[("bank", bnk)], writes=[("xnA", par, t)])

            def stage_tables(b):
                par = par_of(b)
                rope_tables(posB[par], 512, ("posB", par), Ct[par], St[par], A1, A2, ("tab", par))

            cache = not os.environ.get("KNOCACHE")

            def store_tab(b):
                par = par_of(b)
                P.add("sp", I("dma_start", out=tabd[sq_name][b, :, 0:512], in_=Ct[par]),
                      reads=[(("tab", par), "C")], writes=[("tabd", b, 0)], dma=("tst", 0))
                P.add("sp", I("dma_start", out=tabd[sq_name][b, :, 512:1024], in_=St[par]),
                      reads=[(("tab", par), "S")], writes=[("tabd", b, 1)], dma=("tst", 1))

            def store_x(b):
                par = par_of(b)
                P.add("sp", I("dma_start", out=xnd[sq_name][b], in_=xnA[par].rearrange("p c n -> p (c n)")),
                      reads=[("xnA", par, t) for t in range(4)], writes=[("xnd", b)], dma=("xst", par))

            def stage_reload(b):
                par = par_of(b)
                P.add("sp", I("dma_start", out=xnA[par].rearrange("p c n -> p (c n)"), in_=xnd[sq_name][b]),
                      reads=[("xnd", b)], writes=[("xnA", par, t) for t in range(4)], dma=("xrl", par))
                P.add("sp", I("dma_start", out=Ct[par], in_=tabd[sq_name][b, :, 0:512]),
                      reads=[("tabd", b, 0)], writes=[(("tab", par), "C")], dma=("trl", par, 0))
                P.add("sp", I("dma_start", out=St[par], in_=tabd[sq_name][b, :, 512:1024]),
                      reads=[("tabd", b, 1)], writes=[(("tab", par), "S")], dma=("trl", par, 1))

            first_pass = (hd == 0) or not cache
            if first_pass:
                stage_load(0)
                if NBr > 1:
                    stage_load(1)
                stage_stats(0)
                stage_tables(0)
                if cache:
                    store_tab(0)
            else:
                stage_reload(0)
            for b in range(NBr):
                par = par_of(b)
                if first_pass:
                    stage_mid(b)
                    if b + 2 < NBr:
                        stage_load(b + 2)
                    if b + 1 < NBr:
                        stage_stats(b + 1)
                        stage_tables(b + 1)
                    if cache:
                        store_x(b)
                        if b + 1 < NBr:
                            store_tab(b + 1)
                elif b + 1 < NBr:
                    stage_reload(b + 1)
                xkeys = [("xnA", par, t) for t in range(4)]
                xr = [xnA[par][:, c, :] for c in range(8)]
                kv = os.environ.get("KVAR", "")
                if kv != "noK":
                    rope_proj(1, xr, 512, Ct[par], St[par], ("tab", par), Kt[:, b * 512:(b + 1) * 512], ("Kt", b), xkeys)
                if kv == "KbK":
                    P.barrier()
                    kv = "KK"
                if kv == "KK":
                    for _ in range(int(os.environ.get("KSKIP", "0"))):
                        next_ps()
                    rope_proj(1, xr, 512, Ct[par], St[par], ("tab", par), Kt[:, b * 512:(b + 1) * 512], ("Kt", b), xkeys,
                              part=int(os.environ.get("KPART", "6")))
                    continue
                if os.environ.get("KNOV"):
                    if b < 4 and os.environ.get("KNOV") == "q":
                        rope_proj(0, xr, 512, Ct[par], St[par], ("tab", par), Qt[:, b * 512:(b + 1) * 512], ("Qt", b), xkeys)
                    continue
                bv = next_ps()
                vm = []
                for t in range(4):
                    vm += mm(bank(bv)[:, t * 128:(t + 1) * 128], [xnA[par][:, c, t * 128:(t + 1) * 128] for c in range(8)],
                             [W3[:, 2, c, :] for c in range(8)])
                P.add("pe", vm, reads=xkeys + [wkey], writes=[("bank", bv)])
                for t in range(4):
                    P.add("dve", I("tensor_copy", out=Vaug[:, b * 4 + t, 0:128], in_=bank(bv)[:, t * 128:(t + 1) * 128]),
                          reads=[("bank", bv), "Vaug1"], writes=[("Vaug", b, t)])
                if b < 4:
                    rope_proj(0, xr, 512, Ct[par], St[par], ("tab", par), Qt[:, b * 512:(b + 1) * 512], ("Qt", b), xkeys)
                if hd == 0 and not os.environ.get("KNOAV"):
                    for ti, ch in enumerate(tail_chunks):
                        if ch // 4 == b:
                            t = ch % 4
                            ba = next_ps()
                            P.add("pe", mm(bank(ba), [xnA[par][:, c, t * 128:(t + 1) * 128] for c in range(8)],
                                           [Wav[:, c, :] for c in range(8)]),
                                  reads=xkeys + [wavkey], writes=[("bank", ba)])
                            gmlp_v(bank(ba), ("bank", ba), 128, gv, sq, vstash[:, ti, :], ("vstash", ti))

            if os.environ.get("KSTOP", "9") == "2":
                return
            qblocks = [(0, 512), (512, 512), (1024, 512), (1536, 512), (OWN, 4)]
            allK = [("Kt", b) for b in range(NB)]
            allV = [("Vaug", b, t) for b in range(NB) for t in range(4)]
            allQ = [("Qt", b) for b in range(5)]
            def accv(j, qt):
                i = j * 4 + qt
                return accS[:, i // 3, (i % 3) * 129:(i % 3) * 129 + 129]

            def make_fin(q0, nq, nqt, r):
                def fin():
                    akeys = [("accS", bi) for bi in range(3)]
                    s_ = new_stat()
                    sk = ("stat", s_)
                    for qt in range(nqt):
                        P.add("dve", I("reciprocal", out=stat[:r, s_, 0:1], in_=accv(0, qt)[0:r, 128:129]),
                              reads=akeys, writes=[(sk, 0)])
                        P.add("dve", I("reciprocal", out=stat[:r, s_, 1:2], in_=accv(1, qt)[0:r, 128:129]),
                              reads=akeys, writes=[(sk, 1)])
                        P.add("dve", I("tensor_tensor", out=stat[:r, s_, 2:3], in0=stat[:r, s_, 1:2], in1=lsc[:r, 5:6],
                                       op=ALU.mult), reads=[(sk, 1)], writes=[(sk, 2)])
                        P.add("dve", I("tensor_scalar", out=otmp[:r, :], in0=accv(0, qt)[0:r, 0:128],
                                       scalar1=stat[:r, s_, 0:1], scalar2=None, op0=ALU.mult),
                              reads=akeys + [(sk, 0)], writes=["otmp"])
                        P.add("dve", I("scalar_tensor_tensor", out=obS[:r, qt, :], in0=accv(1, qt)[0:r, 0:128],
                                       scalar=stat[:r, s_, 2:3], in1=otmp[:r, :], op0=ALU.mult, op1=ALU.add),
                              reads=akeys + [(sk, 2), "otmp"], writes=[("obS", qt)])
                        P.add("dve", I("tensor_tensor", out=otmp[:r, :], in0=obS[:r, qt, :], in1=obS[:r, qt, :], op=ALU.mult),
                              reads=[("obS", qt)], writes=["otmp"])
                        P.add("dve", I("tensor_reduce", out=stat[:r, s_, 4 + qt:5 + qt], in_=otmp[:r, :], axis=AX.X, op=ALU.add),
                              reads=["otmp"], writes=[(sk, 4, qt)])
                    return (s_, sk)

                def fin2(s_, sk):
                    pool_rstd(stat[:r, s_, 4:4 + nqt], stat[:r, s_, 8:8 + nqt], r, nqt, 4,
                              [(sk, 4, qt) for qt in range(nqt)], (sk, 9))
                    ptb = bank(7).bitcast(BF16)
                    for qt in range(nqt):
                        P.add("dve", I("scalar_tensor_tensor", out=obn4[:r, qt, :], in0=obS[:r, qt, :],
                                       scalar=stat[:r, s_, 8 + qt:9 + qt], in1=sublnB[:r, :], op0=ALU.mult, op1=ALU.mult),
                              reads=[("obS", qt), (sk, 9)], writes=[("obn", qt)])
                    P.add("pe", [I("transpose", out=ptb[:, qt * 128:qt * 128 + r], in_=obn4[:r, qt, :], identity=ident[:r, :r])
                                 for qt in range(nqt)], reads=[("obn", qt) for qt in range(nqt)], writes=[("bank", 7)])
                    P.add("dve", I("tensor_copy", out=out_bT[:, hd, q0:q0 + nq], in_=ptb[:, 0:nq]),
                          reads=[("bank", 7)], writes=[("obT", hd, q0)])
                return fin, fin2

            pending_fin = []
            for (q0, nq) in qblocks:
                nqt = (nq + 127) // 128
                r = min(128, nq)
                pq = []
                for kb in range(NKB + 2):
                    if kb == 14 and pending_fin:
                        f2_, args_ = pending_fin.pop(0)
                        f2_(*args_)
                    pend = None
                    if kb < NKB:
                        sb = (kb % 2) * 2
                        P.add("pe", [I("matmul", psum[:, sb, 0:nq], lhsT=Kt[0:64, kb * 128:(kb + 1) * 128],
                                       rhs=Qt[0:64, q0:q0 + nq], start=True, stop=True),
                                     I("matmul", psum[:, sb + 1, 0:nq], lhsT=Kt[64:128, kb * 128:(kb + 1) * 128],
                                       rhs=Qt[64:128, q0:q0 + nq], start=True, stop=True)],
                              reads=allK + allQ, writes=[("bank", sb), ("bank", sb + 1)])
                        psl = cnt["pt"] % NPT
                        cnt["pt"] += 1
                        P.add("act", I("activation", out=Pt[psl][:, :, 0:nq], in_=psum[:, sb:sb + 2, 0:nq], func=AF.Exp,
                                       scale=0.125), reads=[("bank", sb), ("bank", sb + 1)], writes=[("Pt", psl)])
                        pq.append((kb, psl))
                    if pq and (len(pq) > 2 or kb >= NKB):
                        pend = pq.pop(0)
                    if pend is not None:
                        pkb, pps = pend
                        pv = []
                        seen_banks = set()
                        for j in range(2):
                            for qt in range(nqt):
                                ai = j * 4 + qt
                                first_in_bank = (ai // 3) not in seen_banks
                                seen_banks.add(ai // 3)
                                pv.append(I("matmul", acc(j, qt)[0:r, :], lhsT=Pt[pps][:, j, qt * 128:qt * 128 + r],
                                            rhs=Vaug[:, pkb, 0:129], start=(pkb == 0 and first_in_bank),
                                            stop=(pkb == NKB - 1), skip_group_check=True))
                        P.add("pe", pv, reads=[("Pt", pps)] + allV, writes=accs)
                for bi in range(3):
                    ncol = 387 if bi < 2 else 258
                    P.add("dve", I("tensor_copy", out=accS[:r, bi, 0:ncol], in_=psum[0:r, 4 + bi, 0:ncol]),
                          reads=[("bank", 4 + bi)], writes=[("accS", bi)])
                f1_, f2_ = make_fin(q0, nq, nqt, r)
                pending_fin.append((f2_, f1_()))
            while pending_fin:
                f2_, args_ = pending_fin.pop(0)
                f2_(*args_)

    def phase_C(sq_name, ri):
        xd = xseq[sq_name]
        roff = ri * RNG
        sidx = 0 if sq_name == "s" else 1
        ar = Arena(arena_ap, ARENA_W)
        h = ar.alloc((9, D), F32)
        xnT = ar.alloc((8, RNG + 2), BF16)
        uT = ar.alloc((4, RNG + 2), BF16)
        oaT = ar.alloc((4, RNG + 2), BF16)
        hTall = ar.alloc((8, RNG + 2), BF16)
        hT = [hTall[:, 0:4, :], hTall[:, 4:8, :]]
        mTb = hTall
        xsb = [ar.alloc((D,), BF16) for _ in range(9)]
        uni = ar.alloc((4640,), F32)
        gv2 = [uni[:, i * 512:(i + 1) * 512] for i in range(3)]
        sq2 = [uni[:, 1536 + i * 512:1536 + (i + 1) * 512] for i in range(3)]
        mixt2 = [uni[:, 3072 + i * 512:3072 + (i + 1) * 512].rearrange("p (g n) -> p g n", g=4) for i in range(3)]
        vb2 = [ar.alloc((512,), BF16) for _ in range(3)]
        mixt = mixt2[0]
        rl = [ar.alloc((512,), F32) for _ in range(2)]
        bgs = uni[:, 0:RNG]
        cgs = uni[:, 1024:1024 + RNG + 2]
        tb = uni[:, 2056:2056 + RNG + 2]
        yb = uni[:, 3088:3088 + RNG]
        ring = WRing(ar)
        cnt = {"xs": 0, "rl": 0, "hT": 0}
        tiles = [(t, 128) for t in range(8)] + [(8, 2)]
        cblocks = [(0, 512), (512, 512), (RNG, 2)]

        def tile_cols(t):
            return (t * 128, 128) if t < 8 else (RNG, 2)

        def hkey(t):
            return ("h", t)

        for t in range(8):
            P.add("sp", I("dma_start", out=h[:, t, :], in_=xd[roff + t * 128:roff + (t + 1) * 128, :]),
                  writes=[hkey(t)], dma=("hl", t))
        P.add("sp", I("dma_start", out=h[0:2, 8, :], in_=xtail[sq_name][2 * ri:2 * ri + 2, :]),
              writes=[hkey(8)], dma=("hl", 8))

        def do_norm(layer, tl):
            nt = len(tl)
            for (t, r) in tl:
                P.add("act", I("activation", out=junk[:r, :], in_=h[0:r, t, :], func=AF.Square,
                               accum_out=statC[:r, 0, t:t + 1]), reads=[hkey(t)], writes=[("sC", 0, t)])
            pool_rstd(statC[:, 0, 0:nt], statC[:, 2, 0:nt], 128, nt, 3, [("sC", 0, t) for (t, r) in tl], ("sC", 2))
            for i, (t, r) in enumerate(tl):
                rs_ap = statC[:r, 2, t:t + 1]
                src = h[0:r, t, :]
                if i % 2 == 0:
                    P.add("act", I("activation", out=xsb[t][:r, :], in_=src, func=AF.Identity, scale=rs_ap),
                          reads=[hkey(t), ("sC", 2)], writes=[("xsbC", t)])
                else:
                    P.add("dve", I("tensor_scalar", out=xsb[t][:r, :], in0=src, scalar1=rs_ap, scalar2=None, op0=ALU.mult),
                          reads=[hkey(t), ("sC", 2)], writes=[("xsbC", t)])
            for (t, r) in tl:
                c0, n = tile_cols(t)
                bnk = next_pt()
                ptv = bank(bnk).bitcast(BF16).rearrange("p (c n) -> p c n", c=8)
                P.add("pe", [I("transpose", out=ptv[:, c, 0:r], in_=xsb[t][:r, c * 128:(c + 1) * 128], identity=ident[:r, :r])
                             for c in range(8)], reads=[("xsbC", t)], writes=[("bank", bnk)])
                P.add("dve", I("tensor_tensor", out=xnT[:, :, c0:c0 + n], in0=ptv[:, :, 0:r],
                               in1=gcols[:, layer, :].unsqueeze(2).broadcast_to([128, 8, r]), op=ALU.mult),
                      reads=[("bank", bnk)], writes=[("xnT", t)])

        def xkeys_cb(cb):
            c0, n = cb
            if n == 2:
                return [("xnT", 8)]
            return [("xnT", t) for t in range(c0 // 128, c0 // 128 + 4)]

        def resid_add(t, r, half, b):
            P.add("dve", I("tensor_tensor", out=h[0:r, t, half * 512:(half + 1) * 512],
                           in0=h[0:r, t, half * 512:(half + 1) * 512], in1=bank(b)[0:r, :], op=ALU.add),
                  reads=[("bank", b), hkey(t)], writes=[hkey(t)])

        do_norm(0, tiles)
        wau_sl, waukey = ring.load([(v8, ewin[:, :, 0:512])])
        wav_sl, wavkey = ring.load([(v8, ewin[:, :, 512:1024])])
        Wau = v8(wau_sl)
        Wav = v8(wav_sl)
        wo = []
        for half in range(2):
            sl, k = ring.load([(v4, ewout[:, half * 4:(half + 1) * 4, :])])
            wo.append((v4(sl), k))
        for cb in cblocks:
            c0, n = cb
            for g in range(4):
                b = next_ps()
                P.add("pe", mm(bank(b)[:, 0:n], [Wau[:, c, g * 128:(g + 1) * 128] for c in range(8)],
                               [xnT[:, c, c0:c0 + n] for c in range(8)]),
                      reads=xkeys_cb(cb) + [waukey], writes=[("bank", b)])
                P.add("act", I("activation", out=uT[:, g, c0:c0 + n], in_=bank(b)[:, 0:n], func=AF.Gelu_apprx_tanh),
                      reads=[("bank", b)], writes=[("uT", g, c0)])

        def ukeys(c0):
            cc = RNG if c0 >= RNG else (c0 // 512) * 512
            return [("uT", g, cc) for g in range(4)]
        gst = {}

        def g_a1(t):
            b = next_ps()
            P.add("pe", mm(bank(b), [xnT[:, c, t * 128:(t + 1) * 128] for c in range(8)], [Wav[:, c, :] for c in range(8)]),
                  reads=[("xnT", t), wavkey], writes=[("bank", b)])
            kx = t % 3
            gst[t] = gmlp_v1(bank(b), ("bank", b), 128, gv2[kx], sq2[kx], kx=kx)

        def g_a2(t):
            kx = t % 3
            gmlp_v2(gst[t], 128, gv2[kx], vb2[kx], ("vb", kx), kx=kx)
            b2 = next_ps()
            P.add("pe", [I("matmul", bank(b2)[:, g * 128:(g + 1) * 128], lhsT=vb2[kx][:, g * 128:(g + 1) * 128],
                           rhs=wsT[:, g, :], start=True, stop=True) for g in range(4)],
                  reads=[("vb", kx)], writes=[("bank", b2)])
            gst[("b2", t)] = b2

        def g_b(t):
            kx = t % 3
            b2 = gst[("b2", t)]
            mx = mixt2[kx]
            P.add("dve", I("tensor_tensor", out=mx, in0=bank(b2).rearrange("p (g n) -> p g n", g=4), in1=bsB, op=ALU.add),
                  reads=[("bank", b2)], writes=[("mixt", kx)])
            P.add("dve", I("tensor_tensor", out=oaT[:, :, t * 128:(t + 1) * 128], in0=mx,
                           in1=uT[:, :, t * 128:(t + 1) * 128], op=ALU.mult),
                  reads=[("mixt", kx)] + ukeys(t * 128), writes=[("oaT", t)])

        if os.environ.get("KGSEQ"):
            for t in range(8):
                g_a1(t)
                g_a2(t)
                g_b(t)
        else:
            g_a1(0)
            for t in range(8):
                if t + 1 < 8:
                    g_a1(t + 1)
                g_a2(t)
                if t >= 1:
                    g_b(t - 1)
            g_b(7)
        for j in range(2):
            ti = 2 * ri + j
            pcol = 127 if j == 0 else 0
            b2 = next_ps()
            P.add("pe", [I("matmul", bank(b2)[:, g:g + 1], lhsT=vstash[:, ti, g * 128:(g + 1) * 128],
                           rhs=wsT[:, g, pcol:pcol + 1], start=True, stop=True) for g in range(4)],
                  reads=[], writes=[("bank", b2)])
            P.add("dve", I("tensor_tensor", out=mixt[:, :, 0], in0=bank(b2)[:, 0:4], in1=bsB[:, :, pcol], op=ALU.add),
                  reads=[("bank", b2)], writes=[("mixt", 0)])
            P.add("dve", I("tensor_tensor", out=oaT[:, :, RNG + j], in0=mixt[:, :, 0], in1=uT[:, :, RNG + j], op=ALU.mult),
                  reads=[("mixt", 0)] + ukeys(RNG), writes=[("oaT", 8, j)])
        for (t, r) in tiles:
            c0, n = tile_cols(t)
            for half in range(2):
                b = next_ps()
                lhs = []
                for c in range(8):
                    if c < 4:
                        lhs.append(oaT[:, c, c0:c0 + r])
                    elif t < 8:
                        lhs.append(out_bT[:, c - 4, roff + c0:roff + c0 + r])
                    else:
                        lhs.append(out_bT[:, c - 4, OWN + 2 * ri:OWN + 2 * ri + 2])
                okeys = [("oaT", t)] if t < 8 else [("oaT", 8, 0), ("oaT", 8, 1)]
                P.add("pe", mm(bank(b)[0:r, :], lhs, [wo[c // 4][0][:, c % 4, half * 512:(half + 1) * 512] for c in range(8)]),
                      reads=okeys + [wo[0][1], wo[1][1]], writes=[("bank", b)])
                resid_add(t, r, half, b)

        ALLHT = [("hT", hs) for hs in range(2)]

        def ffn(layer, tl, cbl):
            do_norm(1 + 2 * layer, tl)
            st = {}

            def ld1(fg):
                st[("w1", fg)] = ring.load([(v8, w1d[layer][:, :, fg * 512:(fg + 1) * 512])])

            def ld2(fg):
                st[("w2", fg)] = ring.load([(v4, w2d[layer][:, fg * 4:(fg + 1) * 4, :])])

            def hidden(fg):
                s1, k1 = st[("w1", fg)]
                W1 = v8(s1)
                hs = fg % 2
                hb = hT[hs]
                for f in range(4):
                    for cb in cbl:
                        c0, n = cb
                        b = next_ps()
                        P.add("pe", mm(bank(b)[:, 0:n], [W1[:, c, f * 128:(f + 1) * 128] for c in range(8)],
                                       [xnT[:, c, c0:c0 + n] for c in range(8)]),
                              reads=xkeys_cb(cb) + [k1], writes=[("bank", b)])
                        rsl = cnt["rl"] % 2
                        cnt["rl"] += 1
                        P.add("act", I("activation", out=rl[rsl][:, 0:n], in_=bank(b)[:, 0:n], func=AF.Relu),
                              reads=[("bank", b)], writes=[("rl", rsl)])
                        P.add("dve", I("tensor_tensor", out=hb[:, f, c0:c0 + n], in0=rl[rsl][:, 0:n], in1=rl[rsl][:, 0:n],
                                       op=ALU.mult), reads=[("rl", rsl)], writes=[("hTw", hs, f, c0)])

            def second(fg):
                s2, k2 = st[("w2", fg)]
                W2 = v4(s2)
                hs = fg % 2
                hb = hT[hs]
                hkeys = [("hTw", hs, f, c0) for f in range(4) for (c0, n) in cbl]
                for (t, r) in tl:
                    c0, n = tile_cols(t)
                    for half in range(2):
                        b = next_ps()
                        P.add("pe", mm(bank(b)[0:r, :], [hb[:, f, c0:c0 + r] for f in range(4)],
                                       [W2[:, f, half * 512:(half + 1) * 512] for f in range(4)]),
                              reads=hkeys + [k2], writes=[("bank", b)])
                        resid_add(t, r, half, b)

            ld1(0)
            ld2(0)
            ld1(1)
            hidden(0)
            for fg in range(8):
                if fg + 1 < 8:
                    ld2(fg + 1)
                    hidden(fg + 1)
                if fg + 2 < 8:
                    ld1(fg + 2)
                second(fg)

        ffn(0, tiles, cblocks)

        do_norm(2, tiles)
        m0 = sidx * 4 + 2 * ri
        for c in range(8):
            wsl, wk = ring.load([(lambda sl, j=j: v3(sl)[:, j], cwin[:, :, j * D + c * 128:j * D + (c + 1) * 128])
                                 for j in range(3)])
            Wc = v3(wsl)
            for cb in cblocks:
                c0, n = cb
                xk = xkeys_cb(cb)
                xr = [xnT[:, k, c0:c0 + n] for k in range(8)]
                if n > 2:
                    b = next_ps()
                    P.add("pe", mm(bank(b)[:, 0:n], [Wc[:, 0, k, :] for k in range(8)], xr),
                          reads=xk + [wk], writes=[("bank", b)])
                    P.add("act", I("activation", out=bgs[:, c0:c0 + n], in_=bank(b)[:, 0:n], func=AF.Copy),
                          reads=[("bank", b)], writes=[("bgs", c0)])
                b = next_ps()
                P.add("pe", mm(bank(b)[:, 0:n], [Wc[:, 1, k, :] for k in range(8)], xr),
                      reads=xk + [wk], writes=[("bank", b)])
                P.add("act", I("activation", out=cgs[:, c0:c0 + n], in_=bank(b)[:, 0:n], func=AF.Copy),
                      reads=[("bank", b)], writes=[("cgs", c0)])
                b = next_ps()
                P.add("pe", mm(bank(b)[:, 0:n], [Wc[:, 2, k, :] for k in range(8)], xr),
                      reads=xk + [wk], writes=[("bank", b)])
                if n > 2:
                    P.add("dve", I("tensor_tensor", out=tb[:, 1 + c0:1 + c0 + n], in0=bank(b)[:, 0:n], in1=cgs[:, c0:c0 + n],
                                   op=ALU.mult), reads=[("bank", b), ("cgs", c0)], writes=[("tb", c0)])
                else:
                    P.add("dve", I("tensor_tensor", out=cgs[:, RNG:RNG + 2], in0=bank(b)[:, 0:2], in1=cgs[:, RNG:RNG + 2],
                                   op=ALU.mult), reads=[("bank", b), ("cgs", RNG)], writes=[("cgs2", RNG)])
                    P.add("dve", I("tensor_tensor", out=tb[:, 0:1], in0=cgs[:, RNG:RNG + 1], in1=maskB[:, m0:m0 + 1],
                                   op=ALU.mult), reads=[("cgs2", RNG)], writes=[("tb", "l")])
                    P.add("dve", I("tensor_tensor", out=tb[:, RNG + 1:RNG + 2], in0=cgs[:, RNG + 1:RNG + 2],
                                   in1=maskB[:, m0 + 1:m0 + 2], op=ALU.mult), reads=[("cgs2", RNG)], writes=[("tb", "r")])
            tkeys = [("tb", 0), ("tb", 512), ("tb", "l"), ("tb", "r")]
            P.add("dve", I("tensor_scalar", out=yb, in0=tb[:, 1:RNG + 1], scalar1=cw[:, 1, c:c + 1], scalar2=None,
                           op0=ALU.mult), reads=tkeys, writes=["yb"])
            P.add("dve", I("scalar_tensor_tensor", out=yb, in0=tb[:, 0:RNG], scalar=cw[:, 0, c:c + 1], in1=yb,
                           op0=ALU.mult, op1=ALU.add), reads=tkeys + ["yb"], writes=["yb"])
            P.add("dve", I("scalar_tensor_tensor", out=yb, in0=tb[:, 2:RNG + 2], scalar=cw[:, 2, c:c + 1], in1=yb,
                           op0=ALU.mult, op1=ALU.add), reads=tkeys + ["yb"], writes=["yb"])
            P.add("dve", I("tensor_tensor", out=mTb[:, c, 0:RNG], in0=bgs, in1=yb, op=ALU.mult),
                  reads=["yb", ("bgs", 0), ("bgs", 512)],
                  writes=[("mT", c), ("hTw", c // 4, c % 4, 0), ("hTw", c // 4, c % 4, 512)])
        wco = []
        for half in range(2):
            sl, k = ring.load([(v4, cwout[:, half * 4:(half + 1) * 4, :])])
            wco.append((v4(sl), k))
        mkeys = [("mT", c) for c in range(8)] + [("hTw", hs_, f_, c0_) for hs_ in range(2) for f_ in range(4) for c0_ in (0, 512)]
        own_tiles = tiles[:8]
        for (t, r) in own_tiles:
            for half in range(2):
                b = next_ps()
                P.add("pe", mm(bank(b), [mTb[:, c, t * 128:(t + 1) * 128] for c in range(8)],
                               [wco[c // 4][0][:, c % 4, half * 512:(half + 1) * 512] for c in range(8)]),
                      reads=mkeys + [wco[0][1], wco[1][1]], writes=[("bank", b)])
                resid_add(t, r, half, b)

        ffn(1, own_tiles, cblocks[:2])

        ystage = [bgs, yb]
        yskeys = [[("bgs", 0), ("bgs", 512)], ["yb"]]
        for half in range(2):
            for t in range(half * 4, half * 4 + 4):
                P.add("act", I("activation", out=junk[:, :], in_=h[:, t, :], func=AF.Square,
                               accum_out=statC[:, 0, t:t + 1]), reads=[hkey(t)], writes=[("sC", 0, t)])
            pool_rstd(statC[:, 0, half * 4:half * 4 + 4], statC[:, 2, half * 4:half * 4 + 4], 128, 4, 3,
                      [("sC", 0, t) for t in range(half * 4, half * 4 + 4)], ("sCf", half))
        for (t, r) in own_tiles:
            rs = statC[:, 2, t:t + 1]
            rsk = ("sCf", t // 4)
            ys = t % 2
            P.add("dve", I("scalar_tensor_tensor", out=ystage[ys], in0=h[:, t, :], scalar=rs, in1=fgB,
                           op0=ALU.mult, op1=ALU.mult), reads=[hkey(t), rsk], writes=yskeys[ys])
            P.add("sp", I("dma_start", out=yout[sq_name][roff + t * 128:roff + (t + 1) * 128, :], in_=ystage[ys]),
                  reads=yskeys[ys], writes=[("yout", sq_name, ri, t)], dma=("yst", ys))

    import os
    dbg = os.environ.get("KDBG", "")
    for sq_name in ("s", "p"):
        if dbg and sq_name not in dbg:
            continue
        if not dbg or "A" in dbg:
            phase_AB(sq_name)
            P.barrier()
        if not dbg or "C" in dbg:
            phase_C(sq_name, 0)
            P.barrier()
        if not dbg or "D" in dbg:
            phase_C(sq_name, 1)
            P.barrier()
    for e in Prog.ENGS:
        P.add(e, None)
    stuck, per_, pos_ = P.simulate()
    if stuck:
        for e, (p_, n_) in stuck.items():
            op = per_[e][p_]
            print("DEADLOCK", e, p_, n_, "op idx", op.idx, "waits", [(d.eng, d.idx, d.dma, d.tick, d.signal) for d in op.waits])
        raise RuntimeError("semaphore protocol deadlock")
    print("program ops:", len(P.ops), {e: len(v) for e, v in per_.items()})
    if os.environ.get("KDUMP"):
        for op in P.ops:
            print(op.idx, op.eng, op.name, "tick", op.tick if (op.signal or op.dma) else None,
                  "W:", [(d.eng, d.idx, d.tick) for d in op.waits], "w=", op.rw[1][:3])
    P.emit(nc, stack)
    stack.close()
    return nc


_NC_CACHE = {}


def _host_constants():
    ident = np.eye(128, dtype=np.float32)
    rt = np.zeros((128, 128), np.float32)
    invf = np.zeros(128, np.float32)
    sgn = np.zeros(128, np.float32)
    inv = (500000.0 ** (-np.arange(0, 16, 2, dtype=np.float32) / 16.0)).astype(np.float32)
    for base in (0, 64):
        for d in range(16):
            p = base + d
            partner = p + 8 if d < 8 else p - 8
            rt[partner, p] = 1.0
            invf[p] = inv[d % 8] / np.float32(2 * np.pi)
            sgn[p] = -1.0 if d < 8 else 1.0
    rc = np.stack([invf, (-2.0 * np.pi * sgn).astype(np.float32)], axis=1).astype(np.float32)
    return ident, rt, rc


def kernel(**inputs):
    f32 = lambda a: np.ascontiguousarray(np.asarray(a, dtype=np.float32))
    xpr = f32(inputs["x_prompt"])
    xsa = f32(inputs["x_sample"])
    ident, rt, rc = _host_constants()
    gains = np.stack([f32(inputs["norm_mix_g"])[0], f32(inputs["norm_ffn_g"])[0],
                      f32(inputs["norm_mix_g"])[1], f32(inputs["norm_ffn_g"])[1]], axis=0)
    gcols = np.ascontiguousarray(gains.reshape(4, 8, 128).transpose(2, 0, 1).reshape(128, 32))
    cwh = np.ascontiguousarray(f32(inputs["c_conv_w"])[0].reshape(3, 8, 128).transpose(2, 0, 1).reshape(128, 24))
    lvec = np.concatenate([f32(inputs["b_lq1"])[0], f32(inputs["b_lk1"])[0],
                           f32(inputs["b_lq2"])[0], f32(inputs["b_lk2"])[0]])[None, :]
    shared = {
        "e_w_in": f32(inputs["e_w_in"])[0], "e_w_out": f32(inputs["e_w_out"])[0],
        "ffn_w1_0": f32(inputs["ffn_w1"])[0], "ffn_w1_1": f32(inputs["ffn_w1"])[1],
        "ffn_w2_0": f32(inputs["ffn_w2"])[0], "ffn_w2_1": f32(inputs["ffn_w2"])[1],
        "c_w_in": f32(inputs["c_w_in"])[0], "c_w_out": f32(inputs["c_w_out"])[0],
        "wsT": np.ascontiguousarray(f32(inputs["a_w_s"])[0].transpose(0, 2, 1)),
        "ident": ident, "rt": rt, "gcols": gcols, "cw": cwh, "rc": rc, "lvec": np.ascontiguousarray(lvec),
        "vng": f32(inputs["a_vnorm_g"]).reshape(1, 512), "bs": f32(inputs["a_b_s"]).reshape(1, 512),
        "subln": f32(inputs["b_subln_g"]).reshape(1, 128), "fg": f32(inputs["final_g"]).reshape(1, D),
    }
    in_maps = []
    meta = []
    for c in range(NCORES):
        bp, op_ = c // 2, (c % 2) * OWN
        bs_, os_ = c // 4, (c % 4) * OWN
        m = dict(shared)
        masks = np.zeros((1, 8), np.float32)
        for key, x, b, off, S, mi in (("s", xsa, bs_, os_, SEQ_S, 0), ("p", xpr, bp, op_, SEQ_P, 4)):
            xr = np.ascontiguousarray(np.roll(x[b], -off, axis=0))
            tidx = np.array([S - 1, RNG, RNG - 1, OWN])
            pos = ((np.arange(S) + off) % S).astype(np.float32)
            post = ((tidx + off) % S).astype(np.float32)
            m["x" + key] = xr
            m["xt" + key] = np.ascontiguousarray(xr[tidx])
            m["pos" + key] = np.concatenate([pos, post])[None, :].astype(np.float32)
            masks[0, mi + 0] = 1.0 if off > 0 else 0.0
            masks[0, mi + 1] = 1.0
            masks[0, mi + 2] = 1.0
            masks[0, mi + 3] = 1.0 if off + OWN < S else 0.0
        m["masks"] = masks
        in_maps.append(m)
        meta.append((bp, op_, bs_, os_))
    if "nc" not in _NC_CACHE:
        _NC_CACHE["nc"] = build_program()
    import os as _os
    ncr = int(_os.environ.get("KCORES", NCORES))
    if _os.environ.get("KTRACE"):
        res = run_bass_kernel_spmd(_NC_CACHE["nc"], in_maps[:ncr], core_ids=list(range(ncr)), trace=True)
        print("KTRACE exec_time_ns", res.exec_time_ns)
    else:
        res = run_bass_kernel_spmd(_NC_CACHE["nc"], in_maps[:ncr], core_ids=list(range(ncr)))
    y_p = np.zeros_like(xpr)
    y_s = np.zeros_like(xsa)
    for c, (bp, op_, bs_, os_) in enumerate(meta[:ncr]):
        y_p[bp, op_:op_ + OWN] = res.results[c]["yp"]
        y_s[bs_, os_:os_ + OWN] = res.results[c]["ys"]
    return (y_p, y_s)
```

```python
import math
import os
import numpy as np
import concourse.bass as bass
import concourse.mybir as mybir
from concourse.bass_utils import run_bass_kernel_spmd

F32 = mybir.dt.float32
BF16 = mybir.dt.bfloat16
I32 = mybir.dt.int32
AF = mybir.ActivationFunctionType
ALU = mybir.AluOpType
AX = mybir.AxisListType

D = 1024
DFF = 4096
EPS = 1e-5
NCORES = 8
SEQ_S = 8192
SEQ_P = 4096
OWN = 2048
RNG = 1024
LAM_INIT = 0.8 - 0.6 * math.exp(0.0)
TWO_PI = 2.0 * math.pi


class Op:
    __slots__ = ("eng", "fn", "waits", "signal", "tick", "dma", "idx", "name", "rw")


def I(name, *args, **kw):
    return (name, args, kw)


def _mkfn(spec):
    if spec is None:
        return None
    if callable(spec):
        return spec
    if isinstance(spec, tuple):
        spec = [spec]

    def f(e, spec=spec):
        ins = None
        for (name, args, kw) in spec:
            ins = getattr(e, name)(*args, **kw)
        return ins
    return f


class Prog:
    ENGS = ("pe", "act", "dve", "pool", "sp")

    def __init__(self):
        self.ops = []
        self.last_w = {}
        self.readers = {}
        self.dma_count = {}
        self.total_groups = set()
        self.extra = {e: [] for e in self.ENGS}
        self.dma_last = {}

    def barrier(self):
        lasts = {}
        for op in reversed(self.ops):
            if op.dma is None and op.eng not in lasts and op.fn is not None:
                lasts[op.eng] = op
        deps = list(lasts.values()) + list(self.dma_last.values())
        for e in self.ENGS:
            self.extra[e] = [d for d in deps if not (d.dma is None and d.eng == e)]
        for d in lasts.values():
            d.signal = True
        self.last_w = {}
        self.readers = {}

    def add(self, eng, fn, reads=(), writes=(), dma=None, wait_total=False):
        op = Op()
        op.eng = eng
        op.fn = _mkfn(fn)
        try:
            op.name = fn[0] if isinstance(fn, tuple) else (fn[-1][0] + "x%d" % len(fn) if isinstance(fn, list) else str(fn))
        except Exception:
            op.name = "?"
        op.signal = False
        op.tick = None
        op.dma = dma
        op.idx = len(self.ops)
        bank_r = [k for k in reads if isinstance(k, tuple) and k and k[0] == "bank" and k not in writes]
        if bank_r:
            writes = list(writes) + bank_r
        deps = {}
        for k in reads:
            w = self.last_w.get(k)
            if w is not None:
                deps[w.idx] = (w, True)
        for k in writes:
            w = self.last_w.get(k)
            if w is not None and w.idx not in deps:
                deps[w.idx] = (w, False)
            for r in self.readers.get(k, ()):
                if r.idx not in deps:
                    deps[r.idx] = (r, False)
        waits = []
        for d, raw in deps.values():
            if d.dma is None and d.eng == eng:
                if eng == "pe" or not raw:
                    continue
            waits.append(d)
            if d.dma is None:
                d.signal = True
        if self.extra[eng]:
            waits = waits + self.extra[eng]
            self.extra[eng] = []
        op.waits = waits
        op.rw = (list(reads), list(writes))
        for k in reads:
            self.readers.setdefault(k, []).append(op)
        for k in writes:
            self.last_w[k] = op
            self.readers[k] = []
        if dma is not None:
            self.dma_count[dma] = self.dma_count.get(dma, 0) + 1
            op.tick = 16 * self.dma_count[dma]
            self.dma_last[dma] = op
            if wait_total:
                self.total_groups.add(dma)
        self.ops.append(op)
        return op

    def simulate(self):
        cnt = {e: 0 for e in self.ENGS}
        for op in self.ops:
            if op.dma is None and op.signal:
                cnt[op.eng] += 1
                op.tick = cnt[op.eng]
        per = {e: [op for op in self.ops if op.eng == e] for e in self.ENGS}
        pos = {e: 0 for e in self.ENGS}
        sem = {}
        progress = True
        while progress:
            progress = False
            for e in self.ENGS:
                while pos[e] < len(per[e]):
                    op = per[e][pos[e]]
                    ok = True
                    for d in op.waits:
                        if d.dma is not None:
                            key = ("d", d.dma)
                            val = 16 * self.dma_count[d.dma] if d.dma in self.total_groups else d.tick
                        else:
                            key = ("e", d.eng)
                            val = d.tick
                        if sem.get(key, 0) < val:
                            ok = False
                            break
                    if not ok:
                        break
                    if op.fn is not None:
                        if op.dma is not None:
                            sem[("d", op.dma)] = sem.get(("d", op.dma), 0) + 16
                        elif op.signal:
                            sem[("e", op.eng)] = sem.get(("e", op.eng), 0) + 1
                    pos[e] += 1
                    progress = True
        stuck = {e: (pos[e], len(per[e])) for e in self.ENGS if pos[e] < len(per[e])}
        return stuck, per, pos

    def emit(self, nc, stack):
        cnt = {e: 0 for e in self.ENGS}
        for op in self.ops:
            if op.dma is None and op.signal:
                cnt[op.eng] += 1
                op.tick = cnt[op.eng]
        sems = {e: stack.enter_context(nc.semaphore("s_" + e)) for e in self.ENGS}
        dsems = {}
        for i, g in enumerate(self.dma_count):
            dsems[g] = stack.enter_context(nc.semaphore("d%d" % i))
        per = {e: [] for e in self.ENGS}
        for op in self.ops:
            per[op.eng].append(op)
        block = stack.enter_context(nc.Block())

        def run(engh, ops):
            seen = {}
            for op in ops:
                need = {}
                for d in op.waits:
                    if d.dma is not None:
                        key = ("d", d.dma)
                        val = 16 * self.dma_count[d.dma] if d.dma in self.total_groups else d.tick
                    else:
                        key = ("e", d.eng)
                        val = d.tick
                    if val > need.get(key, 0):
                        need[key] = val
                for key, val in need.items():
                    if seen.get(key, 0) >= val:
                        continue
                    seen[key] = val
                    sem = dsems[key[1]] if key[0] == "d" else sems[key[1]]
                    engh.wait_ge(sem, val)
                if op.fn is None:
                    continue
                ins = op.fn(engh)
                if op.dma is not None:
                    ins.then_inc(dsems[op.dma], 16)
                elif op.signal:
                    ins.then_inc(sems[op.eng], 1)

        @block.tensor
        def _(e):
            run(e, per["pe"])

        @block.scalar
        def _(e):
            run(e, per["act"])

        @block.vector
        def _(e):
            run(e, per["dve"])

        @block.gpsimd
        def _(e):
            run(e, per["pool"])

        @block.sync
        def _(e):
            run(e, per["sp"])


class Arena:
    def __init__(self, ap, nwords):
        self.ap = ap
        self.n = nwords
        self.off = 0

    def alloc(self, free_shape, dtype):
        nel = 1
        for s in free_shape:
            nel *= s
        bpe = 2 if dtype == BF16 else 4
        words = (nel * bpe + 3) // 4
        words = (words + 7) // 8 * 8
        assert self.off + words <= self.n, ("arena overflow", self.off, words, self.n)
        v = self.ap[:, self.off:self.off + words]
        self.off += words
        if dtype != F32:
            v = v.bitcast(dtype)
        v = v[:, 0:nel]
        if len(free_shape) == 2:
            v = v.rearrange("p (a b) -> p a b", a=free_shape[0])
        elif len(free_shape) == 3:
            v = v.rearrange("p (a b c) -> p a b c", a=free_shape[0], b=free_shape[1])
        return v


def build_program():
    from contextlib import ExitStack
    nc = bass.Bass("TRN2", target_bir_lowering=False)
    stack = ExitStack()
    P = Prog()

    def din(name, shape):
        return nc.dram_tensor(name, list(shape), F32, kind="ExternalInput").ap()

    xseq = {"s": din("xs", (SEQ_S, D)), "p": din("xp", (SEQ_P, D))}
    xtail = {"s": din("xts", (4, D)), "p": din("xtp", (4, D))}
    posd = {"s": din("poss", (1, SEQ_S + 4)), "p": din("posp", (1, SEQ_P + 4))}
    maskd = din("masks", (1, 8))
    ewin = din("e_w_in", (D, 2560)).rearrange("(c p) n -> p c n", p=128)
    ewout = din("e_w_out", (D, D)).rearrange("(c p) n -> p c n", p=128)
    w1d = [din("ffn_w1_%d" % l, (D, DFF)).rearrange("(c p) n -> p c n", p=128) for l in range(2)]
    w2d = [din("ffn_w2_%d" % l, (DFF, D)).rearrange("(f p) n -> p f n", p=128) for l in range(2)]
    cwin = din("c_w_in", (D, 3 * D)).rearrange("(c p) n -> p c n", p=128)
    cwout = din("c_w_out", (D, D)).rearrange("(c p) n -> p c n", p=128)
    wstd = din("wsT", (4, 128, 128)).rearrange("g q p -> q g p")
    identd = din("ident", (128, 128))
    rtd = din("rt", (128, 128))
    gcolsd = din("gcols", (128, 32))
    cwd = din("cw", (128, 24))
    rcd = din("rc", (128, 2))
    lvecd = din("lvec", (1, 256))
    vngd = din("vng", (1, 512))
    bsd = din("bs", (1, 512))
    sublnd = din("subln", (1, 128))
    fgd = din("fg", (1, D))
    xnd = {"s": nc.dram_tensor("xnd_s", [SEQ_S // 512, 128, 8 * 512], BF16, kind="Internal").ap(),
           "p": nc.dram_tensor("xnd_p", [SEQ_P // 512, 128, 8 * 512], BF16, kind="Internal").ap()}
    tabd = {"s": nc.dram_tensor("tabd_s", [SEQ_S // 512, 128, 2 * 512], F32, kind="Internal").ap(),
            "p": nc.dram_tensor("tabd_p", [SEQ_P // 512, 128, 2 * 512], F32, kind="Internal").ap()}
    yout = {"s": nc.dram_tensor("ys", [OWN, D], F32, kind="ExternalOutput").ap(),
            "p": nc.dram_tensor("yp", [OWN, D], F32, kind="ExternalOutput").ap()}

    ARENA_W = 43520
    PERS_W = 9216
    arena_t = stack.enter_context(nc.sbuf_tensor("arena", [128, ARENA_W], F32))
    pers_t = stack.enter_context(nc.sbuf_tensor("pers", [128, PERS_W], F32))
    psum = stack.enter_context(nc.psum_tensor("ps", [128, 8, 512], F32))
    pers = Arena(pers_t[:], PERS_W)
    arena_ap = arena_t[:]

    out_bT = pers.alloc((4, OWN + 4), BF16)
    ident = pers.alloc((128,), BF16)
    rt = pers.alloc((128,), BF16)
    wsT = pers.alloc((4, 128), BF16)
    gcols = pers.alloc((4, 8), F32)
    cw = pers.alloc((3, 8), F32)
    rc = pers.alloc((2,), F32)
    lvec = pers.alloc((256,), F32)
    ltmp = pers.alloc((64,), F32)
    lsc = pers.alloc((8,), F32)
    gBv = pers.alloc((512,), F32)
    bsB = pers.alloc((4, 128), F32)
    sublnB = pers.alloc((128,), F32)
    fgB = pers.alloc((D,), F32)
    maskB = pers.alloc((8,), F32)
    NSTAT = 16
    stat = pers.alloc((NSTAT, 12), F32)
    vstash = pers.alloc((4, 512), BF16)
    junk = pers.alloc((D,), BF16)
    epsb = pers.alloc((1,), F32)
    halfpi = pers.alloc((1,), F32)
    cst = pers.alloc((8,), F32)
    statC = pers.alloc((3, 12), F32)

    CONST = "const"
    ALLC = []

    def cload(eng, dst, src, name):
        P.add(eng, I("dma_start", out=dst, in_=src), writes=[("c", name)], dma=CONST + "_" + eng, wait_total=True)
        ALLC.append(("c", name))

    cload("pool", ident, identd[:, :], "ident")
    cload("pool", rt, rtd[:, :], "rt")
    cload("pool", wsT, wstd, "wsT")
    cload("sp", gcols, gcolsd.rearrange("p (a b) -> p a b", a=4), "gcols")
    cload("sp", cw, cwd.rearrange("p (a b) -> p a b", a=3), "cw")
    cload("sp", rc, rcd[:, :], "rc")
    cload("sp", lvec, lvecd.partition_broadcast(128), "lvec")
    cload("sp", gBv, vngd.partition_broadcast(128), "gBv")
    cload("sp", bsB, bsd.partition_broadcast(128).rearrange("p o (a b) -> p (o a) b", a=4), "bsB")
    cload("sp", sublnB, sublnd.partition_broadcast(128), "sublnB")
    cload("sp", fgB, fgd.partition_broadcast(128), "fgB")
    cload("sp", maskB, maskd.partition_broadcast(128), "maskB")

    P.add("dve", I("memset", epsb, EPS), writes=["epsb"])
    P.add("dve", I("memset", halfpi, math.pi / 2), writes=["halfpi"])
    P.add("dve", I("memset", cst[:, 0:1], 0.5), writes=["cst0"])
    P.add("dve", I("memset", statC, 1.0), writes=["statC"])
    P.add("dve", I("memset", cst[:, 1:2], 0.0), writes=["cst1"])
    P.add("dve", I("memset", cst[:, 2:3], -0.5), writes=["cst2"])
    P.add("dve", I("memset", cst[:, 3:4], D * EPS), writes=["cst3"])
    P.add("dve", I("memset", cst[:, 4:5], 128 * EPS), writes=["cst4"])
    for j in range(2):
        P.add("dve", I("tensor_tensor", out=ltmp, in0=lvec[:, 128 * j:128 * j + 64],
                       in1=lvec[:, 128 * j + 64:128 * j + 128], op=ALU.mult), reads=ALLC, writes=["ltmp"])
        P.add("dve", I("tensor_reduce", out=lsc[:, j:j + 1], in_=ltmp, axis=AX.X, op=ALU.add),
              reads=["ltmp"], writes=[("lsc", j)])
    P.add("act", I("activation", out=lsc[:, 2:4], in_=lsc[:, 0:2], func=AF.Exp),
          reads=[("lsc", 0), ("lsc", 1)], writes=["lsce"])
    P.add("dve", I("tensor_tensor", out=lsc[:, 4:5], in0=lsc[:, 2:3], in1=lsc[:, 3:4], op=ALU.subtract),
          reads=["lsce"], writes=["lam0"])
    P.add("dve", I("tensor_scalar", out=lsc[:, 5:6], in0=lsc[:, 4:5], scalar1=LAM_INIT, scalar2=-1.0,
                   op0=ALU.add, op1=ALU.mult), reads=["lam0"], writes=["neglam"])
    P.add("dve", I("tensor_scalar", out=sublnB, in0=sublnB, scalar1=(1.0 - LAM_INIT) * math.sqrt(128.0), scalar2=None,
                   op0=ALU.mult), reads=ALLC, writes=["sublnS"])
    P.add("dve", I("tensor_scalar", out=gcols, in0=gcols, scalar1=math.sqrt(float(D)), scalar2=None, op0=ALU.mult),
          reads=ALLC, writes=["gcolsS"])
    P.add("dve", I("tensor_scalar", out=fgB, in0=fgB, scalar1=math.sqrt(float(D)), scalar2=None, op0=ALU.mult),
          reads=ALLC, writes=["fgBS"])
    P.add("dve", I("tensor_scalar", out=gBv, in0=gBv, scalar1=math.sqrt(128.0), scalar2=None, op0=ALU.mult),
          reads=ALLC, writes=["gBvS"])
    P.barrier()

    def pool_rstd(src, dst, r, n, ccol, reads, wkey):
        P.add("pool", I("tensor_tensor", out=dst, in0=src, in1=cst[:r, ccol:ccol + 1].broadcast_to([r, n]), op=ALU.add),
              reads=reads, writes=[(wkey, "a")])
        P.add("pool", I("tensor_tensor", out=dst, in0=dst, in1=cst[:r, 2:3].broadcast_to([r, n]), op=ALU.pow),
              reads=[(wkey, "a")], writes=[wkey])

    state = {"stat": 0, "pt": 0, "ps": 0}

    def new_stat():
        s = state["stat"]
        state["stat"] = (s + 1) % NSTAT
        return s

    def bank(i):
        return psum[:, i, :]

    PS_RING = [0, 1, 2, 3, 6, 7]

    def next_ps():
        s = state["ps"]
        state["ps"] = (s + 1) % len(PS_RING)
        return PS_RING[s]

    def next_pt():
        s = state["pt"]
        state["pt"] = (s + 1) % 2
        return 4 + s

    def rstd_chain(src, r, srckey, scale):
        s = new_stat()
        sk = ("stat", s)
        n = src.shape[-1]
        P.add("act", I("activation", out=junk[:r, 0:n], in_=src, func=AF.Square, accum_out=stat[:r, s, 0:1]),
              reads=[srckey], writes=[(sk, 0)])
        pool_rstd(stat[:r, s, 0:1], stat[:r, s, 2:3], r, 1, 3 if n == D else 4, [(sk, 0)], (sk, 2))
        return stat[:r, s, 2:3], (sk, 2)

    def norm_to_xnT(src, r, srckey, xs_buf, xs_key, layer, dst, dst_key):
        rs, rsk = rstd_chain(src, r, srckey, 1.0 / D)
        P.add("act", I("activation", out=xs_buf[:r, :], in_=src, func=AF.Identity, scale=rs),
              reads=[srckey, rsk], writes=[xs_key])
        b = next_pt()
        ptv = bank(b).bitcast(BF16).rearrange("p (c n) -> p c n", c=8)
        P.add("pe", [I("transpose", out=ptv[:, c, 0:r], in_=xs_buf[:r, c * 128:(c + 1) * 128], identity=ident[:r, :r])
                     for c in range(8)], reads=[xs_key], writes=[("bank", b)])
        P.add("dve", I("tensor_tensor", out=dst, in0=ptv[:, :, 0:r],
                       in1=gcols[:, layer, :].unsqueeze(2).broadcast_to([128, 8, r]), op=ALU.mult),
              reads=[("bank", b)], writes=[dst_key])

    def mm(out_ap, lhs, rhs):
        nk = len(lhs)
        return [I("matmul", out_ap, lhsT=lhs[c], rhs=rhs[c], start=(c == 0), stop=(c == nk - 1)) for c in range(nk)]

    NW = 4
    WSLOT = 4096

    class WRing:
        def __init__(self, ar):
            self.slots = [ar.alloc((WSLOT,), BF16) for _ in range(NW)]
            self.i = 0

        def load(self, pieces):
            s = self.i % NW
            self.i += 1
            sl = self.slots[s]
            key = ("w", s)
            for dv, src in pieces:
                P.add("pool", I("dma_start", out=dv(sl), in_=src), writes=[key], dma=("w", s))
            return sl, key

    def v8(sl):
        return sl.rearrange("p (c n) -> p c n", c=8)

    def v4(sl):
        return sl.rearrange("p (c n) -> p c n", c=4)

    def v3(sl):
        return sl[:, 0:3072].rearrange("p (j c n) -> p j c n", j=3, c=8)

    def rope_tables(posB, n, poskey, Ct, St, A1, A2, tkey):
        A2i = A2.bitcast(I32)
        TE = os.environ.get("KTE", "dve")
        bc = lambda ap: ap.broadcast_to([128, n])
        P.add(TE, I("tensor_tensor", out=A1[:, 0:n], in0=posB[:, 0:n], in1=bc(rc[:, 0:1]), op=ALU.mult),
              reads=[poskey], writes=["tabA1"])
        P.add(TE, I("tensor_copy", out=A2i[:, 0:n], in_=A1[:, 0:n]), reads=["tabA1"], writes=["tabA2"])
        P.add(TE, I("tensor_tensor", out=A1[:, 0:n], in0=A1[:, 0:n], in1=A2i[:, 0:n], op=ALU.subtract),
              reads=["tabA1", "tabA2"], writes=["tabA1"])
        P.add("dve", I("scalar_tensor_tensor", out=A2[:, 0:n], in0=A1[:, 0:n], scalar=0.5, in1=A1[:, 0:n],
                       op0=ALU.is_gt, op1=ALU.subtract), reads=["tabA1"], writes=["tabA2"])
        P.add("act", I("activation", out=St[:, 0:n], in_=A2[:, 0:n], func=AF.Sin, scale=rc[:, 1:2]),
              reads=["tabA2"], writes=[(tkey, "S")])
        P.add("dve", I("scalar_tensor_tensor", out=A1[:, 0:n], in0=A2[:, 0:n], scalar=-1.0, in1=A2[:, 0:n],
                       op0=ALU.mult, op1=ALU.min), reads=["tabA2"], writes=["tabA1"])
        P.add("act", I("activation", out=Ct[:, 0:n], in_=A1[:, 0:n], func=AF.Sin, scale=TWO_PI, bias=halfpi[:, 0:1]),
              reads=["tabA1"], writes=[(tkey, "C")])

    def gmlp_v1(ps_ap, pskey, r, gv, sq, kx=0):
        P.add("act", I("activation", out=gv[:r, :], in_=ps_ap, func=AF.Gelu_apprx_tanh), reads=[pskey],
              writes=[("gv", kx)])
        s = new_stat()
        sk = ("stat", s)
        for g in range(4):
            P.add("act", I("activation", out=junk[:r, 0:128], in_=gv[:r, g * 128:(g + 1) * 128], func=AF.Square,
                           accum_out=stat[:r, s, g:g + 1]), reads=[("gv", kx)], writes=[(sk, 0, g)])
        return s

    def gmlp_v2(s, r, gv, vout, vkey, kx=0):
        sk = ("stat", s)
        pool_rstd(stat[:r, s, 0:4], stat[:r, s, 8:12], r, 4, 4, [(sk, 0, g) for g in range(4)], (sk, 2))
        for g in range(4):
            P.add("dve", I("scalar_tensor_tensor", out=vout[:r, g * 128:(g + 1) * 128], in0=gv[:r, g * 128:(g + 1) * 128],
                           scalar=stat[:r, s, 8 + g:9 + g], in1=gBv[:r, g * 128:(g + 1) * 128], op0=ALU.mult, op1=ALU.mult),
                  reads=[("gv", kx), (sk, 2)], writes=[vkey])

    def gmlp_v(ps_ap, pskey, r, gv, sq, vout, vkey, kx=0):
        s = gmlp_v1(ps_ap, pskey, r, gv, sq, kx)
        gmlp_v2(s, r, gv, vout, vkey, kx)

    def phase_AB(sq_name):
        S = SEQ_S if sq_name == "s" else SEQ_P
        NB = S // 512
        NKB = S // 128
        xd = xseq[sq_name]
        ar = Arena(arena_ap, ARENA_W)
        Kt = ar.alloc((S,), BF16)
        Vaug = ar.alloc((NKB, 130), BF16)
        Qt = ar.alloc((OWN + 4,), BF16)
        NXR = 8
        NXS = 4
        xring = [ar.alloc((D,), F32) for _ in range(NXR)]
        xsb = [ar.alloc((D,), BF16) for _ in range(NXS)]
        xnA = [ar.alloc((8, 512), BF16) for _ in range(2)]
        xnTl = ar.alloc((8, 4), BF16)
        kraw = [ar.alloc((512,), BF16) for _ in range(2)]
        t1 = ar.alloc((512,), F32)
        t2 = ar.alloc((512,), F32)
        Ct = [ar.alloc((512,), F32) for _ in range(2)]
        St = [ar.alloc((512,), F32) for _ in range(2)]
        A1 = ar.alloc((512,), F32)
        A2 = ar.alloc((512,), F32)
        posB = [ar.alloc((512,), F32) for _ in range(2)]
        NPT = 4
        Pt = [ar.alloc((2, 512), BF16) for _ in range(NPT)]
        otmp = ar.alloc((128,), F32)
        accS = ar.alloc((3, 388), F32)
        obS = ar.alloc((4, 128), F32)
        obn4 = ar.alloc((4, 128), BF16)
        gv = ar.alloc((512,), F32)
        sq = ar.alloc((512,), F32)
        ring = WRing(ar)
        tail_chunks = [NKB - 1, 8, 7, 16]
        cnt = {"x": 0, "xs": 0, "kraw": 0, "pt": 0}
        accs = [("bank", 4), ("bank", 5), ("bank", 6)]

        def acc(j, qt):
            i = j * 4 + qt
            return psum[:, 4 + i // 3, (i % 3) * 129:(i % 3) * 129 + 129]

        import os
        for hd in range(int(os.environ.get("KHEADS", "4"))):
            wsl, wkey = ring.load([(lambda sl, j=j: v3(sl)[:, j],
                                    ewin[:, :, 1024 + 512 * j + hd * 128:1024 + 512 * j + (hd + 1) * 128]) for j in range(3)])
            W3 = v3(wsl)
            if hd == 0:
                wav_sl, wavkey = ring.load([(v8, ewin[:, :, 512:1024])])
                Wav = v8(wav_sl)
            P.add("dve", I("memset", Vaug[:, :, 128:129], 1.0), writes=["Vaug1"])

            def rope_proj(j, rhs, n, ctab, stab, tabkey, dst, dstkey, xkeys, part=6):
                b1 = next_ps()
                P.add("pe", mm(bank(b1)[:, 0:n], [W3[:, j, c, :] for c in range(8)], rhs),
                      reads=xkeys + [wkey], writes=[("bank", b1)])
                if part < 2:
                    return
                ks = cnt["kraw"] % 2
                cnt["kraw"] += 1
                P.add("act", I("activation", out=kraw[ks][:, 0:n], in_=bank(b1)[:, 0:n], func=AF.Copy),
                      reads=[("bank", b1)], writes=[("kraw", ks)])
                if part < 3:
                    return
                b2 = next_ps()
                P.add("pe", I("matmul", bank(b2)[:, 0:n], lhsT=rt, rhs=kraw[ks][:, 0:n], start=True, stop=True),
                      reads=[("kraw", ks)], writes=[("bank", b2)])
                if part < 4:
                    return
                P.add("dve", I("tensor_tensor", out=t1[:, 0:n], in0=bank(b1)[:, 0:n], in1=ctab[:, 0:n], op=ALU.mult),
                      reads=[("bank", b1), (tabkey, "C")], writes=["t1"])
                if part < 5:
                    return
                P.add("dve", I("tensor_tensor", out=t2[:, 0:n], in0=bank(b2)[:, 0:n], in1=stab[:, 0:n], op=ALU.mult),
                      reads=[("bank", b2), (tabkey, "S")], writes=["t2"])
                if part < 6:
                    return
                P.add("dve", I("tensor_tensor", out=dst, in0=t1[:, 0:n], in1=t2[:, 0:n], op=ALU.add),
                      reads=["t1", "t2"], writes=[dstkey])

            xsl = cnt["x"] % NXR
            cnt["x"] += 1
            xt = xring[xsl]
            xk = ("xr", xsl)
            P.add("sp", I("dma_start", out=xt[0:4, :], in_=xtail[sq_name][:, :]), writes=[xk], dma=xk)
            xss = cnt["xs"] % NXS
            cnt["xs"] += 1
            norm_to_xnT(xt[0:4, :], 4, xk, xsb[xss], ("xsb", xss), 0, xnTl, "xnTl")
            P.add("sp", I("dma_start", out=posB[0][:, 0:4], in_=posd[sq_name][:, S:S + 4].partition_broadcast(128)),
                  writes=[("posB", 0)], dma=("posB", 0))
            rope_tables(posB[0], 4, ("posB", 0), Ct[0], St[0], A1, A2, ("tab", 0))
            rope_proj(0, [xnTl[:, c, :] for c in range(8)], 4, Ct[0], St[0], ("tab", 0), Qt[:, OWN:OWN + 4],
                      ("Qt", 4), ["xnTl"])

            if os.environ.get("KSTOP", "9") == "1":
                return
            NBr = int(os.environ.get("KNB", NB))
            blk = {}

            def par_of(b):
                return (b + 1) % 2

            def stage_load(b):
                par = par_of(b)
                P.add("sp", I("dma_start", out=posB[par],
                              in_=posd[sq_name][:, b * 512:(b + 1) * 512].partition_broadcast(128)),
                      writes=[("posB", par)], dma=("posB", par))
                xs_ = []
                for t in range(4):
                    xsl = (b % 2) * 4 + t
                    row = b * 512 + t * 128
                    P.add("sp", I("dma_start", out=xring[xsl], in_=xd[row:row + 128, :]), writes=[("xr", xsl)],
                          dma=("xr", xsl))
                    xs_.append(xsl)
                blk[b] = {"x": xs_}

            def stage_stats(b):
                s_ = new_stat()
                sk = ("stat", s_)
                for t in range(4):
                    xsl = blk[b]["x"][t]
                    P.add("act", I("activation", out=junk[:, :], in_=xring[xsl], func=AF.Square,
                                   accum_out=stat[:, s_, t:t + 1]), reads=[("xr", xsl)], writes=[(sk, 0, t)])
                pool_rstd(stat[:, s_, 0:4], stat[:, s_, 8:12], 128, 4, 3, [(sk, 0, t) for t in range(4)], (sk, 2))
                blk[b]["stat"] = s_

            def stage_mid(b):
                par = par_of(b)
                s_ = blk[b]["stat"]
                sk = ("stat", s_)
                for t in range(4):
                    xsl = blk[b]["x"][t]
                    rs_ap = stat[:, s_, 8 + t:9 + t]
                    if t == 2:
                        P.add("act", I("activation", out=xsb[t], in_=xring[xsl], func=AF.Identity, scale=rs_ap),
                              reads=[("xr", xsl), (sk, 2)], writes=[("xsb", t)])
                    elif t == 3:
                        P.add("act", I("activation", out=xsb[t], in_=xring[xsl], func=AF.Identity, scale=rs_ap),
                              reads=[("xr", xsl), (sk, 2)], writes=[("xsb", t)])
                    else:
                        P.add("dve", I("tensor_scalar", out=xsb[t], in0=xring[xsl], scalar1=rs_ap,
                                       scalar2=None, op0=ALU.mult), reads=[("xr", xsl), (sk, 2)], writes=[("xsb", t)])
                for t in range(4):
                    bnk = next_pt()
                    ptv = bank(bnk).bitcast(BF16).rearrange("p (c n) -> p c n", c=8)
                    P.add("pe", [I("transpose", out=ptv[:, c, :], in_=xsb[t][:, c * 128:(c + 1) * 128], identity=ident)
                                 for c in range(8)], reads=[("xsb", t)], writes=[("bank", bnk)])
                    P.add("dve", I("tensor_tensor", out=xnA[par][:, :, t * 128:(t + 1) * 128], in0=ptv,
                                   in1=gcols[:, 0, :].unsqueeze(2).broadcast_to([128, 8, 128]), op=ALU.mult),
                          reads=[("bank", bnk)], writes=[("xnA", par, t)])

            def stage_tables(b):
                par = par_of(b)
                rope_tables(posB[par], 512, ("posB", par), Ct[par], St[par], A1, A2, ("tab", par))

            cache = not os.environ.get("KNOCACHE")

            def store_tab(b):
                par = par_of(b)
                P.add("sp", I("dma_start", out=tabd[sq_name][b, :, 0:512], in_=Ct[par]),
                      reads=[(("tab", par), "C")], writes=[("tabd", b, 0)], dma=("tst", 0))
                P.add("sp", I("dma_start", out=tabd[sq_name][b, :, 512:1024], in_=St[par]),
                      reads=[(("tab", par), "S")], writes=[("tabd", b, 1)], dma=("tst", 1))

            def store_x(b):
                par = par_of(b)
                P.add("sp", I("dma_start", out=xnd[sq_name][b], in_=xnA[par].rearrange("p c n -> p (c n)")),
                      reads=[("xnA", par, t) for t in range(4)], writes=[("xnd", b)], dma=("xst", par))

            def stage_reload(b):
                par = par_of(b)
                P.add("sp", I("dma_start", out=xnA[par].rearrange("p c n -> p (c n)"), in_=xnd[sq_name][b]),
                      reads=[("xnd", b)], writes=[("xnA", par, t) for t in range(4)], dma=("xrl", par))
                P.add("sp", I("dma_start", out=Ct[par], in_=tabd[sq_name][b, :, 0:512]),
                      reads=[("tabd", b, 0)], writes=[(("tab", par), "C")], dma=("trl", par, 0))
                P.add("sp", I("dma_start", out=St[par], in_=tabd[sq_name][b, :, 512:1024]),
                      reads=[("tabd", b, 1)], writes=[(("tab", par), "S")], dma=("trl", par, 1))

            first_pass = (hd == 0) or not cache
            if first_pass:
                stage_load(0)
                if NBr > 1:
                    stage_load(1)
                stage_stats(0)
                stage_tables(0)
                if cache:
                    store_tab(0)
            else:
                stage_reload(0)
            for b in range(NBr):
                par = par_of(b)
                if first_pass:
                    stage_mid(b)
                    if b + 2 < NBr:
                        stage_load(b + 2)
                    if b + 1 < NBr:
                        stage_stats(b + 1)
                        stage_tables(b + 1)
                    if cache:
                        store_x(b)
                        if b + 1 < NBr:
                            store_tab(b + 1)
                elif b + 1 < NBr:
                    stage_reload(b + 1)
                xkeys = [("xnA", par, t) for t in range(4)]
                xr = [xnA[par][:, c, :] for c in range(8)]
                kv = os.environ.get("KVAR", "")
                if kv != "noK":
                    rope_proj(1, xr, 512, Ct[par], St[par], ("tab", par), Kt[:, b * 512:(b + 1) * 512], ("Kt", b), xkeys)
                if kv == "KbK":
                    P.barrier()
                    kv = "KK"
                if kv == "KK":
                    for _ in range(int(os.environ.get("KSKIP", "0"))):
                        next_ps()
                    rope_proj(1, xr, 512, Ct[par], St[par], ("tab", par), Kt[:, b * 512:(b + 1) * 512], ("Kt", b), xkeys,
                              part=int(os.environ.get("KPART", "6")))
                    continue
                if os.environ.get("KNOV"):
                    if b < 4 and os.environ.get("KNOV") == "q":
                        rope_proj(0, xr, 512, Ct[par], St[par], ("tab", par), Qt[:, b * 512:(b + 1) * 512], ("Qt", b), xkeys)
                    continue
                bv = next_ps()
                vm = []
                for t in range(4):
                    vm += mm(bank(bv)[:, t * 128:(t + 1) * 128], [xnA[par][:, c, t * 128:(t + 1) * 128] for c in range(8)],
                             [W3[:, 2, c, :] for c in range(8)])
                P.add("pe", vm, reads=xkeys + [wkey], writes=[("bank", bv)])
                for t in range(4):
                    P.add("dve", I("tensor_copy", out=Vaug[:, b * 4 + t, 0:128], in_=bank(bv)[:, t * 128:(t + 1) * 128]),
                          reads=[("bank", bv), "Vaug1"], writes=[("Vaug", b, t)])
                if b < 4:
                    rope_proj(0, xr, 512, Ct[par], St[par], ("tab", par), Qt[:, b * 512:(b + 1) * 512], ("Qt", b), xkeys)
                if hd == 0 and not os.environ.get("KNOAV"):
                    for ti, ch in enumerate(tail_chunks):
                        if ch // 4 == b:
                            t = ch % 4
                            ba = next_ps()
                            P.add("pe", mm(bank(ba), [xnA[par][:, c, t * 128:(t + 1) * 128] for c in range(8)],
                                           [Wav[:, c, :] for c in range(8)]),
                                  reads=xkeys + [wavkey], writes=[("bank", ba)])
                            gmlp_v(bank(ba), ("bank", ba), 128, gv, sq, vstash[:, ti, :], ("vstash", ti))

            if os.environ.get("KSTOP", "9") == "2":
                return
            qblocks = [(0, 512), (512, 512), (1024, 512), (1536, 512), (OWN, 4)]
            allK = [("Kt", b) for b in range(NB)]
            allV = [("Vaug", b, t) for b in range(NB) for t in range(4)]
            allQ = [("Qt", b) for b in range(5)]
            def accv(j, qt):
                i = j * 4 + qt
                return accS[:, i // 3, (i % 3) * 129:(i % 3) * 129 + 129]

            def make_fin(q0, nq, nqt, r):
                def fin():
                    akeys = [("accS", bi) for bi in range(3)]
                    s_ = new_stat()
                    sk = ("stat", s_)
                    for qt in range(nqt):
                        P.add("dve", I("reciprocal", out=stat[:r, s_, 0:1], in_=accv(0, qt)[0:r, 128:129]),
                              reads=akeys, writes=[(sk, 0)])
                        P.add("dve", I("reciprocal", out=stat[:r, s_, 1:2], in_=accv(1, qt)[0:r, 128:129]),
                              reads=akeys, writes=[(sk, 1)])
                        P.add("dve", I("tensor_tensor", out=stat[:r, s_, 2:3], in0=stat[:r, s_, 1:2], in1=lsc[:r, 5:6],
                                       op=ALU.mult), reads=[(sk, 1)], writes=[(sk, 2)])
                        P.add("dve", I("tensor_scalar", out=otmp[:r, :], in0=accv(0, qt)[0:r, 0:128],
                                       scalar1=stat[:r, s_, 0:1], scalar2=None, op0=ALU.mult),
                              reads=akeys + [(sk, 0)], writes=["otmp"])
                        P.add("dve", I("scalar_tensor_tensor", out=obS[:r, qt, :], in0=accv(1, qt)[0:r, 0:128],
                                       scalar=stat[:r, s_, 2:3], in1=otmp[:r, :], op0=ALU.mult, op1=ALU.add),
                              reads=akeys + [(sk, 2), "otmp"], writes=[("obS", qt)])
                        P.add("dve", I("tensor_tensor", out=otmp[:r, :], in0=obS[:r, qt, :], in1=obS[:r, qt, :], op=ALU.mult),
                              reads=[("obS", qt)], writes=["otmp"])
                        P.add("dve", I("tensor_reduce", out=stat[:r, s_, 4 + qt:5 + qt], in_=otmp[:r, :], axis=AX.X, op=ALU.add),
                              reads=["otmp"], writes=[(sk, 4, qt)])
                    return (s_, sk)

                def fin2(s_, sk):
                    pool_rstd(stat[:r, s_, 4:4 + nqt], stat[:r, s_, 8:8 + nqt], r, nqt, 4,
                              [(sk, 4, qt) for qt in range(nqt)], (sk, 9))
                    ptb = bank(7).bitcast(BF16)
                    for qt in range(nqt):
                        P.add("dve", I("scalar_tensor_tensor", out=obn4[:r, qt, :], in0=obS[:r, qt, :],
                                       scalar=stat[:r, s_, 8 + qt:9 + qt], in1=sublnB[:r, :], op0=ALU.mult, op1=ALU.mult),
                              reads=[("obS", qt), (sk, 9)], writes=[("obn", qt)])
                    P.add("pe", [I("transpose", out=ptb[:, qt * 128:qt * 128 + r], in_=obn4[:r, qt, :], identity=ident[:r, :r])
                                 for qt in range(nqt)], reads=[("obn", qt) for qt in range(nqt)], writes=[("bank", 7)])
                    P.add("dve", I("tensor_copy", out=out_bT[:, hd, q0:q0 + nq], in_=ptb[:, 0:nq]),
                          reads=[("bank", 7)], writes=[("obT", hd, q0)])
                return fin, fin2

            pending_fin = []
            for (q0, nq) in qblocks:
                nqt = (nq + 127) // 128
                r = min(128, nq)
                pq = []
                for kb in range(NKB + 2):
                    if kb == 14 and pending_fin:
                        f2_, args_ = pending_fin.pop(0)
                        f2_(*args_)
                    pend = None
                    if kb < NKB:
                        sb = (kb % 2) * 2
                        P.add("pe", [I("matmul", psum[:, sb, 0:nq], lhsT=Kt[0:64, kb * 128:(kb + 1) * 128],
                                       rhs=Qt[0:64, q0:q0 + nq], start=True, stop=True),
                                     I("matmul", psum[:, sb + 1, 0:nq], lhsT=Kt[64:128, kb * 128:(kb + 1) * 128],
                                       rhs=Qt[64:128, q0:q0 + nq], start=True, stop=True)],
                              reads=allK + allQ, writes=[("bank", sb), ("bank", sb + 1)])
                        psl = cnt["pt"] % NPT
                        cnt["pt"] += 1
                        P.add("act", I("activation", out=Pt[psl][:, :, 0:nq], in_=psum[:, sb:sb + 2, 0:nq], func=AF.Exp,
                                       scale=0.125), reads=[("bank", sb), ("bank", sb + 1)], writes=[("Pt", psl)])
                        pq.append((kb, psl))
                    if pq and (len(pq) > 2 or kb >= NKB):
                        pend = pq.pop(0)
                    if pend is not None:
                        pkb, pps = pend
                        pv = []
                        seen_banks = set()
                        for j in range(2):
                            for qt in range(nqt):
                                ai = j * 4 + qt
                                first_in_bank = (ai // 3) not in seen_banks
                                seen_banks.add(ai // 3)
                                pv.append(I("matmul", acc(j, qt)[0:r, :], lhsT=Pt[pps][:, j, qt * 128:qt * 128 + r],
                                            rhs=Vaug[:, pkb, 0:129], start=(pkb == 0 and first_in_bank),
                                            stop=(pkb == NKB - 1), skip_group_check=True))
                        P.add("pe", pv, reads=[("Pt", pps)] + allV, writes=accs)
                for bi in range(3):
                    ncol = 387 if bi < 2 else 258
                    P.add("dve", I("tensor_copy", out=accS[:r, bi, 0:ncol], in_=psum[0:r, 4 + bi, 0:ncol]),
                          reads=[("bank", 4 + bi)], writes=[("accS", bi)])
                f1_, f2_ = make_fin(q0, nq, nqt, r)
                pending_fin.append((f2_, f1_()))
            while pending_fin:
                f2_, args_ = pending_fin.pop(0)
                f2_(*args_)

    def phase_C(sq_name, ri):
        xd = xseq[sq_name]
        roff = ri * RNG
        sidx = 0 if sq_name == "s" else 1
        ar = Arena(arena_ap, ARENA_W)
        h = ar.alloc((9, D), F32)
        xnT = ar.alloc((8, RNG + 2), BF16)
        uT = ar.alloc((4, RNG + 2), BF16)
        oaT = ar.alloc((4, RNG + 2), BF16)
        hTall = ar.alloc((8, RNG + 2), BF16)
        hT = [hTall[:, 0:4, :], hTall[:, 4:8, :]]
        mTb = hTall
        xsb = [ar.alloc((D,), BF16) for _ in range(9)]
        uni = ar.alloc((4640,), F32)
        gv2 = [uni[:, i * 512:(i + 1) * 512] for i in range(3)]
        sq2 = [uni[:, 1536 + i * 512:1536 + (i + 1) * 512] for i in range(3)]
        mixt2 = [uni[:, 3072 + i * 512:3072 + (i + 1) * 512].rearrange("p (g n) -> p g n", g=4) for i in range(3)]
        vb2 = [ar.alloc((512,), BF16) for _ in range(3)]
        mixt = mixt2[0]
        rl = [ar.alloc((512,), F32) for _ in range(2)]
        bgs = uni[:, 0:RNG]
        cgs = uni[:, 1024:1024 + RNG + 2]
        tb = uni[:, 2056:2056 + RNG + 2]
        yb = uni[:, 3088:3088 + RNG]
        ring = WRing(ar)
        cnt = {"xs": 0, "rl": 0, "hT": 0}
        tiles = [(t, 128) for t in range(8)] + [(8, 2)]
        cblocks = [(0, 512), (512, 512), (RNG, 2)]

        def tile_cols(t):
            return (t * 128, 128) if t < 8 else (RNG, 2)

        def hkey(t):
            return ("h", t)

        for t in range(8):
            P.add("sp", I("dma_start", out=h[:, t, :], in_=xd[roff + t * 128:roff + (t + 1) * 128, :]),
                  writes=[hkey(t)], dma=("hl", t))
        P.add("sp", I("dma_start", out=h[0:2, 8, :], in_=xtail[sq_name][2 * ri:2 * ri + 2, :]),
              writes=[hkey(8)], dma=("hl", 8))

        def do_norm(layer, tl):
            nt = len(tl)
            for (t, r) in tl:
                P.add("act", I("activation", out=junk[:r, :], in_=h[0:r, t, :], func=AF.Square,
                               accum_out=statC[:r, 0, t:t + 1]), reads=[hkey(t)], writes=[("sC", 0, t)])
            pool_rstd(statC[:, 0, 0:nt], statC[:, 2, 0:nt], 128, nt, 3, [("sC", 0, t) for (t, r) in tl], ("sC", 2))
            for i, (t, r) in enumerate(tl):
                rs_ap = statC[:r, 2, t:t + 1]
                src = h[0:r, t, :]
                if i % 2 == 0:
                    P.add("act", I("activation", out=xsb[t][:r, :], in_=src, func=AF.Identity, scale=rs_ap),
                          reads=[hkey(t), ("sC", 2)], writes=[("xsbC", t)])
                else:
                    P.add("dve", I("tensor_scalar", out=xsb[t][:r, :], in0=src, scalar1=rs_ap, scalar2=None, op0=ALU.mult),
                          reads=[hkey(t), ("sC", 2)], writes=[("xsbC", t)])
            for (t, r) in tl:
                c0, n = tile_cols(t)
                bnk = next_pt()
                ptv = bank(bnk).bitcast(BF16).rearrange("p (c n) -> p c n", c=8)
                P.add("pe", [I("transpose", out=ptv[:, c, 0:r], in_=xsb[t][:r, c * 128:(c + 1) * 128], identity=ident[:r, :r])
                             for c in range(8)], reads=[("xsbC", t)], writes=[("bank", bnk)])
                P.add("dve", I("tensor_tensor", out=xnT[:, :, c0:c0 + n], in0=ptv[:, :, 0:r],
                               in1=gcols[:, layer, :].unsqueeze(2).broadcast_to([128, 8, r]), op=ALU.mult),
                      reads=[("bank", bnk)], writes=[("xnT", t)])

        def xkeys_cb(cb):
            c0, n = cb
            if n == 2:
                return [("xnT", 8)]
            return [("xnT", t) for t in range(c0 // 128, c0 // 128 + 4)]

        def resid_add(t, r, half, b):
            P.add("dve", I("tensor_tensor", out=h[0:r, t, half * 512:(half + 1) * 512],
                           in0=h[0:r, t, half * 512:(half + 1) * 512], in1=bank(b)[0:r, :], op=ALU.add),
                  reads=[("bank", b), hkey(t)], writes=[hkey(t)])

        do_norm(0, tiles)
        wau_sl, waukey = ring.load([(v8, ewin[:, :, 0:512])])
        wav_sl, wavkey = ring.load([(v8, ewin[:, :, 512:1024])])
        Wau = v8(wau_sl)
        Wav = v8(wav_sl)
        wo = []
        for half in range(2):
            sl, k = ring.load([(v4, ewout[:, half * 4:(half + 1) * 4, :])])
            wo.append((v4(sl), k))
        for cb in cblocks:
            c0, n = cb
            for g in range(4):
                b = next_ps()
                P.add("pe", mm(bank(b)[:, 0:n], [Wau[:, c, g * 128:(g + 1) * 128] for c in range(8)],
                               [xnT[:, c, c0:c0 + n] for c in range(8)]),
                      reads=xkeys_cb(cb) + [waukey], writes=[("bank", b)])
                P.add("act", I("activation", out=uT[:, g, c0:c0 + n], in_=bank(b)[:, 0:n], func=AF.Gelu_apprx_tanh),
                      reads=[("bank", b)], writes=[("uT", g, c0)])

        def ukeys(c0):
            cc = RNG if c0 >= RNG else (c0 // 512) * 512
            return [("uT", g, cc) for g in range(4)]
        gst = {}

        def g_a1(t):
            b = next_ps()
            P.add("pe", mm(bank(b), [xnT[:, c, t * 128:(t + 1) * 128] for c in range(8)], [Wav[:, c, :] for c in range(8)]),
                  reads=[("xnT", t), wavkey], writes=[("bank", b)])
            kx = t % 3
            gst[t] = gmlp_v1(bank(b), ("bank", b), 128, gv2[kx], sq2[kx], kx=kx)

        def g_a2(t):
            kx = t % 3
            gmlp_v2(gst[t], 128, gv2[kx], vb2[kx], ("vb", kx), kx=kx)
            b2 = next_ps()
            P.add("pe", [I("matmul", bank(b2)[:, g * 128:(g + 1) * 128], lhsT=vb2[kx][:, g * 128:(g + 1) * 128],
                           rhs=wsT[:, g, :], start=True, stop=True) for g in range(4)],
                  reads=[("vb", kx)], writes=[("bank", b2)])
            gst[("b2", t)] = b2

        def g_b(t):
            kx = t % 3
            b2 = gst[("b2", t)]
            mx = mixt2[kx]
            P.add("dve", I("tensor_tensor", out=mx, in0=bank(b2).rearrange("p (g n) -> p g n", g=4), in1=bsB, op=ALU.add),
                  reads=[("bank", b2)], writes=[("mixt", kx)])
            P.add("dve", I("tensor_tensor", out=oaT[:, :, t * 128:(t + 1) * 128], in0=mx,
                           in1=uT[:, :, t * 128:(t + 1) * 128], op=ALU.mult),
                  reads=[("mixt", kx)] + ukeys(t * 128), writes=[("oaT", t)])

        if os.environ.get("KGSEQ"):
            for t in range(8):
                g_a1(t)
                g_a2(t)
                g_b(t)
        else:
            g_a1(0)
            for t in range(8):
                if t + 1 < 8:
                    g_a1(t + 1)
                g_a2(t)
                if t >= 1:
                    g_b(t - 1)
            g_b(7)
        for j in range(2):
            ti = 2 * ri + j
            pcol = 127 if j == 0 else 0
            b2 = next_ps()
            P.add("pe", [I("matmul", bank(b2)[:, g:g + 1], lhsT=vstash[:, ti, g * 128:(g + 1) * 128],
                           rhs=wsT[:, g, pcol:pcol + 1], start=True, stop=True) for g in range(4)],
                  reads=[], writes=[("bank", b2)])
            P.add("dve", I("tensor_tensor", out=mixt[:, :, 0], in0=bank(b2)[:, 0:4], in1=bsB[:, :, pcol], op=ALU.add),
                  reads=[("bank", b2)], writes=[("mixt", 0)])
            P.add("dve", I("tensor_tensor", out=oaT[:, :, RNG + j], in0=mixt[:, :, 0], in1=uT[:, :, RNG + j], op=ALU.mult),
                  reads=[("mixt", 0)] + ukeys(RNG), writes=[("oaT", 8, j)])
        for (t, r) in tiles:
            c0, n = tile_cols(t)
            for half in range(2):
                b = next_ps()
                lhs = []
                for c in range(8):
                    if c < 4:
                        lhs.append(oaT[:, c, c0:c0 + r])
                    elif t < 8:
                        lhs.append(out_bT[:, c - 4, roff + c0:roff + c0 + r])
                    else:
                        lhs.append(out_bT[:, c - 4, OWN + 2 * ri:OWN + 2 * ri + 2])
                okeys = [("oaT", t)] if t < 8 else [("oaT", 8, 0), ("oaT", 8, 1)]
                P.add("pe", mm(bank(b)[0:r, :], lhs, [wo[c // 4][0][:, c % 4, half * 512:(half + 1) * 512] for c in range(8)]),
                      reads=okeys + [wo[0][1], wo[1][1]], writes=[("bank", b)])
                resid_add(t, r, half, b)

        ALLHT = [("hT", hs) for hs in range(2)]

        def ffn(layer, tl, cbl):
            do_norm(1 + 2 * layer, tl)
            st = {}

            def ld1(fg):
                st[("w1", fg)] = ring.load([(v8, w1d[layer][:, :, fg * 512:(fg + 1) * 512])])

            def ld2(fg):
                st[("w2", fg)] = ring.load([(v4, w2d[layer][:, fg * 4:(fg + 1) * 4, :])])

            def hidden(fg):
                s1, k1 = st[("w1", fg)]
                W1 = v8(s1)
                hs = fg % 2
                hb = hT[hs]
                for f in range(4):
                    for cb in cbl:
                        c0, n = cb
                        b = next_ps()
                        P.add("pe", mm(bank(b)[:, 0:n], [W1[:, c, f * 128:(f + 1) * 128] for c in range(8)],
                                       [xnT[:, c, c0:c0 + n] for c in range(8)]),
                              reads=xkeys_cb(cb) + [k1], writes=[("bank", b)])
                        rsl = cnt["rl"] % 2
                        cnt["rl"] += 1
                        P.add("act", I("activation", out=rl[rsl][:, 0:n], in_=bank(b)[:, 0:n], func=AF.Relu),
                              reads=[("bank", b)], writes=[("rl", rsl)])
                        P.add("dve", I("tensor_tensor", out=hb[:, f, c0:c0 + n], in0=rl[rsl][:, 0:n], in1=rl[rsl][:, 0:n],
                                       op=ALU.mult), reads=[("rl", rsl)], writes=[("hTw", hs, f, c0)])

            def second(fg):
                s2, k2 = st[("w2", fg)]
                W2 = v4(s2)
                hs = fg % 2
                hb = hT[hs]
                hkeys = [("hTw", hs, f, c0) for f in range(4) for (c0, n) in cbl]
                for (t, r) in tl:
                    c0, n = tile_cols(t)
                    for half in range(2):
                        b = next_ps()
                        P.add("pe", mm(bank(b)[0:r, :], [hb[:, f, c0:c0 + r] for f in range(4)],
                                       [W2[:, f, half * 512:(half + 1) * 512] for f in range(4)]),
                              reads=hkeys + [k2], writes=[("bank", b)])
                        resid_add(t, r, half, b)

            ld1(0)
            ld2(0)
            ld1(1)
            hidden(0)
            for fg in range(8):
                if fg + 1 < 8:
                    ld2(fg + 1)
                    hidden(fg + 1)
                if fg + 2 < 8:
                    ld1(fg + 2)
                second(fg)

        ffn(0, tiles, cblocks)

        do_norm(2, tiles)
        m0 = sidx * 4 + 2 * ri
        for c in range(8):
            wsl, wk = ring.load([(lambda sl, j=j: v3(sl)[:, j], cwin[:, :, j * D + c * 128:j * D + (c + 1) * 128])
                                 for j in range(3)])
            Wc = v3(wsl)
            for cb in cblocks:
                c0, n = cb
                xk = xkeys_cb(cb)
                xr = [xnT[:, k, c0:c0 + n] for k in range(8)]
                if n > 2:
                    b = next_ps()
                    P.add("pe", mm(bank(b)[:, 0:n], [Wc[:, 0, k, :] for k in range(8)], xr),
                          reads=xk + [wk], writes=[("bank", b)])
                    P.add("act", I("activation", out=bgs[:, c0:c0 + n], in_=bank(b)[:, 0:n], func=AF.Copy),
                          reads=[("bank", b)], writes=[("bgs", c0)])
                b = next_ps()
                P.add("pe", mm(bank(b)[:, 0:n], [Wc[:, 1, k, :] for k in range(8)], xr),
                      reads=xk + [wk], writes=[("bank", b)])
                P.add("act", I("activation", out=cgs[:, c0:c0 + n], in_=bank(b)[:, 0:n], func=AF.Copy),
                      reads=[("bank", b)], writes=[("cgs", c0)])
                b = next_ps()
                P.add("pe", mm(bank(b)[:, 0:n], [Wc[:, 2, k, :] for k in range(8)], xr),
                      reads=xk + [wk], writes=[("bank", b)])
                if n > 2:
                    P.add("dve", I("tensor_tensor", out=tb[:, 1 + c0:1 + c0 + n], in0=bank(b)[:, 0:n], in1=cgs[:, c0:c0 + n],
                                   op=ALU.mult), reads=[("bank", b), ("cgs", c0)], writes=[("tb", c0)])
                else:
                    P.add("dve", I("tensor_tensor", out=cgs[:, RNG:RNG + 2], in0=bank(b)[:, 0:2], in1=cgs[:, RNG:RNG + 2],
                                   op=ALU.mult), reads=[("bank", b), ("cgs", RNG)], writes=[("cgs2", RNG)])
                    P.add("dve", I("tensor_tensor", out=tb[:, 0:1], in0=cgs[:, RNG:RNG + 1], in1=maskB[:, m0:m0 + 1],
                                   op=ALU.mult), reads=[("cgs2", RNG)], writes=[("tb", "l")])
                    P.add("dve", I("tensor_tensor", out=tb[:, RNG + 1:RNG + 2], in0=cgs[:, RNG + 1:RNG + 2],
                                   in1=maskB[:, m0 + 1:m0 + 2], op=ALU.mult), reads=[("cgs2", RNG)], writes=[("tb", "r")])
            tkeys = [("tb", 0), ("tb", 512), ("tb", "l"), ("tb", "r")]
            P.add("dve", I("tensor_scalar", out=yb, in0=tb[:, 1:RNG + 1], scalar1=cw[:, 1, c:c + 1], scalar2=None,
                           op0=ALU.mult), reads=tkeys, writes=["yb"])
            P.add("dve", I("scalar_tensor_tensor", out=yb, in0=tb[:, 0:RNG], scalar=cw[:, 0, c:c + 1], in1=yb,
                           op0=ALU.mult, op1=ALU.add), reads=tkeys + ["yb"], writes=["yb"])
            P.add("dve", I("scalar_tensor_tensor", out=yb, in0=tb[:, 2:RNG + 2], scalar=cw[:, 2, c:c + 1], in1=yb,
                           op0=ALU.mult, op1=ALU.add), reads=tkeys + ["yb"], writes=["yb"])
            P.add("dve", I("tensor_tensor", out=mTb[:, c, 0:RNG], in0=bgs, in1=yb, op=ALU.mult),
                  reads=["yb", ("bgs", 0), ("bgs", 512)],
                  writes=[("mT", c), ("hTw", c // 4, c % 4, 0), ("hTw", c // 4, c % 4, 512)])
        wco = []
        for half in range(2):
            sl, k = ring.load([(v4, cwout[:, half * 4:(half + 1) * 4, :])])
            wco.append((v4(sl), k))
        mkeys = [("mT", c) for c in range(8)] + [("hTw", hs_, f_, c0_) for hs_ in range(2) for f_ in range(4) for c0_ in (0, 512)]
        own_tiles = tiles[:8]
        for (t, r) in own_tiles:
            for half in range(2):
                b = next_ps()
                P.add("pe", mm(bank(b), [mTb[:, c, t * 128:(t + 1) * 128] for c in range(8)],
                               [wco[c // 4][0][:, c % 4, half * 512:(half + 1) * 512] for c in range(8)]),
                      reads=mkeys + [wco[0][1], wco[1][1]], writes=[("bank", b)])
                resid_add(t, r, half, b)

        ffn(1, own_tiles, cblocks[:2])

        ystage = [bgs, yb]
        yskeys = [[("bgs", 0), ("bgs", 512)], ["yb"]]
        for half in range(2):
            for t in range(half * 4, half * 4 + 4):
                P.add("act", I("activation", out=junk[:, :], in_=h[:, t, :], func=AF.Square,
                               accum_out=statC[:, 0, t:t + 1]), reads=[hkey(t)], writes=[("sC", 0, t)])
            pool_rstd(statC[:, 0, half * 4:half * 4 + 4], statC[:, 2, half * 4:half * 4 + 4], 128, 4, 3,
                      [("sC", 0, t) for t in range(half * 4, half * 4 + 4)], ("sCf", half))
        for (t, r) in own_tiles:
            rs = statC[:, 2, t:t + 1]
            rsk = ("sCf", t // 4)
            ys = t % 2
            P.add("dve", I("scalar_tensor_tensor", out=ystage[ys], in0=h[:, t, :], scalar=rs, in1=fgB,
                           op0=ALU.mult, op1=ALU.mult), reads=[hkey(t), rsk], writes=yskeys[ys])
            P.add("sp", I("dma_start", out=yout[sq_name][roff + t * 128:roff + (t + 1) * 128, :], in_=ystage[ys]),
                  reads=yskeys[ys], writes=[("yout", sq_name, ri, t)], dma=("yst", ys))

    import os
    dbg = os.environ.get("KDBG", "")
    for sq_name in ("s", "p"):
        if dbg and sq_name not in dbg:
            continue
        if not dbg or "A" in dbg:
            phase_AB(sq_name)
            P.barrier()
        if not dbg or "C" in dbg:
            phase_C(sq_name, 0)
            P.barrier()
        if not dbg or "D" in dbg:
            phase_C(sq_name, 1)
            P.barrier()
    for e in Prog.ENGS:
        P.add(e, None)
    stuck, per_, pos_ = P.simulate()
    if stuck:
        for e, (p_, n_) in stuck.items():
            op = per_[e][p_]
            print("DEADLOCK", e, p_, n_, "op idx", op.idx, "waits", [(d.eng, d.idx, d.dma, d.tick, d.signal) for d in op.waits])
        raise RuntimeError("semaphore protocol deadlock")
    print("program ops:", len(P.ops), {e: len(v) for e, v in per_.items()})
    if os.environ.get("KDUMP"):
        for op in P.ops:
            print(op.idx, op.eng, op.name, "tick", op.tick if (op.signal or op.dma) else None,
                  "W:", [(d.eng, d.idx, d.tick) for d in op.waits], "w=", op.rw[1][:3])
    P.emit(nc, stack)
    stack.close()
    return nc


_NC_CACHE = {}


def _host_constants():
    ident = np.eye(128, dtype=np.float32)
    rt = np.zeros((128, 128), np.float32)
    invf = np.zeros(128, np.float32)
    sgn = np.zeros(128, np.float32)
    inv = (500000.0 ** (-np.arange(0, 16, 2, dtype=np.float32) / 16.0)).astype(np.float32)
    for base in (0, 64):
        for d in range(16):
            p = base + d
            partner = p + 8 if d < 8 else p - 8
            rt[partner, p] = 1.0
            invf[p] = inv[d % 8] / np.float32(2 * np.pi)
            sgn[p] = -1.0 if d < 8 else 1.0
    rc = np.stack([invf, (-2.0 * np.pi * sgn).astype(np.float32)], axis=1).astype(np.float32)
    return ident, rt, rc


def kernel(**inputs):
    f32 = lambda a: np.ascontiguousarray(np.asarray(a, dtype=np.float32))
    xpr = f32(inputs["x_prompt"])
    xsa = f32(inputs["x_sample"])
    ident, rt, rc = _host_constants()
    gains = np.stack([f32(inputs["norm_mix_g"])[0], f32(inputs["norm_ffn_g"])[0],
                      f32(inputs["norm_mix_g"])[1], f32(inputs["norm_ffn_g"])[1]], axis=0)
    gcols = np.ascontiguousarray(gains.reshape(4, 8, 128).transpose(2, 0, 1).reshape(128, 32))
    cwh = np.ascontiguousarray(f32(inputs["c_conv_w"])[0].reshape(3, 8, 128).transpose(2, 0, 1).reshape(128, 24))
    lvec = np.concatenate([f32(inputs["b_lq1"])[0], f32(inputs["b_lk1"])[0],
                           f32(inputs["b_lq2"])[0], f32(inputs["b_lk2"])[0]])[None, :]
    shared = {
        "e_w_in": f32(inputs["e_w_in"])[0], "e_w_out": f32(inputs["e_w_out"])[0],
        "ffn_w1_0": f32(inputs["ffn_w1"])[0], "ffn_w1_1": f32(inputs["ffn_w1"])[1],
        "ffn_w2_0": f32(inputs["ffn_w2"])[0], "ffn_w2_1": f32(inputs["ffn_w2"])[1],
        "c_w_in": f32(inputs["c_w_in"])[0], "c_w_out": f32(inputs["c_w_out"])[0],
        "wsT": np.ascontiguousarray(f32(inputs["a_w_s"])[0].transpose(0, 2, 1)),
        "ident": ident, "rt": rt, "gcols": gcols, "cw": cwh, "rc": rc, "lvec": np.ascontiguousarray(lvec),
        "vng": f32(inputs["a_vnorm_g"]).reshape(1, 512), "bs": f32(inputs["a_b_s"]).reshape(1, 512),
        "subln": f32(inputs["b_subln_g"]).reshape(1, 128), "fg": f32(inputs["final_g"]).reshape(1, D),
    }
    in_maps = []
    meta = []
    for c in range(NCORES):
        bp, op_ = c // 2, (c % 2) * OWN
        bs_, os_ = c // 4, (c % 4) * OWN
        m = dict(shared)
        masks = np.zeros((1, 8), np.float32)
        for key, x, b, off, S, mi in (("s", xsa, bs_, os_, SEQ_S, 0), ("p", xpr, bp, op_, SEQ_P, 4)):
            xr = np.ascontiguousarray(np.roll(x[b], -off, axis=0))
            tidx = np.array([S - 1, RNG, RNG - 1, OWN])
            pos = ((np.arange(S) + off) % S).astype(np.float32)
            post = ((tidx + off) % S).astype(np.float32)
            m["x" + key] = xr
            m["xt" + key] = np.ascontiguousarray(xr[tidx])
            m["pos" + key] = np.concatenate([pos, post])[None, :].astype(np.float32)
            masks[0, mi + 0] = 1.0 if off > 0 else 0.0
            masks[0, mi + 1] = 1.0
            masks[0, mi + 2] = 1.0
            masks[0, mi + 3] = 1.0 if off + OWN < S else 0.0
        m["masks"] = masks
        in_maps.append(m)
        meta.append((bp, op_, bs_, os_))
    if "nc" not in _NC_CACHE:
        _NC_CACHE["nc"] = build_program()
    import os as _os
    ncr = int(_os.environ.get("KCORES", NCORES))
    if _os.environ.get("KTRACE"):
        res = run_bass_kernel_spmd(_NC_CACHE["nc"], in_maps[:ncr], core_ids=list(range(ncr)), trace=True)
        print("KTRACE exec_time_ns", res.exec_time_ns)
    else:
        res = run_bass_kernel_spmd(_NC_CACHE["nc"], in_maps[:ncr], core_ids=list(range(ncr)))
    y_p = np.zeros_like(xpr)
    y_s = np.zeros_like(xsa)
    for c, (bp, op_, bs_, os_) in enumerate(meta[:ncr]):
        y_p[bp, op_:op_ + OWN] = res.results[c]["yp"]
        y_s[bs_, os_:os_ + OWN] = res.results[c]["ys"]
    return (y_p, y_s)
```

```python
import math
import os
import numpy as np
import concourse.bass as bass
import concourse.mybir as mybir
from concourse.bass_utils import run_bass_kernel_spmd

F32 = mybir.dt.float32
BF16 = mybir.dt.bfloat16
I32 = mybir.dt.int32
AF = mybir.ActivationFunctionType
ALU = mybir.AluOpType
AX = mybir.AxisListType

D = 1024
DFF = 4096
EPS = 1e-5
NCORES = 8
SEQ_S = 8192
SEQ_P = 4096
OWN = 2048
RNG = 1024
LAM_INIT = 0.8 - 0.6 * math.exp(0.0)
TWO_PI = 2.0 * math.pi


class Op:
    __slots__ = ("eng", "fn", "waits", "signal", "tick", "dma", "idx", "name", "rw")


def I(name, *args, **kw):
    return (name, args, kw)


def _mkfn(spec):
    if spec is None:
        return None
    if callable(spec):
        return spec
    if isinstance(spec, tuple):
        spec = [spec]

    def f(e, spec=spec):
        ins = None
        for (name, args, kw) in spec:
            ins = getattr(e, name)(*args, **kw)
        return ins
    return f


class Prog:
    ENGS = ("pe", "act", "dve", "pool", "sp")

    def __init__(self):
        self.ops = []
        self.last_w = {}
        self.readers = {}
        self.dma_count = {}
        self.total_groups = set()
        self.extra = {e: [] for e in self.ENGS}
        self.dma_last = {}

    def barrier(self):
        lasts = {}
        for op in reversed(self.ops):
            if op.dma is None and op.eng not in lasts and op.fn is not None:
                lasts[op.eng] = op
        deps = list(lasts.values()) + list(self.dma_last.values())
        for e in self.ENGS:
            self.extra[e] = [d for d in deps if not (d.dma is None and d.eng == e)]
        for d in lasts.values():
            d.signal = True
        self.last_w = {}
        self.readers = {}

    def add(self, eng, fn, reads=(), writes=(), dma=None, wait_total=False):
        op = Op()
        op.eng = eng
        op.fn = _mkfn(fn)
        try:
            op.name = fn[0] if isinstance(fn, tuple) else (fn[-1][0] + "x%d" % len(fn) if isinstance(fn, list) else str(fn))
        except Exception:
            op.name = "?"
        op.signal = False
        op.tick = None
        op.dma = dma
        op.idx = len(self.ops)
        bank_r = [k for k in reads if isinstance(k, tuple) and k and k[0] == "bank" and k not in writes]
        if bank_r:
            writes = list(writes) + bank_r
        deps = {}
        for k in reads:
            w = self.last_w.get(k)
            if w is not None:
                deps[w.idx] = (w, True)
        for k in writes:
            w = self.last_w.get(k)
            if w is not None and w.idx not in deps:
                deps[w.idx] = (w, False)
            for r in self.readers.get(k, ()):
                if r.idx not in deps:
                    deps[r.idx] = (r, False)
        waits = []
        for d, raw in deps.values():
            if d.dma is None and d.eng == eng:
                if eng == "pe" or not raw:
                    continue
            waits.append(d)
            if d.dma is None:
                d.signal = True
        if self.extra[eng]:
            waits = waits + self.extra[eng]
            self.extra[eng] = []
        op.waits = waits
        op.rw = (list(reads), list(writes))
        for k in reads:
            self.readers.setdefault(k, []).append(op)
        for k in writes:
            self.last_w[k] = op
            self.readers[k] = []
        if dma is not None:
            self.dma_count[dma] = self.dma_count.get(dma, 0) + 1
            op.tick = 16 * self.dma_count[dma]
            self.dma_last[dma] = op
            if wait_total:
                self.total_groups.add(dma)
        self.ops.append(op)
        return op

    def simulate(self):
        cnt = {e: 0 for e in self.ENGS}
        for op in self.ops:
            if op.dma is None and op.signal:
                cnt[op.eng] += 1
                op.tick = cnt[op.eng]
        per = {e: [op for op in self.ops if op.eng == e] for e in self.ENGS}
        pos = {e: 0 for e in self.ENGS}
        sem = {}
        progress = True
        while progress:
            progress = False
            for e in self.ENGS:
                while pos[e] < len(per[e]):
                    op = per[e][pos[e]]
                    ok = True
                    for d in op.waits:
                        if d.dma is not None:
                            key = ("d", d.dma)
                            val = 16 * self.dma_count[d.dma] if d.dma in self.total_groups else d.tick
                        else:
                            key = ("e", d.eng)
                            val = d.tick
                        if sem.get(key, 0) < val:
                            ok = False
                            break
                    if not ok:
                        break
                    if op.fn is not None:
                        if op.dma is not None:
                            sem[("d", op.dma)] = sem.get(("d", op.dma), 0) + 16
                        elif op.signal:
                            sem[("e", op.eng)] = sem.get(("e", op.eng), 0) + 1
                    pos[e] += 1
                    progress = True
        stuck = {e: (pos[e], len(per[e])) for e in self.ENGS if pos[e] < len(per[e])}
        return stuck, per, pos

    def emit(self, nc, stack):
        cnt = {e: 0 for e in self.ENGS}
        for op in self.ops:
            if op.dma is None and op.signal:
                cnt[op.eng] += 1
                op.tick = cnt[op.eng]
        sems = {e: stack.enter_context(nc.semaphore("s_" + e)) for e in self.ENGS}
        dsems = {}
        for i, g in enumerate(self.dma_count):
            dsems[g] = stack.enter_context(nc.semaphore("d%d" % i))
        per = {e: [] for e in self.ENGS}
        for op in self.ops:
            per[op.eng].append(op)
        block = stack.enter_context(nc.Block())

        def run(engh, ops):
            seen = {}
            for op in ops:
                need = {}
                for d in op.waits:
                    if d.dma is not None:
                        key = ("d", d.dma)
                        val = 16 * self.dma_count[d.dma] if d.dma in self.total_groups else d.tick
                    else:
                        key = ("e", d.eng)
                        val = d.tick
                    if val > need.get(key, 0):
                        need[key] = val
                for key, val in need.items():
                    if seen.get(key, 0) >= val:
                        continue
                    seen[key] = val
                    sem = dsems[key[1]] if key[0] == "d" else sems[key[1]]
                    engh.wait_ge(sem, val)
                if op.fn is None:
                    continue
                ins = op.fn(engh)
                if op.dma is not None:
                    ins.then_inc(dsems[op.dma], 16)
                elif op.signal:
                    ins.then_inc(sems[op.eng], 1)

        @block.tensor
        def _(e):
            run(e, per["pe"])

        @block.scalar
        def _(e):
            run(e, per["act"])

        @block.vector
        def _(e):
            run(e, per["dve"])

        @block.gpsimd
        def _(e):
            run(e, per["pool"])

        @block.sync
        def _(e):
            run(e, per["sp"])


class Arena:
    def __init__(self, ap, nwords):
        self.ap = ap
        self.n = nwords
        self.off = 0

    def alloc(self, free_shape, dtype):
        nel = 1
        for s in free_shape:
            nel *= s
        bpe = 2 if dtype == BF16 else 4
        words = (nel * bpe + 3) // 4
        words = (words + 7) // 8 * 8
        assert self.off + words <= self.n, ("arena overflow", self.off, words, self.n)
        v = self.ap[:, self.off:self.off + words]
        self.off += words
        if dtype != F32:
            v = v.bitcast(dtype)
        v = v[:, 0:nel]
        if len(free_shape) == 2:
            v = v.rearrange("p (a b) -> p a b", a=free_shape[0])
        elif len(free_shape) == 3:
            v = v.rearrange("p (a b c) -> p a b c", a=free_shape[0], b=free_shape[1])
        return v


def build_program():
    from contextlib import ExitStack
    nc = bass.Bass("TRN2", target_bir_lowering=False)
    stack = ExitStack()
    P = Prog()

    def din(name, shape):
        return nc.dram_tensor(name, list(shape), F32, kind="ExternalInput").ap()

    xseq = {"s": din("xs", (SEQ_S, D)), "p": din("xp", (SEQ_P, D))}
    xtail = {"s": din("xts", (4, D)), "p": din("xtp", (4, D))}
    posd = {"s": din("poss", (1, SEQ_S + 4)), "p": din("posp", (1, SEQ_P + 4))}
    maskd = din("masks", (1, 8))
    ewin = din("e_w_in", (D, 2560)).rearrange("(c p) n -> p c n", p=128)
    ewout = din("e_w_out", (D, D)).rearrange("(c p) n -> p c n", p=128)
    w1d = [din("ffn_w1_%d" % l, (D, DFF)).rearrange("(c p) n -> p c n", p=128) for l in range(2)]
    w2d = [din("ffn_w2_%d" % l, (DFF, D)).rearrange("(f p) n -> p f n", p=128) for l in range(2)]
    cwin = din("c_w_in", (D, 3 * D)).rearrange("(c p) n -> p c n", p=128)
    cwout = din("c_w_out", (D, D)).rearrange("(c p) n -> p c n", p=128)
    wstd = din("wsT", (4, 128, 128)).rearrange("g q p -> q g p")
    identd = din("ident", (128, 128))
    rtd = din("rt", (128, 128))
    gcolsd = din("gcols", (128, 32))
    cwd = din("cw", (128, 24))
    rcd = din("rc", (128, 2))
    lvecd = din("lvec", (1, 256))
    vngd = din("vng", (1, 512))
    bsd = din("bs", (1, 512))
    sublnd = din("subln", (1, 128))
    fgd = din("fg", (1, D))
    xnd = {"s": nc.dram_tensor("xnd_s", [SEQ_S // 512, 128, 8 * 512], BF16, kind="Internal").ap(),
           "p": nc.dram_tensor("xnd_p", [SEQ_P // 512, 128, 8 * 512], BF16, kind="Internal").ap()}
    tabd = {"s": nc.dram_tensor("tabd_s", [SEQ_S // 512, 128, 2 * 512], F32, kind="Internal").ap(),
            "p": nc.dram_tensor("tabd_p", [SEQ_P // 512, 128, 2 * 512], F32, kind="Internal").ap()}
    yout = {"s": nc.dram_tensor("ys", [OWN, D], F32, kind="ExternalOutput").ap(),
            "p": nc.dram_tensor("yp", [OWN, D], F32, kind="ExternalOutput").ap()}

    ARENA_W = 43520
    PERS_W = 9216
    arena_t = stack.enter_context(nc.sbuf_tensor("arena", [128, ARENA_W], F32))
    pers_t = stack.enter_context(nc.sbuf_tensor("pers", [128, PERS_W], F32))
    psum = stack.enter_context(nc.psum_tensor("ps", [128, 8, 512], F32))
    pers = Arena(pers_t[:], PERS_W)
    arena_ap = arena_t[:]

    out_bT = pers.alloc((4, OWN + 4), BF16)
    ident = pers.alloc((128,), BF16)
    rt = pers.alloc((128,), BF16)
    wsT = pers.alloc((4, 128), BF16)
    gcols = pers.alloc((4, 8), F32)
    cw = pers.alloc((3, 8), F32)
    rc = pers.alloc((2,), F32)
    lvec = pers.alloc((256,), F32)
    ltmp = pers.alloc((64,), F32)
    lsc = pers.alloc((8,), F32)
    gBv = pers.alloc((512,), F32)
    bsB = pers.alloc((4, 128), F32)
    sublnB = pers.alloc((128,), F32)
    fgB = pers.alloc((D,), F32)
    maskB = pers.alloc((8,), F32)
    NSTAT = 16
    stat = pers.alloc((NSTAT, 12), F32)
    vstash = pers.alloc((4, 512), BF16)
    junk = pers.alloc((D,), BF16)
    epsb = pers.alloc((1,), F32)
    halfpi = pers.alloc((1,), F32)
    cst = pers.alloc((8,), F32)
    statC = pers.alloc((3, 12), F32)

    CONST = "const"
    ALLC = []

    def cload(eng, dst, src, name):
        P.add(eng, I("dma_start", out=dst, in_=src), writes=[("c", name)], dma=CONST + "_" + eng, wait_total=True)
        ALLC.append(("c", name))

    cload("pool", ident, identd[:, :], "ident")
    cload("pool", rt, rtd[:, :], "rt")
    cload("pool", wsT, wstd, "wsT")
    cload("sp", gcols, gcolsd.rearrange("p (a b) -> p a b", a=4), "gcols")
    cload("sp", cw, cwd.rearrange("p (a b) -> p a b", a=3), "cw")
    cload("sp", rc, rcd[:, :], "rc")
    cload("sp", lvec, lvecd.partition_broadcast(128), "lvec")
    cload("sp", gBv, vngd.partition_broadcast(128), "gBv")
    cload("sp", bsB, bsd.partition_broadcast(128).rearrange("p o (a b) -> p (o a) b", a=4), "bsB")
    cload("sp", sublnB, sublnd.partition_broadcast(128), "sublnB")
    cload("sp", fgB, fgd.partition_broadcast(128), "fgB")
    cload("sp", maskB, maskd.partition_broadcast(128), "maskB")

    P.add("dve", I("memset", epsb, EPS), writes=["epsb"])
    P.add("dve", I("memset", halfpi, math.pi / 2), writes=["halfpi"])
    P.add("dve", I("memset", cst[:, 0:1], 0.5), writes=["cst0"])
    P.add("dve", I("memset", statC, 1.0), writes=["statC"])
    P.add("dve", I("memset", cst[:, 1:2], 0.0), writes=["cst1"])
    P.add("dve", I("memset", cst[:, 2:3], -0.5), writes=["cst2"])
    P.add("dve", I("memset", cst[:, 3:4], D * EPS), writes=["cst3"])
    P.add("dve", I("memset", cst[:, 4:5], 128 * EPS), writes=["cst4"])
    for j in range(2):
        P.add("dve", I("tensor_tensor", out=ltmp, in0=lvec[:, 128 * j:128 * j + 64],
                       in1=lvec[:, 128 * j + 64:128 * j + 128], op=ALU.mult), reads=ALLC, writes=["ltmp"])
        P.add("dve", I("tensor_reduce", out=lsc[:, j:j + 1], in_=ltmp, axis=AX.X, op=ALU.add),
              reads=["ltmp"], writes=[("lsc", j)])
    P.add("act", I("activation", out=lsc[:, 2:4], in_=lsc[:, 0:2], func=AF.Exp),
          reads=[("lsc", 0), ("lsc", 1)], writes=["lsce"])
    P.add("dve", I("tensor_tensor", out=lsc[:, 4:5], in0=lsc[:, 2:3], in1=lsc[:, 3:4], op=ALU.subtract),
          reads=["lsce"], writes=["lam0"])
    P.add("dve", I("tensor_scalar", out=lsc[:, 5:6], in0=lsc[:, 4:5], scalar1=LAM_INIT, scalar2=-1.0,
                   op0=ALU.add, op1=ALU.mult), reads=["lam0"], writes=["neglam"])
    P.add("dve", I("tensor_scalar", out=sublnB, in0=sublnB, scalar1=(1.0 - LAM_INIT) * math.sqrt(128.0), scalar2=None,
                   op0=ALU.mult), reads=ALLC, writes=["sublnS"])
    P.add("dve", I("tensor_scalar", out=gcols, in0=gcols, scalar1=math.sqrt(float(D)), scalar2=None, op0=ALU.mult),
          reads=ALLC, writes=["gcolsS"])
    P.add("dve", I("tensor_scalar", out=fgB, in0=fgB, scalar1=math.sqrt(float(D)), scalar2=None, op0=ALU.mult),
          reads=ALLC, writes=["fgBS"])
    P.add("dve", I("tensor_scalar", out=gBv, in0=gBv, scalar1=math.sqrt(128.0), scalar2=None, op0=ALU.mult),
          reads=ALLC, writes=["gBvS"])
    P.barrier()

    def pool_rstd(src, dst, r, n, ccol, reads, wkey):
        P.add("pool", I("tensor_tensor", out=dst, in0=src, in1=cst[:r, ccol:ccol + 1].broadcast_to([r, n]), op=ALU.add),
              reads=reads, writes=[(wkey, "a")])
        P.add("pool", I("tensor_tensor", out=dst, in0=dst, in1=cst[:r, 2:3].broadcast_to([r, n]), op=ALU.pow),
              reads=[(wkey, "a")], writes=[wkey])

    state = {"stat": 0, "pt": 0, "ps": 0}

    def new_stat():
        s = state["stat"]
        state["stat"] = (s + 1) % NSTAT
        return s

    def bank(i):
        return psum[:, i, :]

    PS_RING = [0, 1, 2, 3, 6, 7]

    def next_ps():
        s = state["ps"]
        state["ps"] = (s + 1) % len(PS_RING)
        return PS_RING[s]

    def next_pt():
        s = state["pt"]
        state["pt"] = (s + 1) % 2
        return 4 + s

    def rstd_chain(src, r, srckey, scale):
        s = new_stat()
        sk = ("stat", s)
        n = src.shape[-1]
        P.add("act", I("activation", out=junk[:r, 0:n], in_=src, func=AF.Square, accum_out=stat[:r, s, 0:1]),
              reads=[srckey], writes=[(sk, 0)])
        pool_rstd(stat[:r, s, 0:1], stat[:r, s, 2:3], r, 1, 3 if n == D else 4, [(sk, 0)], (sk, 2))
        return stat[:r, s, 2:3], (sk, 2)

    def norm_to_xnT(src, r, srckey, xs_buf, xs_key, layer, dst, dst_key):
        rs, rsk = rstd_chain(src, r, srckey, 1.0 / D)
        P.add("act", I("activation", out=xs_buf[:r, :], in_=src, func=AF.Identity, scale=rs),
              reads=[srckey, rsk], writes=[xs_key])
        b = next_pt()
        ptv = bank(b).bitcast(BF16).rearrange("p (c n) -> p c n", c=8)
        P.add("pe", [I("transpose", out=ptv[:, c, 0:r], in_=xs_buf[:r, c * 128:(c + 1) * 128], identity=ident[:r, :r])
                     for c in range(8)], reads=[xs_key], writes=[("bank", b)])
        P.add("dve", I("tensor_tensor", out=dst, in0=ptv[:, :, 0:r],
                       in1=gcols[:, layer, :].unsqueeze(2).broadcast_to([128, 8, r]), op=ALU.mult),
              reads=[("bank", b)], writes=[dst_key])

    def mm(out_ap, lhs, rhs):
        nk = len(lhs)
        return [I("matmul", out_ap, lhsT=lhs[c], rhs=rhs[c], start=(c == 0), stop=(c == nk - 1)) for c in range(nk)]

    NW = 4
    WSLOT = 4096

    class WRing:
        def __init__(self, ar):
            self.slots = [ar.alloc((WSLOT,), BF16) for _ in range(NW)]
            self.i = 0

        def load(self, pieces):
            s = self.i % NW
            self.i += 1
            sl = self.slots[s]
            key = ("w", s)
            for dv, src in pieces:
                P.add("pool", I("dma_start", out=dv(sl), in_=src), writes=[key], dma=("w", s))
            return sl, key

    def v8(sl):
        return sl.rearrange("p (c n) -> p c n", c=8)

    def v4(sl):
        return sl.rearrange("p (c n) -> p c n", c=4)

    def v3(sl):
        return sl[:, 0:3072].rearrange("p (j c n) -> p j c n", j=3, c=8)

    def rope_tables(posB, n, poskey, Ct, St, A1, A2, tkey):
        A2i = A2.bitcast(I32)
        TE = os.environ.get("KTE", "dve")
        bc = lambda ap: ap.broadcast_to([128, n])
        P.add(TE, I("tensor_tensor", out=A1[:, 0:n], in0=posB[:, 0:n], in1=bc(rc[:, 0:1]), op=ALU.mult),
              reads=[poskey], writes=["tabA1"])
        P.add(TE, I("tensor_copy", out=A2i[:, 0:n], in_=A1[:, 0:n]), reads=["tabA1"], writes=["tabA2"])
        P.add(TE, I("tensor_tensor", out=A1[:, 0:n], in0=A1[:, 0:n], in1=A2i[:, 0:n], op=ALU.subtract),
              reads=["tabA1", "tabA2"], writes=["tabA1"])
        P.add("dve", I("scalar_tensor_tensor", out=A2[:, 0:n], in0=A1[:, 0:n], scalar=0.5, in1=A1[:, 0:n],
                       op0=ALU.is_gt, op1=ALU.subtract), reads=["tabA1"], writes=["tabA2"])
        P.add("act", I("activation", out=St[:, 0:n], in_=A2[:, 0:n], func=AF.Sin, scale=rc[:, 1:2]),
              reads=["tabA2"], writes=[(tkey, "S")])
        P.add("dve", I("scalar_tensor_tensor", out=A1[:, 0:n], in0=A2[:, 0:n], scalar=-1.0, in1=A2[:, 0:n],
                       op0=ALU.mult, op1=ALU.min), reads=["tabA2"], writes=["tabA1"])
        P.add("act", I("activation", out=Ct[:, 0:n], in_=A1[:, 0:n], func=AF.Sin, scale=TWO_PI, bias=halfpi[:, 0:1]),
              reads=["tabA1"], writes=[(tkey, "C")])

    def gmlp_v1(ps_ap, pskey, r, gv, sq, kx=0):
        P.add("act", I("activation", out=gv[:r, :], in_=ps_ap, func=AF.Gelu_apprx_tanh), reads=[pskey],
              writes=[("gv", kx)])
        s = new_stat()
        sk = ("stat", s)
        for g in range(4):
            P.add("act", I("activation", out=junk[:r, 0:128], in_=gv[:r, g * 128:(g + 1) * 128], func=AF.Square,
                           accum_out=stat[:r, s, g:g + 1]), reads=[("gv", kx)], writes=[(sk, 0, g)])
        return s

    def gmlp_v2(s, r, gv, vout, vkey, kx=0):
        sk = ("stat", s)
        pool_rstd(stat[:r, s, 0:4], stat[:r, s, 8:12], r, 4, 4, [(sk, 0, g) for g in range(4)], (sk, 2))
        for g in range(4):
            P.add("dve", I("scalar_tensor_tensor", out=vout[:r, g * 128:(g + 1) * 128], in0=gv[:r, g * 128:(g + 1) * 128],
                           scalar=stat[:r, s, 8 + g:9 + g], in1=gBv[:r, g * 128:(g + 1) * 128], op0=ALU.mult, op1=ALU.mult),
                  reads=[("gv", kx), (sk, 2)], writes=[vkey])

    def gmlp_v(ps_ap, pskey, r, gv, sq, vout, vkey, kx=0):
        s = gmlp_v1(ps_ap, pskey, r, gv, sq, kx)
        gmlp_v2(s, r, gv, vout, vkey, kx)

    def phase_AB(sq_name):
        S = SEQ_S if sq_name == "s" else SEQ_P
        NB = S // 512
        NKB = S // 128
        xd = xseq[sq_name]
        ar = Arena(arena_ap, ARENA_W)
        Kt = ar.alloc((S,), BF16)
        Vaug = ar.alloc((NKB, 130), BF16)
        Qt = ar.alloc((OWN + 4,), BF16)
        NXR = 8
        NXS = 4
        xring = [ar.alloc((D,), F32) for _ in range(NXR)]
        xsb = [ar.alloc((D,), BF16) for _ in range(NXS)]
        xnA = [ar.alloc((8, 512), BF16) for _ in range(2)]
        xnTl = ar.alloc((8, 4), BF16)
        kraw = [ar.alloc((512,), BF16) for _ in range(2)]
        t1 = ar.alloc((512,), F32)
        t2 = ar.alloc((512,), F32)
        Ct = [ar.alloc((512,), F32) for _ in range(2)]
        St = [ar.alloc((512,), F32) for _ in range(2)]
        A1 = ar.alloc((512,), F32)
        A2 = ar.alloc((512,), F32)
        posB = [ar.alloc((512,), F32) for _ in range(2)]
        NPT = 4
        Pt = [ar.alloc((2, 512), BF16) for _ in range(NPT)]
        otmp = ar.alloc((128,), F32)
        accS = ar.alloc((3, 388), F32)
        obS = ar.alloc((4, 128), F32)
        obn4 = ar.alloc((4, 128), BF16)
        gv = ar.alloc((512,), F32)
        sq = ar.alloc((512,), F32)
        ring = WRing(ar)
        tail_chunks = [NKB - 1, 8, 7, 16]
        cnt = {"x": 0, "xs": 0, "kraw": 0, "pt": 0}
        accs = [("bank", 4), ("bank", 5), ("bank", 6)]

        def acc(j, qt):
            i = j * 4 + qt
            return psum[:, 4 + i // 3, (i % 3) * 129:(i % 3) * 129 + 129]

        import os
        for hd in range(int(os.environ.get("KHEADS", "4"))):
            wsl, wkey = ring.load([(lambda sl, j=j: v3(sl)[:, j],
                                    ewin[:, :, 1024 + 512 * j + hd * 128:1024 + 512 * j + (hd + 1) * 128]) for j in range(3)])
            W3 = v3(wsl)
            if hd == 0:
                wav_sl, wavkey = ring.load([(v8, ewin[:, :, 512:1024])])
                Wav = v8(wav_sl)
            P.add("dve", I("memset", Vaug[:, :, 128:129], 1.0), writes=["Vaug1"])

            def rope_proj(j, rhs, n, ctab, stab, tabkey, dst, dstkey, xkeys, part=6):
                b1 = next_ps()
                P.add("pe", mm(bank(b1)[:, 0:n], [W3[:, j, c, :] for c in range(8)], rhs),
                      reads=xkeys + [wkey], writes=[("bank", b1)])
                if part < 2:
                    return
                ks = cnt["kraw"] % 2
                cnt["kraw"] += 1
                P.add("act", I("activation", out=kraw[ks][:, 0:n], in_=bank(b1)[:, 0:n], func=AF.Copy),
                      reads=[("bank", b1)], writes=[("kraw", ks)])
                if part < 3:
                    return
                b2 = next_ps()
                P.add("pe", I("matmul", bank(b2)[:, 0:n], lhsT=rt, rhs=kraw[ks][:, 0:n], start=True, stop=True),
                      reads=[("kraw", ks)], writes=[("bank", b2)])
                if part < 4:
                    return
                P.add("dve", I("tensor_tensor", out=t1[:, 0:n], in0=bank(b1)[:, 0:n], in1=ctab[:, 0:n], op=ALU.mult),
                      reads=[("bank", b1), (tabkey, "C")], writes=["t1"])
                if part < 5:
                    return
                P.add("dve", I("tensor_tensor", out=t2[:, 0:n], in0=bank(b2)[:, 0:n], in1=stab[:, 0:n], op=ALU.mult),
                      reads=[("bank", b2), (tabkey, "S")], writes=["t2"])
                if part < 6:
                    return
                P.add("dve", I("tensor_tensor", out=dst, in0=t1[:, 0:n], in1=t2[:, 0:n], op=ALU.add),
                      reads=["t1", "t2"], writes=[dstkey])

            xsl = cnt["x"] % NXR
            cnt["x"] += 1
            xt = xring[xsl]
            xk = ("xr", xsl)
            P.add("sp", I("dma_start", out=xt[0:4, :], in_=xtail[sq_name][:, :]), writes=[xk], dma=xk)
            xss = cnt["xs"] % NXS
            cnt["xs"] += 1
            norm_to_xnT(xt[0:4, :], 4, xk, xsb[xss], ("xsb", xss), 0, xnTl, "xnTl")
            P.add("sp", I("dma_start", out=posB[0][:, 0:4], in_=posd[sq_name][:, S:S + 4].partition_broadcast(128)),
                  writes=[("posB", 0)], dma=("posB", 0))
            rope_tables(posB[0], 4, ("posB", 0), Ct[0], St[0], A1, A2, ("tab", 0))
            rope_proj(0, [xnTl[:, c, :] for c in range(8)], 4, Ct[0], St[0], ("tab", 0), Qt[:, OWN:OWN + 4],
                      ("Qt", 4), ["xnTl"])

            if os.environ.get("KSTOP", "9") == "1":
                return
            NBr = int(os.environ.get("KNB", NB))
            blk = {}

            def par_of(b):
                return (b + 1) % 2

            def stage_load(b):
                par = par_of(b)
                P.add("sp", I("dma_start", out=posB[par],
                              in_=posd[sq_name][:, b * 512:(b + 1) * 512].partition_broadcast(128)),
                      writes=[("posB", par)], dma=("posB", par))
                xs_ = []
                for t in range(4):
                    xsl = (b % 2) * 4 + t
                    row = b * 512 + t * 128
                    P.add("sp", I("dma_start", out=xring[xsl], in_=xd[row:row + 128, :]), writes=[("xr", xsl)],
                          dma=("xr", xsl))
                    xs_.append(xsl)
                blk[b] = {"x": xs_}

            def stage_stats(b):
                s_ = new_stat()
                sk = ("stat", s_)
                for t in range(4):
                    xsl = blk[b]["x"][t]
                    P.add("act", I("activation", out=junk[:, :], in_=xring[xsl], func=AF.Square,
                                   accum_out=stat[:, s_, t:t + 1]), reads=[("xr", xsl)], writes=[(sk, 0, t)])
                pool_rstd(stat[:, s_, 0:4], stat[:, s_, 8:12], 128, 4, 3, [(sk, 0, t) for t in range(4)], (sk, 2))
                blk[b]["stat"] = s_

            def stage_mid(b):
                par = par_of(b)
                s_ = blk[b]["stat"]
                sk = ("stat", s_)
                for t in range(4):
                    xsl = blk[b]["x"][t]
                    rs_ap = stat[:, s_, 8 + t:9 + t]
                    if t == 2:
                        P.add("act", I("activation", out=xsb[t], in_=xring[xsl], func=AF.Identity, scale=rs_ap),
                              reads=[("xr", xsl), (sk, 2)], writes=[("xsb", t)])
                    elif t == 3:
                        P.add("act", I("activation", out=xsb[t], in_=xring[xsl], func=AF.Identity, scale=rs_ap),
                              reads=[("xr", xsl), (sk, 2)], writes=[("xsb", t)])
                    else:
                        P.add("dve", I("tensor_scalar", out=xsb[t], in0=xring[xsl], scalar1=rs_ap,
                                       scalar2=None, op0=ALU.mult), reads=[("xr", xsl), (sk, 2)], writes=[("xsb", t)])
                for t in range(4):
                    bnk = next_pt()
                    ptv = bank(bnk).bitcast(BF16).rearrange("p (c n) -> p c n", c=8)
                    P.add("pe", [I("transpose", out=ptv[:, c, :], in_=xsb[t][:, c * 128:(c + 1) * 128], identity=ident)
                                 for c in range(8)], reads=[("xsb", t)], writes=[("bank", bnk)])
                    P.add("dve", I("tensor_tensor", out=xnA[par][:, :, t * 128:(t + 1) * 128], in0=ptv,
                                   in1=gcols[:, 0, :].unsqueeze(2).broadcast_to([128, 8, 128]), op=ALU.mult),
                          reads=[("bank", bnk)], writes=[("xnA", par, t)])

            def stage_tables(b):
                par = par_of(b)
                rope_tables(posB[par], 512, ("posB", par), Ct[par], St[par], A1, A2, ("tab", par))

            cache = not os.environ.get("KNOCACHE")

            def store_tab(b):
                par = par_of(b)
                P.add("sp", I("dma_start", out=tabd[sq_name][b, :, 0:512], in_=Ct[par]),
                      reads=[(("tab", par), "C")], writes=[("tabd", b, 0)], dma=("tst", 0))
                P.add("sp", I("dma_start", out=tabd[sq_name][b, :, 512:1024], in_=St[par]),
                      reads=[(("tab", par), "S")], writes=[("tabd", b, 1)], dma=("tst", 1))

            def store_x(b):
                par = par_of(b)
                P.add("sp", I("dma_start", out=xnd[sq_name][b], in_=xnA[par].rearrange("p c n -> p (c n)")),
                      reads=[("xnA", par, t) for t in range(4)], writes=[("xnd", b)], dma=("xst", par))

            def stage_reload(b):
                par = par_of(b)
                P.add("sp", I("dma_start", out=xnA[par].rearrange("p c n -> p (c n)"), in_=xnd[sq_name][b]),
                      reads=[("xnd", b)], writes=[("xnA", par, t) for t in range(4)], dma=("xrl", par))
                P.add("sp", I("dma_start", out=Ct[par], in_=tabd[sq_name][b, :, 0:512]),
                      reads=[("tabd", b, 0)], writes=[(("tab", par), "C")], dma=("trl", par, 0))
                P.add("sp", I("dma_start", out=St[par], in_=tabd[sq_name][b, :, 512:1024]),
                      reads=[("tabd", b, 1)], writes=[(("tab", par), "S")], dma=("trl", par, 1))

            first_pass = (hd == 0) or not cache
            if first_pass:
                stage_load(0)
                if NBr > 1:
                    stage_load(1)
                stage_stats(0)
                stage_tables(0)
                if cache:
                    store_tab(0)
            else:
                stage_reload(0)
            for b in range(NBr):
                par = par_of(b)
                if first_pass:
                    stage_mid(b)
                    if b + 2 < NBr:
                        stage_load(b + 2)
                    if b + 1 < NBr:
                        stage_stats(b + 1)
                        stage_tables(b + 1)
                    if cache:
                        store_x(b)
                        if b + 1 < NBr:
                            store_tab(b + 1)
                elif b + 1 < NBr:
                    stage_reload(b + 1)
                xkeys = [("xnA", par, t) for t in range(4)]
                xr = [xnA[par][:, c, :] for c in range(8)]
                kv = os.environ.get("KVAR", "")
                if kv != "noK":
                    rope_proj(1, xr, 512, Ct[par], St[par], ("tab", par), Kt[:, b * 512:(b + 1) * 512], ("Kt", b), xkeys)
                if kv == "KbK":
                    P.barrier()
                    kv = "KK"
                if kv == "KK":
                    for _ in range(int(os.environ.get("KSKIP", "0"))):
                        next_ps()
                    rope_proj(1, xr, 512, Ct[par], St[par], ("tab", par), Kt[:, b * 512:(b + 1) * 512], ("Kt", b), xkeys,
                              part=int(os.environ.get("KPART", "6")))
                    continue
                if os.environ.get("KNOV"):
                    if b < 4 and os.environ.get("KNOV") == "q":
                        rope_proj(0, xr, 512, Ct[par], St[par], ("tab", par), Qt[:, b * 512:(b + 1) * 512], ("Qt", b), xkeys)
                    continue
                bv = next_ps()
                vm = []
                for t in range(4):
                    vm += mm(bank(bv)[:, t * 128:(t + 1) * 128], [xnA[par][:, c, t * 128:(t + 1) * 128] for c in range(8)],
                             [W3[:, 2, c, :] for c in range(8)])
                P.add("pe", vm, reads=xkeys + [wkey], writes=[("bank", bv)])
                for t in range(4):
                    P.add("dve", I("tensor_copy", out=Vaug[:, b * 4 + t, 0:128], in_=bank(bv)[:, t * 128:(t + 1) * 128]),
                          reads=[("bank", bv), "Vaug1"], writes=[("Vaug", b, t)])
                if b < 4:
                    rope_proj(0, xr, 512, Ct[par], St[par], ("tab", par), Qt[:, b * 512:(b + 1) * 512], ("Qt", b), xkeys)
                if hd == 0 and not os.environ.get("KNOAV"):
                    for ti, ch in enumerate(tail_chunks):
                        if ch // 4 == b:
                            t = ch % 4
                            ba = next_ps()
                            P.add("pe", mm(bank(ba), [xnA[par][:, c, t * 128:(t + 1) * 128] for c in range(8)],
                                           [Wav[:, c, :] for c in range(8)]),
                                  reads=xkeys + [wavkey], writes=[("bank", ba)])
                            gmlp_v(bank(ba), ("bank", ba), 128, gv, sq, vstash[:, ti, :], ("vstash", ti))

            if os.environ.get("KSTOP", "9") == "2":
                return
            qblocks = [(0, 512), (512, 512), (1024, 512), (1536, 512), (OWN, 4)]
            allK = [("Kt", b) for b in range(NB)]
            allV = [("Vaug", b, t) for b in range(NB) for t in range(4)]
            allQ = [("Qt", b) for b in range(5)]
            def accv(j, qt):
                i = j * 4 + qt
                return accS[:, i // 3, (i % 3) * 129:(i % 3) * 129 + 129]

            def make_fin(q0, nq, nqt, r):
                def fin():
                    akeys = [("accS", bi) for bi in range(3)]
                    s_ = new_stat()
                    sk = ("stat", s_)
                    for qt in range(nqt):
                        P.add("dve", I("reciprocal", out=stat[:r, s_, 0:1], in_=accv(0, qt)[0:r, 128:129]),
                              reads=akeys, writes=[(sk, 0)])
                        P.add("dve", I("reciprocal", out=stat[:r, s_, 1:2], in_=accv(1, qt)[0:r, 128:129]),
                              reads=akeys, writes=[(sk, 1)])
                        P.add("dve", I("tensor_tensor", out=stat[:r, s_, 2:3], in0=stat[:r, s_, 1:2], in1=lsc[:r, 5:6],
                                       op=ALU.mult), reads=[(sk, 1)], writes=[(sk, 2)])
                        P.add("dve", I("tensor_scalar", out=otmp[:r, :], in0=accv(0, qt)[0:r, 0:128],
                                       scalar1=stat[:r, s_, 0:1], scalar2=None, op0=ALU.mult),
                              reads=akeys + [(sk, 0)], writes=["otmp"])
                        P.add("dve", I("scalar_tensor_tensor", out=obS[:r, qt, :], in0=accv(1, qt)[0:r, 0:128],
                                       scalar=stat[:r, s_, 2:3], in1=otmp[:r, :], op0=ALU.mult, op1=ALU.add),
                              reads=akeys + [(sk, 2), "otmp"], writes=[("obS", qt)])
                        P.add("dve", I("tensor_tensor", out=otmp[:r, :], in0=obS[:r, qt, :], in1=obS[:r, qt, :], op=ALU.mult),
                              reads=[("obS", qt)], writes=["otmp"])
                        P.add("dve", I("tensor_reduce", out=stat[:r, s_, 4 + qt:5 + qt], in_=otmp[:r, :], axis=AX.X, op=ALU.add),
                              reads=["otmp"], writes=[(sk, 4, qt)])
                    return (s_, sk)

                def fin2(s_, sk):
                    pool_rstd(stat[:r, s_, 4:4 + nqt], stat[:r, s_, 8:8 + nqt], r, nqt, 4,
                              [(sk, 4, qt) for qt in range(nqt)], (sk, 9))
                    ptb = bank(7).bitcast(BF16)
                    for qt in range(nqt):
                        P.add("dve", I("scalar_tensor_tensor", out=obn4[:r, qt, :], in0=obS[:r, qt, :],
                                       scalar=stat[:r, s_, 8 + qt:9 + qt], in1=sublnB[:r, :], op0=ALU.mult, op1=ALU.mult),
                              reads=[("obS", qt), (sk, 9)], writes=[("obn", qt)])
                    P.add("pe", [I("transpose", out=ptb[:, qt * 128:qt * 128 + r], in_=obn4[:r, qt, :], identity=ident[:r, :r])
                                 for qt in range(nqt)], reads=[("obn", qt) for qt in range(nqt)], writes=[("bank", 7)])
                    P.add("dve", I("tensor_copy", out=out_bT[:, hd, q0:q0 + nq], in_=ptb[:, 0:nq]),
                          reads=[("bank", 7)], writes=[("obT", hd, q0)])
                return fin, fin2

            pending_fin = []
            for (q0, nq) in qblocks:
                nqt = (nq + 127) // 128
                r = min(128, nq)
                pq = []
                for kb in range(NKB + 2):
                    if kb == 14 and pending_fin:
                        f2_, args_ = pending_fin.pop(0)
                        f2_(*args_)
                    pend = None
                    if kb < NKB:
                        sb = (kb % 2) * 2
                        P.add("pe", [I("matmul", psum[:, sb, 0:nq], lhsT=Kt[0:64, kb * 128:(kb + 1) * 128],
                                       rhs=Qt[0:64, q0:q0 + nq], start=True, stop=True),
                                     I("matmul", psum[:, sb + 1, 0:nq], lhsT=Kt[64:128, kb * 128:(kb + 1) * 128],
                                       rhs=Qt[64:128, q0:q0 + nq], start=True, stop=True)],
                              reads=allK + allQ, writes=[("bank", sb), ("bank", sb + 1)])
                        psl = cnt["pt"] % NPT
                        cnt["pt"] += 1
                        P.add("act", I("activation", out=Pt[psl][:, :, 0:nq], in_=psum[:, sb:sb + 2, 0:nq], func=AF.Exp,
                                       scale=0.125), reads=[("bank", sb), ("bank", sb + 1)], writes=[("Pt", psl)])
                        pq.append((kb, psl))
                    if pq and (len(pq) > 2 or kb >= NKB):
                        pend = pq.pop(0)
                    if pend is not None:
                        pkb, pps = pend
                        pv = []
                        seen_banks = set()
                        for j in range(2):
                            for qt in range(nqt):
                                ai = j * 4 + qt
                                first_in_bank = (ai // 3) not in seen_banks
                                seen_banks.add(ai // 3)
                                pv.append(I("matmul", acc(j, qt)[0:r, :], lhsT=Pt[pps][:, j, qt * 128:qt * 128 + r],
                                            rhs=Vaug[:, pkb, 0:129], start=(pkb == 0 and first_in_bank),
                                            stop=(pkb == NKB - 1), skip_group_check=True))
                        P.add("pe", pv, reads=[("Pt", pps)] + allV, writes=accs)
                for bi in range(3):
                    ncol = 387 if bi < 2 else 258
                    P.add("dve", I("tensor_copy", out=accS[:r, bi, 0:ncol], in_=psum[0:r, 4 + bi, 0:ncol]),
                          reads=[("bank", 4 + bi)], writes=[("accS", bi)])
                f1_, f2_ = make_fin(q0, nq, nqt, r)
                pending_fin.append((f2_, f1_()))
            while pending_fin:
                f2_, args_ = pending_fin.pop(0)
                f2_(*args_)

    def phase_C(sq_name, ri):
        xd = xseq[sq_name]
        roff = ri * RNG
        sidx = 0 if sq_name == "s" else 1
        ar = Arena(arena_ap, ARENA_W)
        h = ar.alloc((9, D), F32)
        xnT = ar.alloc((8, RNG + 2), BF16)
        uT = ar.alloc((4, RNG + 2), BF16)
        oaT = ar.alloc((4, RNG + 2), BF16)
        hTall = ar.alloc((8, RNG + 2), BF16)
        hT = [hTall[:, 0:4, :], hTall[:, 4:8, :]]
        mTb = hTall
        xsb = [ar.alloc((D,), BF16) for _ in range(9)]
        uni = ar.alloc((4640,), F32)
        gv2 = [uni[:, i * 512:(i + 1) * 512] for i in range(3)]
        sq2 = [uni[:, 1536 + i * 512:1536 + (i + 1) * 512] for i in range(3)]
        mixt2 = [uni[:, 3072 + i * 512:3072 + (i + 1) * 512].rearrange("p (g n) -> p g n", g=4) for i in range(3)]
        vb2 = [ar.alloc((512,), BF16) for _ in range(3)]
        mixt = mixt2[0]
        rl = [ar.alloc((512,), F32) for _ in range(2)]
        bgs = uni[:, 0:RNG]
        cgs = uni[:, 1024:1024 + RNG + 2]
        tb = uni[:, 2056:2056 + RNG + 2]
        yb = uni[:, 3088:3088 + RNG]
        ring = WRing(ar)
        cnt = {"xs": 0, "rl": 0, "hT": 0}
        tiles = [(t, 128) for t in range(8)] + [(8, 2)]
        cblocks = [(0, 512), (512, 512), (RNG, 2)]

        def tile_cols(t):
            return (t * 128, 128) if t < 8 else (RNG, 2)

        def hkey(t):
            return ("h", t)

        for t in range(8):
            P.add("sp", I("dma_start", out=h[:, t, :], in_=xd[roff + t * 128:roff + (t + 1) * 128, :]),
                  writes=[hkey(t)], dma=("hl", t))
        P.add("sp", I("dma_start", out=h[0:2, 8, :], in_=xtail[sq_name][2 * ri:2 * ri + 2, :]),
              writes=[hkey(8)], dma=("hl", 8))

        def do_norm(layer, tl):
            nt = len(tl)
            for (t, r) in tl:
                P.add("act", I("activation", out=junk[:r, :], in_=h[0:r, t, :], func=AF.Square,
                               accum_out=statC[:r, 0, t:t + 1]), reads=[hkey(t)], writes=[("sC", 0, t)])
            t_lo = tl[0][0]
            pool_rstd(statC[:, 0, t_lo:t_lo + nt], statC[:, 2, t_lo:t_lo + nt], 128, nt, 3,
                      [("sC", 0, t) for (t, r) in tl], ("sC", 2))
            for i, (t, r) in enumerate(tl):
                rs_ap = statC[:r, 2, t:t + 1]
                src = h[0:r, t, :]
                if i % 2 == 0:
                    P.add("act", I("activation", out=xsb[t][:r, :], in_=src, func=AF.Identity, scale=rs_ap),
                          reads=[hkey(t), ("sC", 2)], writes=[("xsbC", t)])
                else:
                    P.add("dve", I("tensor_scalar", out=xsb[t][:r, :], in0=src, scalar1=rs_ap, scalar2=None, op0=ALU.mult),
                          reads=[hkey(t), ("sC", 2)], writes=[("xsbC", t)])
            for (t, r) in tl:
                c0, n = tile_cols(t)
                bnk = next_pt()
                ptv = bank(bnk).bitcast(BF16).rearrange("p (c n) -> p c n", c=8)
                P.add("pe", [I("transpose", out=ptv[:, c, 0:r], in_=xsb[t][:r, c * 128:(c + 1) * 128], identity=ident[:r, :r])
                             for c in range(8)], reads=[("xsbC", t)], writes=[("bank", bnk)])
                P.add("dve", I("tensor_tensor", out=xnT[:, :, c0:c0 + n], in0=ptv[:, :, 0:r],
                               in1=gcols[:, layer, :].unsqueeze(2).broadcast_to([128, 8, r]), op=ALU.mult),
                      reads=[("bank", bnk)], writes=[("xnT", t)])

        def xkeys_cb(cb):
            c0, n = cb
            if n == 2:
                return [("xnT", 8)]
            return [("xnT", t) for t in range(c0 // 128, c0 // 128 + 4)]

        def resid_add(t, r, half, b):
            P.add("dve", I("tensor_tensor", out=h[0:r, t, half * 512:(half + 1) * 512],
                           in0=h[0:r, t, half * 512:(half + 1) * 512], in1=bank(b)[0:r, :], op=ALU.add),
                  reads=[("bank", b), hkey(t)], writes=[hkey(t)])

        if os.environ.get("KNOXND"):
            do_norm(0, tiles)
        else:
            for cbi in range(2):
                P.add("sp", I("dma_start", out=xnT[:, :, cbi * 512:(cbi + 1) * 512],
                              in_=xnd[sq_name][2 * ri + cbi].rearrange("p (c n) -> p c n", c=8)),
                      writes=[("xnT", t) for t in range(cbi * 4, cbi * 4 + 4)], dma=("xnl", cbi))
            do_norm(0, tiles[8:])
        wau_sl, waukey = ring.load([(v8, ewin[:, :, 0:512])])
        wav_sl, wavkey = ring.load([(v8, ewin[:, :, 512:1024])])
        Wau = v8(wau_sl)
        Wav = v8(wav_sl)
        wo = []
        for half in range(2):
            sl, k = ring.load([(v4, ewout[:, half * 4:(half + 1) * 4, :])])
            wo.append((v4(sl), k))
        for cb in cblocks:
            c0, n = cb
            for g in range(4):
                b = next_ps()
                P.add("pe", mm(bank(b)[:, 0:n], [Wau[:, c, g * 128:(g + 1) * 128] for c in range(8)],
                               [xnT[:, c, c0:c0 + n] for c in range(8)]),
                      reads=xkeys_cb(cb) + [waukey], writes=[("bank", b)])
                P.add("act", I("activation", out=uT[:, g, c0:c0 + n], in_=bank(b)[:, 0:n], func=AF.Gelu_apprx_tanh),
                      reads=[("bank", b)], writes=[("uT", g, c0)])

        def ukeys(c0):
            cc = RNG if c0 >= RNG else (c0 // 512) * 512
            return [("uT", g, cc) for g in range(4)]
        gst = {}

        def g_a1(t):
            b = next_ps()
            P.add("pe", mm(bank(b), [xnT[:, c, t * 128:(t + 1) * 128] for c in range(8)], [Wav[:, c, :] for c in range(8)]),
                  reads=[("xnT", t), wavkey], writes=[("bank", b)])
            kx = t % 3
            gst[t] = gmlp_v1(bank(b), ("bank", b), 128, gv2[kx], sq2[kx], kx=kx)

        def g_a2(t):
            kx = t % 3
            gmlp_v2(gst[t], 128, gv2[kx], vb2[kx], ("vb", kx), kx=kx)
            b2 = next_ps()
            P.add("pe", [I("matmul", bank(b2)[:, g * 128:(g + 1) * 128], lhsT=vb2[kx][:, g * 128:(g + 1) * 128],
                           rhs=wsT[:, g, :], start=True, stop=True) for g in range(4)],
                  reads=[("vb", kx)], writes=[("bank", b2)])
            gst[("b2", t)] = b2

        def g_b(t):
            kx = t % 3
            b2 = gst[("b2", t)]
            mx = mixt2[kx]
            P.add("dve", I("tensor_tensor", out=mx, in0=bank(b2).rearrange("p (g n) -> p g n", g=4), in1=bsB, op=ALU.add),
                  reads=[("bank", b2)], writes=[("mixt", kx)])
            P.add("dve", I("tensor_tensor", out=oaT[:, :, t * 128:(t + 1) * 128], in0=mx,
                           in1=uT[:, :, t * 128:(t + 1) * 128], op=ALU.mult),
                  reads=[("mixt", kx)] + ukeys(t * 128), writes=[("oaT", t)])

        if os.environ.get("KGSEQ"):
            for t in range(8):
                g_a1(t)
                g_a2(t)
                g_b(t)
        else:
            g_a1(0)
            for t in range(8):
                if t + 1 < 8:
                    g_a1(t + 1)
                g_a2(t)
                if t >= 1:
                    g_b(t - 1)
            g_b(7)
        for j in range(2):
            ti = 2 * ri + j
            pcol = 127 if j == 0 else 0
            b2 = next_ps()
            P.add("pe", [I("matmul", bank(b2)[:, g:g + 1], lhsT=vstash[:, ti, g * 128:(g + 1) * 128],
                           rhs=wsT[:, g, pcol:pcol + 1], start=True, stop=True) for g in range(4)],
                  reads=[], writes=[("bank", b2)])
            P.add("dve", I("tensor_tensor", out=mixt[:, :, 0], in0=bank(b2)[:, 0:4], in1=bsB[:, :, pcol], op=ALU.add),
                  reads=[("bank", b2)], writes=[("mixt", 0)])
            P.add("dve", I("tensor_tensor", out=oaT[:, :, RNG + j], in0=mixt[:, :, 0], in1=uT[:, :, RNG + j], op=ALU.mult),
                  reads=[("mixt", 0)] + ukeys(RNG), writes=[("oaT", 8, j)])
        for (t, r) in tiles:
            c0, n = tile_cols(t)
            for half in range(2):
                b = next_ps()
                lhs = []
                for c in range(8):
                    if c < 4:
                        lhs.append(oaT[:, c, c0:c0 + r])
                    elif t < 8:
                        lhs.append(out_bT[:, c - 4, roff + c0:roff + c0 + r])
                    else:
                        lhs.append(out_bT[:, c - 4, OWN + 2 * ri:OWN + 2 * ri + 2])
                okeys = [("oaT", t)] if t < 8 else [("oaT", 8, 0), ("oaT", 8, 1)]
                P.add("pe", mm(bank(b)[0:r, :], lhs, [wo[c // 4][0][:, c % 4, half * 512:(half + 1) * 512] for c in range(8)]),
                      reads=okeys + [wo[0][1], wo[1][1]], writes=[("bank", b)])
                resid_add(t, r, half, b)

        ALLHT = [("hT", hs) for hs in range(2)]

        def ffn(layer, tl, cbl):
            do_norm(1 + 2 * layer, tl)
            st = {}

            def ld1(fg):
                st[("w1", fg)] = ring.load([(v8, w1d[layer][:, :, fg * 512:(fg + 1) * 512])])

            def ld2(fg):
                st[("w2", fg)] = ring.load([(v4, w2d[layer][:, fg * 4:(fg + 1) * 4, :])])

            def hidden(fg):
                s1, k1 = st[("w1", fg)]
                W1 = v8(s1)
                hs = fg % 2
                hb = hT[hs]
                for f in range(4):
                    for cb in cbl:
                        c0, n = cb
                        b = next_ps()
                        P.add("pe", mm(bank(b)[:, 0:n], [W1[:, c, f * 128:(f + 1) * 128] for c in range(8)],
                                       [xnT[:, c, c0:c0 + n] for c in range(8)]),
                              reads=xkeys_cb(cb) + [k1], writes=[("bank", b)])
                        rsl = cnt["rl"] % 2
                        cnt["rl"] += 1
                        P.add("act", I("activation", out=rl[rsl][:, 0:n], in_=bank(b)[:, 0:n], func=AF.Relu),
                              reads=[("bank", b)], writes=[("rl", rsl)])
                        P.add("dve", I("tensor_tensor", out=hb[:, f, c0:c0 + n], in0=rl[rsl][:, 0:n], in1=rl[rsl][:, 0:n],
                                       op=ALU.mult), reads=[("rl", rsl)], writes=[("hTw", hs, f, c0)])

            def second(fg):
                s2, k2 = st[("w2", fg)]
                W2 = v4(s2)
                hs = fg % 2
                hb = hT[hs]
                hkeys = [("hTw", hs, f, c0) for f in range(4) for (c0, n) in cbl]
                for (t, r) in tl:
                    c0, n = tile_cols(t)
                    for half in range(2):
                        b = next_ps()
                        P.add("pe", mm(bank(b)[0:r, :], [hb[:, f, c0:c0 + r] for f in range(4)],
                                       [W2[:, f, half * 512:(half + 1) * 512] for f in range(4)]),
                              reads=hkeys + [k2], writes=[("bank", b)])
                        resid_add(t, r, half, b)

            ld1(0)
            ld2(0)
            ld1(1)
            hidden(0)
            for fg in range(8):
                if fg + 1 < 8:
                    ld2(fg + 1)
                    hidden(fg + 1)
                if fg + 2 < 8:
                    ld1(fg + 2)
                second(fg)

        ffn(0, tiles, cblocks)

        do_norm(2, tiles)
        m0 = sidx * 4 + 2 * ri
        for c in range(8):
            wsl, wk = ring.load([(lambda sl, j=j: v3(sl)[:, j], cwin[:, :, j * D + c * 128:j * D + (c + 1) * 128])
                                 for j in range(3)])
            Wc = v3(wsl)
            for cb in cblocks:
                c0, n = cb
                xk = xkeys_cb(cb)
                xr = [xnT[:, k, c0:c0 + n] for k in range(8)]
                if n > 2:
                    b = next_ps()
                    P.add("pe", mm(bank(b)[:, 0:n], [Wc[:, 0, k, :] for k in range(8)], xr),
                          reads=xk + [wk], writes=[("bank", b)])
                    P.add("act", I("activation", out=bgs[:, c0:c0 + n], in_=bank(b)[:, 0:n], func=AF.Copy),
                          reads=[("bank", b)], writes=[("bgs", c0)])
                b = next_ps()
                P.add("pe", mm(bank(b)[:, 0:n], [Wc[:, 1, k, :] for k in range(8)], xr),
                      reads=xk + [wk], writes=[("bank", b)])
                P.add("act", I("activation", out=cgs[:, c0:c0 + n], in_=bank(b)[:, 0:n], func=AF.Copy),
                      reads=[("bank", b)], writes=[("cgs", c0)])
                b = next_ps()
                P.add("pe", mm(bank(b)[:, 0:n], [Wc[:, 2, k, :] for k in range(8)], xr),
                      reads=xk + [wk], writes=[("bank", b)])
                if n > 2:
                    P.add("dve", I("tensor_tensor", out=tb[:, 1 + c0:1 + c0 + n], in0=bank(b)[:, 0:n], in1=cgs[:, c0:c0 + n],
                                   op=ALU.mult), reads=[("bank", b), ("cgs", c0)], writes=[("tb", c0)])
                else:
                    P.add("dve", I("tensor_tensor", out=cgs[:, RNG:RNG + 2], in0=bank(b)[:, 0:2], in1=cgs[:, RNG:RNG + 2],
                                   op=ALU.mult), reads=[("bank", b), ("cgs", RNG)], writes=[("cgs2", RNG)])
                    P.add("dve", I("tensor_tensor", out=tb[:, 0:1], in0=cgs[:, RNG:RNG + 1], in1=maskB[:, m0:m0 + 1],
                                   op=ALU.mult), reads=[("cgs2", RNG)], writes=[("tb", "l")])
                    P.add("dve", I("tensor_tensor", out=tb[:, RNG + 1:RNG + 2], in0=cgs[:, RNG + 1:RNG + 2],
                                   in1=maskB[:, m0 + 1:m0 + 2], op=ALU.mult), reads=[("cgs2", RNG)], writes=[("tb", "r")])
            tkeys = [("tb", 0), ("tb", 512), ("tb", "l"), ("tb", "r")]
            P.add("dve", I("tensor_scalar", out=yb, in0=tb[:, 1:RNG + 1], scalar1=cw[:, 1, c:c + 1], scalar2=None,
                           op0=ALU.mult), reads=tkeys, writes=["yb"])
            P.add("dve", I("scalar_tensor_tensor", out=yb, in0=tb[:, 0:RNG], scalar=cw[:, 0, c:c + 1], in1=yb,
                           op0=ALU.mult, op1=ALU.add), reads=tkeys + ["yb"], writes=["yb"])
            P.add("dve", I("scalar_tensor_tensor", out=yb, in0=tb[:, 2:RNG + 2], scalar=cw[:, 2, c:c + 1], in1=yb,
                           op0=ALU.mult, op1=ALU.add), reads=tkeys + ["yb"], writes=["yb"])
            P.add("dve", I("tensor_tensor", out=mTb[:, c, 0:RNG], in0=bgs, in1=yb, op=ALU.mult),
                  reads=["yb", ("bgs", 0), ("bgs", 512)],
                  writes=[("mT", c), ("hTw", c // 4, c % 4, 0), ("hTw", c // 4, c % 4, 512)])
        wco = []
        for half in range(2):
            sl, k = ring.load([(v4, cwout[:, half * 4:(half + 1) * 4, :])])
            wco.append((v4(sl), k))
        mkeys = [("mT", c) for c in range(8)] + [("hTw", hs_, f_, c0_) for hs_ in range(2) for f_ in range(4) for c0_ in (0, 512)]
        own_tiles = tiles[:8]
        for (t, r) in own_tiles:
            for half in range(2):
                b = next_ps()
                P.add("pe", mm(bank(b), [mTb[:, c, t * 128:(t + 1) * 128] for c in range(8)],
                               [wco[c // 4][0][:, c % 4, half * 512:(half + 1) * 512] for c in range(8)]),
                      reads=mkeys + [wco[0][1], wco[1][1]], writes=[("bank", b)])
                resid_add(t, r, half, b)

        ffn(1, own_tiles, cblocks[:2])

        ystage = [bgs, yb]
        yskeys = [[("bgs", 0), ("bgs", 512)], ["yb"]]
        for half in range(2):
            for t in range(half * 4, half * 4 + 4):
                P.add("act", I("activation", out=junk[:, :], in_=h[:, t, :], func=AF.Square,
                               accum_out=statC[:, 0, t:t + 1]), reads=[hkey(t)], writes=[("sC", 0, t)])
            pool_rstd(statC[:, 0, half * 4:half * 4 + 4], statC[:, 2, half * 4:half * 4 + 4], 128, 4, 3,
                      [("sC", 0, t) for t in range(half * 4, half * 4 + 4)], ("sCf", half))
        for (t, r) in own_tiles:
            rs = statC[:, 2, t:t + 1]
            rsk = ("sCf", t // 4)
            ys = t % 2
            P.add("dve", I("scalar_tensor_tensor", out=ystage[ys], in0=h[:, t, :], scalar=rs, in1=fgB,
                           op0=ALU.mult, op1=ALU.mult), reads=[hkey(t), rsk], writes=yskeys[ys])
            P.add("sp", I("dma_start", out=yout[sq_name][roff + t * 128:roff + (t + 1) * 128, :], in_=ystage[ys]),
                  reads=yskeys[ys], writes=[("yout", sq_name, ri, t)], dma=("yst", ys))

    import os
    dbg = os.environ.get("KDBG", "")
    for sq_name in ("s", "p"):
        if dbg and sq_name not in dbg:
            continue
        if not dbg or "A" in dbg:
            phase_AB(sq_name)
            P.barrier()
        if not dbg or "C" in dbg:
            phase_C(sq_name, 0)
            P.barrier()
        if not dbg or "D" in dbg:
            phase_C(sq_name, 1)
            P.barrier()
    for e in Prog.ENGS:
        P.add(e, None)
    stuck, per_, pos_ = P.simulate()
    if stuck:
        for e, (p_, n_) in stuck.items():
            op = per_[e][p_]
            print("DEADLOCK", e, p_, n_, "op idx", op.idx, "waits", [(d.eng, d.idx, d.dma, d.tick, d.signal) for d in op.waits])
        raise RuntimeError("semaphore protocol deadlock")
    print("program ops:", len(P.ops), {e: len(v) for e, v in per_.items()})
    if os.environ.get("KDUMP"):
        for op in P.ops:
            print(op.idx, op.eng, op.name, "tick", op.tick if (op.signal or op.dma) else None,
                  "W:", [(d.eng, d.idx, d.tick) for d in op.waits], "w=", op.rw[1][:3])
    P.emit(nc, stack)
    stack.close()
    return nc


_NC_CACHE = {}


def _host_constants():
    ident = np.eye(128, dtype=np.float32)
    rt = np.zeros((128, 128), np.float32)
    invf = np.zeros(128, np.float32)
    sgn = np.zeros(128, np.float32)
    inv = (500000.0 ** (-np.arange(0, 16, 2, dtype=np.float32) / 16.0)).astype(np.float32)
    for base in (0, 64):
        for d in range(16):
            p = base + d
            partner = p + 8 if d < 8 else p - 8
            rt[partner, p] = 1.0
            invf[p] = inv[d % 8] / np.float32(2 * np.pi)
            sgn[p] = -1.0 if d < 8 else 1.0
    rc = np.stack([invf, (-2.0 * np.pi * sgn).astype(np.float32)], axis=1).astype(np.float32)
    return ident, rt, rc


def kernel(**inputs):
    f32 = lambda a: np.ascontiguousarray(np.asarray(a, dtype=np.float32))
    xpr = f32(inputs["x_prompt"])
    xsa = f32(inputs["x_sample"])
    ident, rt, rc = _host_constants()
    gains = np.stack([f32(inputs["norm_mix_g"])[0], f32(inputs["norm_ffn_g"])[0],
                      f32(inputs["norm_mix_g"])[1], f32(inputs["norm_ffn_g"])[1]], axis=0)
    gcols = np.ascontiguousarray(gains.reshape(4, 8, 128).transpose(2, 0, 1).reshape(128, 32))
    cwh = np.ascontiguousarray(f32(inputs["c_conv_w"])[0].reshape(3, 8, 128).transpose(2, 0, 1).reshape(128, 24))
    lvec = np.concatenate([f32(inputs["b_lq1"])[0], f32(inputs["b_lk1"])[0],
                           f32(inputs["b_lq2"])[0], f32(inputs["b_lk2"])[0]])[None, :]
    shared = {
        "e_w_in": f32(inputs["e_w_in"])[0], "e_w_out": f32(inputs["e_w_out"])[0],
        "ffn_w1_0": f32(inputs["ffn_w1"])[0], "ffn_w1_1": f32(inputs["ffn_w1"])[1],
        "ffn_w2_0": f32(inputs["ffn_w2"])[0], "ffn_w2_1": f32(inputs["ffn_w2"])[1],
        "c_w_in": f32(inputs["c_w_in"])[0], "c_w_out": f32(inputs["c_w_out"])[0],
        "wsT": np.ascontiguousarray(f32(inputs["a_w_s"])[0].transpose(0, 2, 1)),
        "ident": ident, "rt": rt, "gcols": gcols, "cw": cwh, "rc": rc, "lvec": np.ascontiguousarray(lvec),
        "vng": f32(inputs["a_vnorm_g"]).reshape(1, 512), "bs": f32(inputs["a_b_s"]).reshape(1, 512),
        "subln": f32(inputs["b_subln_g"]).reshape(1, 128), "fg": f32(inputs["final_g"]).reshape(1, D),
    }
    in_maps = []
    meta = []
    for c in range(NCORES):
        bp, op_ = c // 2, (c % 2) * OWN
        bs_, os_ = c // 4, (c % 4) * OWN
        m = dict(shared)
        masks = np.zeros((1, 8), np.float32)
        for key, x, b, off, S, mi in (("s", xsa, bs_, os_, SEQ_S, 0), ("p", xpr, bp, op_, SEQ_P, 4)):
            xr = np.ascontiguousarray(np.roll(x[b], -off, axis=0))
            tidx = np.array([S - 1, RNG, RNG - 1, OWN])
            pos = ((np.arange(S) + off) % S).astype(np.float32)
            post = ((tidx + off) % S).astype(np.float32)
            m["x" + key] = xr
            m["xt" + key] = np.ascontiguousarray(xr[tidx])
            m["pos" + key] = np.concatenate([pos, post])[None, :].astype(np.float32)
            masks[0, mi + 0] = 1.0 if off > 0 else 0.0
            masks[0, mi + 1] = 1.0
            masks[0, mi + 2] = 1.0
            masks[0, mi + 3] = 1.0 if off + OWN < S else 0.0
        m["masks"] = masks
        in_maps.append(m)
        meta.append((bp, op_, bs_, os_))
    if "nc" not in _NC_CACHE:
        _NC_CACHE["nc"] = build_program()
    import os as _os
    ncr = int(_os.environ.get("KCORES", NCORES))
    if _os.environ.get("KTRACE"):
        res = run_bass_kernel_spmd(_NC_CACHE["nc"], in_maps[:ncr], core_ids=list(range(ncr)), trace=True)
        print("KTRACE exec_time_ns", res.exec_time_ns)
    else:
        res = run_bass_kernel_spmd(_NC_CACHE["nc"], in_maps[:ncr], core_ids=list(range(ncr)))
    y_p = np.zeros_like(xpr)
    y_s = np.zeros_like(xsa)
    for c, (bp, op_, bs_, os_) in enumerate(meta[:ncr]):
        y_p[bp, op_:op_ + OWN] = res.results[c]["yp"]
        y_s[bs_, os_:os_ + OWN] = res.results[c]["ys"]
    return (y_p, y_s)
```
